# Optimizing a Trainium2 kernel written in Bass

```python
import jax, jax.numpy as jnp
from jax import lax
import numpy as np

D_MODEL = 1024
BATCH = 4
SEQ = 8192
DEPTH = 1
DEC_BATCH = 128
DEC_SEQ = 1
PAST_LEN = 16384
PAGE_SIZE = 128

N_Q_HEADS = 8
N_KV_HEADS = 2
GROUP = N_Q_HEADS // N_KV_HEADS
HEAD_DIM = 64
WINDOW = 128
ATTN_Q_WIDTH = N_Q_HEADS * HEAD_DIM
KV_WIDTH = N_KV_HEADS * HEAD_DIM
N_GDN_HEADS = 4
GDN_DK = 128
GDN_DV = 128
GDN_QK_WIDTH = N_GDN_HEADS * GDN_DK
GDN_V_WIDTH = N_GDN_HEADS * GDN_DV
GDN_CONV_DIM = 2 * GDN_QK_WIDTH + GDN_V_WIDTH
CONV_WIDTH = 4
CHUNK = 64
MIX_WIDTH = ATTN_Q_WIDTH + GDN_V_WIDTH
IN_WIDTH = ATTN_Q_WIDTH + 2 * KV_WIDTH + 2 * GDN_QK_WIDTH + 2 * GDN_V_WIDTH + 2 * N_GDN_HEADS
D_FF = 2816
EPS = 1e-6

kernel_name = "hymba_swa_sink_alibi_gdn_macaron"


def _split_points():
    sizes = (ATTN_Q_WIDTH, KV_WIDTH, KV_WIDTH, GDN_QK_WIDTH, GDN_QK_WIDTH,
             GDN_V_WIDTH, GDN_V_WIDTH, N_GDN_HEADS, N_GDN_HEADS)
    pts, acc = [], 0
    for s in sizes[:-1]:
        acc += s
        pts.append(acc)
    return pts


def rmsnorm(x, g):
    xf = x.astype(jnp.float32)
    y = xf * lax.rsqrt(jnp.mean(xf * xf, axis=-1, keepdims=True) + EPS)
    return (y * g.astype(jnp.float32)).astype(x.dtype)


def l2norm(x):
    return x * lax.rsqrt(jnp.sum(x * x, axis=-1, keepdims=True) + EPS)


def swiglu(x, w_in, w_out):
    gate, up = jnp.split(x @ w_in, 2, axis=-1)
    return (jax.nn.silu(gate) * up) @ w_out


def alibi_slopes():
    h = jnp.arange(1, N_Q_HEADS + 1, dtype=jnp.float32)
    return jnp.exp2(-8.0 * h / N_Q_HEADS).reshape(N_KV_HEADS, GROUP)


def band_attention(q, k, v, dist, valid, sinks, slopes):
    s = jnp.einsum("...qhgd,...khd->...hgqk", q, k).astype(jnp.float32) * (HEAD_DIM ** -0.5)
    s = s - slopes[:, :, None, None] * dist[..., None, None, :, :].astype(jnp.float32)
    s = jnp.where(valid[..., None, None, :, :], s, -jnp.inf)
    sink = jnp.broadcast_to(sinks.astype(jnp.float32)[:, :, None, None], s.shape[:-1] + (1,))
    p = jax.nn.softmax(jnp.concatenate([s, sink], axis=-1), axis=-1)[..., :-1]
    return jnp.einsum("...hgqk,...khd->...qhgd", p.astype(v.dtype), v)


def swa_prompt(q, k, v, sinks, slopes):
    B, T, KV, G, HD = q.shape
    nb = T // WINDOW
    qb = q.reshape(B, nb, WINDOW, KV, G, HD)
    pad = jnp.zeros((B, WINDOW, KV, HD), k.dtype)

    def band(t):
        tp = jnp.concatenate([pad, t], axis=1).reshape(B, nb + 1, WINDOW, KV, HD)
        return jnp.concatenate([tp[:, :-1], tp[:, 1:]], axis=2)

    r = jnp.arange(WINDOW)[:, None]
    c = jnp.arange(2 * WINDOW)[None, :]
    dist = r - c + WINDOW
    blk = jnp.arange(nb)[:, None, None]
    valid = (dist >= 0) & (dist <= WINDOW) & (blk * WINDOW + c - WINDOW >= 0)
    o = band_attention(qb, band(k), band(v), dist, valid, sinks, slopes)
    return o.reshape(B, T, KV * G * HD)


def swa_sample(q, k_new, v_new, k_hist, v_hist, sinks, slopes):
    DB, T, KV, G, HD = q.shape
    wb = k_hist.shape[1]
    kk = jnp.concatenate([k_hist, k_new], axis=1)
    vv = jnp.concatenate([v_hist, v_new], axis=1)
    dist = jnp.arange(T)[:, None] - (jnp.arange(wb + T)[None, :] - wb)
    valid = (dist >= 0) & (dist <= WINDOW)
    o = band_attention(q, kk, vv, dist, valid, sinks, slopes)
    return o.reshape(DB, T, KV * G * HD), kk[:, -wb:], vv[:, -wb:]


def gdn_chunked(q, k, v, g, beta, S0):
    B, T, H, _ = q.shape
    dv = v.shape[-1]
    n = T // CHUNK

    def blk(t):
        return jnp.moveaxis(t.reshape((B, n, CHUNK, H) + t.shape[3:]), 3, 1)

    q, k, v, g, beta = blk(q), blk(k), blk(v), blk(g), blk(beta)
    gc = jnp.cumsum(g, axis=-1)
    idx = jnp.arange(CHUNK)
    tril = idx[:, None] >= idx[None, :]
    strict = idx[:, None] > idx[None, :]
    decay = jnp.exp(jnp.where(tril, gc[..., :, None] - gc[..., None, :], -jnp.inf))
    kk = jnp.einsum("bhnid,bhnjd->bhnij", k, k)
    A = jnp.where(strict, beta[..., None] * kk * decay, 0.0)
    eye = jnp.eye(CHUNK, dtype=A.dtype)
    Tm = lax.linalg.triangular_solve(eye + A, jnp.broadcast_to(eye, A.shape),
                                     left_side=True, lower=True, unit_diagonal=True)
    u = Tm @ (v * beta[..., None])
    w = Tm @ (k * (beta * jnp.exp(gc))[..., None])
    qk = jnp.where(tril, jnp.einsum("bhnid,bhnjd->bhnij", q, k) * decay, 0.0)
    q_dec = q * jnp.exp(gc)[..., None]
    k_dec = k * jnp.exp(gc[..., -1:] - gc)[..., None]
    g_last = jnp.exp(gc[..., -1])

    def step(S, xs):
        u_c, w_c, qk_c, qd_c, kd_c, gl_c = xs
        v_new = u_c - w_c @ S
        o = qd_c @ S + qk_c @ v_new
        S = S * gl_c[..., None, None] + jnp.swapaxes(kd_c, -1, -2) @ v_new
        return S, o

    xs = tuple(jnp.moveaxis(t, 2, 0) for t in (u, w, qk, q_dec, k_dec, g_last))
    S, o = lax.scan(step, S0, xs)
    o = jnp.moveaxis(jnp.moveaxis(o, 0, 2), 1, 3).reshape(B, T, H, dv)
    return o, S


def gdn_recurrent(q, k, v, g, beta, S0):
    def step(S, xs):
        q_t, k_t, v_t, g_t, b_t = xs
        S = S * jnp.exp(g_t)[..., None, None]
        kS = jnp.einsum("bhk,bhkv->bhv", k_t, S)
        S = S + jnp.einsum("bhk,bhv->bhkv", k_t, b_t[..., None] * (v_t - kS))
        return S, jnp.einsum("bhk,bhkv->bhv", q_t, S)

    xs = tuple(jnp.moveaxis(t, 1, 0) for t in (q, k, v, g, beta))
    S, o = lax.scan(step, S0, xs)
    return jnp.moveaxis(o, 0, 1), S


def token_mix(n, k_hist, v_hist, conv_hist, S0, w_in_mix, attn_sinks, conv_w,
              gdn_A_log, gdn_dt_bias, gdn_norm_g, w_out_mix):
    B, T, _ = n.shape
    aq, ak, av, gq, gk, gv, gz, gb, ga = jnp.split(n @ w_in_mix, _split_points(), axis=-1)
    slopes = alibi_slopes()
    sinks = attn_sinks.reshape(N_KV_HEADS, GROUP)
    q = aq.reshape(B, T, N_KV_HEADS, GROUP, HEAD_DIM)
    k = ak.reshape(B, T, N_KV_HEADS, HEAD_DIM)
    v = av.reshape(B, T, N_KV_HEADS, HEAD_DIM)
    if k_hist is None:
        attn_out = swa_prompt(q, k, v, sinks, slopes)
        wb = min(WINDOW, T)
        new_k, new_v = k[:, -wb:], v[:, -wb:]
        hist = jnp.zeros((B, CONV_WIDTH - 1, GDN_CONV_DIM), n.dtype)
    else:
        attn_out, new_k, new_v = swa_sample(q, k, v, k_hist, v_hist, sinks, slopes)
        hist = conv_hist
    xh = jnp.concatenate([hist, jnp.concatenate([gq, gk, gv], axis=-1)], axis=1)
    conv = conv_w[0] * xh[:, 0:T]
    for j in range(1, CONV_WIDTH):
        conv = conv + conv_w[j] * xh[:, j:j + T]
    new_conv = xh[:, -(CONV_WIDTH - 1):]
    act = jax.nn.silu(conv).astype(jnp.float32)
    cq, ck, cv = jnp.split(act, [GDN_QK_WIDTH, 2 * GDN_QK_WIDTH], axis=-1)
    dq = l2norm(cq.reshape(B, T, N_GDN_HEADS, GDN_DK)) * (GDN_DK ** -0.5)
    dk = l2norm(ck.reshape(B, T, N_GDN_HEADS, GDN_DK))
    dvv = cv.reshape(B, T, N_GDN_HEADS, GDN_DV)
    beta = jax.nn.sigmoid(gb.astype(jnp.float32))
    g = -jnp.exp(gdn_A_log.astype(jnp.float32)) * jax.nn.softplus(
        ga.astype(jnp.float32) + gdn_dt_bias.astype(jnp.float32))
    if S0 is None:
        o, new_S = gdn_chunked(dq, dk, dvv, g, beta,
                               jnp.zeros((B, N_GDN_HEADS, GDN_DK, GDN_DV), jnp.float32))
    else:
        o, new_S = gdn_recurrent(dq, dk, dvv, g, beta, S0.astype(jnp.float32))
    o = rmsnorm(o, gdn_norm_g) * jax.nn.silu(gz.astype(jnp.float32).reshape(B, T, N_GDN_HEADS, GDN_DV))
    gdn_out = o.reshape(B, T, GDN_V_WIDTH).astype(n.dtype)
    mix = jnp.concatenate([attn_out, gdn_out], axis=-1) @ w_out_mix
    return mix, new_k, new_v, new_conv, new_S.astype(n.dtype)


def decoder_layer(x, k_hist, v_hist, conv_hist, S0, ffn1_norm_g, ffn1_w_in, ffn1_w_out,
                  mix_norm_g, w_in_mix, attn_sinks, conv_w, gdn_A_log, gdn_dt_bias,
                  gdn_norm_g, w_out_mix, ffn2_norm_g, ffn2_w_in, ffn2_w_out):
    x = x + 0.5 * swiglu(rmsnorm(x, ffn1_norm_g), ffn1_w_in, ffn1_w_out)
    mix, new_k, new_v, new_conv, new_S = token_mix(
        rmsnorm(x, mix_norm_g), k_hist, v_hist, conv_hist, S0, w_in_mix, attn_sinks, conv_w,
        gdn_A_log, gdn_dt_bias, gdn_norm_g, w_out_mix)
    x = x + mix
    x = x + 0.5 * swiglu(rmsnorm(x, ffn2_norm_g), ffn2_w_in, ffn2_w_out)
    return x, new_k, new_v, new_conv, new_S


def setup_inputs(seed: int = 0) -> dict:
    key = jax.random.key(seed)
    ks = jax.random.split(key, 24)
    f32 = jnp.float32
    wb = min(WINDOW, PAST_LEN)
    nrm = lambda k, shape, scale: jax.random.normal(k, shape, f32) * scale
    gain = lambda k, shape: 1.0 + 0.02 * jax.random.normal(k, shape, f32)
    dt = jnp.exp(jax.random.uniform(ks[0], (DEPTH, N_GDN_HEADS), f32, np.log(1e-3), np.log(1e-1)))
    return {
        "x_prompt": nrm(ks[1], (BATCH, SEQ, D_MODEL), 1.0),
        "x_sample": nrm(ks[2], (DEC_BATCH, DEC_SEQ, D_MODEL), 1.0),
        "cache_attn_k": nrm(ks[3], (DEPTH, DEC_BATCH, wb, N_KV_HEADS, HEAD_DIM), 1.0),
        "cache_attn_v": nrm(ks[4], (DEPTH, DEC_BATCH, wb, N_KV_HEADS, HEAD_DIM), 1.0),
        "state_conv": nrm(ks[5], (DEPTH, DEC_BATCH, CONV_WIDTH - 1, GDN_CONV_DIM), 1.0),
        "state_gdn": nrm(ks[6], (DEPTH, DEC_BATCH, N_GDN_HEADS, GDN_DK, GDN_DV), GDN_DK ** -0.5),
        "ffn1_norm_g": gain(ks[7], (DEPTH, D_MODEL)),
        "ffn1_w_in": nrm(ks[8], (DEPTH, D_MODEL, 2 * D_FF), D_MODEL ** -0.5),
        "ffn1_w_out": nrm(ks[9], (DEPTH, D_FF, D_MODEL), D_FF ** -0.5),
        "mix_norm_g": gain(ks[10], (DEPTH, D_MODEL)),
        "w_in_mix": nrm(ks[11], (DEPTH, D_MODEL, IN_WIDTH), D_MODEL ** -0.5),
        "attn_sinks": nrm(ks[12], (DEPTH, N_Q_HEADS), 0.5),
        "conv_w": nrm(ks[13], (DEPTH, CONV_WIDTH, GDN_CONV_DIM), CONV_WIDTH ** -0.5),
        "gdn_A_log": jnp.log(jax.random.uniform(ks[14], (DEPTH, N_GDN_HEADS), f32, 1.0, 16.0)),
        "gdn_dt_bias": dt + jnp.log(-jnp.expm1(-dt)),
        "gdn_norm_g": gain(ks[15], (DEPTH, GDN_DV)),
        "w_out_mix": nrm(ks[16], (DEPTH, MIX_WIDTH, D_MODEL), MIX_WIDTH ** -0.5),
        "ffn2_norm_g": gain(ks[17], (DEPTH, D_MODEL)),
        "ffn2_w_in": nrm(ks[18], (DEPTH, D_MODEL, 2 * D_FF), D_MODEL ** -0.5),
        "ffn2_w_out": nrm(ks[19], (DEPTH, D_FF, D_MODEL), D_FF ** -0.5),
        "final_norm_g": gain(ks[20], (D_MODEL,)),
    }


def reference(x_prompt, x_sample, cache_attn_k, cache_attn_v, state_conv, state_gdn,
              ffn1_norm_g, ffn1_w_in, ffn1_w_out, mix_norm_g, w_in_mix, attn_sinks, conv_w,
              gdn_A_log, gdn_dt_bias, gdn_norm_g, w_out_mix, ffn2_norm_g, ffn2_w_in, ffn2_w_out,
              final_norm_g):
    xp, xs = x_prompt, x_sample
    kp_l, vp_l, cp_l, sp_l, ks_l, vs_l, cs_l, ss_l = [], [], [], [], [], [], [], []
    for l in range(DEPTH):
        lw = (ffn1_norm_g[l], ffn1_w_in[l], ffn1_w_out[l], mix_norm_g[l], w_in_mix[l],
              attn_sinks[l], conv_w[l], gdn_A_log[l], gdn_dt_bias[l], gdn_norm_g[l],
              w_out_mix[l], ffn2_norm_g[l], ffn2_w_in[l], ffn2_w_out[l])
        xp, kp, vp, cp, sp = decoder_layer(xp, None, None, None, None, *lw)
        xs, kss, vss, css, sss = decoder_layer(xs, cache_attn_k[l], cache_attn_v[l],
                                               state_conv[l], state_gdn[l], *lw)
        kp_l.append(kp); vp_l.append(vp); cp_l.append(cp); sp_l.append(sp)
        ks_l.append(kss); vs_l.append(vss); cs_l.append(css); ss_l.append(sss)
    y_prompt = rmsnorm(xp, final_norm_g)
    y_sample = rmsnorm(xs, final_norm_g)
    new_k_prompt = jnp.stack(kp_l)
    new_v_prompt = jnp.stack(vp_l)
    new_conv_prompt = jnp.stack(cp_l)
    new_gdn_prompt = jnp.stack(sp_l)
    new_k_sample = jnp.stack(ks_l)
    new_v_sample = jnp.stack(vs_l)
    new_conv_sample = jnp.stack(cs_l)
    new_gdn_sample = jnp.stack(ss_l)
    return (y_prompt, y_sample, new_k_prompt, new_v_prompt, new_conv_prompt, new_gdn_prompt,
            new_k_sample, new_v_sample, new_conv_sample, new_gdn_sample)
```

```python
import os
import numpy as np
from contextlib import ExitStack
KLIMS = float(os.environ.get('KLIMS', '99'))
KLIMP = float(os.environ.get('KLIMP', '99'))
import concourse.bass as bass
import concourse.mybir as mybir
from concourse.bass_utils import run_bass_kernel_spmd

F32 = mybir.dt.float32
BF16 = mybir.dt.bfloat16
ALU = mybir.AluOpType
AF = mybir.ActivationFunctionType
AX = mybir.AxisListType

D = 1024
FF = 2816
NJ = 22
EPS = 1e-6
TT = 512
NBLK = 4
NS = 16
INW = 2824


class Prog:
    ENG = ("pe", "act", "dve", "pool", "sp")

    def __init__(self, nc):
        self.nc = nc
        self.ops = []
        self.lastw = {}
        self.readers = {}
        self.chan_cnt = {}

    def op(self, eng, fn, r=(), w=(), chan=None, nd=1):
        i = len(self.ops)
        deps = set()
        for k in r:
            if k in self.lastw:
                deps.add(self.lastw[k])
        for k in w:
            if k in self.lastw:
                deps.add(self.lastw[k])
            deps.update(self.readers.get(k, ()))
        for k in r:
            self.readers.setdefault(k, []).append(i)
        for k in w:
            self.lastw[k] = i
            self.readers[k] = []
        o = dict(eng=eng, fn=fn, deps=deps, chan=chan, nd=nd, sig=False, val=0)
        if chan is not None:
            self.chan_cnt[chan] = self.chan_cnt.get(chan, 0) + 16 * nd
            o["val"] = self.chan_cnt[chan]
        self.ops.append(o)
        return i

    def allkeys(self, pred):
        ks = set(self.lastw) | set(self.readers)
        return [k for k in ks if pred(k)]

    def run(self, sems):
        nc = self.nc
        ops = self.ops
        for o in ops:
            for d in o["deps"]:
                ops[d]["sig"] = True
        cnt = {e: 0 for e in self.ENG}
        for o in ops:
            if o["chan"] is None and o["sig"]:
                cnt[o["eng"]] += 1
                o["val"] = cnt[o["eng"]]
        streams = {e: [] for e in self.ENG}
        for i, o in enumerate(ops):
            streams[o["eng"]].append(i)

        def runner(ename):
            def f(e):
                waited = {}
                for i in streams[ename]:
                    o = ops[i]
                    for d in sorted(o["deps"]):
                        do = ops[d]
                        if do["chan"] is not None:
                            key = ("c", do["chan"])
                        else:
                            if do["eng"] == ename and ename in ("pe", "sp"):
                                continue
                            key = ("e", do["eng"])
                        if waited.get(key, 0) >= do["val"]:
                            continue
                        waited[key] = do["val"]
                        e.wait_ge(sems[key], do["val"])
                    res = o["fn"](e)
                    if o["chan"] is not None:
                        lst = res if isinstance(res, (list, tuple)) else [res]
                        assert len(lst) == o["nd"], (len(lst), o["nd"])
                        for ins in lst:
                            ins.then_inc(sems[("c", o["chan"])], 16)
                    elif o["sig"]:
                        res.then_inc(sems[("e", ename)], 1)
                if ename == "sp":
                    for ch, v in self.chan_cnt.items():
                        if waited.get(("c", ch), 0) < v:
                            e.wait_ge(sems[("c", ch)], v)
            return f

        with nc.Block() as block:
            block.tensor(runner("pe"))
            block.scalar(runner("act"))
            block.vector(runner("dve"))
            block.gpsimd(runner("pool"))
            block.sync(runner("sp"))


def host_consts():
    i = np.arange(128)
    c = {}
    c["ident"] = np.eye(128, dtype=np.float32)
    c["triu"] = (i[:, None] <= i[None, :]).astype(np.float32)
    c["strl"] = (i[:, None] > i[None, :]).astype(np.float32)
    slopes = np.exp2(-8.0 * np.arange(1, 9, dtype=np.float32) / 8.0).astype(np.float32)
    jj = i[:, None, None].astype(np.float32)
    ii = i[None, None, :].astype(np.float32)
    sl = slopes[None, :, None]
    bc = np.where(ii >= jj, -sl * (ii - jj), -30000.0).astype(np.float32)
    bp = np.where(jj >= ii, -sl * (ii - jj + 128.0), -30000.0).astype(np.float32)
    c["bcur"] = np.ascontiguousarray(bc.reshape(128, 1024))
    c["bprev"] = np.ascontiguousarray(bp.reshape(128, 1024))
    tm = np.zeros((128, 1), np.float32)
    tm[0, 0] = 1.0
    c["tok0"] = tm
    cm = np.zeros((128, 512), np.float32)
    cm[:, 0::128] = 1.0
    c["col0"] = cm
    qm = np.zeros((128, 4), np.float32)
    qm[:64, 0] = 0.125; qm[64:, 1] = 0.125; qm[:64, 2] = 1.0; qm[64:, 3] = 1.0
    c["qm"] = qm
    return c


def build(n_tiles, stage=3, n_pre=0):
    nc = bass.Bass("TRN2", target_bir_lowering=False)
    NTOK = n_tiles * TT

    def din(name, shape):
        return nc.dram_tensor(name, list(shape), F32, kind="ExternalInput").ap()

    def dout(name, shape):
        return nc.dram_tensor(name, list(shape), F32, kind="ExternalOutput").ap()

    xp = din("xp", [NTOK, D])
    xpre = din("xpre", [n_pre * TT, D]) if n_pre > 0 else None
    hasprev = din("hasprev", [128, 1])
    xs = din("xs", [NS, D])
    ck = din("ck", [NS, 128, 128])
    cv = din("cv", [NS, 128, 128])
    sconv = din("sconv", [NS * 3, 1536])
    sgdn = din("sgdn", [NS, 4, 128, 128])
    g1 = din("g1", [D]); wi1 = din("wi1", [D, 2 * FF]); wo1 = din("wo1", [FF, D])
    g2 = din("g2", [D]); wmi = din("wmi", [D, INW]); sinks = din("sinks", [8])
    convw = din("convw", [4, 1536]); alog = din("alog", [4]); dtb = din("dtb", [4])
    gng = din("gng", [128]); wmo = din("wmo", [D, D])
    g3 = din("g3", [D]); wi2 = din("wi2", [D, 2 * FF]); wo2 = din("wo2", [FF, D])
    gfin = din("gfin", [D])
    c_ident = din("c_ident", [128, 128]); c_triu = din("c_triu", [128, 128]); c_strl = din("c_strl", [128, 128])
    c_bcur = din("c_bcur", [128, 1024]); c_bprev = din("c_bprev", [128, 1024])
    c_tok0 = din("c_tok0", [128, 1]); c_col0 = din("c_col0", [128, 512]); c_qm = din("c_qm", [128, 4])

    yp = dout("yp", [NTOK, D]); ys = dout("ys", [NS, D])
    nkp = dout("nkp", [128, 128]); nvp = dout("nvp", [128, 128])
    ncp = dout("ncp", [3, 1536]); ngp = dout("ngp", [4, 128, 128])
    nks = dout("nks", [NS, 128, 128]); nvs = dout("nvs", [NS, 128, 128])
    ncs = dout("ncs", [NS, 3, 1536]); ngs = dout("ngs", [NS, 4, 128, 128])

    wi_s = [nc.dram_tensor(f"wi_s{i}", [NJ, 128, 8, 256], BF16).ap() for i in range(2)]
    wo_s = [nc.dram_tensor(f"wo_s{i}", [128, NJ, D], BF16).ap() for i in range(2)]
    wm_s = nc.dram_tensor("wm_s", [11, 128, 8, 256], BF16).ap()
    wtm_s = nc.dram_tensor("wtm_s", [128, 8, 264], BF16).ap()
    wmo_s = nc.dram_tensor("wmo_s", [128, 8, D], BF16).ap()

    A = nc.alloc_sbuf_tensor
    xt = [A(f"xt{i}", [128, NBLK, D], F32) for i in range(2)]
    xts = xt[1][0:NS, 0:1, :]
    x1f = xt[1][:, 1:4, :].rearrange("p b d -> p (b d)")
    xn2 = [A(f"xn{i}", [128, D], BF16) for i in range(2)]
    xnT = A("xnT", [128, 8, TT], BF16)
    nTs = A("nTs", [128, 8, NS], BF16)
    wi = [A(f"wibuf{i}", [128, 8, 256], BF16) for i in range(3)]
    wmo_t = A("wmo_t", [128, 8, D], BF16)
    wtm = A("wtm", [128, 8, 264], BF16)
    arena = A("arena", [128, 16896], F32)
    hT = arena[:, 0:5632].bitcast(BF16).rearrange("p (j t) -> p j t", j=NJ)
    wo = arena[:, 5632:16896].bitcast(BF16).rearrange("p (j n) -> p j n", j=NJ)
    off = [0]

    def carve(ncols_f32):
        a = off[0]
        off[0] += ncols_f32
        assert off[0] <= 16896
        return arena[:, a:a + ncols_f32]

    pre = carve(12 * 524).rearrange("p (m t) -> p m t", m=12)
    oT = carve(2048).rearrange("p (h t) -> p h t", h=4)
    ncrow = oT[0:4, :, :].rearrange("p h t -> p (h t)")[:, 0:1536]
    ebuf = carve(1024)
    tC = ebuf[:, 0:512]; tD = ebuf[:, 512:1024]
    tA = carve(512); tB = carve(512); tE = carve(512)
    u_t = carve(512)
    rbuf = tB
    gqn = carve(1024).bitcast(BF16).rearrange("p (h t) -> p h t", h=4)
    gkn = carve(1024).bitcast(BF16).rearrange("p (h t) -> p h t", h=4)
    zs = carve(1024).bitcast(BF16).rearrange("p (h t) -> p h t", h=4)
    Pp = carve(512).bitcast(BF16).rearrange("p (h t) -> p h t", h=8)
    qT = A("qT", [128, 4, TT], BF16)
    kTd = [A(f"kTd{i}", [128, 8, 128], BF16) for i in range(2)]
    Vd = [A(f"Vd{i}", [128, 8, 128], BF16) for i in range(2)]
    aoT = A("aoT", [128, 4, TT], BF16)
    goT = A("goT", [128, 4, TT], BF16)
    aoTs = A("aoTs", [128, 4, NS], BF16)
    goTs = A("goTs", [128, 4, NS], BF16)
    Pc = A("Pc", [128, 8, 128], BF16)
    Xb = [A("Xb0", [128, 4, 128], F32), carve(512).rearrange("p (h t) -> p h t", h=4)]
    Yb = [A("Yb0", [128, 4, 128], F32), carve(512).rearrange("p (h t) -> p h t", h=4)]
    Nf = carve(512).rearrange("p (h t) -> p h t", h=4)
    Nb = A("Nb", [128, 4, 128], BF16)
    vb = A("vb", [128, 4, 128], BF16)
    kb = A("kb", [128, 4, 128], BF16)
    kd = A("kd", [128, 4, 128], BF16)
    wT = A("wT", [128, 4, 128], BF16)
    qdT = A("qdT", [128, 4, 128], BF16)
    qkT = A("qkT", [128, 4, 128], BF16)
    vnew = A("vnew", [128, 4, 128], BF16)
    Sbf = A("Sbf", [128, 4, 128], BF16)
    ctmp = A("ctmp", [128, TT], F32)
    junk = ctmp[:, :].bitcast(BF16)
    Sst = A("Sst", [128, 4, 128], F32)
    ccar = A("ccar", [128, 12, 3], F32)
    kvo = A("kvo", [128, 256], F32)
    gbga = A("gbga", [128, NBLK, 8], F32)
    sm = A("sm", [128, 64], F32)
    ss = A("ss", [128, 8], F32)
    sg = [A("sg0", [128, TT], F32)] * 2
    ident_f = A("ident_f", [128, 128], F32)
    ident = A("ident_b", [128, 128], BF16)
    triu = A("triu", [128, 128], F32)
    strl = A("strl", [128, 128], F32)
    ones_f = A("ones_f", [128, 128], F32)
    ones_b = A("ones_b", [128, 128], BF16)
    bcur = A("bcur", [128, 1024], BF16)
    bprev = A("bprev", [128, 1024], BF16)
    tok0 = A("tok0", [128, 1], F32)
    hp_t = A("hp_t", [128, 1], F32)
    qm = A("qm", [128, 4], F32)
    qT2 = A("qT2", [128, 4, TT], BF16)
    col0 = A("col0", [128, 512], F32)
    gT = [A(f"gT{i}", [128, 8], F32) for i in range(3)]
    gfin_b = A("gfin_b", [128, D], F32)
    cw = A("cw", [128, 4, 12], F32)
    esink = A("esink", [128, 8], F32)
    negA = A("negA", [128, 4], F32)
    dtb_b = A("dtb_b", [128, 4], F32)
    gng_t = A("gng_t", [128, 1], F32)
    sct = x1f[0:NS * 3, 0:1536]
    ckt = x1f[:, 1536:1664]
    cvt = x1f[:, 1664:1792]
    ckd = x1f[:, 1792:2048].rearrange("p (a b d) -> p a b d", a=2, b=2)
    ps = nc.alloc_psum_tensor("ps", [128, 8, 512], F32)

    P = Prog(nc)
    MK = lambda *a: ("m_" + a[0],) + tuple(a[1:])
    bank = [0]

    def nb():
        bank[0] = (bank[0] + 1) % 8
        return bank[0]

    def bc(ap, shape, axis):
        return ap.unsqueeze(axis).broadcast_to(shape)

    def cast_ffn(i, w_in, w_out):
        v = w_in.rearrange("(c p) n -> p c n", p=128)
        for j in range(NJ):
            def f(e, j=j):
                return [e.dma_start(out=wi_s[i][j, :, :, g * 128:(g + 1) * 128],
                                    in_=v[:, :, g * FF + j * 128: g * FF + (j + 1) * 128]) for g in range(2)]
            P.op("pool", f, w=[("wi_s", i, j)], chan=f"cast{i}", nd=2)
        P.op("pool", lambda e: e.dma_start(out=wo_s[i], in_=w_out.rearrange("(j p) n -> p j n", p=128)),
             w=[("wo_s", i), ("castgrp", i)], chan=f"cast{i}")

    cast_ffn(0, wi1, wo1)
    wmv = wmi.rearrange("(c p) n -> p c n", p=128)
    groups = [(128 * p, False) for p in range(4)] + [(512, True), (576, True)] + \
             [(768 + 128 * m, False) for m in range(12)] + [(2304 + 128 * h, False) for h in range(4)]
    for s in range(11):
        def f(e, s=s):
            r = []
            for g in range(2):
                c0, dup = groups[2 * s + g]
                if dup:
                    for q in range(2):
                        r.append(e.dma_start(out=wm_s[s, :, :, g * 128 + q * 64: g * 128 + (q + 1) * 64], in_=wmv[:, :, c0:c0 + 64]))
                else:
                    r.append(e.dma_start(out=wm_s[s, :, :, g * 128:(g + 1) * 128], in_=wmv[:, :, c0:c0 + 128]))
            return r
        nd = sum(2 if groups[2 * s + g][1] else 1 for g in range(2))
        P.op("pool", f, w=[("wm_s", s)], chan="castm", nd=nd)

    def f(e):
        return [e.dma_start(out=wtm_s[:, :, 0:128], in_=wmv[:, :, 640:768]),
                e.dma_start(out=wtm_s[:, :, 128:256], in_=wmv[:, :, 512:640]),
                e.dma_start(out=wtm_s[:, :, 256:264], in_=wmv[:, :, 2816:2824])]
    P.op("pool", f, w=[("wtm_s",)], chan="castm", nd=3)
    P.op("pool", lambda e: e.dma_start(out=wmo_s, in_=wmo.rearrange("(c p) n -> p c n", p=128)), w=[("wmo_s",), ("castgrp", "m")], chan="castm")
    cast_ffn(1, wi2, wo2)

    for sq in range(NS):
        def f(e, sq=sq):
            return [e.dma_start(out=nks[sq, 0:127, :], in_=ck[sq, 1:128, :]),
                    e.dma_start(out=nvs[sq, 0:127, :], in_=cv[sq, 1:128, :]),
                    e.dma_start(out=ncs[sq, 0:2, :], in_=sconv[sq * 3 + 1: sq * 3 + 3, :])]
        P.op("pool", f, w=[("o_hist", sq)], chan="ohist", nd=3)

    ldn = [0]

    def ld(dst, src, key, **kw):
        ldn[0] += 1
        P.op("sp", lambda e: e.dma_start(out=dst, in_=src, **kw), w=[key], chan=f"const{ldn[0]}")

    ld(ident_f[:, :], c_ident, ("identf",)); ld(triu[:, :], c_triu, ("triu",)); ld(strl[:, :], c_strl, ("strl",))
    P.op("pool", lambda e: e.dma_start(out=bcur[:, :], in_=c_bcur), w=[("bcur",)], chan="constb1")
    P.op("pool", lambda e: e.dma_start(out=bprev[:, :], in_=c_bprev), w=[("bprev",)], chan="constb2")
    ld(tok0[:, :], c_tok0, ("tok0",)); ld(hp_t[:, :], hasprev, ("hp",)); ld(qm[:, :], c_qm, ("qm",)); ld(col0[:, :], c_col0, ("col0",))
    for i, g in enumerate((g1, g2, g3)):
        ld(gT[i][:, :], g.rearrange("(c p) -> p c", p=128), ("gT", i), allow_slow_non_contiguous=True)
    ld(gfin_b[:, :], gfin.partition_broadcast(128), ("gfin",))
    P.op("sp", lambda e: [e.dma_start(out=cw[:, j, :], in_=convw[j].rearrange("(m p) -> p m", p=128), allow_slow_non_contiguous=True) for j in range(4)], w=[("cw",)], chan="constcw", nd=4)
    ld(esink[:, :], sinks.partition_broadcast(128), ("esink",))
    ld(negA[:, :], alog.partition_broadcast(128), ("negA",))
    ld(dtb_b[:, :], dtb.partition_broadcast(128), ("dtb",))
    ld(gng_t[:, :], gng.rearrange("(p o) -> p o", o=1), ("gng",))
    ld(wtm[:, :, :], wtm_s, ("wtm",))
    P.ops[-1]["deps"].add(P.lastw[("castgrp", "m")])
    ld(wmo_t[:, :, :], wmo_s, ("wmo",))
    P.ops[-1]["deps"].add(P.lastw[("castgrp", "m")])
    P.op("dve", lambda e: e.tensor_copy(ident[:, :], ident_f[:, :]), r=[("identf",)], w=[("ident",)])
    P.op("dve", lambda e: e.memset(ones_f[:, :], 1.0), w=[("ones_f",)])
    P.op("dve", lambda e: e.memset(ones_b[:, :], 1.0), w=[("ones_b",)])
    P.op("act", lambda e: e.activation(out=esink[:, :], in_=esink[:, :], func=AF.Exp), r=[("esink",)], w=[("esink",)])
    P.op("act", lambda e: e.activation(out=negA[:, :], in_=negA[:, :], func=AF.Exp), r=[("negA",)], w=[("negA",)])
    P.op("dve", lambda e: e.tensor_scalar(negA[:, :], negA[:, :], -1.0, 0.0, ALU.mult, ALU.add), r=[("negA",)], w=[("negA",)])

    def rstd_cols(n, np_, scale, keyss):
        P.op("act", lambda e: e.activation(out=ss[0:np_, 0:n], in_=ss[0:np_, 0:n], func=AF.Ln, scale=scale, bias=EPS),
             r=keyss, w=keyss)
        P.op("act", lambda e: e.activation(out=ss[0:np_, 0:n], in_=ss[0:np_, 0:n], func=AF.Exp, scale=-0.5),
             r=keyss, w=keyss)

    def norm_T(xsrc, xkeys, nblk, np_, gi, dstT, dkeys):
        keyss = [("ss",)]
        for b in range(nblk):
            P.op("act", lambda e, b=b: e.activation(out=junk[0:np_, :], in_=xsrc(b), func=AF.Square, accum_out=ss[0:np_, b:b + 1]),
                 r=[xkeys[b]], w=[MK("ctmp")] + keyss)
        rstd_cols(nblk, np_, 1.0 / D, keyss)
        for b in range(nblk):
            xn = xn2[b % 2]
            xnk = ("xn", b % 2)
            P.op("pool", lambda e, b=b, xn=xn: e.tensor_scalar(xn[0:np_, :], xsrc(b), ss[0:np_, b:b + 1], 1.0, ALU.mult, ALU.mult),
                 r=keyss + [xkeys[b]], w=[xnk])
            k = nb()
            pst = ps[:, k, :].bitcast(BF16)
            for c in range(8):
                P.op("pe", lambda e, c=c, pst=pst, xn=xn: e.transpose(pst[:, c * 128:c * 128 + np_], xn[0:np_, c * 128:(c + 1) * 128], ident[0:np_, 0:np_]),
                     r=[xnk, ("ident",)], w=[("ps", k)])
            P.op("dve", lambda e, b=b, pst=pst: e.tensor_tensor(
                dstT[:, :, b * np_:(b + 1) * np_], pst.rearrange("p (c t) -> p c t", c=8)[:, :, 0:np_],
                bc(gT[gi][:, :], [128, 8, np_], 2), ALU.mult),
                r=[("ps", k), ("gT", gi)], w=[dkeys[b]])

    pref = {"on": False}

    def ffn_prefetch(fi):
        for j in range(3):
            P.op("sp", lambda e, j=j: e.dma_start(out=wi[j][:, :, :], in_=wi_s[fi][j]),
                 r=[("castgrp", fi)], w=[("wi", j)], chan=f"wi{j}")
        pref["on"] = True

    def ffn(fi, srcT, skeys, ntok, xdst, xkeys, nblk, np_):
        hkeys = [("hT", j) for j in range(NJ)]
        skip_first = pref["on"]
        pref["on"] = False
        P.op("sp", lambda e: e.dma_start(out=wo, in_=wo_s[fi]), r=[("wo_s", fi), ("castgrp", fi)], w=[("wo",)], chan="wo")
        for j in range(NJ):
            s = j % 3
            if not (skip_first and j < 3):
                P.op("sp", lambda e, j=j, s=s: e.dma_start(out=wi[s][:, :, :], in_=wi_s[fi][j]),
                     r=[("wi_s", fi, j), ("castgrp", fi)], w=[("wi", s)], chan=f"wi{s}")
            kg, ku = nb(), nb()
            for c in range(8):
                P.op("pe", lambda e, c=c, s=s, kg=kg: e.matmul(ps[:, kg, 0:ntok], wi[s][:, c, 0:128], srcT[:, c, 0:ntok], start=(c == 0), stop=(c == 7)),
                     r=[("wi", s)] + skeys, w=[("ps", kg)])
            for c in range(8):
                P.op("pe", lambda e, c=c, s=s, ku=ku: e.matmul(ps[:, ku, 0:ntok], wi[s][:, c, 128:256], srcT[:, c, 0:ntok], start=(c == 0), stop=(c == 7)),
                     r=[("wi", s)] + skeys, w=[("ps", ku)])
            q = 0
            P.op("act", lambda e, kg=kg, q=q: e.activation(out=sg[q][:, 0:ntok], in_=ps[:, kg, 0:ntok], func=AF.Silu),
                 r=[("ps", kg)], w=[("sg", q)])
            P.op("dve", lambda e, ku=ku, q=q, j=j: e.tensor_tensor(hT[:, j, 0:ntok], sg[q][:, 0:ntok], ps[:, ku, 0:ntok], ALU.mult),
                 r=[("ps", ku), ("sg", q)], w=[hkeys[j]])
        for b in range(nblk):
            k0, k1 = nb(), nb()
            for j in range(NJ):
                for dh, kk in ((0, k0), (1, k1)):
                    P.op("pe", lambda e, j=j, dh=dh, kk=kk, b=b: e.matmul(ps[0:np_, kk, :], hT[:, j, b * np_:(b + 1) * np_], wo[:, j, dh * 512:(dh + 1) * 512], start=(j == 0), stop=(j == NJ - 1)),
                         r=[hkeys[j], ("wo",)], w=[("ps", kk)])
            for dh, kk in ((0, k0), (1, k1)):
                P.op("dve", lambda e, dh=dh, kk=kk, b=b: e.scalar_tensor_tensor(xdst(b, dh), ps[0:np_, kk, :], 0.5, xdst(b, dh), ALU.mult, ALU.add),
                     r=[("ps", kk)], w=[xkeys[b]])

    def fence():
        ks = P.allkeys(lambda k: k[0] in ("hT", "wo") or str(k[0]).startswith("m_"))
        P.op("pool", lambda e: e.memset(sm[:, 63:64], 0.0), w=ks + [("fence",)])


    def mix(srcT, skeys, sample, tix, first_tile, last_tile, xres, xkeys, prepass=False, pre_last=False, mask_prev=False):
        bs = 131 if sample else 128

        def kprev(kv, b):
            return kTd[kv][:, 2 * b, :] if sample else kTd[kv][:, b, :]

        def kcur(kv, b):
            return kTd[kv][:, 2 * b + 1, :] if sample else kTd[kv][:, b + 1, :]

        def kix(b):
            return (2 * b, 2 * b + 1) if sample else (b, b + 1)

        KLIM = KLIMS if sample else KLIMP
        if KLIM < 0.5:
            return
        PQ = (lambda *a, **k: None) if prepass else P.op
        PD = (lambda *a, **k: None) if sample else P.op
        if sample:
            for h in range(4):
                P.op("pool", lambda e, h=h: e.tensor_copy(Nb[:, h, :], ident[:, :]), r=[("ident",)], w=[MK("N")])
        slots = range(11) if not prepass else (range(2, 9) if pre_last else range(5, 9))
        for s in slots:
            sl = s % 3
            P.op("sp", lambda e, s=s, sl=sl: e.dma_start(out=wi[sl][:, :, :], in_=wm_s[s]),
                 r=[("wm_s", s), ("castgrp", "m")], w=[("wi", sl)], chan=f"wi{sl}")
            if 3 <= s <= 8 and (sample or last_tile):
                kq = nb()
                lt = srcT[:, :, :].rearrange("p c (b t) -> p c b t", b=4)[:, :, :, 0] if sample else srcT[:, :, 509:512]
                mrows = 4 if sample else 3
                for c in range(8):
                    P.op("pe", lambda e, c=c, sl=sl, kq=kq, lt=lt, mrows=mrows: e.matmul(ps[0:mrows, kq, 0:256], lt[:, c, :], wi[sl][:, c, :], start=(c == 0), stop=(c == 7)),
                         r=[("wi", sl)] + skeys, w=[("ps", kq)])
                P.op("dve", lambda e, kq=kq, s=s, mrows=mrows: e.tensor_copy(ncrow[0:mrows, (s - 3) * 256:(s - 2) * 256], ps[0:mrows, kq, 0:256]), r=[("ps", kq)], w=[MK("ncrow")])
                if s == 8:
                    if sample:
                        P.op("sp", lambda e: e.dma_start(out=ncs[tix * 4:(tix + 1) * 4, 2, :], in_=ncrow[0:4, :]), r=[MK("ncrow")], w=[("o_ncs", tix)], chan="onc")
                    else:
                        P.op("sp", lambda e: e.dma_start(out=ncp, in_=ncrow[0:3, :]), r=[MK("ncrow")], w=[("o_ncp",)], chan="onc")
            for g in range(2):
                gi = 2 * s + g
                k = nb()
                for c in range(8):
                    P.op("pe", lambda e, c=c, sl=sl, g=g, k=k: e.matmul(ps[:, k, :], wi[sl][:, c, g * 128:(g + 1) * 128], srcT[:, c, :], start=(c == 0), stop=(c == 7)),
                         r=[("wi", sl)] + skeys, w=[("ps", k)])
                if gi < 4:
                    P.op("act", lambda e, k=k, gi=gi: e.activation(out=qT[:, gi, :], in_=ps[:, k, :], func=AF.Identity, scale=qm[:, 0:1]),
                         r=[("ps", k), ("qm",)], w=[MK("qT", gi)])
                    P.op("dve", lambda e, k=k, gi=gi: e.tensor_scalar(qT2[:, gi, :], ps[:, k, :], qm[:, 1:2], 0.0, ALU.mult, ALU.add),
                         r=[("ps", k), ("qm",)], w=[MK("qT", gi)])
                elif gi < 6:
                    kv = gi - 4
                    if sample:
                        dst = kTd[kv][:, :, :].rearrange("p (b two) t -> p b two t", two=2)[:, :, 1, :]
                    else:
                        dst = kTd[kv][:, 1:5, :]
                    P.op("act", lambda e, k=k, dst=dst: e.activation(out=dst, in_=ps[:, k, :].rearrange("p (b t) -> p b t", b=4), func=AF.Identity),
                         r=[("ps", k)], w=[MK("kTd", kv, i) for i in ((1, 3, 5, 7) if sample else (1, 2, 3, 4))])
                elif gi < 18:
                    m = gi - 6
                    dst = pre[:, m, 0:4 * bs].rearrange("p (b t) -> p b t", b=4)[:, :, 3:131] if sample else None
                    if sample:
                        P.op("act", lambda e, k=k, dst=dst: e.activation(out=dst, in_=ps[:, k, :].rearrange("p (b t) -> p b t", b=4), func=AF.Identity),
                             r=[("ps", k)], w=[MK("pre", m)])
                    else:
                        P.op("act", lambda e, k=k, m=m: e.activation(out=pre[:, m, 3:515], in_=ps[:, k, :], func=AF.Identity),
                             r=[("ps", k)], w=[MK("pre", m)])
                else:
                    h = gi - 18
                    P.op("act", lambda e, k=k, h=h: e.activation(out=zs[:, h, :], in_=ps[:, k, :], func=AF.Silu),
                         r=[("ps", k)], w=[MK("zs", h)])
        if KLIM < 1:
            return
        for b in range(4):
            k = nb()
            for c in range(8):
                P.op("pe", lambda e, c=c, k=k, b=b: e.matmul(ps[:, k, 0:264], srcT[:, c, b * 128:(b + 1) * 128], wtm[:, c, :], start=(c == 0), stop=(c == 7)),
                     r=[("wtm",)] + skeys, w=[("ps", k)])
            pi, ci = kix(b)
            for kv in range(2):
                P.op("act", lambda e, k=k, kv=kv, ci=ci: e.activation(out=Vd[kv][:, ci, 0:64], in_=ps[:, k, kv * 64:(kv + 1) * 64], func=AF.Identity),
                     r=[("ps", k)], w=[MK("Vd", kv, ci)])
                P.op("dve", lambda e, k=k, kv=kv, ci=ci: e.tensor_copy(Vd[kv][:, ci, 64:128], ps[:, k, kv * 64:(kv + 1) * 64]),
                     r=[("ps", k)], w=[MK("Vd", kv, ci)])
            P.op("dve", lambda e, k=k, b=b: e.tensor_copy(gbga[:, b, :], ps[:, k, 256:264]), r=[("ps", k)], w=[MK("gbga", b)])
            want_kv = sample or (last_tile and b == 3)
            if want_kv:
                P.op("dve", lambda e, k=k: e.tensor_copy(kvo[:, :], ps[:, k, 0:256]), r=[("ps", k)], w=[MK("kvo")])
                if sample:
                    sq = tix * 4 + b
                    def f(e, sq=sq):
                        return [e.dma_start(out=nvs[sq, 127:128, :], in_=kvo[0:1, 0:128]),
                                e.dma_start(out=nks[sq, 127:128, :], in_=kvo[0:1, 128:256])]
                    P.op("sp", f, r=[MK("kvo")], w=[("o_kvs", sq)], chan="okv", nd=2)
                else:
                    def f(e):
                        return [e.dma_start(out=nvp, in_=kvo[:, 0:128]), e.dma_start(out=nkp, in_=kvo[:, 128:256])]
                    P.op("sp", f, r=[MK("kvo")], w=[("o_kvp",)], chan="okv", nd=2)
        if KLIM < 2:
            return
        if sample:
            for b in range(4):
                sq = tix * 4 + b
                P.op("sp", lambda e, sq=sq: [e.dma_start(out=ckt[:, :], in_=ck[sq]), e.dma_start(out=cvt[:, :], in_=cv[sq])],
                     w=[MK("ckt"), MK("cvt")], chan="ckld", nd=2)
                for kv in range(2):
                    k = nb()
                    for q in range(2):
                        P.op("dve", lambda e, kv=kv, q=q: e.tensor_copy(ckd[:, kv, q, :], ckt[:, kv * 64:(kv + 1) * 64]), r=[MK("ckt")], w=[MK("ckd", kv)])
                    P.op("pe", lambda e, k=k, kv=kv: e.transpose(ps[:, k, 0:128], ckd[:, kv, :, :].rearrange("p a d -> p (a d)"), ident_f[:, :]),
                         r=[MK("ckd", kv), ("identf",)], w=[("ps", k)])
                    P.op("act", lambda e, k=k, kv=kv, b=b: e.activation(out=kTd[kv][:, 2 * b, :], in_=ps[:, k, 0:128], func=AF.Identity),
                         r=[("ps", k)], w=[MK("kTd", kv, 2 * b)])
                    for q in range(2):
                        P.op("dve", lambda e, kv=kv, b=b, q=q: e.tensor_copy(Vd[kv][:, 2 * b, q * 64:(q + 1) * 64], cvt[:, kv * 64:(kv + 1) * 64]),
                             r=[MK("cvt")], w=[MK("Vd", kv, 2 * b)])
            if tix == 0:
                P.op("sp", lambda e: e.dma_start(out=sct[:, :], in_=sconv), w=[("sct",)], chan="sctld")
            for m in range(12):
                k = nb()
                P.op("pe", lambda e, k=k, m=m: e.transpose(ps[:, k, 0:NS * 3], sct[:, m * 128:(m + 1) * 128], ident_f[0:NS * 3, 0:NS * 3]),
                     r=[("sct",), ("identf",)], w=[("ps", k)])
                P.op("dve", lambda e, k=k, m=m: e.tensor_copy(
                    pre[:, m, 0:4 * bs].rearrange("p (b t) -> p b t", b=4)[:, :, 0:3],
                    ps[:, k, tix * 12: tix * 12 + 12].rearrange("p (b r) -> p b r", b=4)),
                    r=[("ps", k)], w=[MK("pre", m)])
        elif not first_tile:
            P.op("pool", lambda e: e.tensor_copy(pre[:, :, 0:3], ccar[:, :, :]), r=[("ccar",)], w=[MK("pre", m) for m in range(12)])
        if (not sample) and first_tile:
            for kv in range(2):
                P.op("pool", lambda e, kv=kv: e.memset(kTd[kv][:, 0, :], 0.0), w=[MK("kTd", kv, 0)])
                P.op("pool", lambda e, kv=kv: e.memset(Vd[kv][:, 0, :], 0.0), w=[MK("Vd", kv, 0)])
            P.op("pool", lambda e: e.memset(pre[:, :, 0:3], 0.0), w=[MK("pre", m) for m in range(12)])
            P.op("pool", lambda e: e.memset(ccar[:, :, :], 0.0), w=[("ccar",)])
            P.op("pool", lambda e: e.memset(Sst[:, :, :], 0.0), w=[MK("S")])
            P.op("pool", lambda e: e.memset(Sbf[:, :, :], 0.0), w=[MK("Sbf")])

        if KLIM < 3:
            return
        for b in (range(4) if not prepass else ()):
            pi, ci = kix(b)
            has_prev = sample or not (first_tile and b == 0)
            kc = [nb(), nb()]
            for h in range(8):
                kv, p, hh = h // 4, h // 2, h % 2
                P.op("pe", lambda e, h=h, kv=kv, p=p, hh=hh, ci=ci, b=b, kc=kc: e.matmul(
                    ps[:, kc[h // 4], (h % 4) * 128:(h % 4 + 1) * 128], kTd[kv][:, ci, :],
                    (qT2 if hh else qT)[:, p, b * 128:(b + 1) * 128], start=True, stop=True),
                    r=[MK("kTd", kv, ci), MK("qT", p)], w=[("ps", kc[h // 4])])
            for half in range(2):
                P.op("dve", lambda e, half=half, kc=kc: e.tensor_tensor(ebuf[:, half * 512:(half + 1) * 512], ps[:, kc[half], :], bcur[:, half * 512:(half + 1) * 512], ALU.add),
                     r=[("ps", kc[half]), ("bcur",)], w=[MK("tC" if half == 0 else "tD")])
            P.op("act", lambda e: e.activation(out=Pc[:, :, :].rearrange("p h t -> p (h t)"), in_=ebuf, func=AF.Exp),
                 r=[MK("tC"), MK("tD")], w=[MK("Pc")])
            if has_prev:
                kp = [nb(), nb()]
                for h in range(8):
                    kv, p, hh = h // 4, h // 2, h % 2
                    P.op("pe", lambda e, h=h, kv=kv, p=p, hh=hh, pi=pi, b=b, kp=kp: e.matmul(
                        ps[:, kp[h // 4], (h % 4) * 128:(h % 4 + 1) * 128], kTd[kv][:, pi, :],
                        (qT2 if hh else qT)[:, p, b * 128:(b + 1) * 128], start=True, stop=True),
                        r=[MK("kTd", kv, pi), MK("qT", p)], w=[("ps", kp[h // 4])])
                for half in range(2):
                    P.op("dve", lambda e, half=half, kp=kp: e.tensor_tensor(ebuf[:, half * 512:(half + 1) * 512], ps[:, kp[half], :], bprev[:, half * 512:(half + 1) * 512], ALU.add),
                         r=[("ps", kp[half]), ("bprev",)], w=[MK("tC" if half == 0 else "tD")])
                P.op("act", lambda e: e.activation(out=Pp[:, :, :].rearrange("p h t -> p (h t)"), in_=ebuf, func=AF.Exp),
                     r=[MK("tC"), MK("tD")], w=[MK("Pp")])
                if mask_prev and b == 0:
                    P.op("dve", lambda e: e.tensor_scalar(Pp[:, :, :].rearrange("p h t -> p (h t)"), Pp[:, :, :].rearrange("p h t -> p (h t)"), hp_t[:, 0:1], 0.0, ALU.mult, ALU.add),
                         r=[MK("Pp"), ("hp",)], w=[MK("Pp")])
            for kv in range(2):
                kn, kdn = nb(), nb()
                pcs = Pc[:, kv * 4:(kv + 1) * 4, :].rearrange("p h t -> p (h t)")
                pps = Pp[:, kv * 4:(kv + 1) * 4, :].rearrange("p h t -> p (h t)")
                P.op("pe", lambda e, kn=kn, kv=kv, ci=ci, pcs=pcs, hp=has_prev: e.matmul(ps[:, kn, :], Vd[kv][:, ci, :], pcs, start=True, stop=not hp),
                     r=[MK("Vd", kv, ci), MK("Pc")], w=[("ps", kn)])
                if has_prev:
                    P.op("pe", lambda e, kn=kn, kv=kv, pi=pi, pps=pps: e.matmul(ps[:, kn, :], Vd[kv][:, pi, :], pps, start=False, stop=True),
                         r=[MK("Vd", kv, pi), MK("Pp")], w=[("ps", kn)])
                P.op("pe", lambda e, kdn=kdn, pcs=pcs, hp=has_prev: e.matmul(ps[:, kdn, :], ones_b[:, :], pcs, start=True, stop=not hp),
                     r=[("ones_b",), MK("Pc")], w=[("ps", kdn)])
                if has_prev:
                    P.op("pe", lambda e, kdn=kdn, pps=pps: e.matmul(ps[:, kdn, :], ones_b[:, :], pps, start=False, stop=True),
                         r=[("ones_b",), MK("Pp")], w=[("ps", kdn)])
                P.op("dve", lambda e, kdn=kdn, kv=kv: e.tensor_tensor(rbuf.rearrange("p (h t) -> p h t", h=4), ps[:, kdn, :].rearrange("p (h t) -> p h t", h=4),
                                                                      bc(esink[:, kv * 4:(kv + 1) * 4], [128, 4, 128], 2), ALU.add),
                     r=[("ps", kdn), ("esink",)], w=[MK("tB")])
                P.op("dve", lambda e: e.reciprocal(rbuf, rbuf), r=[MK("tB")], w=[MK("tB")])
                nv = ps[:, kn, :].rearrange("p (m q t) -> p m q t", m=2, q=2)
                rv = rbuf.rearrange("p (m q t) -> p m q t", m=2, q=2)
                te = tA.rearrange("p (m t) -> p m t", m=4)[:, 0:2, :]
                to = tA.rearrange("p (m t) -> p m t", m=4)[:, 2:4, :]
                P.op("dve", lambda e, nv=nv, rv=rv, te=te: e.tensor_tensor(te, nv[:, :, 0, :], rv[:, :, 0, :], ALU.mult), r=[("ps", kn), MK("tB")], w=[MK("tA")])
                P.op("dve", lambda e, nv=nv, rv=rv, to=to: e.tensor_tensor(to, nv[:, :, 1, :], rv[:, :, 1, :], ALU.mult), r=[("ps", kn), MK("tB")], w=[MK("tA")])
                P.op("dve", lambda e, te=te: e.tensor_scalar(te, te, qm[:, 2:3], 0.0, ALU.mult, ALU.add), r=[MK("tA"), ("qm",)], w=[MK("tA")])
                P.op("dve", lambda e, kv=kv, b=b, te=te, to=to: e.scalar_tensor_tensor(aoT[:, kv * 2:(kv + 1) * 2, b * 128:(b + 1) * 128], to, qm[:, 3:4], te, ALU.mult, ALU.add),
                     r=[MK("tA"), ("qm",)], w=[MK("aoT", b)])
        if (not sample) and (not prepass or pre_last):
            for kv in range(2):
                P.op("pool", lambda e, kv=kv: e.tensor_copy(kTd[kv][:, 0, :], kTd[kv][:, 4, :]), r=[MK("kTd", kv, 4)], w=[MK("kTd", kv, 0)])
                P.op("pool", lambda e, kv=kv: e.tensor_copy(Vd[kv][:, 0, :], Vd[kv][:, 4, :]), r=[MK("Vd", kv, 4)], w=[MK("Vd", kv, 0)])

        if KLIM < 4:
            return
        if KLIM < 5:
            return
        def pv(m, j):
            if sample:
                return pre[:, m, 0:4 * bs].rearrange("p (b t) -> p b t", b=4)[:, :, j:j + 128]
            return pre[:, m, j:j + 512].rearrange("p (b t) -> p b t", b=4)
        cbufs = [(ctmp[:, :], MK("ctmp")), (tA, MK("tA")), (tE, MK("tE"))]
        for ci_, m in enumerate(range(12) if (not prepass or pre_last) else range(4, 12)):
            cb, ck_ = cbufs[ci_ % 3]
            ct3 = cb.rearrange("p (b t) -> p b t", b=4)
            P.op("act", lambda e, m=m, ct3=ct3: e.activation(out=ct3, in_=pv(m, 0), func=AF.Identity, scale=cw[:, 0, m:m + 1]), r=[MK("pre", m), ("cw",)], w=[ck_])
            for j in (1, 2, 3):
                P.op("dve", lambda e, m=m, j=j, ct3=ct3: e.scalar_tensor_tensor(ct3, pv(m, j), cw[:, j, m:m + 1], ct3, ALU.mult, ALU.add),
                     r=[MK("pre", m), ("cw",), ck_], w=[ck_])
            if not sample:
                P.op("pool", lambda e, m=m: e.tensor_copy(ccar[:, m, :], pre[:, m, 512:515]), r=[MK("pre", m)], w=[("ccar",)])
            P.op("act", lambda e, m=m, cb=cb: e.activation(out=pre[:, m, 3:515], in_=cb, func=AF.Silu), r=[ck_], w=[MK("pre", m)])
            if sample:
                P.op("pool", lambda e, m=m: e.tensor_tensor(pre[:, m, 3:515], pre[:, m, 3:515], col0[:, :], ALU.mult), r=[MK("pre", m), ("col0",)], w=[MK("pre", m)])
        cact = lambda m: pre[:, m, 3:515]
        if KLIM < 6:
            return
        for li, m in enumerate(range(8) if not prepass else range(4, 8)):
            sqb, sqk = ((ctmp[:, :], MK("ctmp")), (tE, MK("tE")))[li % 2]
            rsb, rsk = ((tA, MK("tA")), (tB, MK("tB")))[li % 2]
            P.op("pool", lambda e, m=m, sqb=sqb: e.tensor_tensor(sqb, cact(m), cact(m), ALU.mult), r=[MK("pre", m)], w=[sqk])
            k = nb()
            P.op("pe", lambda e, k=k, sqb=sqb: e.matmul(ps[:, k, :], ones_f[:, :], sqb, start=True, stop=True), r=[("ones_f",), sqk], w=[("ps", k)])
            P.op("act", lambda e, k=k, rsb=rsb: e.activation(out=rsb, in_=ps[:, k, :], func=AF.Ln, bias=EPS), r=[("ps", k)], w=[rsk])
            P.op("act", lambda e, rsb=rsb: e.activation(out=rsb, in_=rsb, func=AF.Exp, scale=-0.5), r=[rsk], w=[rsk])
            dst = gqn[:, m, :] if m < 4 else gkn[:, m - 4, :]
            sc = 128.0 ** -0.5 if m < 4 else 1.0
            P.op("dve", lambda e, m=m, dst=dst, sc=sc, rsb=rsb: e.scalar_tensor_tensor(dst, cact(m), sc, rsb, ALU.mult, ALU.mult),
                 r=[MK("pre", m), rsk], w=[MK("gqn" if m < 4 else "gkn", m % 4)])
        if KLIM < 7:
            return
        gb_v = gbga[:, :, 0:4]
        ga_v = gbga[:, :, 4:8]
        be = sm[:, 0:16].rearrange("p (b h) -> p b h", b=4)
        gg = sm[:, 16:32].rearrange("p (b h) -> p b h", b=4)
        t1 = sm[:, 32:48].rearrange("p (b h) -> p b h", b=4)
        gkeys = [MK("gbga", b) for b in range(4)]
        P.op("act", lambda e: e.activation(out=be, in_=gb_v, func=AF.Exp, scale=-1.0), r=gkeys, w=[MK("be")])
        P.op("dve", lambda e: e.tensor_scalar(be, be, 1.0, 1.0, ALU.add, ALU.mult), r=[MK("be")], w=[MK("be")])
        P.op("dve", lambda e: e.reciprocal(be, be), r=[MK("be")], w=[MK("be")])
        P.op("dve", lambda e: e.tensor_tensor(gg, ga_v, bc(dtb_b[:, :], [128, 4, 4], 1), ALU.add), r=gkeys + [("dtb",)], w=[MK("gg")])
        P.op("dve", lambda e: e.tensor_scalar(t1, gg, -1.0, 0.0, ALU.mult, ALU.add), r=[MK("gg")], w=[MK("t1")])
        P.op("dve", lambda e: e.tensor_tensor(t1, t1, gg, ALU.max), r=[MK("gg"), MK("t1")], w=[MK("t1")])
        P.op("act", lambda e: e.activation(out=t1, in_=t1, func=AF.Exp, scale=-1.0), r=[MK("t1")], w=[MK("t1")])
        P.op("act", lambda e: e.activation(out=t1, in_=t1, func=AF.Ln, bias=1.0), r=[MK("t1")], w=[MK("t1")])
        P.op("dve", lambda e: e.scalar_tensor_tensor(gg, gg, 0.0, t1, ALU.max, ALU.add), r=[MK("gg"), MK("t1")], w=[MK("gg")])
        P.op("dve", lambda e: e.tensor_tensor(gg, gg, bc(negA[:, :], [128, 4, 4], 1), ALU.mult), r=[MK("gg"), ("negA",)], w=[MK("gg")])
        if sample:
            P.op("dve", lambda e: e.tensor_scalar(gg, gg, tok0[:, 0:1], 0.0, ALU.mult, ALU.add), r=[MK("gg"), ("tok0",)], w=[MK("gg")])

        if KLIM < 8:
            return
        v4 = lambda ap: ap.rearrange("p (h t) -> p h t", h=4)
        for b in range(4):
            cols = slice(b * 128, (b + 1) * 128)
            gcc = sm[:, 48:52]
            gl = sm[:, 52:56]
            s1 = sm[:, 56:60]
            s2 = sm[:, 60:63]
            k = nb()
            P.op("pe", lambda e, k=k, b=b: e.matmul(ps[:, k, 0:4], triu[:, :], gg[:, b, :], start=True, stop=True), r=[("triu",), MK("gg")], w=[("ps", k)])
            P.op("dve", lambda e, k=k: e.tensor_copy(gcc, ps[:, k, 0:4]), r=[("ps", k)], w=[MK("gcc")])
            P.op("dve", lambda e, b=b: e.tensor_tensor(v4(tA), bc(triu[:, :], [128, 4, 128], 1), bc(gg[:, b, :], [128, 4, 128], 2), ALU.mult),
                 r=[("triu",), MK("gg")], w=[MK("tA")])
            kr = nb()
            P.op("pe", lambda e, kr=kr: e.matmul(ps[:, kr, :], ones_f[:, :], tA, start=True, stop=True), r=[("ones_f",), MK("tA")], w=[("ps", kr)])
            P.op("dve", lambda e, kr=kr: e.tensor_tensor(v4(tB), v4(ps[:, kr, :]), bc(gcc, [128, 4, 128], 2), ALU.subtract), r=[("ps", kr), MK("gcc")], w=[MK("tB")])
            PQ("dve", lambda e: e.tensor_scalar(tC, tB, 0.0, 0.0, ALU.min, ALU.add), r=[MK("tB")], w=[MK("tC")])
            PQ("act", lambda e: e.activation(out=tC, in_=tC, func=AF.Exp), r=[MK("tC")], w=[MK("tC")])
            PQ("dve", lambda e: e.tensor_tensor(v4(tC), v4(tC), bc(triu[:, :], [128, 4, 128], 1), ALU.mult), r=[MK("tC"), ("triu",)], w=[MK("tC")])
            PD("dve", lambda e: e.tensor_scalar(tD, tB, 0.0, 0.0, ALU.max, ALU.add), r=[MK("tB")], w=[MK("tD")])
            PD("act", lambda e: e.activation(out=tD, in_=tD, func=AF.Exp, scale=-1.0), r=[MK("tD")], w=[MK("tD")])
            PD("dve", lambda e: e.tensor_tensor(v4(tD), v4(tD), bc(strl[:, :], [128, 4, 128], 1), ALU.mult), r=[MK("tD"), ("strl",)], w=[MK("tD")])
            PQ("act", lambda e, kr=kr: e.activation(out=tE, in_=ps[:, kr, :], func=AF.Exp), r=[("ps", kr)], w=[MK("tE")])
            P.op("act", lambda e, kr=kr: e.activation(out=gl, in_=v4(ps[:, kr, :])[:, :, 127], func=AF.Exp), r=[("ps", kr)], w=[MK("gl")])
            P.op("dve", lambda e, kr=kr: e.tensor_tensor(s1, v4(ps[:, kr, :])[:, :, 127], gcc, ALU.subtract), r=[("ps", kr), MK("gcc")], w=[MK("s1")])
            P.op("act", lambda e: e.activation(out=s1, in_=s1, func=AF.Exp), r=[MK("s1")], w=[MK("s1")])
            P.op("act", lambda e: e.activation(out=gcc, in_=gcc, func=AF.Exp), r=[MK("gcc")], w=[MK("gcc")])
            P.op("dve", lambda e, b=b: e.tensor_tensor(gcc, gcc, be[:, b, :], ALU.mult), r=[MK("gcc"), MK("be")], w=[MK("gcc")])
            kt = nb()
            ktb = ps[:, kt, :].bitcast(BF16)
            for h in range(4):
                P.op("pe", lambda e, h=h, ktb=ktb, cols=cols: e.transpose(ktb[:, h * 128:(h + 1) * 128], gkn[:, h, cols], ident[:, :]),
                     r=[MK("gkn", h), ("ident",)], w=[("ps", kt)])
            P.op("dve", lambda e, ktb=ktb: e.tensor_tensor(kb[:, :, :], v4(ktb[:, 0:512]), bc(gcc, [128, 4, 128], 2), ALU.mult), r=[("ps", kt), MK("gcc")], w=[MK("kb")])
            P.op("dve", lambda e, ktb=ktb: e.tensor_tensor(kd[:, :, :], v4(ktb[:, 0:512]), bc(s1, [128, 4, 128], 2), ALU.mult), r=[("ps", kt), MK("s1")], w=[MK("kd")])
            kvv = nb()
            for h in range(4):
                P.op("pe", lambda e, h=h, kvv=kvv, cols=cols: e.transpose(ps[:, kvv, h * 128:(h + 1) * 128], cact(8 + h)[:, cols], ident_f[:, :]),
                     r=[MK("pre", 8 + h), ("identf",)], w=[("ps", kvv)])
            P.op("dve", lambda e, kvv=kvv, b=b: e.tensor_tensor(vb[:, :, :], v4(ps[:, kvv, :]), bc(be[:, b, :], [128, 4, 128], 2), ALU.mult), r=[("ps", kvv), MK("be")], w=[MK("vb")])
            kkk, kqk = nb(), nb()
            for h in range(4):
                PD("pe", lambda e, h=h, kkk=kkk, cols=cols: e.matmul(ps[:, kkk, h * 128:(h + 1) * 128], gkn[:, h, cols], gkn[:, h, cols], start=True, stop=True),
                     r=[MK("gkn", h)], w=[("ps", kkk)])
            for h in range(4):
                PQ("pe", lambda e, h=h, kqk=kqk, cols=cols: e.matmul(ps[:, kqk, h * 128:(h + 1) * 128], gkn[:, h, cols], gqn[:, h, cols], start=True, stop=True),
                     r=[MK("gkn", h), MK("gqn", h)], w=[("ps", kqk)])
            PD("dve", lambda e, kkk=kkk: e.tensor_tensor(tD, ps[:, kkk, :], tD, ALU.mult), r=[("ps", kkk), MK("tD")], w=[MK("tD")])
            PD("dve", lambda e, b=b: e.scalar_tensor_tensor(Xb[0][:, :, :], v4(tD), -1.0, bc(be[:, b, :], [128, 4, 128], 2), ALU.mult, ALU.mult),
                 r=[MK("tD"), MK("be")], w=[MK("X", 0)])
            PQ("dve", lambda e, kqk=kqk: e.tensor_tensor(qkT[:, :, :], v4(ps[:, kqk, :]), v4(tC), ALU.mult), r=[("ps", kqk), MK("tC")], w=[MK("qkT")])
            PQ("pool", lambda e, cols=cols: e.tensor_tensor(qdT[:, :, :], gqn[:, :, cols], v4(tE), ALU.mult), r=[MK("gqn", h) for h in range(4)] + [MK("tE")], w=[MK("qdT")])
            ky = nb()
            kyb = ps[:, ky, :].bitcast(BF16)
            for h in range(4):
                PD("pe", lambda e, h=h, ky=ky: e.transpose(ps[:, ky, h * 128:(h + 1) * 128], Xb[0][:, h, :], ident_f[:, :]), r=[MK("X", 0), ("identf",)], w=[("ps", ky)])
            PD("act", lambda e, ky=ky: e.activation(out=Yb[0][:, :, :], in_=v4(ps[:, ky, :]), func=AF.Identity), r=[("ps", ky)], w=[MK("Y", 0)])
            PD("dve", lambda e: e.tensor_tensor(Nf, Yb[0][:, :, :], bc(ident_f[:, :], [128, 4, 128], 1), ALU.add), r=[MK("Y", 0), ("identf",)], w=[MK("Nf")])
            cur = 0
            pend = None
            for st in range(1, 7):
                nx = 1 - cur
                kx, kyy = nb(), nb()
                for h in range(4):
                    PD("pe", lambda e, h=h, kx=kx, cur=cur: e.matmul(ps[:, kx, h * 128:(h + 1) * 128], Yb[cur][:, h, :], Xb[cur][:, h, :], start=True, stop=True),
                         r=[MK("X", cur), MK("Y", cur)], w=[("ps", kx)])
                for h in (range(4) if st < 6 else ()):
                    PD("pe", lambda e, h=h, kyy=kyy, cur=cur: e.matmul(ps[:, kyy, h * 128:(h + 1) * 128], Xb[cur][:, h, :], Yb[cur][:, h, :], start=True, stop=True),
                         r=[MK("X", cur), MK("Y", cur)], w=[("ps", kyy)])
                PD("act", lambda e, kx=kx, nx=nx: e.activation(out=Xb[nx][:, :, :], in_=v4(ps[:, kx, :]), func=AF.Identity), r=[("ps", kx)], w=[MK("X", nx)])
                if st < 6:
                    PD("dve", lambda e, kyy=kyy, nx=nx: e.tensor_copy(Yb[nx][:, :, :], v4(ps[:, kyy, :])), r=[("ps", kyy)], w=[MK("Y", nx)])
                if pend is not None:
                    pend()

                def mk(nx=nx):
                    def f():
                        kn2 = nb()
                        for h in range(4):
                            PD("pe", lambda e, h=h, kn2=kn2, nx=nx: e.matmul(ps[:, kn2, h * 128:(h + 1) * 128], Xb[nx][:, h, :], Nf[:, h, :], start=True, stop=True),
                               r=[MK("X", nx), MK("Nf")], w=[("ps", kn2)])
                        PD("dve", lambda e, kn2=kn2: e.tensor_tensor(Nf, Nf, v4(ps[:, kn2, :]), ALU.add), r=[("ps", kn2), MK("Nf")], w=[MK("Nf")])
                    return f
                pend = mk()
                cur = nx
            pend()
            PD("act", lambda e: e.activation(out=Nb[:, :, :], in_=Nf, func=AF.Identity), r=[MK("Nf")], w=[MK("N")])
            ku, kw = nb(), nb()
            for h in range(4):
                P.op("pe", lambda e, h=h, ku=ku: e.matmul(ps[:, ku, h * 128:(h + 1) * 128], Nb[:, h, :], vb[:, h, :], start=True, stop=True), r=[MK("N"), MK("vb")], w=[("ps", ku)])
            for h in range(4):
                P.op("pe", lambda e, h=h, kw=kw: e.matmul(ps[:, kw, h * 128:(h + 1) * 128], kb[:, h, :], Nb[:, h, :], start=True, stop=True), r=[MK("N"), MK("kb")], w=[("ps", kw)])
            P.op("act", lambda e, ku=ku: e.activation(out=u_t, in_=ps[:, ku, :], func=AF.Identity), r=[("ps", ku)], w=[MK("u")])
            P.op("act", lambda e, kw=kw: e.activation(out=wT[:, :, :], in_=v4(ps[:, kw, :]), func=AF.Identity), r=[("ps", kw)], w=[MK("wT")])
            if sample:
                sq = tix * 4 + b
                P.op("sp", lambda e, sq=sq: e.dma_start(out=Sst[:, :, :], in_=sgdn[sq].rearrange("h k v -> k h v")), w=[MK("S")], chan="sld")
                P.op("act", lambda e: e.activation(out=Sbf[:, :, :], in_=Sst[:, :, :], func=AF.Identity), r=[MK("S")], w=[MK("Sbf")])
            k1 = nb()
            for h in range(4):
                P.op("pe", lambda e, h=h, k1=k1: e.matmul(ps[:, k1, h * 128:(h + 1) * 128], wT[:, h, :], Sbf[:, h, :], start=True, stop=True), r=[MK("wT"), MK("Sbf")], w=[("ps", k1)])
            P.op("dve", lambda e, k1=k1: e.tensor_tensor(vnew[:, :, :], v4(u_t), v4(ps[:, k1, :]), ALU.subtract), r=[("ps", k1), MK("u")], w=[MK("vnew")])
            k3, k4 = nb(), nb()
            for h in range(4):
                PQ("pe", lambda e, h=h, k3=k3: e.matmul(ps[:, k3, h * 128:(h + 1) * 128], Sbf[:, h, :], qdT[:, h, :], start=True, stop=False), r=[MK("Sbf"), MK("qdT")], w=[("ps", k3)])
                PQ("pe", lambda e, h=h, k3=k3: e.matmul(ps[:, k3, h * 128:(h + 1) * 128], vnew[:, h, :], qkT[:, h, :], start=False, stop=True), r=[MK("vnew"), MK("qkT")], w=[("ps", k3)])
            for h in range(4):
                P.op("pe", lambda e, h=h, k4=k4: e.matmul(ps[:, k4, h * 128:(h + 1) * 128], kd[:, h, :], vnew[:, h, :], start=True, stop=True), r=[MK("kd"), MK("vnew")], w=[("ps", k4)])
            PQ("act", lambda e, k3=k3, cols=cols: e.activation(out=oT[:, :, cols], in_=v4(ps[:, k3, :]), func=AF.Identity), r=[("ps", k3)], w=[MK("oT", b), MK("ncrow")])
            for h in range(4):
                P.op("dve", lambda e, h=h, k4=k4: e.scalar_tensor_tensor(Sst[:, h, :], Sst[:, h, :], gl[:, h:h + 1], ps[:, k4, h * 128:(h + 1) * 128], ALU.mult, ALU.add),
                     r=[("ps", k4), MK("gl"), MK("S")], w=[MK("S")])
            if sample:
                P.op("sp", lambda e, sq=sq: e.dma_start(out=ngs[sq].rearrange("h k v -> k h v"), in_=Sst[:, :, :]), r=[MK("S")], w=[("o_ngs", sq)], chan="ongs")
            else:
                P.op("act", lambda e: e.activation(out=Sbf[:, :, :], in_=Sst[:, :, :], func=AF.Identity), r=[MK("S")], w=[MK("Sbf")])
                if last_tile and b == 3:
                    P.op("sp", lambda e: e.dma_start(out=ngp.rearrange("h k v -> k h v"), in_=Sst[:, :, :]), r=[MK("S")], w=[("o_ngp",)], chan="ongs")
        if KLIM < 9:
            return
        if prepass:
            return
        for h in range(4):
            P.op("pool", lambda e, h=h: e.tensor_tensor(ctmp[:, :], oT[:, h, :], oT[:, h, :], ALU.mult), r=[MK("oT", b) for b in range(4)], w=[MK("ctmp")])
            k = nb()
            P.op("pe", lambda e, k=k: e.matmul(ps[:, k, :], ones_f[:, :], ctmp[:, :], start=True, stop=True), r=[("ones_f",), MK("ctmp")], w=[("ps", k)])
            P.op("act", lambda e, k=k: e.activation(out=tA, in_=ps[:, k, :], func=AF.Ln, scale=1.0 / 128, bias=EPS), r=[("ps", k)], w=[MK("tA")])
            P.op("act", lambda e: e.activation(out=tA, in_=tA, func=AF.Exp, scale=-0.5), r=[MK("tA")], w=[MK("tA")])
            P.op("dve", lambda e, h=h: e.scalar_tensor_tensor(tA, oT[:, h, :], gng_t[:, 0:1], tA, ALU.mult, ALU.mult), r=[MK("oT", b) for b in range(4)] + [MK("tA"), ("gng",)], w=[MK("tA")])
            P.op("dve", lambda e, h=h: e.tensor_tensor(goT[:, h, :], tA, zs[:, h, :], ALU.mult), r=[MK("tA"), MK("zs", h)], w=[MK("goT", h)])
        if KLIM < 10:
            return
        if sample:
            for t_, src in ((aoTs, aoT), (goTs, goT)):
                P.op("pool", lambda e, t_=t_, src=src: e.tensor_copy(t_[:, :, tix * 4:(tix + 1) * 4], src[:, :, :].rearrange("p c (b t) -> p c b t", b=4)[:, :, :, 0]),
                     r=[MK("aoT", b) for b in range(4)] + [MK("goT", h) for h in range(4)], w=[("mixTs", tix)])
        else:
            for b in range(4):
                k0, k1 = nb(), nb()
                for c in range(8):
                    src = aoT if c < 4 else goT
                    for dh, kk in ((0, k0), (1, k1)):
                        P.op("pe", lambda e, c=c, dh=dh, kk=kk, b=b, src=src: e.matmul(ps[:, kk, :], src[:, c % 4, b * 128:(b + 1) * 128], wmo_t[:, c, dh * 512:(dh + 1) * 512], start=(c == 0), stop=(c == 7)),
                             r=[MK("aoT", b), MK("goT", c % 4), ("wmo",)], w=[("ps", kk)])
                for dh, kk in ((0, k0), (1, k1)):
                    P.op("dve", lambda e, dh=dh, kk=kk, b=b: e.tensor_tensor(xres(b, dh), xres(b, dh), ps[:, kk, :], ALU.add), r=[("ps", kk)], w=[xkeys[b]])

    def final_norm(xsrc, xkeys, nblk, np_, ydst, okey, chan):
        keyss = [("ss",)]
        for b in range(nblk):
            P.op("act", lambda e, b=b: e.activation(out=junk[0:np_, :], in_=xsrc(b), func=AF.Square, accum_out=ss[0:np_, b:b + 1]), r=[xkeys[b]], w=[MK("ctmp")] + keyss)
        rstd_cols(nblk, np_, 1.0 / D, keyss)
        for b in range(nblk):
            P.op("dve", lambda e, b=b: e.scalar_tensor_tensor(xsrc(b), xsrc(b), ss[0:np_, b:b + 1], gfin_b[0:np_, :], ALU.mult, ALU.mult), r=keyss + [xkeys[b], ("gfin",)], w=[xkeys[b]])
            P.op("sp", lambda e, b=b: e.dma_start(out=ydst(b), in_=xsrc(b)), r=[xkeys[b]], w=[(okey, b)], chan=chan)

    skeys = [("xs",)]
    P.op("sp", lambda e: e.dma_start(out=xts[:, 0, :], in_=xs), w=skeys, chan="xsld")
    xs_src = lambda b: xts[0:NS, 0, :]
    xs_dst = lambda b, dh: xts[0:NS, 0, dh * 512:(dh + 1) * 512]
    nTs_keys = [("nTs",)]
    ffn_prefetch(0)
    norm_T(xs_src, skeys, 1, NS, 0, nTs, nTs_keys)
    fence()
    ffn(0, nTs, nTs_keys, NS, xs_dst, skeys, 1, NS)
    xk = [("xnT", b) for b in range(4)]
    if stage >= 2:
        norm_T(xs_src, skeys, 1, NS, 1, nTs, nTs_keys)
        fence()
    for tix in range(4 if stage >= 2 else 0):
        P.op("pool", lambda e: e.memset(xnT[:, :, :], 0.0), w=xk)
        P.op("pool", lambda e, tix=tix: e.tensor_copy(xnT[:, :, :].rearrange("p c (b t) -> p c b t", b=4)[:, :, :, 0], nTs[:, :, tix * 4:(tix + 1) * 4]), r=nTs_keys, w=xk)
        mix(xnT, xk, True, tix, False, False, None, None)
    k0, k1 = nb(), nb()
    for c in range(8 if stage >= 2 else 0):
        src = aoTs if c < 4 else goTs
        for dh, kk in ((0, k0), (1, k1)):
            P.op("pe", lambda e, c=c, dh=dh, kk=kk, src=src: e.matmul(ps[0:NS, kk, :], src[:, c % 4, :], wmo_t[:, c, dh * 512:(dh + 1) * 512], start=(c == 0), stop=(c == 7)),
                 r=[("mixTs", t) for t in range(4)] + [("wmo",)], w=[("ps", kk)])
    for dh, kk in (((0, k0), (1, k1)) if stage >= 2 else ()):
        P.op("dve", lambda e, dh=dh, kk=kk: e.tensor_tensor(xs_dst(0, dh), xs_dst(0, dh), ps[0:NS, kk, :], ALU.add), r=[("ps", kk)], w=skeys)
    if stage >= 2:
        ffn_prefetch(1)
        norm_T(xs_src, skeys, 1, NS, 2, nTs, nTs_keys)
        fence()
        ffn(1, nTs, nTs_keys, NS, xs_dst, skeys, 1, NS)
    final_norm(xs_src, skeys, 1, NS, lambda b: ys, "o_ys", "oys")

    tiles = [("pre", i) for i in range(n_pre)] + [("main", i) for i in range(n_tiles)]

    def load_x(gi):
        kind, i = tiles[gi]
        src = xpre if kind == "pre" else xp
        X = xt[gi % 2]
        keys = [("x", gi % 2, b) for b in range(4)]
        if gi == 1:
            keys = keys + [("xs",), ("sct",), MK("ckt"), MK("cvt"), MK("ckd", 0), MK("ckd", 1)]
        P.op("sp", lambda e: [e.dma_start(out=X[:, b, :], in_=src[i * TT + b * 128: i * TT + (b + 1) * 128, :]) for b in range(4)],
             w=keys, chan=f"x{gi % 2}", nd=4)

    if stage >= 3:
        load_x(0)
    for gi, (kind, i) in enumerate(tiles if stage >= 3 else []):
        X = xt[gi % 2]
        xkeys = [("x", gi % 2, b) for b in range(4)]
        xsrc = lambda b, X=X: X[:, b, :]
        xdst = lambda b, dh, X=X: X[:, b, dh * 512:(dh + 1) * 512]
        ffn_prefetch(0)
        norm_T(xsrc, xkeys, 4, 128, 0, xnT, xk)
        fence()
        ffn(0, xnT, xk, TT, xdst, xkeys, 4, 128)
        norm_T(xsrc, xkeys, 4, 128, 1, xnT, xk)
        fence()
        if kind == "pre":
            mix(xnT, xk, False, 0, i == 0, False, xdst, xkeys, prepass=True, pre_last=(i == n_pre - 1))
            if gi + 1 < len(tiles):
                load_x(gi + 1)
            continue
        mix(xnT, xk, False, 0, (n_pre == 0 and i == 0), i == n_tiles - 1, xdst, xkeys, mask_prev=(n_pre > 0 and i == 0))
        if gi + 1 < len(tiles):
            load_x(gi + 1)
        ffn_prefetch(1)
        norm_T(xsrc, xkeys, 4, 128, 2, xnT, xk)
        fence()
        ffn(1, xnT, xk, TT, xdst, xkeys, 4, 128)
        final_norm(xsrc, xkeys, 4, 128, lambda b, i=i: yp[i * TT + b * 128: i * TT + (b + 1) * 128, :], ("o_yp", i), f"oy{gi % 2}")

    with ExitStack() as st:
        sems = {}
        for en in Prog.ENG:
            sems[("e", en)] = st.enter_context(nc.semaphore("e_" + en))
        for ch in P.chan_cnt:
            sems[("c", ch)] = st.enter_context(nc.semaphore("c_" + ch))
        P.run(sems)
    return nc


_NC_CACHE = {}


def run_cores(xp_list, per_core, shared, n_tiles, stage=3, xpre_list=None, hasprev=None):
    n_pre = 0 if xpre_list is None else xpre_list[0].shape[0] // TT
    key = (n_tiles, n_pre, stage)
    if key not in _NC_CACHE:
        _NC_CACHE[key] = build(n_tiles, stage, n_pre)
    nc = _NC_CACHE[key]
    consts = host_consts()
    in_maps = []
    for c in range(len(xp_list)):
        m = dict(shared)
        m.update(per_core[c])
        m["xp"] = xp_list[c]
        if n_pre:
            m["xpre"] = xpre_list[c]
        m["hasprev"] = np.full((128, 1), 0.0 if hasprev is None else hasprev[c], np.float32)
        for k, v in consts.items():
            m["c_" + k] = v
        in_maps.append({k: np.ascontiguousarray(v, dtype=np.float32) for k, v in m.items()})
    res = run_bass_kernel_spmd(nc, in_maps, core_ids=list(range(len(xp_list))))
    return res.results


def kernel(x_prompt, x_sample, cache_attn_k, cache_attn_v, state_conv, state_gdn,
           ffn1_norm_g, ffn1_w_in, ffn1_w_out, mix_norm_g, w_in_mix, attn_sinks, conv_w,
           gdn_A_log, gdn_dt_bias, gdn_norm_g, w_out_mix, ffn2_norm_g, ffn2_w_in, ffn2_w_out,
           final_norm_g):
    f = lambda a: np.asarray(a, dtype=np.float32)
    x_prompt = f(x_prompt)
    B, S, _ = x_prompt.shape
    ncore = 8
    HALF = S // 2
    n_tiles = HALF // TT
    shared = dict(g1=f(ffn1_norm_g)[0], wi1=f(ffn1_w_in)[0], wo1=f(ffn1_w_out)[0], g2=f(mix_norm_g)[0],
                  wmi=f(w_in_mix)[0], sinks=f(attn_sinks)[0], convw=f(conv_w)[0], alog=f(gdn_A_log)[0],
                  dtb=f(gdn_dt_bias)[0], gng=f(gdn_norm_g)[0], wmo=f(w_out_mix)[0], g3=f(ffn2_norm_g)[0],
                  wi2=f(ffn2_w_in)[0], wo2=f(ffn2_w_out)[0], gfin=f(final_norm_g))
    xs_ = f(x_sample)[:, 0, :]
    ck_ = f(cache_attn_k)[0].reshape(-1, 128, 128)
    cv_ = f(cache_attn_v)[0].reshape(-1, 128, 128)
    sc_ = f(state_conv)[0]
    sg_ = f(state_gdn)[0]
    per_core, xp_list, xpre_list, hasprev = [], [], [], []
    zeros = np.zeros((HALF, D), np.float32)
    for c in range(ncore):
        sl = slice(c * NS, (c + 1) * NS)
        per_core.append(dict(xs=xs_[sl], ck=ck_[sl], cv=cv_[sl], sconv=sc_[sl].reshape(NS * 3, 1536), sgdn=sg_[sl]))
        sq, half = c // 2, c % 2
        xp_list.append(x_prompt[sq, half * HALF:(half + 1) * HALF])
        xpre_list.append(x_prompt[sq, 0:HALF] if half else zeros)
        hasprev.append(float(half))
    res = run_cores(xp_list, per_core, shared, n_tiles, 3, xpre_list, hasprev)
    y_prompt = np.stack([np.concatenate([res[2 * b]["yp"], res[2 * b + 1]["yp"]]) for b in range(B)])
    cat = lambda k: np.stack([res[2 * b + 1][k] for b in range(B)])
    y_sample = np.concatenate([res[c]["ys"] for c in range(ncore)])[:, None, :]
    nkp = cat("nkp").reshape(1, B, 128, 2, 64)
    nvp = cat("nvp").reshape(1, B, 128, 2, 64)
    ncp = cat("ncp")[None]
    ngp = cat("ngp")[None]
    nks = np.concatenate([res[c]["nks"] for c in range(ncore)]).reshape(1, ncore * NS, 128, 2, 64)
    nvs = np.concatenate([res[c]["nvs"] for c in range(ncore)]).reshape(1, ncore * NS, 128, 2, 64)
    ncs = np.concatenate([res[c]["ncs"] for c in range(ncore)])[None]
    ngs = np.concatenate([res[c]["ngs"] for c in range(ncore)])[None]
    return (y_prompt, y_sample, nkp, nvp, ncp, ngp, nks, nvs, ncs, ngs)
```

```python
import os
import numpy as np
from contextlib import ExitStack
KLIMS = float(os.environ.get('KLIMS', '99'))
KLIMP = float(os.environ.get('KLIMP', '99'))
import concourse.bass as bass
import concourse.mybir as mybir
from concourse.bass_utils import run_bass_kernel_spmd

F32 = mybir.dt.float32
BF16 = mybir.dt.bfloat16
ALU = mybir.AluOpType
AF = mybir.ActivationFunctionType
AX = mybir.AxisListType

D = 1024
FF = 2816
NJ = 22
EPS = 1e-6
TT = 512
NBLK = 4
NS = 16
INW = 2824


class Prog:
    ENG = ("pe", "act", "dve", "pool", "sp")

    def __init__(self, nc):
        self.nc = nc
        self.ops = []
        self.lastw = {}
        self.readers = {}
        self.chan_cnt = {}

    def op(self, eng, fn, r=(), w=(), chan=None, nd=1):
        i = len(self.ops)
        deps = set()
        for k in r:
            if k in self.lastw:
                deps.add(self.lastw[k])
        for k in w:
            if k in self.lastw:
                deps.add(self.lastw[k])
            deps.update(self.readers.get(k, ()))
        for k in r:
            self.readers.setdefault(k, []).append(i)
        for k in w:
            self.lastw[k] = i
            self.readers[k] = []
        o = dict(eng=eng, fn=fn, deps=deps, chan=chan, nd=nd, sig=False, val=0)
        if chan is not None:
            self.chan_cnt[chan] = self.chan_cnt.get(chan, 0) + 16 * nd
            o["val"] = self.chan_cnt[chan]
        self.ops.append(o)
        return i

    def allkeys(self, pred):
        ks = set(self.lastw) | set(self.readers)
        return [k for k in ks if pred(k)]

    def run(self, sems):
        nc = self.nc
        ops = self.ops
        for o in ops:
            for d in o["deps"]:
                ops[d]["sig"] = True
        cnt = {e: 0 for e in self.ENG}
        for o in ops:
            if o["chan"] is None and o["sig"]:
                cnt[o["eng"]] += 1
                o["val"] = cnt[o["eng"]]
        streams = {e: [] for e in self.ENG}
        for i, o in enumerate(ops):
            streams[o["eng"]].append(i)

        def runner(ename):
            def f(e):
                waited = {}
                for i in streams[ename]:
                    o = ops[i]
                    for d in sorted(o["deps"]):
                        do = ops[d]
                        if do["chan"] is not None:
                            key = ("c", do["chan"])
                        else:
                            if do["eng"] == ename and ename in ("pe", "sp"):
                                continue
                            key = ("e", do["eng"])
                        if waited.get(key, 0) >= do["val"]:
                            continue
                        waited[key] = do["val"]
                        e.wait_ge(sems[key], do["val"])
                    res = o["fn"](e)
                    if o["chan"] is not None:
                        lst = res if isinstance(res, (list, tuple)) else [res]
                        assert len(lst) == o["nd"], (len(lst), o["nd"])
                        for ins in lst:
                            ins.then_inc(sems[("c", o["chan"])], 16)
                    elif o["sig"]:
                        res.then_inc(sems[("e", ename)], 1)
                if ename == "sp":
                    for ch, v in self.chan_cnt.items():
                        if waited.get(("c", ch), 0) < v:
                            e.wait_ge(sems[("c", ch)], v)
            return f

        with nc.Block() as block:
            block.tensor(runner("pe"))
            block.scalar(runner("act"))
            block.vector(runner("dve"))
            block.gpsimd(runner("pool"))
            block.sync(runner("sp"))


def host_consts():
    i = np.arange(128)
    c = {}
    c["ident"] = np.eye(128, dtype=np.float32)
    c["triu"] = (i[:, None] <= i[None, :]).astype(np.float32)
    c["strl"] = (i[:, None] > i[None, :]).astype(np.float32)
    slopes = np.exp2(-8.0 * np.arange(1, 9, dtype=np.float32) / 8.0).astype(np.float32)
    jj = i[:, None, None].astype(np.float32)
    ii = i[None, None, :].astype(np.float32)
    sl = slopes[None, :, None]
    bc = np.where(ii >= jj, -sl * (ii - jj), -30000.0).astype(np.float32)
    bp = np.where(jj >= ii, -sl * (ii - jj + 128.0), -30000.0).astype(np.float32)
    c["bcur"] = np.ascontiguousarray(bc.reshape(128, 1024))
    c["bprev"] = np.ascontiguousarray(bp.reshape(128, 1024))
    tm = np.zeros((128, 1), np.float32)
    tm[0, 0] = 1.0
    c["tok0"] = tm
    cm = np.zeros((128, 512), np.float32)
    cm[:, 0::128] = 1.0
    c["col0"] = cm
    qm = np.zeros((128, 4), np.float32)
    qm[:64, 0] = 0.125; qm[64:, 1] = 0.125; qm[:64, 2] = 1.0; qm[64:, 3] = 1.0
    c["qm"] = qm
    return c


def build(n_tiles, stage=3, n_pre=0):
    nc = bass.Bass("TRN2", target_bir_lowering=False)
    NTOK = n_tiles * TT

    def din(name, shape):
        return nc.dram_tensor(name, list(shape), F32, kind="ExternalInput").ap()

    def dout(name, shape):
        return nc.dram_tensor(name, list(shape), F32, kind="ExternalOutput").ap()

    xp = din("xp", [NTOK, D])
    xpre = din("xpre", [n_pre * TT, D]) if n_pre > 0 else None
    hasprev = din("hasprev", [128, 1])
    xs = din("xs", [NS, D])
    ck = din("ck", [NS, 128, 128])
    cv = din("cv", [NS, 128, 128])
    sconv = din("sconv", [NS * 3, 1536])
    sgdn = din("sgdn", [NS, 4, 128, 128])
    g1 = din("g1", [D]); wi1 = din("wi1", [D, 2 * FF]); wo1 = din("wo1", [FF, D])
    g2 = din("g2", [D]); wmi = din("wmi", [D, INW]); sinks = din("sinks", [8])
    convw = din("convw", [4, 1536]); alog = din("alog", [4]); dtb = din("dtb", [4])
    gng = din("gng", [128]); wmo = din("wmo", [D, D])
    g3 = din("g3", [D]); wi2 = din("wi2", [D, 2 * FF]); wo2 = din("wo2", [FF, D])
    gfin = din("gfin", [D])
    c_ident = din("c_ident", [128, 128]); c_triu = din("c_triu", [128, 128]); c_strl = din("c_strl", [128, 128])
    c_bcur = din("c_bcur", [128, 1024]); c_bprev = din("c_bprev", [128, 1024])
    c_tok0 = din("c_tok0", [128, 1]); c_col0 = din("c_col0", [128, 512]); c_qm = din("c_qm", [128, 4])

    yp = dout("yp", [NTOK, D]); ys = dout("ys", [NS, D])
    nkp = dout("nkp", [128, 128]); nvp = dout("nvp", [128, 128])
    ncp = dout("ncp", [3, 1536]); ngp = dout("ngp", [4, 128, 128])
    nks = dout("nks", [NS, 128, 128]); nvs = dout("nvs", [NS, 128, 128])
    ncs = dout("ncs", [NS, 3, 1536]); ngs = dout("ngs", [NS, 4, 128, 128])

    wi_s = [nc.dram_tensor(f"wi_s{i}", [NJ, 128, 8, 256], BF16).ap() for i in range(2)]
    wo_s = [nc.dram_tensor(f"wo_s{i}", [128, NJ, D], BF16).ap() for i in range(2)]
    wm_s = nc.dram_tensor("wm_s", [11, 128, 8, 256], BF16).ap()
    wtm_s = nc.dram_tensor("wtm_s", [128, 8, 264], BF16).ap()
    wmo_s = nc.dram_tensor("wmo_s", [128, 8, D], BF16).ap()

    A = nc.alloc_sbuf_tensor
    xt = [A(f"xt{i}", [128, NBLK, D], F32) for i in range(2)]
    xts = xt[1][0:NS, 0:1, :]
    x1f = xt[1][:, 1:4, :].rearrange("p b d -> p (b d)")
    xn2 = [A(f"xn{i}", [128, D], BF16) for i in range(2)]
    xnT = A("xnT", [128, 8, TT], BF16)
    nTs = A("nTs", [128, 8, NS], BF16)
    wi = [A(f"wibuf{i}", [128, 8, 256], BF16) for i in range(3)]
    wmo_t = A("wmo_t", [128, 8, D], BF16)
    wtm = A("wtm", [128, 8, 264], BF16)
    arena = A("arena", [128, 16896], F32)
    hT = arena[:, 0:5632].bitcast(BF16).rearrange("p (j t) -> p j t", j=NJ)
    wo = arena[:, 5632:16896].bitcast(BF16).rearrange("p (j n) -> p j n", j=NJ)
    off = [0]

    def carve(ncols_f32):
        a = off[0]
        off[0] += ncols_f32
        assert off[0] <= 16896
        return arena[:, a:a + ncols_f32]

    pre = carve(12 * 524).rearrange("p (m t) -> p m t", m=12)
    oT = carve(2048).rearrange("p (h t) -> p h t", h=4)
    ncrow = oT[0:4, :, :].rearrange("p h t -> p (h t)")[:, 0:1536]
    ebuf = carve(1024)
    tC = ebuf[:, 0:512]; tD = ebuf[:, 512:1024]
    tA = carve(512); tB = carve(512); tE = carve(512)
    u_t = carve(512)
    rbuf = tB
    gqn = carve(1024).bitcast(BF16).rearrange("p (h t) -> p h t", h=4)
    gkn = carve(1024).bitcast(BF16).rearrange("p (h t) -> p h t", h=4)
    zs = carve(1024).bitcast(BF16).rearrange("p (h t) -> p h t", h=4)
    Pp = carve(512).bitcast(BF16).rearrange("p (h t) -> p h t", h=8)
    qT = A("qT", [128, 4, TT], BF16)
    kTd = [A(f"kTd{i}", [128, 8, 128], BF16) for i in range(2)]
    Vd = [A(f"Vd{i}", [128, 8, 128], BF16) for i in range(2)]
    aoT = A("aoT", [128, 4, TT], BF16)
    goT = A("goT", [128, 4, TT], BF16)
    aoTs = A("aoTs", [128, 4, NS], BF16)
    goTs = A("goTs", [128, 4, NS], BF16)
    Pc = A("Pc", [128, 8, 128], BF16)
    Xb = [A("Xb0", [128, 4, 128], F32), carve(512).rearrange("p (h t) -> p h t", h=4)]
    Yb = [A("Yb0", [128, 4, 128], F32), carve(512).rearrange("p (h t) -> p h t", h=4)]
    Nf = carve(512).rearrange("p (h t) -> p h t", h=4)
    Nb = A("Nb", [128, 4, 128], BF16)
    Xh = [A(f"Xh{i}", [128, 4, 128], BF16) for i in range(2)]
    Yh = [A(f"Yh{i}", [128, 4, 128], BF16) for i in range(2)]
    vb = A("vb", [128, 4, 128], BF16)
    kb = A("kb", [128, 4, 128], BF16)
    kd = A("kd", [128, 4, 128], BF16)
    wT = A("wT", [128, 4, 128], BF16)
    qdT = A("qdT", [128, 4, 128], BF16)
    qkT = A("qkT", [128, 4, 128], BF16)
    vnew = A("vnew", [128, 4, 128], BF16)
    Sbf = A("Sbf", [128, 4, 128], BF16)
    ctmp = A("ctmp", [128, TT], F32)
    junk = ctmp[:, :].bitcast(BF16)
    Sst = A("Sst", [128, 4, 128], F32)
    ccar = A("ccar", [128, 12, 3], F32)
    kvo = A("kvo", [128, 256], F32)
    gbga = A("gbga", [128, NBLK, 8], F32)
    sm = A("sm", [128, 64], F32)
    ss = A("ss", [128, 8], F32)
    sg = [A("sg0", [128, TT], F32)] * 2
    ident_f = A("ident_f", [128, 128], F32)
    ident = A("ident_b", [128, 128], BF16)
    triu = A("triu", [128, 128], F32)
    strl = A("strl", [128, 128], F32)
    ones_f = A("ones_f", [128, 128], F32)
    ones_b = A("ones_b", [128, 128], BF16)
    bcur = A("bcur", [128, 1024], BF16)
    bprev = A("bprev", [128, 1024], BF16)
    tok0 = A("tok0", [128, 1], F32)
    hp_t = A("hp_t", [128, 1], F32)
    qm = A("qm", [128, 4], F32)
    qT2 = A("qT2", [128, 4, TT], BF16)
    col0 = A("col0", [128, 512], F32)
    gT = [A(f"gT{i}", [128, 8], F32) for i in range(3)]
    gfin_b = A("gfin_b", [128, D], F32)
    cw = A("cw", [128, 4, 12], F32)
    esink = A("esink", [128, 8], F32)
    negA = A("negA", [128, 4], F32)
    dtb_b = A("dtb_b", [128, 4], F32)
    gng_t = A("gng_t", [128, 1], F32)
    sct = x1f[0:NS * 3, 0:1536]
    ckt = x1f[:, 1536:1664]
    cvt = x1f[:, 1664:1792]
    ckd = x1f[:, 1792:2048].rearrange("p (a b d) -> p a b d", a=2, b=2)
    ps = nc.alloc_psum_tensor("ps", [128, 8, 512], F32)

    P = Prog(nc)
    MK = lambda *a: ("m_" + a[0],) + tuple(a[1:])
    bank = [0]

    def nb():
        bank[0] = (bank[0] + 1) % 8
        return bank[0]

    def bc(ap, shape, axis):
        return ap.unsqueeze(axis).broadcast_to(shape)

    def cast_ffn(i, w_in, w_out):
        v = w_in.rearrange("(c p) n -> p c n", p=128)
        for j in range(NJ):
            def f(e, j=j):
                return [e.dma_start(out=wi_s[i][j, :, :, g * 128:(g + 1) * 128],
                                    in_=v[:, :, g * FF + j * 128: g * FF + (j + 1) * 128]) for g in range(2)]
            P.op("pool", f, w=[("wi_s", i, j)], chan=f"cast{i}", nd=2)
        P.op("pool", lambda e: e.dma_start(out=wo_s[i], in_=w_out.rearrange("(j p) n -> p j n", p=128)),
             w=[("wo_s", i), ("castgrp", i)], chan=f"cast{i}")

    cast_ffn(0, wi1, wo1)
    wmv = wmi.rearrange("(c p) n -> p c n", p=128)
    groups = [(128 * p, False) for p in range(4)] + [(512, True), (576, True)] + \
             [(768 + 128 * m, False) for m in range(12)] + [(2304 + 128 * h, False) for h in range(4)]
    for s in range(11):
        def f(e, s=s):
            r = []
            for g in range(2):
                c0, dup = groups[2 * s + g]
                if dup:
                    for q in range(2):
                        r.append(e.dma_start(out=wm_s[s, :, :, g * 128 + q * 64: g * 128 + (q + 1) * 64], in_=wmv[:, :, c0:c0 + 64]))
                else:
                    r.append(e.dma_start(out=wm_s[s, :, :, g * 128:(g + 1) * 128], in_=wmv[:, :, c0:c0 + 128]))
            return r
        nd = sum(2 if groups[2 * s + g][1] else 1 for g in range(2))
        P.op("pool", f, w=[("wm_s", s)], chan="castm", nd=nd)

    def f(e):
        return [e.dma_start(out=wtm_s[:, :, 0:128], in_=wmv[:, :, 640:768]),
                e.dma_start(out=wtm_s[:, :, 128:256], in_=wmv[:, :, 512:640]),
                e.dma_start(out=wtm_s[:, :, 256:264], in_=wmv[:, :, 2816:2824])]
    P.op("pool", f, w=[("wtm_s",)], chan="castm", nd=3)
    P.op("pool", lambda e: e.dma_start(out=wmo_s, in_=wmo.rearrange("(c p) n -> p c n", p=128)), w=[("wmo_s",), ("castgrp", "m")], chan="castm")
    cast_ffn(1, wi2, wo2)

    for sq in range(NS):
        def f(e, sq=sq):
            return [e.dma_start(out=nks[sq, 0:127, :], in_=ck[sq, 1:128, :]),
                    e.dma_start(out=nvs[sq, 0:127, :], in_=cv[sq, 1:128, :]),
                    e.dma_start(out=ncs[sq, 0:2, :], in_=sconv[sq * 3 + 1: sq * 3 + 3, :])]
        P.op("pool", f, w=[("o_hist", sq)], chan="ohist", nd=3)

    ldn = [0]

    def ld(dst, src, key, **kw):
        ldn[0] += 1
        P.op("sp", lambda e: e.dma_start(out=dst, in_=src, **kw), w=[key], chan=f"const{ldn[0]}")

    ld(ident_f[:, :], c_ident, ("identf",)); ld(triu[:, :], c_triu, ("triu",)); ld(strl[:, :], c_strl, ("strl",))
    P.op("pool", lambda e: e.dma_start(out=bcur[:, :], in_=c_bcur), w=[("bcur",)], chan="constb1")
    P.op("pool", lambda e: e.dma_start(out=bprev[:, :], in_=c_bprev), w=[("bprev",)], chan="constb2")
    ld(tok0[:, :], c_tok0, ("tok0",)); ld(hp_t[:, :], hasprev, ("hp",)); ld(qm[:, :], c_qm, ("qm",)); ld(col0[:, :], c_col0, ("col0",))
    for i, g in enumerate((g1, g2, g3)):
        ld(gT[i][:, :], g.rearrange("(c p) -> p c", p=128), ("gT", i), allow_slow_non_contiguous=True)
    ld(gfin_b[:, :], gfin.partition_broadcast(128), ("gfin",))
    P.op("sp", lambda e: [e.dma_start(out=cw[:, j, :], in_=convw[j].rearrange("(m p) -> p m", p=128), allow_slow_non_contiguous=True) for j in range(4)], w=[("cw",)], chan="constcw", nd=4)
    ld(esink[:, :], sinks.partition_broadcast(128), ("esink",))
    ld(negA[:, :], alog.partition_broadcast(128), ("negA",))
    ld(dtb_b[:, :], dtb.partition_broadcast(128), ("dtb",))
    ld(gng_t[:, :], gng.rearrange("(p o) -> p o", o=1), ("gng",))
    ld(wtm[:, :, :], wtm_s, ("wtm",))
    P.ops[-1]["deps"].add(P.lastw[("castgrp", "m")])
    ld(wmo_t[:, :, :], wmo_s, ("wmo",))
    P.ops[-1]["deps"].add(P.lastw[("castgrp", "m")])
    P.op("dve", lambda e: e.tensor_copy(ident[:, :], ident_f[:, :]), r=[("identf",)], w=[("ident",)])
    P.op("dve", lambda e: e.memset(ones_f[:, :], 1.0), w=[("ones_f",)])
    P.op("dve", lambda e: e.memset(ones_b[:, :], 1.0), w=[("ones_b",)])
    P.op("act", lambda e: e.activation(out=esink[:, :], in_=esink[:, :], func=AF.Exp), r=[("esink",)], w=[("esink",)])
    P.op("act", lambda e: e.activation(out=negA[:, :], in_=negA[:, :], func=AF.Exp), r=[("negA",)], w=[("negA",)])
    P.op("dve", lambda e: e.tensor_scalar(negA[:, :], negA[:, :], -1.0, 0.0, ALU.mult, ALU.add), r=[("negA",)], w=[("negA",)])

    def rstd_cols(n, np_, scale, keyss):
        P.op("act", lambda e: e.activation(out=ss[0:np_, 0:n], in_=ss[0:np_, 0:n], func=AF.Ln, scale=scale, bias=EPS),
             r=keyss, w=keyss)
        P.op("act", lambda e: e.activation(out=ss[0:np_, 0:n], in_=ss[0:np_, 0:n], func=AF.Exp, scale=-0.5),
             r=keyss, w=keyss)

    def norm_T(xsrc, xkeys, nblk, np_, gi, dstT, dkeys):
        keyss = [("ss",)]
        for b in range(nblk):
            P.op("act", lambda e, b=b: e.activation(out=junk[0:np_, :], in_=xsrc(b), func=AF.Square, accum_out=ss[0:np_, b:b + 1]),
                 r=[xkeys[b]], w=[MK("ctmp")] + keyss)
        rstd_cols(nblk, np_, 1.0 / D, keyss)
        for b in range(nblk):
            xn = xn2[b % 2]
            xnk = ("xn", b % 2)
            P.op("dve", lambda e, b=b, xn=xn: e.tensor_scalar(xn[0:np_, :], xsrc(b), ss[0:np_, b:b + 1], 1.0, ALU.mult, ALU.mult),
                 r=keyss + [xkeys[b]], w=[xnk])
            k = nb()
            pst = ps[:, k, :].bitcast(BF16)
            for c in range(8):
                P.op("pe", lambda e, c=c, pst=pst, xn=xn: e.transpose(pst[:, c * 128:c * 128 + np_], xn[0:np_, c * 128:(c + 1) * 128], ident[0:np_, 0:np_]),
                     r=[xnk, ("ident",)], w=[("ps", k)])
            P.op("dve", lambda e, b=b, pst=pst: e.tensor_tensor(
                dstT[:, :, b * np_:(b + 1) * np_], pst.rearrange("p (c t) -> p c t", c=8)[:, :, 0:np_],
                bc(gT[gi][:, :], [128, 8, np_], 2), ALU.mult),
                r=[("ps", k), ("gT", gi)], w=[dkeys[b]])

    pref = {"on": False}

    def ffn_prefetch(fi):
        for j in range(3):
            P.op("sp", lambda e, j=j: e.dma_start(out=wi[j][:, :, :], in_=wi_s[fi][j]),
                 r=[("castgrp", fi)], w=[("wi", j)], chan=f"wi{j}")
        pref["on"] = True

    def ffn(fi, srcT, skeys, ntok, xdst, xkeys, nblk, np_):
        hkeys = [("hT", j) for j in range(NJ)]
        skip_first = pref["on"]
        pref["on"] = False
        P.op("sp", lambda e: e.dma_start(out=wo, in_=wo_s[fi]), r=[("wo_s", fi), ("castgrp", fi)], w=[("wo",)], chan="wo")
        for j in range(NJ):
            s = j % 3
            if not (skip_first and j < 3):
                P.op("sp", lambda e, j=j, s=s: e.dma_start(out=wi[s][:, :, :], in_=wi_s[fi][j]),
                     r=[("wi_s", fi, j), ("castgrp", fi)], w=[("wi", s)], chan=f"wi{s}")
            kg, ku = nb(), nb()
            for c in range(8):
                P.op("pe", lambda e, c=c, s=s, kg=kg: e.matmul(ps[:, kg, 0:ntok], wi[s][:, c, 0:128], srcT[:, c, 0:ntok], start=(c == 0), stop=(c == 7)),
                     r=[("wi", s)] + skeys, w=[("ps", kg)])
            for c in range(8):
                P.op("pe", lambda e, c=c, s=s, ku=ku: e.matmul(ps[:, ku, 0:ntok], wi[s][:, c, 128:256], srcT[:, c, 0:ntok], start=(c == 0), stop=(c == 7)),
                     r=[("wi", s)] + skeys, w=[("ps", ku)])
            q = 0
            P.op("act", lambda e, kg=kg, q=q: e.activation(out=sg[q][:, 0:ntok], in_=ps[:, kg, 0:ntok], func=AF.Silu),
                 r=[("ps", kg)], w=[("sg", q)])
            P.op("dve", lambda e, ku=ku, q=q, j=j: e.tensor_tensor(hT[:, j, 0:ntok], sg[q][:, 0:ntok], ps[:, ku, 0:ntok], ALU.mult),
                 r=[("ps", ku), ("sg", q)], w=[hkeys[j]])
        for b in range(nblk):
            k0, k1 = nb(), nb()
            for j in range(NJ):
                for dh, kk in ((0, k0), (1, k1)):
                    P.op("pe", lambda e, j=j, dh=dh, kk=kk, b=b: e.matmul(ps[0:np_, kk, :], hT[:, j, b * np_:(b + 1) * np_], wo[:, j, dh * 512:(dh + 1) * 512], start=(j == 0), stop=(j == NJ - 1)),
                         r=[hkeys[j], ("wo",)], w=[("ps", kk)])
            for dh, kk in ((0, k0), (1, k1)):
                P.op("dve", lambda e, dh=dh, kk=kk, b=b: e.scalar_tensor_tensor(xdst(b, dh), ps[0:np_, kk, :], 0.5, xdst(b, dh), ALU.mult, ALU.add),
                     r=[("ps", kk)], w=[xkeys[b]])

    def fence():
        ks = P.allkeys(lambda k: k[0] in ("hT", "wo") or str(k[0]).startswith("m_"))
        P.op("pool", lambda e: e.memset(sm[:, 63:64], 0.0), w=ks + [("fence",)])


    def mix(srcT, skeys, sample, tix, first_tile, last_tile, xres, xkeys, prepass=False, pre_last=False, mask_prev=False):
        bs = 131 if sample else 128

        def kprev(kv, b):
            return kTd[kv][:, 2 * b, :] if sample else kTd[kv][:, b, :]

        def kcur(kv, b):
            return kTd[kv][:, 2 * b + 1, :] if sample else kTd[kv][:, b + 1, :]

        def kix(b):
            return (2 * b, 2 * b + 1) if sample else (b, b + 1)

        KLIM = KLIMS if sample else KLIMP
        if KLIM < 0.5:
            return
        PQ = (lambda *a, **k: None) if prepass else P.op
        PD = (lambda *a, **k: None) if sample else P.op
        if sample:
            for h in range(4):
                P.op("pool", lambda e, h=h: e.tensor_copy(Nb[:, h, :], ident[:, :]), r=[("ident",)], w=[MK("N")])
        slots = range(11) if not prepass else (range(2, 9) if pre_last else range(5, 9))
        for s in slots:
            sl = s % 3
            P.op("sp", lambda e, s=s, sl=sl: e.dma_start(out=wi[sl][:, :, :], in_=wm_s[s]),
                 r=[("wm_s", s), ("castgrp", "m")], w=[("wi", sl)], chan=f"wi{sl}")
            if 3 <= s <= 8 and (sample or last_tile):
                kq = nb()
                lt = srcT[:, :, :].rearrange("p c (b t) -> p c b t", b=4)[:, :, :, 0] if sample else srcT[:, :, 509:512]
                mrows = 4 if sample else 3
                for c in range(8):
                    P.op("pe", lambda e, c=c, sl=sl, kq=kq, lt=lt, mrows=mrows: e.matmul(ps[0:mrows, kq, 0:256], lt[:, c, :], wi[sl][:, c, :], start=(c == 0), stop=(c == 7)),
                         r=[("wi", sl)] + skeys, w=[("ps", kq)])
                P.op("dve", lambda e, kq=kq, s=s, mrows=mrows: e.tensor_copy(ncrow[0:mrows, (s - 3) * 256:(s - 2) * 256], ps[0:mrows, kq, 0:256]), r=[("ps", kq)], w=[MK("ncrow")])
                if s == 8:
                    if sample:
                        P.op("sp", lambda e: e.dma_start(out=ncs[tix * 4:(tix + 1) * 4, 2, :], in_=ncrow[0:4, :]), r=[MK("ncrow")], w=[("o_ncs", tix)], chan="onc")
                    else:
                        P.op("sp", lambda e: e.dma_start(out=ncp, in_=ncrow[0:3, :]), r=[MK("ncrow")], w=[("o_ncp",)], chan="onc")
            for g in range(2):
                gi = 2 * s + g
                k = nb()
                for c in range(8):
                    P.op("pe", lambda e, c=c, sl=sl, g=g, k=k: e.matmul(ps[:, k, :], wi[sl][:, c, g * 128:(g + 1) * 128], srcT[:, c, :], start=(c == 0), stop=(c == 7)),
                         r=[("wi", sl)] + skeys, w=[("ps", k)])
                if gi < 4:
                    P.op("act", lambda e, k=k, gi=gi: e.activation(out=qT[:, gi, :], in_=ps[:, k, :], func=AF.Identity, scale=qm[:, 0:1]),
                         r=[("ps", k), ("qm",)], w=[MK("qT", gi)])
                    P.op("dve", lambda e, k=k, gi=gi: e.tensor_scalar(qT2[:, gi, :], ps[:, k, :], qm[:, 1:2], 0.0, ALU.mult, ALU.add),
                         r=[("ps", k), ("qm",)], w=[MK("qT", gi)])
                elif gi < 6:
                    kv = gi - 4
                    if sample:
                        dst = kTd[kv][:, :, :].rearrange("p (b two) t -> p b two t", two=2)[:, :, 1, :]
                    else:
                        dst = kTd[kv][:, 1:5, :]
                    P.op("act", lambda e, k=k, dst=dst: e.activation(out=dst, in_=ps[:, k, :].rearrange("p (b t) -> p b t", b=4), func=AF.Identity),
                         r=[("ps", k)], w=[MK("kTd", kv, i) for i in ((1, 3, 5, 7) if sample else (1, 2, 3, 4))])
                elif gi < 18:
                    m = gi - 6
                    dst = pre[:, m, 0:4 * bs].rearrange("p (b t) -> p b t", b=4)[:, :, 3:131] if sample else None
                    if sample:
                        P.op("act", lambda e, k=k, dst=dst: e.activation(out=dst, in_=ps[:, k, :].rearrange("p (b t) -> p b t", b=4), func=AF.Identity),
                             r=[("ps", k)], w=[MK("pre", m)])
                    else:
                        P.op("act", lambda e, k=k, m=m: e.activation(out=pre[:, m, 3:515], in_=ps[:, k, :], func=AF.Identity),
                             r=[("ps", k)], w=[MK("pre", m)])
                else:
                    h = gi - 18
                    P.op("act", lambda e, k=k, h=h: e.activation(out=zs[:, h, :], in_=ps[:, k, :], func=AF.Silu),
                         r=[("ps", k)], w=[MK("zs", h)])
        if KLIM < 1:
            return
        for b in range(4):
            k = nb()
            for c in range(8):
                P.op("pe", lambda e, c=c, k=k, b=b: e.matmul(ps[:, k, 0:264], srcT[:, c, b * 128:(b + 1) * 128], wtm[:, c, :], start=(c == 0), stop=(c == 7)),
                     r=[("wtm",)] + skeys, w=[("ps", k)])
            pi, ci = kix(b)
            for kv in range(2):
                P.op("act", lambda e, k=k, kv=kv, ci=ci: e.activation(out=Vd[kv][:, ci, 0:64], in_=ps[:, k, kv * 64:(kv + 1) * 64], func=AF.Identity),
                     r=[("ps", k)], w=[MK("Vd", kv, ci)])
                P.op("dve", lambda e, k=k, kv=kv, ci=ci: e.tensor_copy(Vd[kv][:, ci, 64:128], ps[:, k, kv * 64:(kv + 1) * 64]),
                     r=[("ps", k)], w=[MK("Vd", kv, ci)])
            P.op("dve", lambda e, k=k, b=b: e.tensor_copy(gbga[:, b, :], ps[:, k, 256:264]), r=[("ps", k)], w=[MK("gbga", b)])
            want_kv = sample or (last_tile and b == 3)
            if want_kv:
                P.op("dve", lambda e, k=k: e.tensor_copy(kvo[:, :], ps[:, k, 0:256]), r=[("ps", k)], w=[MK("kvo")])
                if sample:
                    sq = tix * 4 + b
                    def f(e, sq=sq):
                        return [e.dma_start(out=nvs[sq, 127:128, :], in_=kvo[0:1, 0:128]),
                                e.dma_start(out=nks[sq, 127:128, :], in_=kvo[0:1, 128:256])]
                    P.op("sp", f, r=[MK("kvo")], w=[("o_kvs", sq)], chan="okv", nd=2)
                else:
                    def f(e):
                        return [e.dma_start(out=nvp, in_=kvo[:, 0:128]), e.dma_start(out=nkp, in_=kvo[:, 128:256])]
                    P.op("sp", f, r=[MK("kvo")], w=[("o_kvp",)], chan="okv", nd=2)
        if KLIM < 2:
            return
        if sample:
            for b in range(4):
                sq = tix * 4 + b
                P.op("sp", lambda e, sq=sq: [e.dma_start(out=ckt[:, :], in_=ck[sq]), e.dma_start(out=cvt[:, :], in_=cv[sq])],
                     w=[MK("ckt"), MK("cvt")], chan="ckld", nd=2)
                for kv in range(2):
                    k = nb()
                    for q in range(2):
                        P.op("dve", lambda e, kv=kv, q=q: e.tensor_copy(ckd[:, kv, q, :], ckt[:, kv * 64:(kv + 1) * 64]), r=[MK("ckt")], w=[MK("ckd", kv)])
                    P.op("pe", lambda e, k=k, kv=kv: e.transpose(ps[:, k, 0:128], ckd[:, kv, :, :].rearrange("p a d -> p (a d)"), ident_f[:, :]),
                         r=[MK("ckd", kv), ("identf",)], w=[("ps", k)])
                    P.op("act", lambda e, k=k, kv=kv, b=b: e.activation(out=kTd[kv][:, 2 * b, :], in_=ps[:, k, 0:128], func=AF.Identity),
                         r=[("ps", k)], w=[MK("kTd", kv, 2 * b)])
                    for q in range(2):
                        P.op("dve", lambda e, kv=kv, b=b, q=q: e.tensor_copy(Vd[kv][:, 2 * b, q * 64:(q + 1) * 64], cvt[:, kv * 64:(kv + 1) * 64]),
                             r=[MK("cvt")], w=[MK("Vd", kv, 2 * b)])
            if tix == 0:
                P.op("sp", lambda e: e.dma_start(out=sct[:, :], in_=sconv), w=[("sct",)], chan="sctld")
            for m in range(12):
                k = nb()
                P.op("pe", lambda e, k=k, m=m: e.transpose(ps[:, k, 0:NS * 3], sct[:, m * 128:(m + 1) * 128], ident_f[0:NS * 3, 0:NS * 3]),
                     r=[("sct",), ("identf",)], w=[("ps", k)])
                P.op("dve", lambda e, k=k, m=m: e.tensor_copy(
                    pre[:, m, 0:4 * bs].rearrange("p (b t) -> p b t", b=4)[:, :, 0:3],
                    ps[:, k, tix * 12: tix * 12 + 12].rearrange("p (b r) -> p b r", b=4)),
                    r=[("ps", k)], w=[MK("pre", m)])
        elif not first_tile:
            P.op("pool", lambda e: e.tensor_copy(pre[:, :, 0:3], ccar[:, :, :]), r=[("ccar",)], w=[MK("pre", m) for m in range(12)])
        if (not sample) and first_tile:
            for kv in range(2):
                P.op("pool", lambda e, kv=kv: e.memset(kTd[kv][:, 0, :], 0.0), w=[MK("kTd", kv, 0)])
                P.op("pool", lambda e, kv=kv: e.memset(Vd[kv][:, 0, :], 0.0), w=[MK("Vd", kv, 0)])
            P.op("pool", lambda e: e.memset(pre[:, :, 0:3], 0.0), w=[MK("pre", m) for m in range(12)])
            P.op("pool", lambda e: e.memset(ccar[:, :, :], 0.0), w=[("ccar",)])
            P.op("pool", lambda e: e.memset(Sst[:, :, :], 0.0), w=[MK("S")])
            P.op("pool", lambda e: e.memset(Sbf[:, :, :], 0.0), w=[MK("Sbf")])

        if KLIM < 3:
            return
        for b in (range(4) if not prepass else ()):
            pi, ci = kix(b)
            has_prev = sample or not (first_tile and b == 0)
            kc = [nb(), nb()]
            for h in range(8):
                kv, p, hh = h // 4, h // 2, h % 2
                P.op("pe", lambda e, h=h, kv=kv, p=p, hh=hh, ci=ci, b=b, kc=kc: e.matmul(
                    ps[:, kc[h // 4], (h % 4) * 128:(h % 4 + 1) * 128], kTd[kv][:, ci, :],
                    (qT2 if hh else qT)[:, p, b * 128:(b + 1) * 128], start=True, stop=True),
                    r=[MK("kTd", kv, ci), MK("qT", p)], w=[("ps", kc[h // 4])])
            for half in range(2):
                P.op("dve", lambda e, half=half, kc=kc: e.tensor_tensor(ebuf[:, half * 512:(half + 1) * 512], ps[:, kc[half], :], bcur[:, half * 512:(half + 1) * 512], ALU.add),
                     r=[("ps", kc[half]), ("bcur",)], w=[MK("tC" if half == 0 else "tD")])
            P.op("act", lambda e: e.activation(out=Pc[:, :, :].rearrange("p h t -> p (h t)"), in_=ebuf, func=AF.Exp),
                 r=[MK("tC"), MK("tD")], w=[MK("Pc")])
            if has_prev:
                kp = [nb(), nb()]
                for h in range(8):
                    kv, p, hh = h // 4, h // 2, h % 2
                    P.op("pe", lambda e, h=h, kv=kv, p=p, hh=hh, pi=pi, b=b, kp=kp: e.matmul(
                        ps[:, kp[h // 4], (h % 4) * 128:(h % 4 + 1) * 128], kTd[kv][:, pi, :],
                        (qT2 if hh else qT)[:, p, b * 128:(b + 1) * 128], start=True, stop=True),
                        r=[MK("kTd", kv, pi), MK("qT", p)], w=[("ps", kp[h // 4])])
                for half in range(2):
                    P.op("dve", lambda e, half=half, kp=kp: e.tensor_tensor(ebuf[:, half * 512:(half + 1) * 512], ps[:, kp[half], :], bprev[:, half * 512:(half + 1) * 512], ALU.add),
                         r=[("ps", kp[half]), ("bprev",)], w=[MK("tC" if half == 0 else "tD")])
                P.op("act", lambda e: e.activation(out=Pp[:, :, :].rearrange("p h t -> p (h t)"), in_=ebuf, func=AF.Exp),
                     r=[MK("tC"), MK("tD")], w=[MK("Pp")])
                if mask_prev and b == 0:
                    P.op("dve", lambda e: e.tensor_scalar(Pp[:, :, :].rearrange("p h t -> p (h t)"), Pp[:, :, :].rearrange("p h t -> p (h t)"), hp_t[:, 0:1], 0.0, ALU.mult, ALU.add),
                         r=[MK("Pp"), ("hp",)], w=[MK("Pp")])
            for kv in range(2):
                kn, kdn = nb(), nb()
                pcs = Pc[:, kv * 4:(kv + 1) * 4, :].rearrange("p h t -> p (h t)")
                pps = Pp[:, kv * 4:(kv + 1) * 4, :].rearrange("p h t -> p (h t)")
                P.op("pe", lambda e, kn=kn, kv=kv, ci=ci, pcs=pcs, hp=has_prev: e.matmul(ps[:, kn, :], Vd[kv][:, ci, :], pcs, start=True, stop=not hp),
                     r=[MK("Vd", kv, ci), MK("Pc")], w=[("ps", kn)])
                if has_prev:
                    P.op("pe", lambda e, kn=kn, kv=kv, pi=pi, pps=pps: e.matmul(ps[:, kn, :], Vd[kv][:, pi, :], pps, start=False, stop=True),
                         r=[MK("Vd", kv, pi), MK("Pp")], w=[("ps", kn)])
                P.op("pe", lambda e, kdn=kdn, pcs=pcs, hp=has_prev: e.matmul(ps[:, kdn, :], ones_b[:, :], pcs, start=True, stop=not hp),
                     r=[("ones_b",), MK("Pc")], w=[("ps", kdn)])
                if has_prev:
                    P.op("pe", lambda e, kdn=kdn, pps=pps: e.matmul(ps[:, kdn, :], ones_b[:, :], pps, start=False, stop=True),
                         r=[("ones_b",), MK("Pp")], w=[("ps", kdn)])
                P.op("dve", lambda e, kdn=kdn, kv=kv: e.tensor_tensor(rbuf.rearrange("p (h t) -> p h t", h=4), ps[:, kdn, :].rearrange("p (h t) -> p h t", h=4),
                                                                      bc(esink[:, kv * 4:(kv + 1) * 4], [128, 4, 128], 2), ALU.add),
                     r=[("ps", kdn), ("esink",)], w=[MK("tB")])
                P.op("dve", lambda e: e.reciprocal(rbuf, rbuf), r=[MK("tB")], w=[MK("tB")])
                nv = ps[:, kn, :].rearrange("p (m q t) -> p m q t", m=2, q=2)
                rv = rbuf.rearrange("p (m q t) -> p m q t", m=2, q=2)
                te = tA.rearrange("p (m t) -> p m t", m=4)[:, 0:2, :]
                to = tA.rearrange("p (m t) -> p m t", m=4)[:, 2:4, :]
                P.op("dve", lambda e, nv=nv, rv=rv, te=te: e.tensor_tensor(te, nv[:, :, 0, :], rv[:, :, 0, :], ALU.mult), r=[("ps", kn), MK("tB")], w=[MK("tA")])
                P.op("dve", lambda e, nv=nv, rv=rv, to=to: e.tensor_tensor(to, nv[:, :, 1, :], rv[:, :, 1, :], ALU.mult), r=[("ps", kn), MK("tB")], w=[MK("tA")])
                P.op("dve", lambda e, te=te: e.tensor_scalar(te, te, qm[:, 2:3], 0.0, ALU.mult, ALU.add), r=[MK("tA"), ("qm",)], w=[MK("tA")])
                P.op("dve", lambda e, kv=kv, b=b, te=te, to=to: e.scalar_tensor_tensor(aoT[:, kv * 2:(kv + 1) * 2, b * 128:(b + 1) * 128], to, qm[:, 3:4], te, ALU.mult, ALU.add),
                     r=[MK("tA"), ("qm",)], w=[MK("aoT", b)])
        if (not sample) and (not prepass or pre_last):
            for kv in range(2):
                P.op("pool", lambda e, kv=kv: e.tensor_copy(kTd[kv][:, 0, :], kTd[kv][:, 4, :]), r=[MK("kTd", kv, 4)], w=[MK("kTd", kv, 0)])
                P.op("pool", lambda e, kv=kv: e.tensor_copy(Vd[kv][:, 0, :], Vd[kv][:, 4, :]), r=[MK("Vd", kv, 4)], w=[MK("Vd", kv, 0)])

        if KLIM < 4:
            return
        if KLIM < 5:
            return
        def pv(m, j):
            if sample:
                return pre[:, m, 0:4 * bs].rearrange("p (b t) -> p b t", b=4)[:, :, j:j + 128]
            return pre[:, m, j:j + 512].rearrange("p (b t) -> p b t", b=4)
        cbufs = [(ctmp[:, :], MK("ctmp")), (tA, MK("tA")), (tE, MK("tE"))]
        for ci_, m in enumerate(range(12) if (not prepass or pre_last) else range(4, 12)):
            cb, ck_ = cbufs[ci_ % 3]
            ct3 = cb.rearrange("p (b t) -> p b t", b=4)
            P.op("pool", lambda e, m=m, ct3=ct3: e.tensor_scalar(ct3, pv(m, 0), cw[:, 0, m:m + 1], 0.0, ALU.mult, ALU.add), r=[MK("pre", m), ("cw",)], w=[ck_])
            for j in (1, 2, 3):
                P.op("dve", lambda e, m=m, j=j, ct3=ct3: e.scalar_tensor_tensor(ct3, pv(m, j), cw[:, j, m:m + 1], ct3, ALU.mult, ALU.add),
                     r=[MK("pre", m), ("cw",), ck_], w=[ck_])
            if not sample:
                P.op("pool", lambda e, m=m: e.tensor_copy(ccar[:, m, :], pre[:, m, 512:515]), r=[MK("pre", m)], w=[("ccar",)])
            P.op("act", lambda e, m=m, cb=cb: e.activation(out=pre[:, m, 3:515], in_=cb, func=AF.Silu), r=[ck_], w=[MK("pre", m)])
            if sample:
                P.op("pool", lambda e, m=m: e.tensor_tensor(pre[:, m, 3:515], pre[:, m, 3:515], col0[:, :], ALU.mult), r=[MK("pre", m), ("col0",)], w=[MK("pre", m)])
        cact = lambda m: pre[:, m, 3:515]
        if KLIM < 6:
            return
        for li, m in enumerate(range(8) if not prepass else range(4, 8)):
            sqb, sqk = ((ctmp[:, :], MK("ctmp")), (tE, MK("tE")))[li % 2]
            rsb, rsk = ((tA, MK("tA")), (tB, MK("tB")))[li % 2]
            P.op("pool", lambda e, m=m, sqb=sqb: e.tensor_tensor(sqb, cact(m), cact(m), ALU.mult), r=[MK("pre", m)], w=[sqk])
            k = nb()
            P.op("pe", lambda e, k=k, sqb=sqb: e.matmul(ps[:, k, :], ones_f[:, :], sqb, start=True, stop=True), r=[("ones_f",), sqk], w=[("ps", k)])
            P.op("act", lambda e, k=k, rsb=rsb: e.activation(out=rsb, in_=ps[:, k, :], func=AF.Ln, bias=EPS), r=[("ps", k)], w=[rsk])
            P.op("act", lambda e, rsb=rsb: e.activation(out=rsb, in_=rsb, func=AF.Exp, scale=-0.5), r=[rsk], w=[rsk])
            dst = gqn[:, m, :] if m < 4 else gkn[:, m - 4, :]
            sc = 128.0 ** -0.5 if m < 4 else 1.0
            P.op("dve", lambda e, m=m, dst=dst, sc=sc, rsb=rsb: e.scalar_tensor_tensor(dst, cact(m), sc, rsb, ALU.mult, ALU.mult),
                 r=[MK("pre", m), rsk], w=[MK("gqn" if m < 4 else "gkn", m % 4)])
        if KLIM < 7:
            return
        gb_v = gbga[:, :, 0:4]
        ga_v = gbga[:, :, 4:8]
        be = sm[:, 0:16].rearrange("p (b h) -> p b h", b=4)
        gg = sm[:, 16:32].rearrange("p (b h) -> p b h", b=4)
        t1 = sm[:, 32:48].rearrange("p (b h) -> p b h", b=4)
        gkeys = [MK("gbga", b) for b in range(4)]
        P.op("act", lambda e: e.activation(out=be, in_=gb_v, func=AF.Exp, scale=-1.0), r=gkeys, w=[MK("be")])
        P.op("dve", lambda e: e.tensor_scalar(be, be, 1.0, 1.0, ALU.add, ALU.mult), r=[MK("be")], w=[MK("be")])
        P.op("dve", lambda e: e.reciprocal(be, be), r=[MK("be")], w=[MK("be")])
        P.op("dve", lambda e: e.tensor_tensor(gg, ga_v, bc(dtb_b[:, :], [128, 4, 4], 1), ALU.add), r=gkeys + [("dtb",)], w=[MK("gg")])
        P.op("dve", lambda e: e.tensor_scalar(t1, gg, -1.0, 0.0, ALU.mult, ALU.add), r=[MK("gg")], w=[MK("t1")])
        P.op("dve", lambda e: e.tensor_tensor(t1, t1, gg, ALU.max), r=[MK("gg"), MK("t1")], w=[MK("t1")])
        P.op("act", lambda e: e.activation(out=t1, in_=t1, func=AF.Exp, scale=-1.0), r=[MK("t1")], w=[MK("t1")])
        P.op("act", lambda e: e.activation(out=t1, in_=t1, func=AF.Ln, bias=1.0), r=[MK("t1")], w=[MK("t1")])
        P.op("dve", lambda e: e.scalar_tensor_tensor(gg, gg, 0.0, t1, ALU.max, ALU.add), r=[MK("gg"), MK("t1")], w=[MK("gg")])
        P.op("dve", lambda e: e.tensor_tensor(gg, gg, bc(negA[:, :], [128, 4, 4], 1), ALU.mult), r=[MK("gg"), ("negA",)], w=[MK("gg")])
        if sample:
            P.op("dve", lambda e: e.tensor_scalar(gg, gg, tok0[:, 0:1], 0.0, ALU.mult, ALU.add), r=[MK("gg"), ("tok0",)], w=[MK("gg")])

        if KLIM < 8:
            return
        v4 = lambda ap: ap.rearrange("p (h t) -> p h t", h=4)
        for b in range(4):
            cols = slice(b * 128, (b + 1) * 128)
            gcc = sm[:, 48:52]
            gl = sm[:, 52:56]
            s1 = sm[:, 56:60]
            s2 = sm[:, 60:63]
            k = nb()
            P.op("pe", lambda e, k=k, b=b: e.matmul(ps[:, k, 0:4], triu[:, :], gg[:, b, :], start=True, stop=True), r=[("triu",), MK("gg")], w=[("ps", k)])
            P.op("dve", lambda e, k=k: e.tensor_copy(gcc, ps[:, k, 0:4]), r=[("ps", k)], w=[MK("gcc")])
            P.op("dve", lambda e, b=b: e.tensor_tensor(v4(tA), bc(triu[:, :], [128, 4, 128], 1), bc(gg[:, b, :], [128, 4, 128], 2), ALU.mult),
                 r=[("triu",), MK("gg")], w=[MK("tA")])
            kr = nb()
            P.op("pe", lambda e, kr=kr: e.matmul(ps[:, kr, :], ones_f[:, :], tA, start=True, stop=True), r=[("ones_f",), MK("tA")], w=[("ps", kr)])
            P.op("dve", lambda e, kr=kr: e.tensor_tensor(v4(tB), v4(ps[:, kr, :]), bc(gcc, [128, 4, 128], 2), ALU.subtract), r=[("ps", kr), MK("gcc")], w=[MK("tB")])
            PQ("dve", lambda e: e.tensor_scalar(tC, tB, 0.0, 0.0, ALU.min, ALU.add), r=[MK("tB")], w=[MK("tC")])
            PQ("act", lambda e: e.activation(out=tC, in_=tC, func=AF.Exp), r=[MK("tC")], w=[MK("tC")])
            PQ("dve", lambda e: e.tensor_tensor(v4(tC), v4(tC), bc(triu[:, :], [128, 4, 128], 1), ALU.mult), r=[MK("tC"), ("triu",)], w=[MK("tC")])
            PD("dve", lambda e: e.tensor_scalar(tD, tB, 0.0, 0.0, ALU.max, ALU.add), r=[MK("tB")], w=[MK("tD")])
            PD("act", lambda e: e.activation(out=tD, in_=tD, func=AF.Exp, scale=-1.0), r=[MK("tD")], w=[MK("tD")])
            PD("dve", lambda e: e.tensor_tensor(v4(tD), v4(tD), bc(strl[:, :], [128, 4, 128], 1), ALU.mult), r=[MK("tD"), ("strl",)], w=[MK("tD")])
            PQ("act", lambda e, kr=kr: e.activation(out=tE, in_=ps[:, kr, :], func=AF.Exp), r=[("ps", kr)], w=[MK("tE")])
            P.op("act", lambda e, kr=kr: e.activation(out=gl, in_=v4(ps[:, kr, :])[:, :, 127], func=AF.Exp), r=[("ps", kr)], w=[MK("gl")])
            P.op("dve", lambda e, kr=kr: e.tensor_tensor(s1, v4(ps[:, kr, :])[:, :, 127], gcc, ALU.subtract), r=[("ps", kr), MK("gcc")], w=[MK("s1")])
            P.op("act", lambda e: e.activation(out=s1, in_=s1, func=AF.Exp), r=[MK("s1")], w=[MK("s1")])
            P.op("act", lambda e: e.activation(out=gcc, in_=gcc, func=AF.Exp), r=[MK("gcc")], w=[MK("gcc")])
            P.op("dve", lambda e, b=b: e.tensor_tensor(gcc, gcc, be[:, b, :], ALU.mult), r=[MK("gcc"), MK("be")], w=[MK("gcc")])
            kt = nb()
            ktb = ps[:, kt, :].bitcast(BF16)
            for h in range(4):
                P.op("pe", lambda e, h=h, ktb=ktb, cols=cols: e.transpose(ktb[:, h * 128:(h + 1) * 128], gkn[:, h, cols], ident[:, :]),
                     r=[MK("gkn", h), ("ident",)], w=[("ps", kt)])
            P.op("dve", lambda e, ktb=ktb: e.tensor_tensor(kb[:, :, :], v4(ktb[:, 0:512]), bc(gcc, [128, 4, 128], 2), ALU.mult), r=[("ps", kt), MK("gcc")], w=[MK("kb")])
            P.op("dve", lambda e, ktb=ktb: e.tensor_tensor(kd[:, :, :], v4(ktb[:, 0:512]), bc(s1, [128, 4, 128], 2), ALU.mult), r=[("ps", kt), MK("s1")], w=[MK("kd")])
            kvv = nb()
            for h in range(4):
                P.op("pe", lambda e, h=h, kvv=kvv, cols=cols: e.transpose(ps[:, kvv, h * 128:(h + 1) * 128], cact(8 + h)[:, cols], ident_f[:, :]),
                     r=[MK("pre", 8 + h), ("identf",)], w=[("ps", kvv)])
            P.op("dve", lambda e, kvv=kvv, b=b: e.tensor_tensor(vb[:, :, :], v4(ps[:, kvv, :]), bc(be[:, b, :], [128, 4, 128], 2), ALU.mult), r=[("ps", kvv), MK("be")], w=[MK("vb")])
            kkk, kqk = nb(), nb()
            for h in range(4):
                PD("pe", lambda e, h=h, kkk=kkk, cols=cols: e.matmul(ps[:, kkk, h * 128:(h + 1) * 128], gkn[:, h, cols], gkn[:, h, cols], start=True, stop=True),
                     r=[MK("gkn", h)], w=[("ps", kkk)])
            for h in range(4):
                PQ("pe", lambda e, h=h, kqk=kqk, cols=cols: e.matmul(ps[:, kqk, h * 128:(h + 1) * 128], gkn[:, h, cols], gqn[:, h, cols], start=True, stop=True),
                     r=[MK("gkn", h), MK("gqn", h)], w=[("ps", kqk)])
            PD("dve", lambda e, kkk=kkk: e.tensor_tensor(tD, ps[:, kkk, :], tD, ALU.mult), r=[("ps", kkk), MK("tD")], w=[MK("tD")])
            PD("dve", lambda e, b=b: e.scalar_tensor_tensor(Xb[0][:, :, :], v4(tD), -1.0, bc(be[:, b, :], [128, 4, 128], 2), ALU.mult, ALU.mult),
                 r=[MK("tD"), MK("be")], w=[MK("X", 0)])
            PQ("dve", lambda e, kqk=kqk: e.tensor_tensor(qkT[:, :, :], v4(ps[:, kqk, :]), v4(tC), ALU.mult), r=[("ps", kqk), MK("tC")], w=[MK("qkT")])
            PQ("pool", lambda e, cols=cols: e.tensor_tensor(qdT[:, :, :], gqn[:, :, cols], v4(tE), ALU.mult), r=[MK("gqn", h) for h in range(4)] + [MK("tE")], w=[MK("qdT")])
            ky = nb()
            kyb = ps[:, ky, :].bitcast(BF16)
            for h in range(4):
                PD("pe", lambda e, h=h, ky=ky: e.transpose(ps[:, ky, h * 128:(h + 1) * 128], Xb[0][:, h, :], ident_f[:, :]), r=[MK("X", 0), ("identf",)], w=[("ps", ky)])
            PD("act", lambda e, ky=ky: e.activation(out=Yb[0][:, :, :], in_=v4(ps[:, ky, :]), func=AF.Identity), r=[("ps", ky)], w=[MK("Y", 0)])
            PD("dve", lambda e: e.tensor_tensor(Nf, Yb[0][:, :, :], bc(ident_f[:, :], [128, 4, 128], 1), ALU.add), r=[MK("Y", 0), ("identf",)], w=[MK("Nf")])
            cur = 0
            pend = None
            for st in range(1, 7):
                kx, kyy = nb(), nb()
                if st <= 3:
                    nx = 1 - cur
                    Xc, Yc, kXc, kYc = Xb[cur], Yb[cur], MK("X", cur), MK("Y", cur)
                else:
                    hc = (st - 4) % 2
                    Xc, Yc, kXc, kYc = Xh[hc], Yh[hc], MK("Xh", hc), MK("Yh", hc)
                for h in range(4):
                    PD("pe", lambda e, h=h, kx=kx, Xc=Xc, Yc=Yc: e.matmul(ps[:, kx, h * 128:(h + 1) * 128], Yc[:, h, :], Xc[:, h, :], start=True, stop=True),
                         r=[kXc, kYc], w=[("ps", kx)])
                for h in (range(4) if st < 6 else ()):
                    PD("pe", lambda e, h=h, kyy=kyy, Xc=Xc, Yc=Yc: e.matmul(ps[:, kyy, h * 128:(h + 1) * 128], Xc[:, h, :], Yc[:, h, :], start=True, stop=True),
                         r=[kXc, kYc], w=[("ps", kyy)])
                if st <= 2:
                    Xn, kXn, rhsN, kN = Xb[nx], MK("X", nx), Nf, MK("Nf")
                    PD("act", lambda e, kx=kx, Xn=Xn: e.activation(out=Xn[:, :, :], in_=v4(ps[:, kx, :]), func=AF.Identity), r=[("ps", kx)], w=[kXn])
                    PD("dve", lambda e, kyy=kyy, nx=nx: e.tensor_copy(Yb[nx][:, :, :], v4(ps[:, kyy, :])), r=[("ps", kyy)], w=[MK("Y", nx)])
                elif st == 3:
                    Xn, kXn, rhsN, kN = Xb[nx], MK("X", nx), Nf, MK("Nf")
                    PD("act", lambda e, kx=kx, Xn=Xn: e.activation(out=Xn[:, :, :], in_=v4(ps[:, kx, :]), func=AF.Identity), r=[("ps", kx)], w=[kXn])
                    PD("act", lambda e, kx=kx: e.activation(out=Xh[0][:, :, :], in_=v4(ps[:, kx, :]), func=AF.Identity), r=[("ps", kx)], w=[MK("Xh", 0)])
                    PD("dve", lambda e, kyy=kyy: e.tensor_copy(Yh[0][:, :, :], v4(ps[:, kyy, :])), r=[("ps", kyy)], w=[MK("Yh", 0)])
                else:
                    hn = 1 - hc
                    Xn, kXn, rhsN, kN = Xh[hn], MK("Xh", hn), Nb, MK("N")
                    PD("act", lambda e, kx=kx, Xn=Xn: e.activation(out=Xn[:, :, :], in_=v4(ps[:, kx, :]), func=AF.Identity), r=[("ps", kx)], w=[kXn])
                    if st < 6:
                        PD("dve", lambda e, kyy=kyy, hn=hn: e.tensor_copy(Yh[hn][:, :, :], v4(ps[:, kyy, :])), r=[("ps", kyy)], w=[MK("Yh", hn)])
                if pend is not None:
                    pend()

                def mk(st=st, Xn=Xn, kXn=kXn, rhsN=rhsN, kN=kN):
                    def f():
                        kn2 = nb()
                        for h in range(4):
                            PD("pe", lambda e, h=h, kn2=kn2: e.matmul(ps[:, kn2, h * 128:(h + 1) * 128], Xn[:, h, :], rhsN[:, h, :], start=True, stop=True),
                               r=[kXn, kN], w=[("ps", kn2)])
                        PD("dve", lambda e, kn2=kn2: e.tensor_tensor(Nf, Nf, v4(ps[:, kn2, :]), ALU.add), r=[("ps", kn2), MK("Nf")], w=[MK("Nf")])
                        if 3 <= st < 6:
                            PD("act", lambda e: e.activation(out=Nb[:, :, :], in_=Nf, func=AF.Identity), r=[MK("Nf")], w=[MK("N")])
                    return f
                pend = mk()
                if st <= 3:
                    cur = nx
            pend()
            PD("act", lambda e: e.activation(out=Nb[:, :, :], in_=Nf, func=AF.Identity), r=[MK("Nf")], w=[MK("N")])
            ku, kw = nb(), nb()
            for h in range(4):
                P.op("pe", lambda e, h=h, ku=ku: e.matmul(ps[:, ku, h * 128:(h + 1) * 128], Nb[:, h, :], vb[:, h, :], start=True, stop=True), r=[MK("N"), MK("vb")], w=[("ps", ku)])
            for h in range(4):
                P.op("pe", lambda e, h=h, kw=kw: e.matmul(ps[:, kw, h * 128:(h + 1) * 128], kb[:, h, :], Nb[:, h, :], start=True, stop=True), r=[MK("N"), MK("kb")], w=[("ps", kw)])
            P.op("act", lambda e, ku=ku: e.activation(out=u_t, in_=ps[:, ku, :], func=AF.Identity), r=[("ps", ku)], w=[MK("u")])
            P.op("act", lambda e, kw=kw: e.activation(out=wT[:, :, :], in_=v4(ps[:, kw, :]), func=AF.Identity), r=[("ps", kw)], w=[MK("wT")])
            if sample:
                sq = tix * 4 + b
                P.op("sp", lambda e, sq=sq: e.dma_start(out=Sst[:, :, :], in_=sgdn[sq].rearrange("h k v -> k h v")), w=[MK("S")], chan="sld")
                P.op("act", lambda e: e.activation(out=Sbf[:, :, :], in_=Sst[:, :, :], func=AF.Identity), r=[MK("S")], w=[MK("Sbf")])
            k1 = nb()
            for h in range(4):
                P.op("pe", lambda e, h=h, k1=k1: e.matmul(ps[:, k1, h * 128:(h + 1) * 128], wT[:, h, :], Sbf[:, h, :], start=True, stop=True), r=[MK("wT"), MK("Sbf")], w=[("ps", k1)])
            P.op("dve", lambda e, k1=k1: e.tensor_tensor(vnew[:, :, :], v4(u_t), v4(ps[:, k1, :]), ALU.subtract), r=[("ps", k1), MK("u")], w=[MK("vnew")])
            k3, k4 = nb(), nb()
            for h in range(4):
                PQ("pe", lambda e, h=h, k3=k3: e.matmul(ps[:, k3, h * 128:(h + 1) * 128], Sbf[:, h, :], qdT[:, h, :], start=True, stop=False), r=[MK("Sbf"), MK("qdT")], w=[("ps", k3)])
                PQ("pe", lambda e, h=h, k3=k3: e.matmul(ps[:, k3, h * 128:(h + 1) * 128], vnew[:, h, :], qkT[:, h, :], start=False, stop=True), r=[MK("vnew"), MK("qkT")], w=[("ps", k3)])
            for h in range(4):
                P.op("pe", lambda e, h=h, k4=k4: e.matmul(ps[:, k4, h * 128:(h + 1) * 128], kd[:, h, :], vnew[:, h, :], start=True, stop=True), r=[MK("kd"), MK("vnew")], w=[("ps", k4)])
            PQ("act", lambda e, k3=k3, cols=cols: e.activation(out=oT[:, :, cols], in_=v4(ps[:, k3, :]), func=AF.Identity), r=[("ps", k3)], w=[MK("oT", b), MK("ncrow")])
            for h in range(4):
                P.op("dve", lambda e, h=h, k4=k4: e.scalar_tensor_tensor(Sst[:, h, :], Sst[:, h, :], gl[:, h:h + 1], ps[:, k4, h * 128:(h + 1) * 128], ALU.mult, ALU.add),
                     r=[("ps", k4), MK("gl"), MK("S")], w=[MK("S")])
            if sample:
                P.op("sp", lambda e, sq=sq: e.dma_start(out=ngs[sq].rearrange("h k v -> k h v"), in_=Sst[:, :, :]), r=[MK("S")], w=[("o_ngs", sq)], chan="ongs")
            else:
                P.op("act", lambda e: e.activation(out=Sbf[:, :, :], in_=Sst[:, :, :], func=AF.Identity), r=[MK("S")], w=[MK("Sbf")])
                if last_tile and b == 3:
                    P.op("sp", lambda e: e.dma_start(out=ngp.rearrange("h k v -> k h v"), in_=Sst[:, :, :]), r=[MK("S")], w=[("o_ngp",)], chan="ongs")
        if KLIM < 9:
            return
        if prepass:
            return
        for h in range(4):
            P.op("pool", lambda e, h=h: e.tensor_tensor(ctmp[:, :], oT[:, h, :], oT[:, h, :], ALU.mult), r=[MK("oT", b) for b in range(4)], w=[MK("ctmp")])
            k = nb()
            P.op("pe", lambda e, k=k: e.matmul(ps[:, k, :], ones_f[:, :], ctmp[:, :], start=True, stop=True), r=[("ones_f",), MK("ctmp")], w=[("ps", k)])
            P.op("act", lambda e, k=k: e.activation(out=tA, in_=ps[:, k, :], func=AF.Ln, scale=1.0 / 128, bias=EPS), r=[("ps", k)], w=[MK("tA")])
            P.op("act", lambda e: e.activation(out=tA, in_=tA, func=AF.Exp, scale=-0.5), r=[MK("tA")], w=[MK("tA")])
            P.op("dve", lambda e, h=h: e.scalar_tensor_tensor(tA, oT[:, h, :], gng_t[:, 0:1], tA, ALU.mult, ALU.mult), r=[MK("oT", b) for b in range(4)] + [MK("tA"), ("gng",)], w=[MK("tA")])
            P.op("dve", lambda e, h=h: e.tensor_tensor(goT[:, h, :], tA, zs[:, h, :], ALU.mult), r=[MK("tA"), MK("zs", h)], w=[MK("goT", h)])
        if KLIM < 10:
            return
        if sample:
            for t_, src in ((aoTs, aoT), (goTs, goT)):
                P.op("pool", lambda e, t_=t_, src=src: e.tensor_copy(t_[:, :, tix * 4:(tix + 1) * 4], src[:, :, :].rearrange("p c (b t) -> p c b t", b=4)[:, :, :, 0]),
                     r=[MK("aoT", b) for b in range(4)] + [MK("goT", h) for h in range(4)], w=[("mixTs", tix)])
        else:
            for b in range(4):
                k0, k1 = nb(), nb()
                for c in range(8):
                    src = aoT if c < 4 else goT
                    for dh, kk in ((0, k0), (1, k1)):
                        P.op("pe", lambda e, c=c, dh=dh, kk=kk, b=b, src=src: e.matmul(ps[:, kk, :], src[:, c % 4, b * 128:(b + 1) * 128], wmo_t[:, c, dh * 512:(dh + 1) * 512], start=(c == 0), stop=(c == 7)),
                             r=[MK("aoT", b), MK("goT", c % 4), ("wmo",)], w=[("ps", kk)])
                for dh, kk in ((0, k0), (1, k1)):
                    P.op("dve", lambda e, dh=dh, kk=kk, b=b: e.tensor_tensor(xres(b, dh), xres(b, dh), ps[:, kk, :], ALU.add), r=[("ps", kk)], w=[xkeys[b]])

    def final_norm(xsrc, xkeys, nblk, np_, ydst, okey, chan):
        keyss = [("ss",)]
        for b in range(nblk):
            P.op("act", lambda e, b=b: e.activation(out=junk[0:np_, :], in_=xsrc(b), func=AF.Square, accum_out=ss[0:np_, b:b + 1]), r=[xkeys[b]], w=[MK("ctmp")] + keyss)
        rstd_cols(nblk, np_, 1.0 / D, keyss)
        for b in range(nblk):
            P.op("dve", lambda e, b=b: e.scalar_tensor_tensor(xsrc(b), xsrc(b), ss[0:np_, b:b + 1], gfin_b[0:np_, :], ALU.mult, ALU.mult), r=keyss + [xkeys[b], ("gfin",)], w=[xkeys[b]])
            P.op("sp", lambda e, b=b: e.dma_start(out=ydst(b), in_=xsrc(b)), r=[xkeys[b]], w=[(okey, b)], chan=chan)

    skeys = [("xs",)]
    P.op("sp", lambda e: e.dma_start(out=xts[:, 0, :], in_=xs), w=skeys, chan="xsld")
    xs_src = lambda b: xts[0:NS, 0, :]
    xs_dst = lambda b, dh: xts[0:NS, 0, dh * 512:(dh + 1) * 512]
    nTs_keys = [("nTs",)]
    ffn_prefetch(0)
    norm_T(xs_src, skeys, 1, NS, 0, nTs, nTs_keys)
    fence()
    ffn(0, nTs, nTs_keys, NS, xs_dst, skeys, 1, NS)
    xk = [("xnT", b) for b in range(4)]
    if stage >= 2:
        norm_T(xs_src, skeys, 1, NS, 1, nTs, nTs_keys)
        fence()
    for tix in range(4 if stage >= 2 else 0):
        P.op("pool", lambda e: e.memset(xnT[:, :, :], 0.0), w=xk)
        P.op("pool", lambda e, tix=tix: e.tensor_copy(xnT[:, :, :].rearrange("p c (b t) -> p c b t", b=4)[:, :, :, 0], nTs[:, :, tix * 4:(tix + 1) * 4]), r=nTs_keys, w=xk)
        mix(xnT, xk, True, tix, False, False, None, None)
    k0, k1 = nb(), nb()
    for c in range(8 if stage >= 2 else 0):
        src = aoTs if c < 4 else goTs
        for dh, kk in ((0, k0), (1, k1)):
            P.op("pe", lambda e, c=c, dh=dh, kk=kk, src=src: e.matmul(ps[0:NS, kk, :], src[:, c % 4, :], wmo_t[:, c, dh * 512:(dh + 1) * 512], start=(c == 0), stop=(c == 7)),
                 r=[("mixTs", t) for t in range(4)] + [("wmo",)], w=[("ps", kk)])
    for dh, kk in (((0, k0), (1, k1)) if stage >= 2 else ()):
        P.op("dve", lambda e, dh=dh, kk=kk: e.tensor_tensor(xs_dst(0, dh), xs_dst(0, dh), ps[0:NS, kk, :], ALU.add), r=[("ps", kk)], w=skeys)
    if stage >= 2:
        ffn_prefetch(1)
        norm_T(xs_src, skeys, 1, NS, 2, nTs, nTs_keys)
        fence()
        ffn(1, nTs, nTs_keys, NS, xs_dst, skeys, 1, NS)
    final_norm(xs_src, skeys, 1, NS, lambda b: ys, "o_ys", "oys")

    tiles = [("pre", i) for i in range(n_pre)] + [("main", i) for i in range(n_tiles)]

    def load_x(gi):
        kind, i = tiles[gi]
        src = xpre if kind == "pre" else xp
        X = xt[gi % 2]
        keys = [("x", gi % 2, b) for b in range(4)]
        if gi == 1:
            keys = keys + [("xs",), ("sct",), MK("ckt"), MK("cvt"), MK("ckd", 0), MK("ckd", 1)]
        P.op("sp", lambda e: [e.dma_start(out=X[:, b, :], in_=src[i * TT + b * 128: i * TT + (b + 1) * 128, :]) for b in range(4)],
             w=keys, chan=f"x{gi % 2}", nd=4)

    if stage >= 3:
        load_x(0)
    for gi, (kind, i) in enumerate(tiles if stage >= 3 else []):
        X = xt[gi % 2]
        xkeys = [("x", gi % 2, b) for b in range(4)]
        xsrc = lambda b, X=X: X[:, b, :]
        xdst = lambda b, dh, X=X: X[:, b, dh * 512:(dh + 1) * 512]
        ffn_prefetch(0)
        norm_T(xsrc, xkeys, 4, 128, 0, xnT, xk)
        fence()
        ffn(0, xnT, xk, TT, xdst, xkeys, 4, 128)
        norm_T(xsrc, xkeys, 4, 128, 1, xnT, xk)
        fence()
        if kind == "pre":
            mix(xnT, xk, False, 0, i == 0, False, xdst, xkeys, prepass=True, pre_last=(i == n_pre - 1))
            if gi + 1 < len(tiles):
                load_x(gi + 1)
            continue
        mix(xnT, xk, False, 0, (n_pre == 0 and i == 0), i == n_tiles - 1, xdst, xkeys, mask_prev=(n_pre > 0 and i == 0))
        if gi + 1 < len(tiles):
            load_x(gi + 1)
        ffn_prefetch(1)
        norm_T(xsrc, xkeys, 4, 128, 2, xnT, xk)
        fence()
        ffn(1, xnT, xk, TT, xdst, xkeys, 4, 128)
        final_norm(xsrc, xkeys, 4, 128, lambda b, i=i: yp[i * TT + b * 128: i * TT + (b + 1) * 128, :], ("o_yp", i), f"oy{gi % 2}")

    with ExitStack() as st:
        sems = {}
        for en in Prog.ENG:
            sems[("e", en)] = st.enter_context(nc.semaphore("e_" + en))
        for ch in P.chan_cnt:
            sems[("c", ch)] = st.enter_context(nc.semaphore("c_" + ch))
        P.run(sems)
    return nc


_NC_CACHE = {}


def run_cores(xp_list, per_core, shared, n_tiles, stage=3, xpre_list=None, hasprev=None):
    n_pre = 0 if xpre_list is None else xpre_list[0].shape[0] // TT
    key = (n_tiles, n_pre, stage)
    if key not in _NC_CACHE:
        _NC_CACHE[key] = build(n_tiles, stage, n_pre)
    nc = _NC_CACHE[key]
    consts = host_consts()
    in_maps = []
    for c in range(len(xp_list)):
        m = dict(shared)
        m.update(per_core[c])
        m["xp"] = xp_list[c]
        if n_pre:
            m["xpre"] = xpre_list[c]
        m["hasprev"] = np.full((128, 1), 0.0 if hasprev is None else hasprev[c], np.float32)
        for k, v in consts.items():
            m["c_" + k] = v
        in_maps.append({k: np.ascontiguousarray(v, dtype=np.float32) for k, v in m.items()})
    res = run_bass_kernel_spmd(nc, in_maps, core_ids=list(range(len(xp_list))))
    return res.results


def kernel(x_prompt, x_sample, cache_attn_k, cache_attn_v, state_conv, state_gdn,
           ffn1_norm_g, ffn1_w_in, ffn1_w_out, mix_norm_g, w_in_mix, attn_sinks, conv_w,
           gdn_A_log, gdn_dt_bias, gdn_norm_g, w_out_mix, ffn2_norm_g, ffn2_w_in, ffn2_w_out,
           final_norm_g):
    f = lambda a: np.asarray(a, dtype=np.float32)
    x_prompt = f(x_prompt)
    B, S, _ = x_prompt.shape
    ncore = 8
    HALF = S // 2
    n_tiles = HALF // TT
    shared = dict(g1=f(ffn1_norm_g)[0], wi1=f(ffn1_w_in)[0], wo1=f(ffn1_w_out)[0], g2=f(mix_norm_g)[0],
                  wmi=f(w_in_mix)[0], sinks=f(attn_sinks)[0], convw=f(conv_w)[0], alog=f(gdn_A_log)[0],
                  dtb=f(gdn_dt_bias)[0], gng=f(gdn_norm_g)[0], wmo=f(w_out_mix)[0], g3=f(ffn2_norm_g)[0],
                  wi2=f(ffn2_w_in)[0], wo2=f(ffn2_w_out)[0], gfin=f(final_norm_g))
    xs_ = f(x_sample)[:, 0, :]
    ck_ = f(cache_attn_k)[0].reshape(-1, 128, 128)
    cv_ = f(cache_attn_v)[0].reshape(-1, 128, 128)
    sc_ = f(state_conv)[0]
    sg_ = f(state_gdn)[0]
    per_core, xp_list, xpre_list, hasprev = [], [], [], []
    zeros = np.zeros((HALF, D), np.float32)
    for c in range(ncore):
        sl = slice(c * NS, (c + 1) * NS)
        per_core.append(dict(xs=xs_[sl], ck=ck_[sl], cv=cv_[sl], sconv=sc_[sl].reshape(NS * 3, 1536), sgdn=sg_[sl]))
        sq, half = c // 2, c % 2
        xp_list.append(x_prompt[sq, half * HALF:(half + 1) * HALF])
        xpre_list.append(x_prompt[sq, 0:HALF] if half else zeros)
        hasprev.append(float(half))
    res = run_cores(xp_list, per_core, shared, n_tiles, 3, xpre_list, hasprev)
    y_prompt = np.stack([np.concatenate([res[2 * b]["yp"], res[2 * b + 1]["yp"]]) for b in range(B)])
    cat = lambda k: np.stack([res[2 * b + 1][k] for b in range(B)])
    y_sample = np.concatenate([res[c]["ys"] for c in range(ncore)])[:, None, :]
    nkp = cat("nkp").reshape(1, B, 128, 2, 64)
    nvp = cat("nvp").reshape(1, B, 128, 2, 64)
    ncp = cat("ncp")[None]
    ngp = cat("ngp")[None]
    nks = np.concatenate([res[c]["nks"] for c in range(ncore)]).reshape(1, ncore * NS, 128, 2, 64)
    nvs = np.concatenate([res[c]["nvs"] for c in range(ncore)]).reshape(1, ncore * NS, 128, 2, 64)
    ncs = np.concatenate([res[c]["ncs"] for c in range(ncore)])[None]
    ngs = np.concatenate([res[c]["ngs"] for c in range(ncore)])[None]
    return (y_prompt, y_sample, nkp, nvp, ncp, ngp, nks, nvs, ncs, ngs)
```

```python
import os
import numpy as np
from contextlib import ExitStack
KLIMS = float(os.environ.get('KLIMS', '99'))
KLIMP = float(os.environ.get('KLIMP', '99'))
import concourse.bass as bass
import concourse.mybir as mybir
from concourse.bass_utils import run_bass_kernel_spmd

F32 = mybir.dt.float32
BF16 = mybir.dt.bfloat16
ALU = mybir.AluOpType
AF = mybir.ActivationFunctionType
AX = mybir.AxisListType

D = 1024
FF = 2816
NJ = 22
EPS = 1e-6
TT = 512
NBLK = 4
NS = 16
INW = 2824


class Prog:
    ENG = ("pe", "act", "dve", "pool", "sp")

    def __init__(self, nc):
        self.nc = nc
        self.ops = []
        self.lastw = {}
        self.readers = {}
        self.chan_cnt = {}

    def op(self, eng, fn, r=(), w=(), chan=None, nd=1):
        i = len(self.ops)
        deps = set()
        for k in r:
            if k in self.lastw:
                deps.add(self.lastw[k])
        for k in w:
            if k in self.lastw:
                deps.add(self.lastw[k])
            deps.update(self.readers.get(k, ()))
        for k in r:
            self.readers.setdefault(k, []).append(i)
        for k in w:
            self.lastw[k] = i
            self.readers[k] = []
        o = dict(eng=eng, fn=fn, deps=deps, chan=chan, nd=nd, sig=False, val=0)
        if chan is not None:
            self.chan_cnt[chan] = self.chan_cnt.get(chan, 0) + 16 * nd
            o["val"] = self.chan_cnt[chan]
        self.ops.append(o)
        return i

    def allkeys(self, pred):
        ks = set(self.lastw) | set(self.readers)
        return [k for k in ks if pred(k)]

    def run(self, sems):
        nc = self.nc
        ops = self.ops
        for o in ops:
            for d in o["deps"]:
                ops[d]["sig"] = True
        cnt = {e: 0 for e in self.ENG}
        for o in ops:
            if o["chan"] is None and o["sig"]:
                cnt[o["eng"]] += 1
                o["val"] = cnt[o["eng"]]
        streams = {e: [] for e in self.ENG}
        for i, o in enumerate(ops):
            streams[o["eng"]].append(i)

        def runner(ename):
            def f(e):
                waited = {}
                for i in streams[ename]:
                    o = ops[i]
                    for d in sorted(o["deps"]):
                        do = ops[d]
                        if do["chan"] is not None:
                            key = ("c", do["chan"])
                        else:
                            if do["eng"] == ename and ename in ("pe", "sp"):
                                continue
                            key = ("e", do["eng"])
                        if waited.get(key, 0) >= do["val"]:
                            continue
                        waited[key] = do["val"]
                        e.wait_ge(sems[key], do["val"])
                    res = o["fn"](e)
                    if o["chan"] is not None:
                        lst = res if isinstance(res, (list, tuple)) else [res]
                        assert len(lst) == o["nd"], (len(lst), o["nd"])
                        for ins in lst:
                            ins.then_inc(sems[("c", o["chan"])], 16)
                    elif o["sig"]:
                        res.then_inc(sems[("e", ename)], 1)
                if ename == "sp":
                    for ch, v in self.chan_cnt.items():
                        if waited.get(("c", ch), 0) < v:
                            e.wait_ge(sems[("c", ch)], v)
            return f

        with nc.Block() as block:
            block.tensor(runner("pe"))
            block.scalar(runner("act"))
            block.vector(runner("dve"))
            block.gpsimd(runner("pool"))
            block.sync(runner("sp"))


def host_consts():
    i = np.arange(128)
    c = {}
    c["ident"] = np.eye(128, dtype=np.float32)
    c["triu"] = (i[:, None] <= i[None, :]).astype(np.float32)
    c["strl"] = (i[:, None] > i[None, :]).astype(np.float32)
    slopes = np.exp2(-8.0 * np.arange(1, 9, dtype=np.float32) / 8.0).astype(np.float32)
    jj = i[:, None, None].astype(np.float32)
    ii = i[None, None, :].astype(np.float32)
    sl = slopes[None, :, None]
    bc = np.where(ii >= jj, -sl * (ii - jj), -30000.0).astype(np.float32)
    bp = np.where(jj >= ii, -sl * (ii - jj + 128.0), -30000.0).astype(np.float32)
    c["bcur"] = np.ascontiguousarray(bc.reshape(128, 1024))
    c["bprev"] = np.ascontiguousarray(bp.reshape(128, 1024))
    tm = np.zeros((128, 1), np.float32)
    tm[0, 0] = 1.0
    c["tok0"] = tm
    cm = np.zeros((128, 512), np.float32)
    cm[:, 0::128] = 1.0
    c["col0"] = cm
    qm = np.zeros((128, 4), np.float32)
    qm[:64, 0] = 0.125; qm[64:, 1] = 0.125; qm[:64, 2] = 1.0; qm[64:, 3] = 1.0
    c["qm"] = qm
    return c


def build(n_tiles, stage=3, n_pre=0):
    nc = bass.Bass("TRN2", target_bir_lowering=False)
    NTOK = n_tiles * TT

    def din(name, shape):
        return nc.dram_tensor(name, list(shape), F32, kind="ExternalInput").ap()

    def dout(name, shape):
        return nc.dram_tensor(name, list(shape), F32, kind="ExternalOutput").ap()

    xp = din("xp", [NTOK, D])
    xpre = din("xpre", [n_pre * TT, D]) if n_pre > 0 else None
    hasprev = din("hasprev", [128, 1])
    xs = din("xs", [NS, D])
    ck = din("ck", [NS, 128, 128])
    cv = din("cv", [NS, 128, 128])
    sconv = din("sconv", [NS * 3, 1536])
    sgdn = din("sgdn", [NS, 4, 128, 128])
    g1 = din("g1", [D]); wi1 = din("wi1", [D, 2 * FF]); wo1 = din("wo1", [FF, D])
    g2 = din("g2", [D]); wmi = din("wmi", [D, INW]); sinks = din("sinks", [8])
    convw = din("convw", [4, 1536]); alog = din("alog", [4]); dtb = din("dtb", [4])
    gng = din("gng", [128]); wmo = din("wmo", [D, D])
    g3 = din("g3", [D]); wi2 = din("wi2", [D, 2 * FF]); wo2 = din("wo2", [FF, D])
    gfin = din("gfin", [D])
    c_ident = din("c_ident", [128, 128]); c_triu = din("c_triu", [128, 128]); c_strl = din("c_strl", [128, 128])
    c_bcur = din("c_bcur", [128, 1024]); c_bprev = din("c_bprev", [128, 1024])
    c_tok0 = din("c_tok0", [128, 1]); c_col0 = din("c_col0", [128, 512]); c_qm = din("c_qm", [128, 4])

    yp = dout("yp", [NTOK, D]); ys = dout("ys", [NS, D])
    nkp = dout("nkp", [128, 128]); nvp = dout("nvp", [128, 128])
    ncp = dout("ncp", [3, 1536]); ngp = dout("ngp", [4, 128, 128])
    nks = dout("nks", [NS, 128, 128]); nvs = dout("nvs", [NS, 128, 128])
    ncs = dout("ncs", [NS, 3, 1536]); ngs = dout("ngs", [NS, 4, 128, 128])

    wi_s = [nc.dram_tensor(f"wi_s{i}", [NJ, 128, 8, 256], BF16).ap() for i in range(2)]
    wo_s = [nc.dram_tensor(f"wo_s{i}", [128, NJ, D], BF16).ap() for i in range(2)]
    wm_s = nc.dram_tensor("wm_s", [11, 128, 8, 256], BF16).ap()
    wtm_s = nc.dram_tensor("wtm_s", [128, 8, 264], BF16).ap()
    wmo_s = nc.dram_tensor("wmo_s", [128, 8, D], BF16).ap()

    A = nc.alloc_sbuf_tensor
    xt = [A(f"xt{i}", [128, NBLK, D], F32) for i in range(2)]
    xts = xt[1][0:NS, 0:1, :]
    x1f = xt[1][:, 1:4, :].rearrange("p b d -> p (b d)")
    xn = A("xn", [128, D], BF16)
    xnT = A("xnT", [128, 8, TT], BF16)
    nTs = A("nTs", [128, 8, NS], BF16)
    wi = [A(f"wibuf{i}", [128, 8, 256], BF16) for i in range(3)]
    wmo_t = A("wmo_t", [128, 8, D], BF16)
    wtm = A("wtm", [128, 8, 264], BF16)
    arena = A("arena", [128, 16896], F32)
    hT = arena[:, 0:5632].bitcast(BF16).rearrange("p (j t) -> p j t", j=NJ)
    wo = arena[:, 5632:16896].bitcast(BF16).rearrange("p (j n) -> p j n", j=NJ)
    off = [0]

    def carve(ncols_f32):
        a = off[0]
        off[0] += ncols_f32
        assert off[0] <= 16896
        return arena[:, a:a + ncols_f32]

    pre = carve(12 * 524).rearrange("p (m t) -> p m t", m=12)
    oT = carve(2048).rearrange("p (h t) -> p h t", h=4)
    ncrow = oT[0:4, :, :].rearrange("p h t -> p (h t)")[:, 0:1536]
    ebuf = carve(1024)
    tC = ebuf[:, 0:512]; tD = ebuf[:, 512:1024]
    tA = carve(512); tB = carve(512); tE = carve(512)
    u_t = carve(512)
    rbuf = tB
    gqn = carve(1024).bitcast(BF16).rearrange("p (h t) -> p h t", h=4)
    gkn = carve(1024).bitcast(BF16).rearrange("p (h t) -> p h t", h=4)
    zs = carve(1024).bitcast(BF16).rearrange("p (h t) -> p h t", h=4)
    Pp = carve(512).bitcast(BF16).rearrange("p (h t) -> p h t", h=8)
    qT = A("qT", [128, 4, TT], BF16)
    kTd = [A(f"kTd{i}", [128, 8, 128], BF16) for i in range(2)]
    Vd = [A(f"Vd{i}", [128, 8, 128], BF16) for i in range(2)]
    aoT = A("aoT", [128, 4, TT], BF16)
    goT = A("goT", [128, 4, TT], BF16)
    aoTs = A("aoTs", [128, 4, NS], BF16)
    goTs = A("goTs", [128, 4, NS], BF16)
    Pc = A("Pc", [128, 8, 128], BF16)
    Xb = [A("Xb0", [128, 4, 128], F32), carve(512).rearrange("p (h t) -> p h t", h=4)]
    Yb = [A("Yb0", [128, 4, 128], F32), carve(512).rearrange("p (h t) -> p h t", h=4)]
    Nf = carve(512).rearrange("p (h t) -> p h t", h=4)
    Nb = A("Nb", [128, 4, 128], BF16)
    vb = A("vb", [128, 4, 128], BF16)
    kb = A("kb", [128, 4, 128], BF16)
    kd = A("kd", [128, 4, 128], BF16)
    wT = A("wT", [128, 4, 128], BF16)
    qdT = A("qdT", [128, 4, 128], BF16)
    qkT = A("qkT", [128, 4, 128], BF16)
    vnew = A("vnew", [128, 4, 128], BF16)
    Sbf = A("Sbf", [128, 4, 128], BF16)
    ctmp = A("ctmp", [128, TT], F32)
    junk = ctmp[:, :].bitcast(BF16)
    Sst = A("Sst", [128, 4, 128], F32)
    ccar = A("ccar", [128, 12, 3], F32)
    kvo = A("kvo", [128, 256], F32)
    gbga = A("gbga", [128, NBLK, 8], F32)
    sm = A("sm", [128, 64], F32)
    ss = A("ss", [128, 8], F32)
    sg = [A("sg0", [128, TT], F32)] * 2
    ident_f = A("ident_f", [128, 128], F32)
    ident = A("ident_b", [128, 128], BF16)
    triu = A("triu", [128, 128], F32)
    strl = A("strl", [128, 128], F32)
    ones_f = A("ones_f", [128, 128], F32)
    ones_b = A("ones_b", [128, 128], BF16)
    bcur = A("bcur", [128, 1024], BF16)
    bprev = A("bprev", [128, 1024], BF16)
    tok0 = A("tok0", [128, 1], F32)
    hp_t = A("hp_t", [128, 1], F32)
    qm = A("qm", [128, 4], F32)
    qT2 = A("qT2", [128, 4, TT], BF16)
    col0 = A("col0", [128, 512], F32)
    gT = [A(f"gT{i}", [128, 8], F32) for i in range(3)]
    gfin_b = A("gfin_b", [128, D], F32)
    cw = A("cw", [128, 4, 12], F32)
    esink = A("esink", [128, 8], F32)
    negA = A("negA", [128, 4], F32)
    dtb_b = A("dtb_b", [128, 4], F32)
    gng_t = A("gng_t", [128, 1], F32)
    sct = x1f[0:NS * 3, 0:1536]
    ckt = x1f[:, 1536:1664]
    cvt = x1f[:, 1664:1792]
    ckd = x1f[:, 1792:2048].rearrange("p (a b d) -> p a b d", a=2, b=2)
    ps = nc.alloc_psum_tensor("ps", [128, 8, 512], F32)

    P = Prog(nc)
    MK = lambda *a: ("m_" + a[0],) + tuple(a[1:])
    bank = [0]

    def nb():
        bank[0] = (bank[0] + 1) % 8
        return bank[0]

    def bc(ap, shape, axis):
        return ap.unsqueeze(axis).broadcast_to(shape)

    def cast_ffn(i, w_in, w_out):
        v = w_in.rearrange("(c p) n -> p c n", p=128)
        for j in range(NJ):
            def f(e, j=j):
                return [e.dma_start(out=wi_s[i][j, :, :, g * 128:(g + 1) * 128],
                                    in_=v[:, :, g * FF + j * 128: g * FF + (j + 1) * 128]) for g in range(2)]
            P.op("pool", f, w=[("wi_s", i, j)], chan=f"cast{i}", nd=2)
        P.op("pool", lambda e: e.dma_start(out=wo_s[i], in_=w_out.rearrange("(j p) n -> p j n", p=128)),
             w=[("wo_s", i), ("castgrp", i)], chan=f"cast{i}")

    cast_ffn(0, wi1, wo1)
    wmv = wmi.rearrange("(c p) n -> p c n", p=128)
    groups = [(128 * p, False) for p in range(4)] + [(512, True), (576, True)] + \
             [(768 + 128 * m, False) for m in range(12)] + [(2304 + 128 * h, False) for h in range(4)]
    for s in range(11):
        def f(e, s=s):
            r = []
            for g in range(2):
                c0, dup = groups[2 * s + g]
                if dup:
                    for q in range(2):
                        r.append(e.dma_start(out=wm_s[s, :, :, g * 128 + q * 64: g * 128 + (q + 1) * 64], in_=wmv[:, :, c0:c0 + 64]))
                else:
                    r.append(e.dma_start(out=wm_s[s, :, :, g * 128:(g + 1) * 128], in_=wmv[:, :, c0:c0 + 128]))
            return r
        nd = sum(2 if groups[2 * s + g][1] else 1 for g in range(2))
        P.op("pool", f, w=[("wm_s", s)], chan="castm", nd=nd)

    def f(e):
        return [e.dma_start(out=wtm_s[:, :, 0:128], in_=wmv[:, :, 640:768]),
                e.dma_start(out=wtm_s[:, :, 128:256], in_=wmv[:, :, 512:640]),
                e.dma_start(out=wtm_s[:, :, 256:264], in_=wmv[:, :, 2816:2824])]
    P.op("pool", f, w=[("wtm_s",)], chan="castm", nd=3)
    P.op("pool", lambda e: e.dma_start(out=wmo_s, in_=wmo.rearrange("(c p) n -> p c n", p=128)), w=[("wmo_s",), ("castgrp", "m")], chan="castm")
    cast_ffn(1, wi2, wo2)

    for sq in range(NS):
        def f(e, sq=sq):
            return [e.dma_start(out=nks[sq, 0:127, :], in_=ck[sq, 1:128, :]),
                    e.dma_start(out=nvs[sq, 0:127, :], in_=cv[sq, 1:128, :]),
                    e.dma_start(out=ncs[sq, 0:2, :], in_=sconv[sq * 3 + 1: sq * 3 + 3, :])]
        P.op("pool", f, w=[("o_hist", sq)], chan="ohist", nd=3)

    ldn = [0]

    def ld(dst, src, key, **kw):
        ldn[0] += 1
        P.op("sp", lambda e: e.dma_start(out=dst, in_=src, **kw), w=[key], chan=f"const{ldn[0]}")

    ld(ident_f[:, :], c_ident, ("identf",)); ld(triu[:, :], c_triu, ("triu",)); ld(strl[:, :], c_strl, ("strl",))
    P.op("pool", lambda e: e.dma_start(out=bcur[:, :], in_=c_bcur), w=[("bcur",)], chan="constb1")
    P.op("pool", lambda e: e.dma_start(out=bprev[:, :], in_=c_bprev), w=[("bprev",)], chan="constb2")
    ld(tok0[:, :], c_tok0, ("tok0",)); ld(hp_t[:, :], hasprev, ("hp",)); ld(qm[:, :], c_qm, ("qm",)); ld(col0[:, :], c_col0, ("col0",))
    for i, g in enumerate((g1, g2, g3)):
        ld(gT[i][:, :], g.rearrange("(c p) -> p c", p=128), ("gT", i), allow_slow_non_contiguous=True)
    ld(gfin_b[:, :], gfin.partition_broadcast(128), ("gfin",))
    P.op("sp", lambda e: [e.dma_start(out=cw[:, j, :], in_=convw[j].rearrange("(m p) -> p m", p=128), allow_slow_non_contiguous=True) for j in range(4)], w=[("cw",)], chan="constcw", nd=4)
    ld(esink[:, :], sinks.partition_broadcast(128), ("esink",))
    ld(negA[:, :], alog.partition_broadcast(128), ("negA",))
    ld(dtb_b[:, :], dtb.partition_broadcast(128), ("dtb",))
    ld(gng_t[:, :], gng.rearrange("(p o) -> p o", o=1), ("gng",))
    ld(wtm[:, :, :], wtm_s, ("wtm",))
    P.ops[-1]["deps"].add(P.lastw[("castgrp", "m")])
    ld(wmo_t[:, :, :], wmo_s, ("wmo",))
    P.ops[-1]["deps"].add(P.lastw[("castgrp", "m")])
    P.op("dve", lambda e: e.tensor_copy(ident[:, :], ident_f[:, :]), r=[("identf",)], w=[("ident",)])
    P.op("dve", lambda e: e.memset(ones_f[:, :], 1.0), w=[("ones_f",)])
    P.op("dve", lambda e: e.memset(ones_b[:, :], 1.0), w=[("ones_b",)])
    P.op("act", lambda e: e.activation(out=esink[:, :], in_=esink[:, :], func=AF.Exp), r=[("esink",)], w=[("esink",)])
    P.op("act", lambda e: e.activation(out=negA[:, :], in_=negA[:, :], func=AF.Exp), r=[("negA",)], w=[("negA",)])
    P.op("dve", lambda e: e.tensor_scalar(negA[:, :], negA[:, :], -1.0, 0.0, ALU.mult, ALU.add), r=[("negA",)], w=[("negA",)])

    def rstd_cols(n, np_, scale, keyss):
        P.op("act", lambda e: e.activation(out=ss[0:np_, 0:n], in_=ss[0:np_, 0:n], func=AF.Ln, scale=scale, bias=EPS),
             r=keyss, w=keyss)
        P.op("act", lambda e: e.activation(out=ss[0:np_, 0:n], in_=ss[0:np_, 0:n], func=AF.Exp, scale=-0.5),
             r=keyss, w=keyss)

    def norm_T(xsrc, xkeys, nblk, np_, gi, dstT, dkeys):
        keyss = [("ss",)]
        for b in range(nblk):
            P.op("act", lambda e, b=b: e.activation(out=junk[0:np_, :], in_=xsrc(b), func=AF.Square, accum_out=ss[0:np_, b:b + 1]),
                 r=[xkeys[b]], w=[MK("ctmp")] + keyss)
        rstd_cols(nblk, np_, 1.0 / D, keyss)
        for b in range(nblk):
            P.op("dve", lambda e, b=b: e.tensor_scalar(xn[0:np_, :], xsrc(b), ss[0:np_, b:b + 1], 1.0, ALU.mult, ALU.mult),
                 r=keyss + [xkeys[b]], w=[("xn",)])
            k = nb()
            pst = ps[:, k, :].bitcast(BF16)
            for c in range(8):
                P.op("pe", lambda e, c=c, pst=pst: e.transpose(pst[:, c * 128:c * 128 + np_], xn[0:np_, c * 128:(c + 1) * 128], ident[0:np_, 0:np_]),
                     r=[("xn",), ("ident",)], w=[("ps", k)])
            P.op("dve", lambda e, b=b, pst=pst: e.tensor_tensor(
                dstT[:, :, b * np_:(b + 1) * np_], pst.rearrange("p (c t) -> p c t", c=8)[:, :, 0:np_],
                bc(gT[gi][:, :], [128, 8, np_], 2), ALU.mult),
                r=[("ps", k), ("gT", gi)], w=[dkeys[b]])

    pref = {"on": False}

    def ffn_prefetch(fi):
        for j in range(3):
            P.op("sp", lambda e, j=j: e.dma_start(out=wi[j][:, :, :], in_=wi_s[fi][j]),
                 r=[("castgrp", fi)], w=[("wi", j)], chan=f"wi{j}")
        pref["on"] = True

    def ffn(fi, srcT, skeys, ntok, xdst, xkeys, nblk, np_):
        hkeys = [("hT", j) for j in range(NJ)]
        skip_first = pref["on"]
        pref["on"] = False
        P.op("sp", lambda e: e.dma_start(out=wo, in_=wo_s[fi]), r=[("wo_s", fi), ("castgrp", fi)], w=[("wo",)], chan="wo")
        for j in range(NJ):
            s = j % 3
            if not (skip_first and j < 3):
                P.op("sp", lambda e, j=j, s=s: e.dma_start(out=wi[s][:, :, :], in_=wi_s[fi][j]),
                     r=[("wi_s", fi, j), ("castgrp", fi)], w=[("wi", s)], chan=f"wi{s}")
            kg, ku = nb(), nb()
            for c in range(8):
                P.op("pe", lambda e, c=c, s=s, kg=kg: e.matmul(ps[:, kg, 0:ntok], wi[s][:, c, 0:128], srcT[:, c, 0:ntok], start=(c == 0), stop=(c == 7)),
                     r=[("wi", s)] + skeys, w=[("ps", kg)])
            for c in range(8):
                P.op("pe", lambda e, c=c, s=s, ku=ku: e.matmul(ps[:, ku, 0:ntok], wi[s][:, c, 128:256], srcT[:, c, 0:ntok], start=(c == 0), stop=(c == 7)),
                     r=[("wi", s)] + skeys, w=[("ps", ku)])
            q = 0
            P.op("act", lambda e, kg=kg, q=q: e.activation(out=sg[q][:, 0:ntok], in_=ps[:, kg, 0:ntok], func=AF.Silu),
                 r=[("ps", kg)], w=[("sg", q)])
            P.op("dve", lambda e, ku=ku, q=q, j=j: e.tensor_tensor(hT[:, j, 0:ntok], sg[q][:, 0:ntok], ps[:, ku, 0:ntok], ALU.mult),
                 r=[("ps", ku), ("sg", q)], w=[hkeys[j]])
        for b in range(nblk):
            k0, k1 = nb(), nb()
            for j in range(NJ):
                for dh, kk in ((0, k0), (1, k1)):
                    P.op("pe", lambda e, j=j, dh=dh, kk=kk, b=b: e.matmul(ps[0:np_, kk, :], hT[:, j, b * np_:(b + 1) * np_], wo[:, j, dh * 512:(dh + 1) * 512], start=(j == 0), stop=(j == NJ - 1)),
                         r=[hkeys[j], ("wo",)], w=[("ps", kk)])
            for dh, kk in ((0, k0), (1, k1)):
                P.op("dve", lambda e, dh=dh, kk=kk, b=b: e.scalar_tensor_tensor(xdst(b, dh), ps[0:np_, kk, :], 0.5, xdst(b, dh), ALU.mult, ALU.add),
                     r=[("ps", kk)], w=[xkeys[b]])

    def fence():
        ks = P.allkeys(lambda k: k[0] in ("hT", "wo") or str(k[0]).startswith("m_"))
        P.op("pool", lambda e: e.memset(sm[:, 63:64], 0.0), w=ks + [("fence",)])


    def mix(srcT, skeys, sample, tix, first_tile, last_tile, xres, xkeys, prepass=False, pre_last=False, mask_prev=False):
        bs = 131 if sample else 128

        def kprev(kv, b):
            return kTd[kv][:, 2 * b, :] if sample else kTd[kv][:, b, :]

        def kcur(kv, b):
            return kTd[kv][:, 2 * b + 1, :] if sample else kTd[kv][:, b + 1, :]

        def kix(b):
            return (2 * b, 2 * b + 1) if sample else (b, b + 1)

        KLIM = KLIMS if sample else KLIMP
        if KLIM < 0.5:
            return
        PQ = (lambda *a, **k: None) if prepass else P.op
        PD = (lambda *a, **k: None) if sample else P.op
        if sample:
            for h in range(4):
                P.op("pool", lambda e, h=h: e.tensor_copy(Nb[:, h, :], ident[:, :]), r=[("ident",)], w=[MK("N")])
        slots = range(11) if not prepass else (range(2, 9) if pre_last else range(5, 9))
        for s in slots:
            sl = s % 3
            P.op("sp", lambda e, s=s, sl=sl: e.dma_start(out=wi[sl][:, :, :], in_=wm_s[s]),
                 r=[("wm_s", s), ("castgrp", "m")], w=[("wi", sl)], chan=f"wi{sl}")
            if 3 <= s <= 8 and (sample or last_tile):
                kq = nb()
                lt = srcT[:, :, :].rearrange("p c (b t) -> p c b t", b=4)[:, :, :, 0] if sample else srcT[:, :, 509:512]
                mrows = 4 if sample else 3
                for c in range(8):
                    P.op("pe", lambda e, c=c, sl=sl, kq=kq, lt=lt, mrows=mrows: e.matmul(ps[0:mrows, kq, 0:256], lt[:, c, :], wi[sl][:, c, :], start=(c == 0), stop=(c == 7)),
                         r=[("wi", sl)] + skeys, w=[("ps", kq)])
                P.op("dve", lambda e, kq=kq, s=s, mrows=mrows: e.tensor_copy(ncrow[0:mrows, (s - 3) * 256:(s - 2) * 256], ps[0:mrows, kq, 0:256]), r=[("ps", kq)], w=[MK("ncrow")])
                if s == 8:
                    if sample:
                        P.op("sp", lambda e: e.dma_start(out=ncs[tix * 4:(tix + 1) * 4, 2, :], in_=ncrow[0:4, :]), r=[MK("ncrow")], w=[("o_ncs", tix)], chan="onc")
                    else:
                        P.op("sp", lambda e: e.dma_start(out=ncp, in_=ncrow[0:3, :]), r=[MK("ncrow")], w=[("o_ncp",)], chan="onc")
            for g in range(2):
                gi = 2 * s + g
                k = nb()
                for c in range(8):
                    P.op("pe", lambda e, c=c, sl=sl, g=g, k=k: e.matmul(ps[:, k, :], wi[sl][:, c, g * 128:(g + 1) * 128], srcT[:, c, :], start=(c == 0), stop=(c == 7)),
                         r=[("wi", sl)] + skeys, w=[("ps", k)])
                if gi < 4:
                    P.op("act", lambda e, k=k, gi=gi: e.activation(out=qT[:, gi, :], in_=ps[:, k, :], func=AF.Identity, scale=qm[:, 0:1]),
                         r=[("ps", k), ("qm",)], w=[MK("qT", gi)])
                    P.op("dve", lambda e, k=k, gi=gi: e.tensor_scalar(qT2[:, gi, :], ps[:, k, :], qm[:, 1:2], 0.0, ALU.mult, ALU.add),
                         r=[("ps", k), ("qm",)], w=[MK("qT", gi)])
                elif gi < 6:
                    kv = gi - 4
                    if sample:
                        dst = kTd[kv][:, :, :].rearrange("p (b two) t -> p b two t", two=2)[:, :, 1, :]
                    else:
                        dst = kTd[kv][:, 1:5, :]
                    P.op("act", lambda e, k=k, dst=dst: e.activation(out=dst, in_=ps[:, k, :].rearrange("p (b t) -> p b t", b=4), func=AF.Identity),
                         r=[("ps", k)], w=[MK("kTd", kv, i) for i in ((1, 3, 5, 7) if sample else (1, 2, 3, 4))])
                elif gi < 18:
                    m = gi - 6
                    dst = pre[:, m, 0:4 * bs].rearrange("p (b t) -> p b t", b=4)[:, :, 3:131] if sample else None
                    if sample:
                        P.op("act", lambda e, k=k, dst=dst: e.activation(out=dst, in_=ps[:, k, :].rearrange("p (b t) -> p b t", b=4), func=AF.Identity),
                             r=[("ps", k)], w=[MK("pre", m)])
                    else:
                        P.op("act", lambda e, k=k, m=m: e.activation(out=pre[:, m, 3:515], in_=ps[:, k, :], func=AF.Identity),
                             r=[("ps", k)], w=[MK("pre", m)])
                else:
                    h = gi - 18
                    P.op("act", lambda e, k=k, h=h: e.activation(out=zs[:, h, :], in_=ps[:, k, :], func=AF.Silu),
                         r=[("ps", k)], w=[MK("zs", h)])
        if KLIM < 1:
            return
        for b in range(4):
            k = nb()
            for c in range(8):
                P.op("pe", lambda e, c=c, k=k, b=b: e.matmul(ps[:, k, 0:264], srcT[:, c, b * 128:(b + 1) * 128], wtm[:, c, :], start=(c == 0), stop=(c == 7)),
                     r=[("wtm",)] + skeys, w=[("ps", k)])
            pi, ci = kix(b)
            for kv in range(2):
                P.op("act", lambda e, k=k, kv=kv, ci=ci: e.activation(out=Vd[kv][:, ci, 0:64], in_=ps[:, k, kv * 64:(kv + 1) * 64], func=AF.Identity),
                     r=[("ps", k)], w=[MK("Vd", kv, ci)])
                P.op("dve", lambda e, k=k, kv=kv, ci=ci: e.tensor_copy(Vd[kv][:, ci, 64:128], ps[:, k, kv * 64:(kv + 1) * 64]),
                     r=[("ps", k)], w=[MK("Vd", kv, ci)])
            P.op("dve", lambda e, k=k, b=b: e.tensor_copy(gbga[:, b, :], ps[:, k, 256:264]), r=[("ps", k)], w=[MK("gbga", b)])
            want_kv = sample or (last_tile and b == 3)
            if want_kv:
                P.op("dve", lambda e, k=k: e.tensor_copy(kvo[:, :], ps[:, k, 0:256]), r=[("ps", k)], w=[MK("kvo")])
                if sample:
                    sq = tix * 4 + b
                    def f(e, sq=sq):
                        return [e.dma_start(out=nvs[sq, 127:128, :], in_=kvo[0:1, 0:128]),
                                e.dma_start(out=nks[sq, 127:128, :], in_=kvo[0:1, 128:256])]
                    P.op("sp", f, r=[MK("kvo")], w=[("o_kvs", sq)], chan="okv", nd=2)
                else:
                    def f(e):
                        return [e.dma_start(out=nvp, in_=kvo[:, 0:128]), e.dma_start(out=nkp, in_=kvo[:, 128:256])]
                    P.op("sp", f, r=[MK("kvo")], w=[("o_kvp",)], chan="okv", nd=2)
        if KLIM < 2:
            return
        if sample:
            for b in range(4):
                sq = tix * 4 + b
                P.op("sp", lambda e, sq=sq: [e.dma_start(out=ckt[:, :], in_=ck[sq]), e.dma_start(out=cvt[:, :], in_=cv[sq])],
                     w=[MK("ckt"), MK("cvt")], chan="ckld", nd=2)
                for kv in range(2):
                    k = nb()
                    for q in range(2):
                        P.op("dve", lambda e, kv=kv, q=q: e.tensor_copy(ckd[:, kv, q, :], ckt[:, kv * 64:(kv + 1) * 64]), r=[MK("ckt")], w=[MK("ckd", kv)])
                    P.op("pe", lambda e, k=k, kv=kv: e.transpose(ps[:, k, 0:128], ckd[:, kv, :, :].rearrange("p a d -> p (a d)"), ident_f[:, :]),
                         r=[MK("ckd", kv), ("identf",)], w=[("ps", k)])
                    P.op("act", lambda e, k=k, kv=kv, b=b: e.activation(out=kTd[kv][:, 2 * b, :], in_=ps[:, k, 0:128], func=AF.Identity),
                         r=[("ps", k)], w=[MK("kTd", kv, 2 * b)])
                    for q in range(2):
                        P.op("dve", lambda e, kv=kv, b=b, q=q: e.tensor_copy(Vd[kv][:, 2 * b, q * 64:(q + 1) * 64], cvt[:, kv * 64:(kv + 1) * 64]),
                             r=[MK("cvt")], w=[MK("Vd", kv, 2 * b)])
            if tix == 0:
                P.op("sp", lambda e: e.dma_start(out=sct[:, :], in_=sconv), w=[("sct",)], chan="sctld")
            for m in range(12):
                k = nb()
                P.op("pe", lambda e, k=k, m=m: e.transpose(ps[:, k, 0:NS * 3], sct[:, m * 128:(m + 1) * 128], ident_f[0:NS * 3, 0:NS * 3]),
                     r=[("sct",), ("identf",)], w=[("ps", k)])
                P.op("dve", lambda e, k=k, m=m: e.tensor_copy(
                    pre[:, m, 0:4 * bs].rearrange("p (b t) -> p b t", b=4)[:, :, 0:3],
                    ps[:, k, tix * 12: tix * 12 + 12].rearrange("p (b r) -> p b r", b=4)),
                    r=[("ps", k)], w=[MK("pre", m)])
        elif not first_tile:
            P.op("pool", lambda e: e.tensor_copy(pre[:, :, 0:3], ccar[:, :, :]), r=[("ccar",)], w=[MK("pre", m) for m in range(12)])
        if (not sample) and first_tile:
            for kv in range(2):
                P.op("pool", lambda e, kv=kv: e.memset(kTd[kv][:, 0, :], 0.0), w=[MK("kTd", kv, 0)])
                P.op("pool", lambda e, kv=kv: e.memset(Vd[kv][:, 0, :], 0.0), w=[MK("Vd", kv, 0)])
            P.op("pool", lambda e: e.memset(pre[:, :, 0:3], 0.0), w=[MK("pre", m) for m in range(12)])
            P.op("pool", lambda e: e.memset(ccar[:, :, :], 0.0), w=[("ccar",)])
            P.op("pool", lambda e: e.memset(Sst[:, :, :], 0.0), w=[MK("S")])
            P.op("pool", lambda e: e.memset(Sbf[:, :, :], 0.0), w=[MK("Sbf")])

        if KLIM < 3:
            return
        for b in (range(4) if not prepass else ()):
            pi, ci = kix(b)
            has_prev = sample or not (first_tile and b == 0)
            kc = [nb(), nb()]
            for h in range(8):
                kv, p, hh = h // 4, h // 2, h % 2
                P.op("pe", lambda e, h=h, kv=kv, p=p, hh=hh, ci=ci, b=b, kc=kc: e.matmul(
                    ps[:, kc[h // 4], (h % 4) * 128:(h % 4 + 1) * 128], kTd[kv][:, ci, :],
                    (qT2 if hh else qT)[:, p, b * 128:(b + 1) * 128], start=True, stop=True),
                    r=[MK("kTd", kv, ci), MK("qT", p)], w=[("ps", kc[h // 4])])
            for half in range(2):
                P.op("dve", lambda e, half=half, kc=kc: e.tensor_tensor(ebuf[:, half * 512:(half + 1) * 512], ps[:, kc[half], :], bcur[:, half * 512:(half + 1) * 512], ALU.add),
                     r=[("ps", kc[half]), ("bcur",)], w=[MK("tC" if half == 0 else "tD")])
            P.op("act", lambda e: e.activation(out=Pc[:, :, :].rearrange("p h t -> p (h t)"), in_=ebuf, func=AF.Exp),
                 r=[MK("tC"), MK("tD")], w=[MK("Pc")])
            if has_prev:
                kp = [nb(), nb()]
                for h in range(8):
                    kv, p, hh = h // 4, h // 2, h % 2
                    P.op("pe", lambda e, h=h, kv=kv, p=p, hh=hh, pi=pi, b=b, kp=kp: e.matmul(
                        ps[:, kp[h // 4], (h % 4) * 128:(h % 4 + 1) * 128], kTd[kv][:, pi, :],
                        (qT2 if hh else qT)[:, p, b * 128:(b + 1) * 128], start=True, stop=True),
                        r=[MK("kTd", kv, pi), MK("qT", p)], w=[("ps", kp[h // 4])])
                for half in range(2):
                    P.op("dve", lambda e, half=half, kp=kp: e.tensor_tensor(ebuf[:, half * 512:(half + 1) * 512], ps[:, kp[half], :], bprev[:, half * 512:(half + 1) * 512], ALU.add),
                         r=[("ps", kp[half]), ("bprev",)], w=[MK("tC" if half == 0 else "tD")])
                P.op("act", lambda e: e.activation(out=Pp[:, :, :].rearrange("p h t -> p (h t)"), in_=ebuf, func=AF.Exp),
                     r=[MK("tC"), MK("tD")], w=[MK("Pp")])
                if mask_prev and b == 0:
                    P.op("dve", lambda e: e.tensor_scalar(Pp[:, :, :].rearrange("p h t -> p (h t)"), Pp[:, :, :].rearrange("p h t -> p (h t)"), hp_t[:, 0:1], 0.0, ALU.mult, ALU.add),
                         r=[MK("Pp"), ("hp",)], w=[MK("Pp")])
            for kv in range(2):
                kn, kdn = nb(), nb()
                pcs = Pc[:, kv * 4:(kv + 1) * 4, :].rearrange("p h t -> p (h t)")
                pps = Pp[:, kv * 4:(kv + 1) * 4, :].rearrange("p h t -> p (h t)")
                P.op("pe", lambda e, kn=kn, kv=kv, ci=ci, pcs=pcs, hp=has_prev: e.matmul(ps[:, kn, :], Vd[kv][:, ci, :], pcs, start=True, stop=not hp),
                     r=[MK("Vd", kv, ci), MK("Pc")], w=[("ps", kn)])
                if has_prev:
                    P.op("pe", lambda e, kn=kn, kv=kv, pi=pi, pps=pps: e.matmul(ps[:, kn, :], Vd[kv][:, pi, :], pps, start=False, stop=True),
                         r=[MK("Vd", kv, pi), MK("Pp")], w=[("ps", kn)])
                P.op("pe", lambda e, kdn=kdn, pcs=pcs, hp=has_prev: e.matmul(ps[:, kdn, :], ones_b[:, :], pcs, start=True, stop=not hp),
                     r=[("ones_b",), MK("Pc")], w=[("ps", kdn)])
                if has_prev:
                    P.op("pe", lambda e, kdn=kdn, pps=pps: e.matmul(ps[:, kdn, :], ones_b[:, :], pps, start=False, stop=True),
                         r=[("ones_b",), MK("Pp")], w=[("ps", kdn)])
                P.op("dve", lambda e, kdn=kdn, kv=kv: e.tensor_tensor(rbuf.rearrange("p (h t) -> p h t", h=4), ps[:, kdn, :].rearrange("p (h t) -> p h t", h=4),
                                                                      bc(esink[:, kv * 4:(kv + 1) * 4], [128, 4, 128], 2), ALU.add),
                     r=[("ps", kdn), ("esink",)], w=[MK("tB")])
                P.op("dve", lambda e: e.reciprocal(rbuf, rbuf), r=[MK("tB")], w=[MK("tB")])
                nv = ps[:, kn, :].rearrange("p (m q t) -> p m q t", m=2, q=2)
                rv = rbuf.rearrange("p (m q t) -> p m q t", m=2, q=2)
                te = tA.rearrange("p (m t) -> p m t", m=4)[:, 0:2, :]
                to = tA.rearrange("p (m t) -> p m t", m=4)[:, 2:4, :]
                P.op("dve", lambda e, nv=nv, rv=rv, te=te: e.tensor_tensor(te, nv[:, :, 0, :], rv[:, :, 0, :], ALU.mult), r=[("ps", kn), MK("tB")], w=[MK("tA")])
                P.op("dve", lambda e, nv=nv, rv=rv, to=to: e.tensor_tensor(to, nv[:, :, 1, :], rv[:, :, 1, :], ALU.mult), r=[("ps", kn), MK("tB")], w=[MK("tA")])
                P.op("dve", lambda e, te=te: e.tensor_scalar(te, te, qm[:, 2:3], 0.0, ALU.mult, ALU.add), r=[MK("tA"), ("qm",)], w=[MK("tA")])
                P.op("dve", lambda e, kv=kv, b=b, te=te, to=to: e.scalar_tensor_tensor(aoT[:, kv * 2:(kv + 1) * 2, b * 128:(b + 1) * 128], to, qm[:, 3:4], te, ALU.mult, ALU.add),
                     r=[MK("tA"), ("qm",)], w=[MK("aoT", b)])
        if (not sample) and (not prepass or pre_last):
            for kv in range(2):
                P.op("pool", lambda e, kv=kv: e.tensor_copy(kTd[kv][:, 0, :], kTd[kv][:, 4, :]), r=[MK("kTd", kv, 4)], w=[MK("kTd", kv, 0)])
                P.op("pool", lambda e, kv=kv: e.tensor_copy(Vd[kv][:, 0, :], Vd[kv][:, 4, :]), r=[MK("Vd", kv, 4)], w=[MK("Vd", kv, 0)])

        if KLIM < 4:
            return
        if KLIM < 5:
            return
        def pv(m, j):
            if sample:
                return pre[:, m, 0:4 * bs].rearrange("p (b t) -> p b t", b=4)[:, :, j:j + 128]
            return pre[:, m, j:j + 512].rearrange("p (b t) -> p b t", b=4)
        cbufs = [(ctmp[:, :], MK("ctmp")), (tA, MK("tA")), (tE, MK("tE"))]
        for ci_, m in enumerate(range(12) if (not prepass or pre_last) else range(4, 12)):
            cb, ck_ = cbufs[ci_ % 3]
            ct3 = cb.rearrange("p (b t) -> p b t", b=4)
            if sample:
                ct1 = ct3[:, :, 0:1]
                P.op("pool", lambda e, m=m, ct1=ct1: e.tensor_scalar(ct1, pv(m, 0)[:, :, 0:1], cw[:, 0, m:m + 1], 0.0, ALU.mult, ALU.add), r=[MK("pre", m), ("cw",)], w=[ck_])
                for j in (1, 2, 3):
                    P.op("dve", lambda e, m=m, j=j, ct1=ct1: e.scalar_tensor_tensor(ct1, pv(m, j)[:, :, 0:1], cw[:, j, m:m + 1], ct1, ALU.mult, ALU.add),
                         r=[MK("pre", m), ("cw",), ck_], w=[ck_])
                P.op("pool", lambda e, m=m: e.memset(pre[:, m, 3:515], 0.0), w=[MK("pre", m)])
                P.op("act", lambda e, m=m, ct1=ct1: e.activation(out=pre[:, m, 3:515].rearrange("p (b t) -> p b t", b=4)[:, :, 0:1], in_=ct1, func=AF.Silu), r=[ck_], w=[MK("pre", m)])
                continue
            P.op("pool", lambda e, m=m, ct3=ct3: e.tensor_scalar(ct3, pv(m, 0), cw[:, 0, m:m + 1], 0.0, ALU.mult, ALU.add), r=[MK("pre", m), ("cw",)], w=[ck_])
            for j in (1, 2, 3):
                P.op("dve", lambda e, m=m, j=j, ct3=ct3: e.scalar_tensor_tensor(ct3, pv(m, j), cw[:, j, m:m + 1], ct3, ALU.mult, ALU.add),
                     r=[MK("pre", m), ("cw",), ck_], w=[ck_])
            P.op("pool", lambda e, m=m: e.tensor_copy(ccar[:, m, :], pre[:, m, 512:515]), r=[MK("pre", m)], w=[("ccar",)])
            P.op("act", lambda e, m=m, cb=cb: e.activation(out=pre[:, m, 3:515], in_=cb, func=AF.Silu), r=[ck_], w=[MK("pre", m)])
        cact = lambda m: pre[:, m, 3:515]
        if KLIM < 6:
            return
        for li, m in enumerate(range(8) if not prepass else range(4, 8)):
            sqb, sqk = ((ctmp[:, :], MK("ctmp")), (tE, MK("tE")))[li % 2]
            rsb, rsk = ((tA, MK("tA")), (tB, MK("tB")))[li % 2]
            P.op("pool", lambda e, m=m, sqb=sqb: e.tensor_tensor(sqb, cact(m), cact(m), ALU.mult), r=[MK("pre", m)], w=[sqk])
            k = nb()
            P.op("pe", lambda e, k=k, sqb=sqb: e.matmul(ps[:, k, :], ones_f[:, :], sqb, start=True, stop=True), r=[("ones_f",), sqk], w=[("ps", k)])
            P.op("act", lambda e, k=k, rsb=rsb: e.activation(out=rsb, in_=ps[:, k, :], func=AF.Ln, bias=EPS), r=[("ps", k)], w=[rsk])
            P.op("act", lambda e, rsb=rsb: e.activation(out=rsb, in_=rsb, func=AF.Exp, scale=-0.5), r=[rsk], w=[rsk])
            dst = gqn[:, m, :] if m < 4 else gkn[:, m - 4, :]
            sc = 128.0 ** -0.5 if m < 4 else 1.0
            P.op("dve", lambda e, m=m, dst=dst, sc=sc, rsb=rsb: e.scalar_tensor_tensor(dst, cact(m), sc, rsb, ALU.mult, ALU.mult),
                 r=[MK("pre", m), rsk], w=[MK("gqn" if m < 4 else "gkn", m % 4)])
        if KLIM < 7:
            return
        gb_v = gbga[:, :, 0:4]
        ga_v = gbga[:, :, 4:8]
        be = sm[:, 0:16].rearrange("p (b h) -> p b h", b=4)
        gg = sm[:, 16:32].rearrange("p (b h) -> p b h", b=4)
        t1 = sm[:, 32:48].rearrange("p (b h) -> p b h", b=4)
        gkeys = [MK("gbga", b) for b in range(4)]
        P.op("act", lambda e: e.activation(out=be, in_=gb_v, func=AF.Exp, scale=-1.0), r=gkeys, w=[MK("be")])
        P.op("dve", lambda e: e.tensor_scalar(be, be, 1.0, 1.0, ALU.add, ALU.mult), r=[MK("be")], w=[MK("be")])
        P.op("dve", lambda e: e.reciprocal(be, be), r=[MK("be")], w=[MK("be")])
        P.op("dve", lambda e: e.tensor_tensor(gg, ga_v, bc(dtb_b[:, :], [128, 4, 4], 1), ALU.add), r=gkeys + [("dtb",)], w=[MK("gg")])
        P.op("dve", lambda e: e.tensor_scalar(t1, gg, -1.0, 0.0, ALU.mult, ALU.add), r=[MK("gg")], w=[MK("t1")])
        P.op("dve", lambda e: e.tensor_tensor(t1, t1, gg, ALU.max), r=[MK("gg"), MK("t1")], w=[MK("t1")])
        P.op("act", lambda e: e.activation(out=t1, in_=t1, func=AF.Exp, scale=-1.0), r=[MK("t1")], w=[MK("t1")])
        P.op("act", lambda e: e.activation(out=t1, in_=t1, func=AF.Ln, bias=1.0), r=[MK("t1")], w=[MK("t1")])
        P.op("dve", lambda e: e.scalar_tensor_tensor(gg, gg, 0.0, t1, ALU.max, ALU.add), r=[MK("gg"), MK("t1")], w=[MK("gg")])
        P.op("dve", lambda e: e.tensor_tensor(gg, gg, bc(negA[:, :], [128, 4, 4], 1), ALU.mult), r=[MK("gg"), ("negA",)], w=[MK("gg")])
        if sample:
            P.op("dve", lambda e: e.tensor_scalar(gg, gg, tok0[:, 0:1], 0.0, ALU.mult, ALU.add), r=[MK("gg"), ("tok0",)], w=[MK("gg")])

        if KLIM < 8:
            return
        v4 = lambda ap: ap.rearrange("p (h t) -> p h t", h=4)
        for b in range(4):
            cols = slice(b * 128, (b + 1) * 128)
            gcc = sm[:, 48:52]
            gl = sm[:, 52:56]
            s1 = sm[:, 56:60]
            s2 = sm[:, 60:63]
            k = nb()
            P.op("pe", lambda e, k=k, b=b: e.matmul(ps[:, k, 0:4], triu[:, :], gg[:, b, :], start=True, stop=True), r=[("triu",), MK("gg")], w=[("ps", k)])
            P.op("dve", lambda e, k=k: e.tensor_copy(gcc, ps[:, k, 0:4]), r=[("ps", k)], w=[MK("gcc")])
            P.op("dve", lambda e, b=b: e.tensor_tensor(v4(tA), bc(triu[:, :], [128, 4, 128], 1), bc(gg[:, b, :], [128, 4, 128], 2), ALU.mult),
                 r=[("triu",), MK("gg")], w=[MK("tA")])
            kr = nb()
            P.op("pe", lambda e, kr=kr: e.matmul(ps[:, kr, :], ones_f[:, :], tA, start=True, stop=True), r=[("ones_f",), MK("tA")], w=[("ps", kr)])
            P.op("dve", lambda e, kr=kr: e.tensor_tensor(v4(tB), v4(ps[:, kr, :]), bc(gcc, [128, 4, 128], 2), ALU.subtract), r=[("ps", kr), MK("gcc")], w=[MK("tB")])
            PQ("dve", lambda e: e.tensor_scalar(tC, tB, 0.0, 0.0, ALU.min, ALU.add), r=[MK("tB")], w=[MK("tC")])
            PQ("act", lambda e: e.activation(out=tC, in_=tC, func=AF.Exp), r=[MK("tC")], w=[MK("tC")])
            PQ("dve", lambda e: e.tensor_tensor(v4(tC), v4(tC), bc(triu[:, :], [128, 4, 128], 1), ALU.mult), r=[MK("tC"), ("triu",)], w=[MK("tC")])
            PD("dve", lambda e: e.tensor_scalar(tD, tB, 0.0, 0.0, ALU.max, ALU.add), r=[MK("tB")], w=[MK("tD")])
            PD("act", lambda e: e.activation(out=tD, in_=tD, func=AF.Exp, scale=-1.0), r=[MK("tD")], w=[MK("tD")])
            PD("dve", lambda e: e.tensor_tensor(v4(tD), v4(tD), bc(strl[:, :], [128, 4, 128], 1), ALU.mult), r=[MK("tD"), ("strl",)], w=[MK("tD")])
            PQ("act", lambda e, kr=kr: e.activation(out=tE, in_=ps[:, kr, :], func=AF.Exp), r=[("ps", kr)], w=[MK("tE")])
            P.op("act", lambda e, kr=kr: e.activation(out=gl, in_=v4(ps[:, kr, :])[:, :, 127], func=AF.Exp), r=[("ps", kr)], w=[MK("gl")])
            P.op("dve", lambda e, kr=kr: e.tensor_tensor(s1, v4(ps[:, kr, :])[:, :, 127], gcc, ALU.subtract), r=[("ps", kr), MK("gcc")], w=[MK("s1")])
            P.op("act", lambda e: e.activation(out=s1, in_=s1, func=AF.Exp), r=[MK("s1")], w=[MK("s1")])
            P.op("act", lambda e: e.activation(out=gcc, in_=gcc, func=AF.Exp), r=[MK("gcc")], w=[MK("gcc")])
            P.op("dve", lambda e, b=b: e.tensor_tensor(gcc, gcc, be[:, b, :], ALU.mult), r=[MK("gcc"), MK("be")], w=[MK("gcc")])
            kt = nb()
            ktb = ps[:, kt, :].bitcast(BF16)
            for h in range(4):
                P.op("pe", lambda e, h=h, ktb=ktb, cols=cols: e.transpose(ktb[:, h * 128:(h + 1) * 128], gkn[:, h, cols], ident[:, :]),
                     r=[MK("gkn", h), ("ident",)], w=[("ps", kt)])
            P.op("dve", lambda e, ktb=ktb: e.tensor_tensor(kb[:, :, :], v4(ktb[:, 0:512]), bc(gcc, [128, 4, 128], 2), ALU.mult), r=[("ps", kt), MK("gcc")], w=[MK("kb")])
            P.op("dve", lambda e, ktb=ktb: e.tensor_tensor(kd[:, :, :], v4(ktb[:, 0:512]), bc(s1, [128, 4, 128], 2), ALU.mult), r=[("ps", kt), MK("s1")], w=[MK("kd")])
            kvv = nb()
            for h in range(4):
                P.op("pe", lambda e, h=h, kvv=kvv, cols=cols: e.transpose(ps[:, kvv, h * 128:(h + 1) * 128], cact(8 + h)[:, cols], ident_f[:, :]),
                     r=[MK("pre", 8 + h), ("identf",)], w=[("ps", kvv)])
            P.op("dve", lambda e, kvv=kvv, b=b: e.tensor_tensor(vb[:, :, :], v4(ps[:, kvv, :]), bc(be[:, b, :], [128, 4, 128], 2), ALU.mult), r=[("ps", kvv), MK("be")], w=[MK("vb")])
            kkk, kqk = nb(), nb()
            for h in range(4):
                PD("pe", lambda e, h=h, kkk=kkk, cols=cols: e.matmul(ps[:, kkk, h * 128:(h + 1) * 128], gkn[:, h, cols], gkn[:, h, cols], start=True, stop=True),
                     r=[MK("gkn", h)], w=[("ps", kkk)])
            for h in range(4):
                PQ("pe", lambda e, h=h, kqk=kqk, cols=cols: e.matmul(ps[:, kqk, h * 128:(h + 1) * 128], gkn[:, h, cols], gqn[:, h, cols], start=True, stop=True),
                     r=[MK("gkn", h), MK("gqn", h)], w=[("ps", kqk)])
            PD("dve", lambda e, kkk=kkk: e.tensor_tensor(tD, ps[:, kkk, :], tD, ALU.mult), r=[("ps", kkk), MK("tD")], w=[MK("tD")])
            PD("dve", lambda e, b=b: e.scalar_tensor_tensor(Xb[0][:, :, :], v4(tD), -1.0, bc(be[:, b, :], [128, 4, 128], 2), ALU.mult, ALU.mult),
                 r=[MK("tD"), MK("be")], w=[MK("X", 0)])
            PQ("dve", lambda e, kqk=kqk: e.tensor_tensor(qkT[:, :, :], v4(ps[:, kqk, :]), v4(tC), ALU.mult), r=[("ps", kqk), MK("tC")], w=[MK("qkT")])
            PQ("pool", lambda e, cols=cols: e.tensor_tensor(qdT[:, :, :], gqn[:, :, cols], v4(tE), ALU.mult), r=[MK("gqn", h) for h in range(4)] + [MK("tE")], w=[MK("qdT")])
            ky = nb()
            kyb = ps[:, ky, :].bitcast(BF16)
            for h in range(4):
                PD("pe", lambda e, h=h, ky=ky: e.transpose(ps[:, ky, h * 128:(h + 1) * 128], Xb[0][:, h, :], ident_f[:, :]), r=[MK("X", 0), ("identf",)], w=[("ps", ky)])
            PD("act", lambda e, ky=ky: e.activation(out=Yb[0][:, :, :], in_=v4(ps[:, ky, :]), func=AF.Identity), r=[("ps", ky)], w=[MK("Y", 0)])
            PD("dve", lambda e: e.tensor_tensor(Nf, Yb[0][:, :, :], bc(ident_f[:, :], [128, 4, 128], 1), ALU.add), r=[MK("Y", 0), ("identf",)], w=[MK("Nf")])
            cur = 0
            for st in range(1, 7):
                nx = 1 - cur
                kx, kyy = nb(), nb()
                for h in range(4):
                    PD("pe", lambda e, h=h, kx=kx, cur=cur: e.matmul(ps[:, kx, h * 128:(h + 1) * 128], Yb[cur][:, h, :], Xb[cur][:, h, :], start=True, stop=True),
                         r=[MK("X", cur), MK("Y", cur)], w=[("ps", kx)])
                for h in (range(4) if st < 6 else ()):
                    PD("pe", lambda e, h=h, kyy=kyy, cur=cur: e.matmul(ps[:, kyy, h * 128:(h + 1) * 128], Xb[cur][:, h, :], Yb[cur][:, h, :], start=True, stop=True),
                         r=[MK("X", cur), MK("Y", cur)], w=[("ps", kyy)])
                PD("act", lambda e, kx=kx, nx=nx: e.activation(out=Xb[nx][:, :, :], in_=v4(ps[:, kx, :]), func=AF.Identity), r=[("ps", kx)], w=[MK("X", nx)])
                if st < 6:
                    PD("dve", lambda e, kyy=kyy, nx=nx: e.tensor_copy(Yb[nx][:, :, :], v4(ps[:, kyy, :])), r=[("ps", kyy)], w=[MK("Y", nx)])
                kn2 = nb()
                for h in range(4):
                    PD("pe", lambda e, h=h, kn2=kn2, nx=nx: e.matmul(ps[:, kn2, h * 128:(h + 1) * 128], Xb[nx][:, h, :], Nf[:, h, :], start=True, stop=True),
                         r=[MK("X", nx), MK("Nf")], w=[("ps", kn2)])
                PD("dve", lambda e, kn2=kn2: e.tensor_tensor(Nf, Nf, v4(ps[:, kn2, :]), ALU.add), r=[("ps", kn2), MK("Nf")], w=[MK("Nf")])
                cur = nx
            PD("act", lambda e: e.activation(out=Nb[:, :, :], in_=Nf, func=AF.Identity), r=[MK("Nf")], w=[MK("N")])
            ku, kw = nb(), nb()
            for h in range(4):
                P.op("pe", lambda e, h=h, ku=ku: e.matmul(ps[:, ku, h * 128:(h + 1) * 128], Nb[:, h, :], vb[:, h, :], start=True, stop=True), r=[MK("N"), MK("vb")], w=[("ps", ku)])
            for h in range(4):
                P.op("pe", lambda e, h=h, kw=kw: e.matmul(ps[:, kw, h * 128:(h + 1) * 128], kb[:, h, :], Nb[:, h, :], start=True, stop=True), r=[MK("N"), MK("kb")], w=[("ps", kw)])
            P.op("act", lambda e, ku=ku: e.activation(out=u_t, in_=ps[:, ku, :], func=AF.Identity), r=[("ps", ku)], w=[MK("u")])
            P.op("act", lambda e, kw=kw: e.activation(out=wT[:, :, :], in_=v4(ps[:, kw, :]), func=AF.Identity), r=[("ps", kw)], w=[MK("wT")])
            if sample:
                sq = tix * 4 + b
                P.op("sp", lambda e, sq=sq: e.dma_start(out=Sst[:, :, :], in_=sgdn[sq].rearrange("h k v -> k h v")), w=[MK("S")], chan="sld")
                P.op("act", lambda e: e.activation(out=Sbf[:, :, :], in_=Sst[:, :, :], func=AF.Identity), r=[MK("S")], w=[MK("Sbf")])
            k1 = nb()
            for h in range(4):
                P.op("pe", lambda e, h=h, k1=k1: e.matmul(ps[:, k1, h * 128:(h + 1) * 128], wT[:, h, :], Sbf[:, h, :], start=True, stop=True), r=[MK("wT"), MK("Sbf")], w=[("ps", k1)])
            P.op("dve", lambda e, k1=k1: e.tensor_tensor(vnew[:, :, :], v4(u_t), v4(ps[:, k1, :]), ALU.subtract), r=[("ps", k1), MK("u")], w=[MK("vnew")])
            k3, k4 = nb(), nb()
            for h in range(4):
                PQ("pe", lambda e, h=h, k3=k3: e.matmul(ps[:, k3, h * 128:(h + 1) * 128], Sbf[:, h, :], qdT[:, h, :], start=True, stop=False), r=[MK("Sbf"), MK("qdT")], w=[("ps", k3)])
                PQ("pe", lambda e, h=h, k3=k3: e.matmul(ps[:, k3, h * 128:(h + 1) * 128], vnew[:, h, :], qkT[:, h, :], start=False, stop=True), r=[MK("vnew"), MK("qkT")], w=[("ps", k3)])
            for h in range(4):
                P.op("pe", lambda e, h=h, k4=k4: e.matmul(ps[:, k4, h * 128:(h + 1) * 128], kd[:, h, :], vnew[:, h, :], start=True, stop=True), r=[MK("kd"), MK("vnew")], w=[("ps", k4)])
            PQ("act", lambda e, k3=k3, cols=cols: e.activation(out=oT[:, :, cols], in_=v4(ps[:, k3, :]), func=AF.Identity), r=[("ps", k3)], w=[MK("oT", b), MK("ncrow")])
            for h in range(4):
                P.op("dve", lambda e, h=h, k4=k4: e.scalar_tensor_tensor(Sst[:, h, :], Sst[:, h, :], gl[:, h:h + 1], ps[:, k4, h * 128:(h + 1) * 128], ALU.mult, ALU.add),
                     r=[("ps", k4), MK("gl"), MK("S")], w=[MK("S")])
            if sample:
                P.op("sp", lambda e, sq=sq: e.dma_start(out=ngs[sq].rearrange("h k v -> k h v"), in_=Sst[:, :, :]), r=[MK("S")], w=[("o_ngs", sq)], chan="ongs")
            else:
                P.op("act", lambda e: e.activation(out=Sbf[:, :, :], in_=Sst[:, :, :], func=AF.Identity), r=[MK("S")], w=[MK("Sbf")])
                if last_tile and b == 3:
                    P.op("sp", lambda e: e.dma_start(out=ngp.rearrange("h k v -> k h v"), in_=Sst[:, :, :]), r=[MK("S")], w=[("o_ngp",)], chan="ongs")
        if KLIM < 9:
            return
        if prepass:
            return
        for h in range(4):
            P.op("pool", lambda e, h=h: e.tensor_tensor(ctmp[:, :], oT[:, h, :], oT[:, h, :], ALU.mult), r=[MK("oT", b) for b in range(4)], w=[MK("ctmp")])
            k = nb()
            P.op("pe", lambda e, k=k: e.matmul(ps[:, k, :], ones_f[:, :], ctmp[:, :], start=True, stop=True), r=[("ones_f",), MK("ctmp")], w=[("ps", k)])
            P.op("act", lambda e, k=k: e.activation(out=tA, in_=ps[:, k, :], func=AF.Ln, scale=1.0 / 128, bias=EPS), r=[("ps", k)], w=[MK("tA")])
            P.op("act", lambda e: e.activation(out=tA, in_=tA, func=AF.Exp, scale=-0.5), r=[MK("tA")], w=[MK("tA")])
            P.op("dve", lambda e, h=h: e.scalar_tensor_tensor(tA, oT[:, h, :], gng_t[:, 0:1], tA, ALU.mult, ALU.mult), r=[MK("oT", b) for b in range(4)] + [MK("tA"), ("gng",)], w=[MK("tA")])
            P.op("dve", lambda e, h=h: e.tensor_tensor(goT[:, h, :], tA, zs[:, h, :], ALU.mult), r=[MK("tA"), MK("zs", h)], w=[MK("goT", h)])
        if KLIM < 10:
            return
        if sample:
            for t_, src in ((aoTs, aoT), (goTs, goT)):
                P.op("pool", lambda e, t_=t_, src=src: e.tensor_copy(t_[:, :, tix * 4:(tix + 1) * 4], src[:, :, :].rearrange("p c (b t) -> p c b t", b=4)[:, :, :, 0]),
                     r=[MK("aoT", b) for b in range(4)] + [MK("goT", h) for h in range(4)], w=[("mixTs", tix)])
        else:
            for b in range(4):
                k0, k1 = nb(), nb()
                for c in range(8):
                    src = aoT if c < 4 else goT
                    for dh, kk in ((0, k0), (1, k1)):
                        P.op("pe", lambda e, c=c, dh=dh, kk=kk, b=b, src=src: e.matmul(ps[:, kk, :], src[:, c % 4, b * 128:(b + 1) * 128], wmo_t[:, c, dh * 512:(dh + 1) * 512], start=(c == 0), stop=(c == 7)),
                             r=[MK("aoT", b), MK("goT", c % 4), ("wmo",)], w=[("ps", kk)])
                for dh, kk in ((0, k0), (1, k1)):
                    P.op("dve", lambda e, dh=dh, kk=kk, b=b: e.tensor_tensor(xres(b, dh), xres(b, dh), ps[:, kk, :], ALU.add), r=[("ps", kk)], w=[xkeys[b]])

    def final_norm(xsrc, xkeys, nblk, np_, ydst, okey, chan):
        keyss = [("ss",)]
        for b in range(nblk):
            P.op("act", lambda e, b=b: e.activation(out=junk[0:np_, :], in_=xsrc(b), func=AF.Square, accum_out=ss[0:np_, b:b + 1]), r=[xkeys[b]], w=[MK("ctmp")] + keyss)
        rstd_cols(nblk, np_, 1.0 / D, keyss)
        for b in range(nblk):
            P.op("dve", lambda e, b=b: e.scalar_tensor_tensor(xsrc(b), xsrc(b), ss[0:np_, b:b + 1], gfin_b[0:np_, :], ALU.mult, ALU.mult), r=keyss + [xkeys[b], ("gfin",)], w=[xkeys[b]])
            P.op("sp", lambda e, b=b: e.dma_start(out=ydst(b), in_=xsrc(b)), r=[xkeys[b]], w=[(okey, b)], chan=chan)

    skeys = [("xs",)]
    P.op("sp", lambda e: e.dma_start(out=xts[:, 0, :], in_=xs), w=skeys, chan="xsld")
    xs_src = lambda b: xts[0:NS, 0, :]
    xs_dst = lambda b, dh: xts[0:NS, 0, dh * 512:(dh + 1) * 512]
    nTs_keys = [("nTs",)]
    ffn_prefetch(0)
    norm_T(xs_src, skeys, 1, NS, 0, nTs, nTs_keys)
    fence()
    ffn(0, nTs, nTs_keys, NS, xs_dst, skeys, 1, NS)
    xk = [("xnT", b) for b in range(4)]
    if stage >= 2:
        norm_T(xs_src, skeys, 1, NS, 1, nTs, nTs_keys)
        fence()
    for tix in range(4 if stage >= 2 else 0):
        P.op("pool", lambda e: e.memset(xnT[:, :, :], 0.0), w=xk)
        P.op("pool", lambda e, tix=tix: e.tensor_copy(xnT[:, :, :].rearrange("p c (b t) -> p c b t", b=4)[:, :, :, 0], nTs[:, :, tix * 4:(tix + 1) * 4]), r=nTs_keys, w=xk)
        mix(xnT, xk, True, tix, False, False, None, None)
    k0, k1 = nb(), nb()
    for c in range(8 if stage >= 2 else 0):
        src = aoTs if c < 4 else goTs
        for dh, kk in ((0, k0), (1, k1)):
            P.op("pe", lambda e, c=c, dh=dh, kk=kk, src=src: e.matmul(ps[0:NS, kk, :], src[:, c % 4, :], wmo_t[:, c, dh * 512:(dh + 1) * 512], start=(c == 0), stop=(c == 7)),
                 r=[("mixTs", t) for t in range(4)] + [("wmo",)], w=[("ps", kk)])
    for dh, kk in (((0, k0), (1, k1)) if stage >= 2 else ()):
        P.op("dve", lambda e, dh=dh, kk=kk: e.tensor_tensor(xs_dst(0, dh), xs_dst(0, dh), ps[0:NS, kk, :], ALU.add), r=[("ps", kk)], w=skeys)
    if stage >= 2:
        ffn_prefetch(1)
        norm_T(xs_src, skeys, 1, NS, 2, nTs, nTs_keys)
        fence()
        ffn(1, nTs, nTs_keys, NS, xs_dst, skeys, 1, NS)
    final_norm(xs_src, skeys, 1, NS, lambda b: ys, "o_ys", "oys")

    tiles = [("pre", i) for i in range(n_pre)] + [("main", i) for i in range(n_tiles)]

    def load_x(gi):
        kind, i = tiles[gi]
        src = xpre if kind == "pre" else xp
        X = xt[gi % 2]
        keys = [("x", gi % 2, b) for b in range(4)]
        if gi == 1:
            keys = keys + [("xs",), ("sct",), MK("ckt"), MK("cvt"), MK("ckd", 0), MK("ckd", 1)]
        P.op("sp", lambda e: [e.dma_start(out=X[:, b, :], in_=src[i * TT + b * 128: i * TT + (b + 1) * 128, :]) for b in range(4)],
             w=keys, chan=f"x{gi % 2}", nd=4)

    if stage >= 3:
        load_x(0)
    for gi, (kind, i) in enumerate(tiles if stage >= 3 else []):
        X = xt[gi % 2]
        xkeys = [("x", gi % 2, b) for b in range(4)]
        xsrc = lambda b, X=X: X[:, b, :]
        xdst = lambda b, dh, X=X: X[:, b, dh * 512:(dh + 1) * 512]
        ffn_prefetch(0)
        norm_T(xsrc, xkeys, 4, 128, 0, xnT, xk)
        fence()
        ffn(0, xnT, xk, TT, xdst, xkeys, 4, 128)
        norm_T(xsrc, xkeys, 4, 128, 1, xnT, xk)
        fence()
        if kind == "pre":
            mix(xnT, xk, False, 0, i == 0, False, xdst, xkeys, prepass=True, pre_last=(i == n_pre - 1))
            if gi + 1 < len(tiles):
                load_x(gi + 1)
            continue
        mix(xnT, xk, False, 0, (n_pre == 0 and i == 0), i == n_tiles - 1, xdst, xkeys, mask_prev=(n_pre > 0 and i == 0))
        if gi + 1 < len(tiles):
            load_x(gi + 1)
        ffn_prefetch(1)
        norm_T(xsrc, xkeys, 4, 128, 2, xnT, xk)
        fence()
        ffn(1, xnT, xk, TT, xdst, xkeys, 4, 128)
        final_norm(xsrc, xkeys, 4, 128, lambda b, i=i: yp[i * TT + b * 128: i * TT + (b + 1) * 128, :], ("o_yp", i), f"oy{gi % 2}")

    with ExitStack() as st:
        sems = {}
        for en in Prog.ENG:
            sems[("e", en)] = st.enter_context(nc.semaphore("e_" + en))
        for ch in P.chan_cnt:
            sems[("c", ch)] = st.enter_context(nc.semaphore("c_" + ch))
        P.run(sems)
    return nc


_NC_CACHE = {}


def run_cores(xp_list, per_core, shared, n_tiles, stage=3, xpre_list=None, hasprev=None):
    n_pre = 0 if xpre_list is None else xpre_list[0].shape[0] // TT
    key = (n_tiles, n_pre, stage)
    if key not in _NC_CACHE:
        _NC_CACHE[key] = build(n_tiles, stage, n_pre)
    nc = _NC_CACHE[key]
    consts = host_consts()
    in_maps = []
    for c in range(len(xp_list)):
        m = dict(shared)
        m.update(per_core[c])
        m["xp"] = xp_list[c]
        if n_pre:
            m["xpre"] = xpre_list[c]
        m["hasprev"] = np.full((128, 1), 0.0 if hasprev is None else hasprev[c], np.float32)
        for k, v in consts.items():
            m["c_" + k] = v
        in_maps.append({k: np.ascontiguousarray(v, dtype=np.float32) for k, v in m.items()})
    res = run_bass_kernel_spmd(nc, in_maps, core_ids=list(range(len(xp_list))))
    return res.results


def kernel(x_prompt, x_sample, cache_attn_k, cache_attn_v, state_conv, state_gdn,
           ffn1_norm_g, ffn1_w_in, ffn1_w_out, mix_norm_g, w_in_mix, attn_sinks, conv_w,
           gdn_A_log, gdn_dt_bias, gdn_norm_g, w_out_mix, ffn2_norm_g, ffn2_w_in, ffn2_w_out,
           final_norm_g):
    f = lambda a: np.asarray(a, dtype=np.float32)
    x_prompt = f(x_prompt)
    B, S, _ = x_prompt.shape
    ncore = 8
    HALF = S // 2
    n_tiles = HALF // TT
    shared = dict(g1=f(ffn1_norm_g)[0], wi1=f(ffn1_w_in)[0], wo1=f(ffn1_w_out)[0], g2=f(mix_norm_g)[0],
                  wmi=f(w_in_mix)[0], sinks=f(attn_sinks)[0], convw=f(conv_w)[0], alog=f(gdn_A_log)[0],
                  dtb=f(gdn_dt_bias)[0], gng=f(gdn_norm_g)[0], wmo=f(w_out_mix)[0], g3=f(ffn2_norm_g)[0],
                  wi2=f(ffn2_w_in)[0], wo2=f(ffn2_w_out)[0], gfin=f(final_norm_g))
    xs_ = f(x_sample)[:, 0, :]
    ck_ = f(cache_attn_k)[0].reshape(-1, 128, 128)
    cv_ = f(cache_attn_v)[0].reshape(-1, 128, 128)
    sc_ = f(state_conv)[0]
    sg_ = f(state_gdn)[0]
    per_core, xp_list, xpre_list, hasprev = [], [], [], []
    zeros = np.zeros((HALF, D), np.float32)
    for c in range(ncore):
        sl = slice(c * NS, (c + 1) * NS)
        per_core.append(dict(xs=xs_[sl], ck=ck_[sl], cv=cv_[sl], sconv=sc_[sl].reshape(NS * 3, 1536), sgdn=sg_[sl]))
        sq, half = c // 2, c % 2
        xp_list.append(x_prompt[sq, half * HALF:(half + 1) * HALF])
        xpre_list.append(x_prompt[sq, 0:HALF] if half else zeros)
        hasprev.append(float(half))
    res = run_cores(xp_list, per_core, shared, n_tiles, 3, xpre_list, hasprev)
    y_prompt = np.stack([np.concatenate([res[2 * b]["yp"], res[2 * b + 1]["yp"]]) for b in range(B)])
    cat = lambda k: np.stack([res[2 * b + 1][k] for b in range(B)])
    y_sample = np.concatenate([res[c]["ys"] for c in range(ncore)])[:, None, :]
    nkp = cat("nkp").reshape(1, B, 128, 2, 64)
    nvp = cat("nvp").reshape(1, B, 128, 2, 64)
    ncp = cat("ncp")[None]
    ngp = cat("ngp")[None]
    nks = np.concatenate([res[c]["nks"] for c in range(ncore)]).reshape(1, ncore * NS, 128, 2, 64)
    nvs = np.concatenate([res[c]["nvs"] for c in range(ncore)]).reshape(1, ncore * NS, 128, 2, 64)
    ncs = np.concatenate([res[c]["ncs"] for c in range(ncore)])[None]
    ngs = np.concatenate([res[c]["ngs"] for c in range(ncore)])[None]
    return (y_prompt, y_sample, nkp, nvp, ncp, ngp, nks, nvs, ncs, ngs)
```

```python
import os
import numpy as np
from contextlib import ExitStack
KLIMS = float(os.environ.get('KLIMS', '99'))
KLIMP = float(os.environ.get('KLIMP', '99'))
import concourse.bass as bass
import concourse.mybir as mybir
from concourse.bass_utils import run_bass_kernel_spmd

F32 = mybir.dt.float32
BF16 = mybir.dt.bfloat16
ALU = mybir.AluOpType
AF = mybir.ActivationFunctionType
AX = mybir.AxisListType

D = 1024
FF = 2816
NJ = 22
EPS = 1e-6
TT = 512
NBLK = 4
NS = 16
INW = 2824


class Prog:
    ENG = ("pe", "act", "dve", "pool", "sp")

    def __init__(self, nc):
        self.nc = nc
        self.ops = []
        self.lastw = {}
        self.readers = {}
        self.chan_cnt = {}

    def op(self, eng, fn, r=(), w=(), chan=None, nd=1):
        i = len(self.ops)
        deps = set()
        for k in r:
            if k in self.lastw:
                deps.add(self.lastw[k])
        for k in w:
            if k in self.lastw:
                deps.add(self.lastw[k])
            deps.update(self.readers.get(k, ()))
        for k in r:
            self.readers.setdefault(k, []).append(i)
        for k in w:
            self.lastw[k] = i
            self.readers[k] = []
        o = dict(eng=eng, fn=fn, deps=deps, chan=chan, nd=nd, sig=False, val=0)
        if chan is not None:
            self.chan_cnt[chan] = self.chan_cnt.get(chan, 0) + 16 * nd
            o["val"] = self.chan_cnt[chan]
        self.ops.append(o)
        return i

    def allkeys(self, pred):
        ks = set(self.lastw) | set(self.readers)
        return [k for k in ks if pred(k)]

    def run(self, sems):
        nc = self.nc
        ops = self.ops
        for o in ops:
            for d in o["deps"]:
                ops[d]["sig"] = True
        cnt = {e: 0 for e in self.ENG}
        for o in ops:
            if o["chan"] is None and o["sig"]:
                cnt[o["eng"]] += 1
                o["val"] = cnt[o["eng"]]
        streams = {e: [] for e in self.ENG}
        for i, o in enumerate(ops):
            streams[o["eng"]].append(i)

        def runner(ename):
            def f(e):
                waited = {}
                for i in streams[ename]:
                    o = ops[i]
                    for d in sorted(o["deps"]):
                        do = ops[d]
                        if do["chan"] is not None:
                            key = ("c", do["chan"])
                        else:
                            if do["eng"] == ename and ename in ("pe", "sp"):
                                continue
                            key = ("e", do["eng"])
                        if waited.get(key, 0) >= do["val"]:
                            continue
                        waited[key] = do["val"]
                        e.wait_ge(sems[key], do["val"])
                    res = o["fn"](e)
                    if o["chan"] is not None:
                        lst = res if isinstance(res, (list, tuple)) else [res]
                        assert len(lst) == o["nd"], (len(lst), o["nd"])
                        for ins in lst:
                            ins.then_inc(sems[("c", o["chan"])], 16)
                    elif o["sig"]:
                        res.then_inc(sems[("e", ename)], 1)
                if ename == "sp":
                    for ch, v in self.chan_cnt.items():
                        if waited.get(("c", ch), 0) < v:
                            e.wait_ge(sems[("c", ch)], v)
            return f

        with nc.Block() as block:
            block.tensor(runner("pe"))
            block.scalar(runner("act"))
            block.vector(runner("dve"))
            block.gpsimd(runner("pool"))
            block.sync(runner("sp"))


def host_consts():
    i = np.arange(128)
    c = {}
    c["ident"] = np.eye(128, dtype=np.float32)
    c["triu"] = (i[:, None] <= i[None, :]).astype(np.float32)
    c["strl"] = (i[:, None] > i[None, :]).astype(np.float32)
    slopes = np.exp2(-8.0 * np.arange(1, 9, dtype=np.float32) / 8.0).astype(np.float32)
    jj = i[:, None, None].astype(np.float32)
    ii = i[None, None, :].astype(np.float32)
    sl = slopes[None, :, None]
    bc = np.where(ii >= jj, -sl * (ii - jj), -30000.0).astype(np.float32)
    bp = np.where(jj >= ii, -sl * (ii - jj + 128.0), -30000.0).astype(np.float32)
    c["bcur"] = np.ascontiguousarray(bc.reshape(128, 1024))
    c["bprev"] = np.ascontiguousarray(bp.reshape(128, 1024))
    tm = np.zeros((128, 1), np.float32)
    tm[0, 0] = 1.0
    c["tok0"] = tm
    cm = np.zeros((128, 512), np.float32)
    cm[:, 0::128] = 1.0
    c["col0"] = cm
    qm = np.zeros((128, 4), np.float32)
    qm[:64, 0] = 0.125; qm[64:, 1] = 0.125; qm[:64, 2] = 1.0; qm[64:, 3] = 1.0
    c["qm"] = qm
    return c


def build(n_tiles, stage=3, n_pre=0):
    nc = bass.Bass("TRN2", target_bir_lowering=False)
    NTOK = n_tiles * TT

    def din(name, shape):
        return nc.dram_tensor(name, list(shape), F32, kind="ExternalInput").ap()

    def dout(name, shape):
        return nc.dram_tensor(name, list(shape), F32, kind="ExternalOutput").ap()

    xp = din("xp", [NTOK, D])
    xpre = din("xpre", [n_pre * TT, D]) if n_pre > 0 else None
    hasprev = din("hasprev", [128, 1])
    xs = din("xs", [NS, D])
    ck = din("ck", [NS, 128, 128])
    cv = din("cv", [NS, 128, 128])
    sconv = din("sconv", [NS * 3, 1536])
    sgdn = din("sgdn", [NS, 4, 128, 128])
    g1 = din("g1", [D]); wi1 = din("wi1", [D, 2 * FF]); wo1 = din("wo1", [FF, D])
    g2 = din("g2", [D]); wmi = din("wmi", [D, INW]); sinks = din("sinks", [8])
    convw = din("convw", [4, 1536]); alog = din("alog", [4]); dtb = din("dtb", [4])
    gng = din("gng", [128]); wmo = din("wmo", [D, D])
    g3 = din("g3", [D]); wi2 = din("wi2", [D, 2 * FF]); wo2 = din("wo2", [FF, D])
    gfin = din("gfin", [D])
    c_ident = din("c_ident", [128, 128]); c_triu = din("c_triu", [128, 128]); c_strl = din("c_strl", [128, 128])
    c_bcur = din("c_bcur", [128, 1024]); c_bprev = din("c_bprev", [128, 1024])
    c_tok0 = din("c_tok0", [128, 1]); c_col0 = din("c_col0", [128, 512]); c_qm = din("c_qm", [128, 4])

    yp = dout("yp", [NTOK, D]); ys = dout("ys", [NS, D])
    nkp = dout("nkp", [128, 128]); nvp = dout("nvp", [128, 128])
    ncp = dout("ncp", [3, 1536]); ngp = dout("ngp", [4, 128, 128])
    nks = dout("nks", [NS, 128, 128]); nvs = dout("nvs", [NS, 128, 128])
    ncs = dout("ncs", [NS, 3, 1536]); ngs = dout("ngs", [NS, 4, 128, 128])

    wi_s = [nc.dram_tensor(f"wi_s{i}", [NJ, 128, 8, 256], BF16).ap() for i in range(2)]
    wo_s = [nc.dram_tensor(f"wo_s{i}", [128, NJ, D], BF16).ap() for i in range(2)]
    wm_s = nc.dram_tensor("wm_s", [11, 128, 8, 256], BF16).ap()
    wtm_s = nc.dram_tensor("wtm_s", [128, 8, 264], BF16).ap()
    wmo_s = nc.dram_tensor("wmo_s", [128, 8, D], BF16).ap()

    A = nc.alloc_sbuf_tensor
    xt = [A(f"xt{i}", [128, NBLK, D], F32) for i in range(2)]
    xts = xt[1][0:NS, 0:1, :]
    x1f = xt[1][:, 1:4, :].rearrange("p b d -> p (b d)")
    xn = A("xn", [128, D], BF16)
    xnT = A("xnT", [128, 8, TT], BF16)
    nTs = A("nTs", [128, 8, NS], BF16)
    wi = [A(f"wibuf{i}", [128, 8, 256], BF16) for i in range(3)]
    wmo_t = A("wmo_t", [128, 8, D], BF16)
    wtm = A("wtm", [128, 8, 264], BF16)
    arena = A("arena", [128, 16896], F32)
    hT = arena[:, 0:5632].bitcast(BF16).rearrange("p (j t) -> p j t", j=NJ)
    wo = arena[:, 5632:16896].bitcast(BF16).rearrange("p (j n) -> p j n", j=NJ)
    off = [0]

    def carve(ncols_f32):
        a = off[0]
        off[0] += ncols_f32
        assert off[0] <= 16896
        return arena[:, a:a + ncols_f32]

    pre = carve(12 * 524).rearrange("p (m t) -> p m t", m=12)
    oT = carve(2048).rearrange("p (h t) -> p h t", h=4)
    ncrow = oT[0:4, :, :].rearrange("p h t -> p (h t)")[:, 0:1536]
    ebuf = carve(1024)
    tC = ebuf[:, 0:512]; tD = ebuf[:, 512:1024]
    tA = carve(512); tB = carve(512); tE = carve(512)
    u_t = carve(512)
    rbuf = tB
    gqn = carve(1024).bitcast(BF16).rearrange("p (h t) -> p h t", h=4)
    gkn = carve(1024).bitcast(BF16).rearrange("p (h t) -> p h t", h=4)
    zs = carve(1024).bitcast(BF16).rearrange("p (h t) -> p h t", h=4)
    Pp = carve(512).bitcast(BF16).rearrange("p (h t) -> p h t", h=8)
    qT = A("qT", [128, 4, TT], BF16)
    kTd = [A(f"kTd{i}", [128, 8, 128], BF16) for i in range(2)]
    Vd = [A(f"Vd{i}", [128, 8, 128], BF16) for i in range(2)]
    aoT = A("aoT", [128, 4, TT], BF16)
    goT = A("goT", [128, 4, TT], BF16)
    aoTs = A("aoTs", [128, 4, NS], BF16)
    goTs = A("goTs", [128, 4, NS], BF16)
    Pc = A("Pc", [128, 8, 128], BF16)
    Xb = [A("Xb0", [128, 4, 128], F32), carve(512).rearrange("p (h t) -> p h t", h=4)]
    Yb = [A("Yb0", [128, 4, 128], F32), carve(512).rearrange("p (h t) -> p h t", h=4)]
    Nf = carve(512).rearrange("p (h t) -> p h t", h=4)
    Nb = A("Nb", [128, 4, 128], BF16)
    vb = A("vb", [128, 4, 128], BF16)
    kb = A("kb", [128, 4, 128], BF16)
    kd = A("kd", [128, 4, 128], BF16)
    wT = A("wT", [128, 4, 128], BF16)
    qdT = A("qdT", [128, 4, 128], BF16)
    qkT = A("qkT", [128, 4, 128], BF16)
    vnew = A("vnew", [128, 4, 128], BF16)
    Sbf = A("Sbf", [128, 4, 128], BF16)
    ctmp = A("ctmp", [128, TT], F32)
    junk = ctmp[:, :].bitcast(BF16)
    Sst = A("Sst", [128, 4, 128], F32)
    ccar = A("ccar", [128, 12, 3], F32)
    kvo = A("kvo", [128, 256], F32)
    gbga = A("gbga", [128, NBLK, 8], F32)
    sm = A("sm", [128, 64], F32)
    ss = A("ss", [128, 8], F32)
    sg = [A("sg0", [128, TT], F32)] * 2
    ident_f = A("ident_f", [128, 128], F32)
    ident = A("ident_b", [128, 128], BF16)
    triu = A("triu", [128, 128], F32)
    strl = A("strl", [128, 128], F32)
    ones_f = A("ones_f", [128, 128], F32)
    ones_b = A("ones_b", [128, 128], BF16)
    bcur = A("bcur", [128, 1024], BF16)
    bprev = A("bprev", [128, 1024], BF16)
    tok0 = A("tok0", [128, 1], F32)
    hp_t = A("hp_t", [128, 1], F32)
    qm = A("qm", [128, 4], F32)
    qT2 = A("qT2", [128, 4, TT], BF16)
    col0 = A("col0", [128, 512], F32)
    gT = [A(f"gT{i}", [128, 8], F32) for i in range(3)]
    gfin_b = A("gfin_b", [128, D], F32)
    cw = A("cw", [128, 4, 12], F32)
    esink = A("esink", [128, 8], F32)
    negA = A("negA", [128, 4], F32)
    dtb_b = A("dtb_b", [128, 4], F32)
    gng_t = A("gng_t", [128, 1], F32)
    sct = x1f[0:NS * 3, 0:1536]
    ckt = x1f[:, 1536:1664]
    cvt = x1f[:, 1664:1792]
    ckd = x1f[:, 1792:2048].rearrange("p (a b d) -> p a b d", a=2, b=2)
    ps = nc.alloc_psum_tensor("ps", [128, 8, 512], F32)

    P = Prog(nc)
    MK = lambda *a: ("m_" + a[0],) + tuple(a[1:])
    bank = [0]

    def nb():
        bank[0] = (bank[0] + 1) % 8
        return bank[0]

    def bc(ap, shape, axis):
        return ap.unsqueeze(axis).broadcast_to(shape)

    def cast_ffn(i, w_in, w_out):
        v = w_in.rearrange("(c p) n -> p c n", p=128)
        for j in range(NJ):
            def f(e, j=j):
                return [e.dma_start(out=wi_s[i][j, :, :, g * 128:(g + 1) * 128],
                                    in_=v[:, :, g * FF + j * 128: g * FF + (j + 1) * 128]) for g in range(2)]
            P.op("pool", f, w=[("wi_s", i, j)], chan=f"cast{i}", nd=2)
        P.op("pool", lambda e: e.dma_start(out=wo_s[i], in_=w_out.rearrange("(j p) n -> p j n", p=128)),
             w=[("wo_s", i), ("castgrp", i)], chan=f"cast{i}")

    cast_ffn(0, wi1, wo1)
    wmv = wmi.rearrange("(c p) n -> p c n", p=128)
    groups = [(128 * p, False) for p in range(4)] + [(512, True), (576, True)] + \
             [(768 + 128 * m, False) for m in range(12)] + [(2304 + 128 * h, False) for h in range(4)]
    for s in range(11):
        def f(e, s=s):
            r = []
            for g in range(2):
                c0, dup = groups[2 * s + g]
                if dup:
                    for q in range(2):
                        r.append(e.dma_start(out=wm_s[s, :, :, g * 128 + q * 64: g * 128 + (q + 1) * 64], in_=wmv[:, :, c0:c0 + 64]))
                else:
                    r.append(e.dma_start(out=wm_s[s, :, :, g * 128:(g + 1) * 128], in_=wmv[:, :, c0:c0 + 128]))
            return r
        nd = sum(2 if groups[2 * s + g][1] else 1 for g in range(2))
        P.op("pool", f, w=[("wm_s", s)], chan="castm", nd=nd)

    def f(e):
        return [e.dma_start(out=wtm_s[:, :, 0:128], in_=wmv[:, :, 640:768]),
                e.dma_start(out=wtm_s[:, :, 128:256], in_=wmv[:, :, 512:640]),
                e.dma_start(out=wtm_s[:, :, 256:264], in_=wmv[:, :, 2816:2824])]
    P.op("pool", f, w=[("wtm_s",)], chan="castm", nd=3)
    P.op("pool", lambda e: e.dma_start(out=wmo_s, in_=wmo.rearrange("(c p) n -> p c n", p=128)), w=[("wmo_s",), ("castgrp", "m")], chan="castm")
    cast_ffn(1, wi2, wo2)

    for sq in range(NS):
        def f(e, sq=sq):
            return [e.dma_start(out=nks[sq, 0:127, :], in_=ck[sq, 1:128, :]),
                    e.dma_start(out=nvs[sq, 0:127, :], in_=cv[sq, 1:128, :]),
                    e.dma_start(out=ncs[sq, 0:2, :], in_=sconv[sq * 3 + 1: sq * 3 + 3, :])]
        P.op("pool", f, w=[("o_hist", sq)], chan="ohist", nd=3)

    ldn = [0]

    def ld(dst, src, key, **kw):
        ldn[0] += 1
        P.op("sp", lambda e: e.dma_start(out=dst, in_=src, **kw), w=[key], chan=f"const{ldn[0]}")

    ld(ident_f[:, :], c_ident, ("identf",)); ld(triu[:, :], c_triu, ("triu",)); ld(strl[:, :], c_strl, ("strl",))
    P.op("pool", lambda e: e.dma_start(out=bcur[:, :], in_=c_bcur), w=[("bcur",)], chan="constb1")
    P.op("pool", lambda e: e.dma_start(out=bprev[:, :], in_=c_bprev), w=[("bprev",)], chan="constb2")
    ld(tok0[:, :], c_tok0, ("tok0",)); ld(hp_t[:, :], hasprev, ("hp",)); ld(qm[:, :], c_qm, ("qm",)); ld(col0[:, :], c_col0, ("col0",))
    for i, g in enumerate((g1, g2, g3)):
        ld(gT[i][:, :], g.rearrange("(c p) -> p c", p=128), ("gT", i), allow_slow_non_contiguous=True)
    ld(gfin_b[:, :], gfin.partition_broadcast(128), ("gfin",))
    P.op("sp", lambda e: [e.dma_start(out=cw[:, j, :], in_=convw[j].rearrange("(m p) -> p m", p=128), allow_slow_non_contiguous=True) for j in range(4)], w=[("cw",)], chan="constcw", nd=4)
    ld(esink[:, :], sinks.partition_broadcast(128), ("esink",))
    ld(negA[:, :], alog.partition_broadcast(128), ("negA",))
    ld(dtb_b[:, :], dtb.partition_broadcast(128), ("dtb",))
    ld(gng_t[:, :], gng.rearrange("(p o) -> p o", o=1), ("gng",))
    ld(wtm[:, :, :], wtm_s, ("wtm",))
    P.ops[-1]["deps"].add(P.lastw[("castgrp", "m")])
    ld(wmo_t[:, :, :], wmo_s, ("wmo",))
    P.ops[-1]["deps"].add(P.lastw[("castgrp", "m")])
    P.op("dve", lambda e: e.tensor_copy(ident[:, :], ident_f[:, :]), r=[("identf",)], w=[("ident",)])
    P.op("dve", lambda e: e.memset(ones_f[:, :], 1.0), w=[("ones_f",)])
    P.op("dve", lambda e: e.memset(ones_b[:, :], 1.0), w=[("ones_b",)])
    P.op("act", lambda e: e.activation(out=esink[:, :], in_=esink[:, :], func=AF.Exp), r=[("esink",)], w=[("esink",)])
    P.op("act", lambda e: e.activation(out=negA[:, :], in_=negA[:, :], func=AF.Exp), r=[("negA",)], w=[("negA",)])
    P.op("dve", lambda e: e.tensor_scalar(negA[:, :], negA[:, :], -1.0, 0.0, ALU.mult, ALU.add), r=[("negA",)], w=[("negA",)])

    def rstd_cols(n, np_, scale, keyss):
        P.op("act", lambda e: e.activation(out=ss[0:np_, 0:n], in_=ss[0:np_, 0:n], func=AF.Ln, scale=scale, bias=EPS),
             r=keyss, w=keyss)
        P.op("act", lambda e: e.activation(out=ss[0:np_, 0:n], in_=ss[0:np_, 0:n], func=AF.Exp, scale=-0.5),
             r=keyss, w=keyss)

    def norm_T(xsrc, xkeys, nblk, np_, gi, dstT, dkeys):
        keyss = [("ss",)]
        for b in range(nblk):
            P.op("act", lambda e, b=b: e.activation(out=junk[0:np_, :], in_=xsrc(b), func=AF.Square, accum_out=ss[0:np_, b:b + 1]),
                 r=[xkeys[b]], w=[MK("ctmp")] + keyss)
        rstd_cols(nblk, np_, 1.0 / D, keyss)
        for b in range(nblk):
            P.op("dve", lambda e, b=b: e.tensor_scalar(xn[0:np_, :], xsrc(b), ss[0:np_, b:b + 1], 1.0, ALU.mult, ALU.mult),
                 r=keyss + [xkeys[b]], w=[("xn",)])
            k = nb()
            pst = ps[:, k, :].bitcast(BF16)
            for c in range(8):
                P.op("pe", lambda e, c=c, pst=pst: e.transpose(pst[:, c * 128:c * 128 + np_], xn[0:np_, c * 128:(c + 1) * 128], ident[0:np_, 0:np_]),
                     r=[("xn",), ("ident",)], w=[("ps", k)])
            P.op("dve", lambda e, b=b, pst=pst: e.tensor_tensor(
                dstT[:, :, b * np_:(b + 1) * np_], pst.rearrange("p (c t) -> p c t", c=8)[:, :, 0:np_],
                bc(gT[gi][:, :], [128, 8, np_], 2), ALU.mult),
                r=[("ps", k), ("gT", gi)], w=[dkeys[b]])

    pref = {"on": False}

    def ffn_prefetch(fi):
        for j in range(3):
            P.op("sp", lambda e, j=j: e.dma_start(out=wi[j][:, :, :], in_=wi_s[fi][j]),
                 r=[("castgrp", fi)], w=[("wi", j)], chan=f"wi{j}")
        pref["on"] = True

    def ffn(fi, srcT, skeys, ntok, xdst, xkeys, nblk, np_):
        hkeys = [("hT", j) for j in range(NJ)]
        skip_first = pref["on"]
        pref["on"] = False
        P.op("sp", lambda e: e.dma_start(out=wo, in_=wo_s[fi]), r=[("wo_s", fi), ("castgrp", fi)], w=[("wo",)], chan="wo")
        for j in range(NJ):
            s = j % 3
            if not (skip_first and j < 3):
                P.op("sp", lambda e, j=j, s=s: e.dma_start(out=wi[s][:, :, :], in_=wi_s[fi][j]),
                     r=[("wi_s", fi, j), ("castgrp", fi)], w=[("wi", s)], chan=f"wi{s}")
            kg, ku = nb(), nb()
            for c in range(8):
                P.op("pe", lambda e, c=c, s=s, kg=kg: e.matmul(ps[:, kg, 0:ntok], wi[s][:, c, 0:128], srcT[:, c, 0:ntok], start=(c == 0), stop=(c == 7)),
                     r=[("wi", s)] + skeys, w=[("ps", kg)])
            for c in range(8):
                P.op("pe", lambda e, c=c, s=s, ku=ku: e.matmul(ps[:, ku, 0:ntok], wi[s][:, c, 128:256], srcT[:, c, 0:ntok], start=(c == 0), stop=(c == 7)),
                     r=[("wi", s)] + skeys, w=[("ps", ku)])
            q = 0
            P.op("act", lambda e, kg=kg, q=q: e.activation(out=sg[q][:, 0:ntok], in_=ps[:, kg, 0:ntok], func=AF.Silu),
                 r=[("ps", kg)], w=[("sg", q)])
            P.op("dve", lambda e, ku=ku, q=q, j=j: e.tensor_tensor(hT[:, j, 0:ntok], sg[q][:, 0:ntok], ps[:, ku, 0:ntok], ALU.mult),
                 r=[("ps", ku), ("sg", q)], w=[hkeys[j]])
        for b in range(nblk):
            k0, k1 = nb(), nb()
            for j in range(NJ):
                for dh, kk in ((0, k0), (1, k1)):
                    P.op("pe", lambda e, j=j, dh=dh, kk=kk, b=b: e.matmul(ps[0:np_, kk, :], hT[:, j, b * np_:(b + 1) * np_], wo[:, j, dh * 512:(dh + 1) * 512], start=(j == 0), stop=(j == NJ - 1)),
                         r=[hkeys[j], ("wo",)], w=[("ps", kk)])
            for dh, kk in ((0, k0), (1, k1)):
                P.op("dve", lambda e, dh=dh, kk=kk, b=b: e.scalar_tensor_tensor(xdst(b, dh), ps[0:np_, kk, :], 0.5, xdst(b, dh), ALU.mult, ALU.add),
                     r=[("ps", kk)], w=[xkeys[b]])

    def fence():
        ks = P.allkeys(lambda k: k[0] in ("hT", "wo") or str(k[0]).startswith("m_"))
        P.op("pool", lambda e: e.memset(sm[:, 63:64], 0.0), w=ks + [("fence",)])


    def mix(srcT, skeys, sample, tix, first_tile, last_tile, xres, xkeys, prepass=False, pre_last=False, mask_prev=False):
        bs = 131 if sample else 128

        def kprev(kv, b):
            return kTd[kv][:, 2 * b, :] if sample else kTd[kv][:, b, :]

        def kcur(kv, b):
            return kTd[kv][:, 2 * b + 1, :] if sample else kTd[kv][:, b + 1, :]

        def kix(b):
            return (2 * b, 2 * b + 1) if sample else (b, b + 1)

        KLIM = KLIMS if sample else KLIMP
        if KLIM < 0.5:
            return
        PQ = (lambda *a, **k: None) if prepass else P.op
        PD = (lambda *a, **k: None) if sample else P.op
        if sample:
            for h in range(4):
                P.op("pool", lambda e, h=h: e.tensor_copy(Nb[:, h, :], ident[:, :]), r=[("ident",)], w=[MK("N")])
        slots = range(11) if not prepass else (range(2, 9) if pre_last else range(5, 9))
        for s in slots:
            sl = s % 3
            P.op("sp", lambda e, s=s, sl=sl: e.dma_start(out=wi[sl][:, :, :], in_=wm_s[s]),
                 r=[("wm_s", s), ("castgrp", "m")], w=[("wi", sl)], chan=f"wi{sl}")
            if 3 <= s <= 8 and (sample or last_tile):
                kq = nb()
                lt = srcT[:, :, :].rearrange("p c (b t) -> p c b t", b=4)[:, :, :, 0] if sample else srcT[:, :, 509:512]
                mrows = 4 if sample else 3
                for c in range(8):
                    P.op("pe", lambda e, c=c, sl=sl, kq=kq, lt=lt, mrows=mrows: e.matmul(ps[0:mrows, kq, 0:256], lt[:, c, :], wi[sl][:, c, :], start=(c == 0), stop=(c == 7)),
                         r=[("wi", sl)] + skeys, w=[("ps", kq)])
                P.op("dve", lambda e, kq=kq, s=s, mrows=mrows: e.tensor_copy(ncrow[0:mrows, (s - 3) * 256:(s - 2) * 256], ps[0:mrows, kq, 0:256]), r=[("ps", kq)], w=[MK("ncrow")])
                if s == 8:
                    if sample:
                        P.op("sp", lambda e: e.dma_start(out=ncs[tix * 4:(tix + 1) * 4, 2, :], in_=ncrow[0:4, :]), r=[MK("ncrow")], w=[("o_ncs", tix)], chan="onc")
                    else:
                        P.op("sp", lambda e: e.dma_start(out=ncp, in_=ncrow[0:3, :]), r=[MK("ncrow")], w=[("o_ncp",)], chan="onc")
            for g in range(2):
                gi = 2 * s + g
                k = nb()
                for c in range(8):
                    P.op("pe", lambda e, c=c, sl=sl, g=g, k=k: e.matmul(ps[:, k, :], wi[sl][:, c, g * 128:(g + 1) * 128], srcT[:, c, :], start=(c == 0), stop=(c == 7)),
                         r=[("wi", sl)] + skeys, w=[("ps", k)])
                if gi < 4:
                    P.op("act", lambda e, k=k, gi=gi: e.activation(out=qT[:, gi, :], in_=ps[:, k, :], func=AF.Identity, scale=qm[:, 0:1]),
                         r=[("ps", k), ("qm",)], w=[MK("qT", gi)])
                    P.op("dve", lambda e, k=k, gi=gi: e.tensor_scalar(qT2[:, gi, :], ps[:, k, :], qm[:, 1:2], 0.0, ALU.mult, ALU.add),
                         r=[("ps", k), ("qm",)], w=[MK("qT", gi)])
                elif gi < 6:
                    kv = gi - 4
                    if sample:
                        dst = kTd[kv][:, :, :].rearrange("p (b two) t -> p b two t", two=2)[:, :, 1, :]
                    else:
                        dst = kTd[kv][:, 1:5, :]
                    P.op("act", lambda e, k=k, dst=dst: e.activation(out=dst, in_=ps[:, k, :].rearrange("p (b t) -> p b t", b=4), func=AF.Identity),
                         r=[("ps", k)], w=[MK("kTd", kv, i) for i in ((1, 3, 5, 7) if sample else (1, 2, 3, 4))])
                elif gi < 18:
                    m = gi - 6
                    dst = pre[:, m, 0:4 * bs].rearrange("p (b t) -> p b t", b=4)[:, :, 3:131] if sample else None
                    if sample:
                        P.op("act", lambda e, k=k, dst=dst: e.activation(out=dst, in_=ps[:, k, :].rearrange("p (b t) -> p b t", b=4), func=AF.Identity),
                             r=[("ps", k)], w=[MK("pre", m)])
                    else:
                        P.op("act", lambda e, k=k, m=m: e.activation(out=pre[:, m, 3:515], in_=ps[:, k, :], func=AF.Identity),
                             r=[("ps", k)], w=[MK("pre", m)])
                else:
                    h = gi - 18
                    P.op("act", lambda e, k=k, h=h: e.activation(out=zs[:, h, :], in_=ps[:, k, :], func=AF.Silu),
                         r=[("ps", k)], w=[MK("zs", h)])
        if KLIM < 1:
            return
        for b in range(4):
            k = nb()
            for c in range(8):
                P.op("pe", lambda e, c=c, k=k, b=b: e.matmul(ps[:, k, 0:264], srcT[:, c, b * 128:(b + 1) * 128], wtm[:, c, :], start=(c == 0), stop=(c == 7)),
                     r=[("wtm",)] + skeys, w=[("ps", k)])
            pi, ci = kix(b)
            for kv in range(2):
                P.op("act", lambda e, k=k, kv=kv, ci=ci: e.activation(out=Vd[kv][:, ci, 0:64], in_=ps[:, k, kv * 64:(kv + 1) * 64], func=AF.Identity),
                     r=[("ps", k)], w=[MK("Vd", kv, ci)])
                P.op("dve", lambda e, k=k, kv=kv, ci=ci: e.tensor_copy(Vd[kv][:, ci, 64:128], ps[:, k, kv * 64:(kv + 1) * 64]),
                     r=[("ps", k)], w=[MK("Vd", kv, ci)])
            P.op("dve", lambda e, k=k, b=b: e.tensor_copy(gbga[:, b, :], ps[:, k, 256:264]), r=[("ps", k)], w=[MK("gbga", b)])
            want_kv = sample or (last_tile and b == 3)
            if want_kv:
                P.op("dve", lambda e, k=k: e.tensor_copy(kvo[:, :], ps[:, k, 0:256]), r=[("ps", k)], w=[MK("kvo")])
                if sample:
                    sq = tix * 4 + b
                    def f(e, sq=sq):
                        return [e.dma_start(out=nvs[sq, 127:128, :], in_=kvo[0:1, 0:128]),
                                e.dma_start(out=nks[sq, 127:128, :], in_=kvo[0:1, 128:256])]
                    P.op("sp", f, r=[MK("kvo")], w=[("o_kvs", sq)], chan="okv", nd=2)
                else:
                    def f(e):
                        return [e.dma_start(out=nvp, in_=kvo[:, 0:128]), e.dma_start(out=nkp, in_=kvo[:, 128:256])]
                    P.op("sp", f, r=[MK("kvo")], w=[("o_kvp",)], chan="okv", nd=2)
        if KLIM < 2:
            return
        if sample:
            for b in range(4):
                sq = tix * 4 + b
                P.op("sp", lambda e, sq=sq: [e.dma_start(out=ckt[:, :], in_=ck[sq]), e.dma_start(out=cvt[:, :], in_=cv[sq])],
                     w=[MK("ckt"), MK("cvt")], chan="ckld", nd=2)
                for kv in range(2):
                    k = nb()
                    for q in range(2):
                        P.op("dve", lambda e, kv=kv, q=q: e.tensor_copy(ckd[:, kv, q, :], ckt[:, kv * 64:(kv + 1) * 64]), r=[MK("ckt")], w=[MK("ckd", kv)])
                    P.op("pe", lambda e, k=k, kv=kv: e.transpose(ps[:, k, 0:128], ckd[:, kv, :, :].rearrange("p a d -> p (a d)"), ident_f[:, :]),
                         r=[MK("ckd", kv), ("identf",)], w=[("ps", k)])
                    P.op("act", lambda e, k=k, kv=kv, b=b: e.activation(out=kTd[kv][:, 2 * b, :], in_=ps[:, k, 0:128], func=AF.Identity),
                         r=[("ps", k)], w=[MK("kTd", kv, 2 * b)])
                    for q in range(2):
                        P.op("dve", lambda e, kv=kv, b=b, q=q: e.tensor_copy(Vd[kv][:, 2 * b, q * 64:(q + 1) * 64], cvt[:, kv * 64:(kv + 1) * 64]),
                             r=[MK("cvt")], w=[MK("Vd", kv, 2 * b)])
            if tix == 0:
                P.op("sp", lambda e: e.dma_start(out=sct[:, :], in_=sconv), w=[("sct",)], chan="sctld")
            for m in range(12):
                k = nb()
                P.op("pe", lambda e, k=k, m=m: e.transpose(ps[:, k, 0:NS * 3], sct[:, m * 128:(m + 1) * 128], ident_f[0:NS * 3, 0:NS * 3]),
                     r=[("sct",), ("identf",)], w=[("ps", k)])
                P.op("dve", lambda e, k=k, m=m: e.tensor_copy(
                    pre[:, m, 0:4 * bs].rearrange("p (b t) -> p b t", b=4)[:, :, 0:3],
                    ps[:, k, tix * 12: tix * 12 + 12].rearrange("p (b r) -> p b r", b=4)),
                    r=[("ps", k)], w=[MK("pre", m)])
        elif not first_tile:
            P.op("pool", lambda e: e.tensor_copy(pre[:, :, 0:3], ccar[:, :, :]), r=[("ccar",)], w=[MK("pre", m) for m in range(12)])
        if (not sample) and first_tile:
            for kv in range(2):
                P.op("pool", lambda e, kv=kv: e.memset(kTd[kv][:, 0, :], 0.0), w=[MK("kTd", kv, 0)])
                P.op("pool", lambda e, kv=kv: e.memset(Vd[kv][:, 0, :], 0.0), w=[MK("Vd", kv, 0)])
            P.op("pool", lambda e: e.memset(pre[:, :, 0:3], 0.0), w=[MK("pre", m) for m in range(12)])
            P.op("pool", lambda e: e.memset(ccar[:, :, :], 0.0), w=[("ccar",)])
            P.op("pool", lambda e: e.memset(Sst[:, :, :], 0.0), w=[MK("S")])
            P.op("pool", lambda e: e.memset(Sbf[:, :, :], 0.0), w=[MK("Sbf")])

        if KLIM < 3:
            return
        for b in (range(4) if not prepass else ()):
            pi, ci = kix(b)
            has_prev = sample or not (first_tile and b == 0)
            kc = [nb(), nb()]
            for h in range(8):
                kv, p, hh = h // 4, h // 2, h % 2
                P.op("pe", lambda e, h=h, kv=kv, p=p, hh=hh, ci=ci, b=b, kc=kc: e.matmul(
                    ps[:, kc[h // 4], (h % 4) * 128:(h % 4 + 1) * 128], kTd[kv][:, ci, :],
                    (qT2 if hh else qT)[:, p, b * 128:(b + 1) * 128], start=True, stop=True),
                    r=[MK("kTd", kv, ci), MK("qT", p)], w=[("ps", kc[h // 4])])
            for half in range(2):
                P.op("dve", lambda e, half=half, kc=kc: e.tensor_tensor(ebuf[:, half * 512:(half + 1) * 512], ps[:, kc[half], :], bcur[:, half * 512:(half + 1) * 512], ALU.add),
                     r=[("ps", kc[half]), ("bcur",)], w=[MK("tC" if half == 0 else "tD")])
            P.op("act", lambda e: e.activation(out=Pc[:, :, :].rearrange("p h t -> p (h t)"), in_=ebuf, func=AF.Exp),
                 r=[MK("tC"), MK("tD")], w=[MK("Pc")])
            if has_prev:
                kp = [nb(), nb()]
                for h in range(8):
                    kv, p, hh = h // 4, h // 2, h % 2
                    P.op("pe", lambda e, h=h, kv=kv, p=p, hh=hh, pi=pi, b=b, kp=kp: e.matmul(
                        ps[:, kp[h // 4], (h % 4) * 128:(h % 4 + 1) * 128], kTd[kv][:, pi, :],
                        (qT2 if hh else qT)[:, p, b * 128:(b + 1) * 128], start=True, stop=True),
                        r=[MK("kTd", kv, pi), MK("qT", p)], w=[("ps", kp[h // 4])])
                for half in range(2):
                    P.op("dve", lambda e, half=half, kp=kp: e.tensor_tensor(ebuf[:, half * 512:(half + 1) * 512], ps[:, kp[half], :], bprev[:, half * 512:(half + 1) * 512], ALU.add),
                         r=[("ps", kp[half]), ("bprev",)], w=[MK("tC" if half == 0 else "tD")])
                P.op("act", lambda e: e.activation(out=Pp[:, :, :].rearrange("p h t -> p (h t)"), in_=ebuf, func=AF.Exp),
                     r=[MK("tC"), MK("tD")], w=[MK("Pp")])
                if mask_prev and b == 0:
                    P.op("dve", lambda e: e.tensor_scalar(Pp[:, :, :].rearrange("p h t -> p (h t)"), Pp[:, :, :].rearrange("p h t -> p (h t)"), hp_t[:, 0:1], 0.0, ALU.mult, ALU.add),
                         r=[MK("Pp"), ("hp",)], w=[MK("Pp")])
            for kv in range(2):
                kn, kdn = nb(), nb()
                pcs = Pc[:, kv * 4:(kv + 1) * 4, :].rearrange("p h t -> p (h t)")
                pps = Pp[:, kv * 4:(kv + 1) * 4, :].rearrange("p h t -> p (h t)")
                P.op("pe", lambda e, kn=kn, kv=kv, ci=ci, pcs=pcs, hp=has_prev: e.matmul(ps[:, kn, :], Vd[kv][:, ci, :], pcs, start=True, stop=not hp),
                     r=[MK("Vd", kv, ci), MK("Pc")], w=[("ps", kn)])
                if has_prev:
                    P.op("pe", lambda e, kn=kn, kv=kv, pi=pi, pps=pps: e.matmul(ps[:, kn, :], Vd[kv][:, pi, :], pps, start=False, stop=True),
                         r=[MK("Vd", kv, pi), MK("Pp")], w=[("ps", kn)])
                P.op("pe", lambda e, kdn=kdn, pcs=pcs, hp=has_prev: e.matmul(ps[:, kdn, :], ones_b[:, :], pcs, start=True, stop=not hp),
                     r=[("ones_b",), MK("Pc")], w=[("ps", kdn)])
                if has_prev:
                    P.op("pe", lambda e, kdn=kdn, pps=pps: e.matmul(ps[:, kdn, :], ones_b[:, :], pps, start=False, stop=True),
                         r=[("ones_b",), MK("Pp")], w=[("ps", kdn)])
                P.op("dve", lambda e, kdn=kdn, kv=kv: e.tensor_tensor(rbuf.rearrange("p (h t) -> p h t", h=4), ps[:, kdn, :].rearrange("p (h t) -> p h t", h=4),
                                                                      bc(esink[:, kv * 4:(kv + 1) * 4], [128, 4, 128], 2), ALU.add),
                     r=[("ps", kdn), ("esink",)], w=[MK("tB")])
                P.op("dve", lambda e: e.reciprocal(rbuf, rbuf), r=[MK("tB")], w=[MK("tB")])
                nv = ps[:, kn, :].rearrange("p (m q t) -> p m q t", m=2, q=2)
                rv = rbuf.rearrange("p (m q t) -> p m q t", m=2, q=2)
                te = tA.rearrange("p (m t) -> p m t", m=4)[:, 0:2, :]
                to = tA.rearrange("p (m t) -> p m t", m=4)[:, 2:4, :]
                P.op("dve", lambda e, nv=nv, rv=rv, te=te: e.tensor_tensor(te, nv[:, :, 0, :], rv[:, :, 0, :], ALU.mult), r=[("ps", kn), MK("tB")], w=[MK("tA")])
                P.op("dve", lambda e, nv=nv, rv=rv, to=to: e.tensor_tensor(to, nv[:, :, 1, :], rv[:, :, 1, :], ALU.mult), r=[("ps", kn), MK("tB")], w=[MK("tA")])
                P.op("dve", lambda e, te=te: e.tensor_scalar(te, te, qm[:, 2:3], 0.0, ALU.mult, ALU.add), r=[MK("tA"), ("qm",)], w=[MK("tA")])
                P.op("dve", lambda e, kv=kv, b=b, te=te, to=to: e.scalar_tensor_tensor(aoT[:, kv * 2:(kv + 1) * 2, b * 128:(b + 1) * 128], to, qm[:, 3:4], te, ALU.mult, ALU.add),
                     r=[MK("tA"), ("qm",)], w=[MK("aoT", b)])
        if (not sample) and (not prepass or pre_last):
            for kv in range(2):
                P.op("pool", lambda e, kv=kv: e.tensor_copy(kTd[kv][:, 0, :], kTd[kv][:, 4, :]), r=[MK("kTd", kv, 4)], w=[MK("kTd", kv, 0)])
                P.op("pool", lambda e, kv=kv: e.tensor_copy(Vd[kv][:, 0, :], Vd[kv][:, 4, :]), r=[MK("Vd", kv, 4)], w=[MK("Vd", kv, 0)])

        if KLIM < 4:
            return
        if KLIM < 5:
            return
        def pv(m, j):
            if sample:
                return pre[:, m, 0:4 * bs].rearrange("p (b t) -> p b t", b=4)[:, :, j:j + 128]
            return pre[:, m, j:j + 512].rearrange("p (b t) -> p b t", b=4)
        cbufs = [(ctmp[:, :], MK("ctmp")), (tA, MK("tA")), (tE, MK("tE"))]
        for ci_, m in enumerate(range(12) if (not prepass or pre_last) else range(4, 12)):
            cb, ck_ = cbufs[ci_ % 3]
            ct3 = cb.rearrange("p (b t) -> p b t", b=4)
            if sample:
                ct1 = ct3[:, :, 0:1]
                P.op("pool", lambda e, m=m, ct1=ct1: e.tensor_scalar(ct1, pv(m, 0)[:, :, 0:1], cw[:, 0, m:m + 1], 0.0, ALU.mult, ALU.add), r=[MK("pre", m), ("cw",)], w=[ck_])
                for j in (1, 2, 3):
                    P.op("dve", lambda e, m=m, j=j, ct1=ct1: e.scalar_tensor_tensor(ct1, pv(m, j)[:, :, 0:1], cw[:, j, m:m + 1], ct1, ALU.mult, ALU.add),
                         r=[MK("pre", m), ("cw",), ck_], w=[ck_])
                P.op("pool", lambda e, m=m: e.memset(pre[:, m, 3:515], 0.0), w=[MK("pre", m)])
                P.op("act", lambda e, m=m, ct1=ct1: e.activation(out=pre[:, m, 3:515].rearrange("p (b t) -> p b t", b=4)[:, :, 0:1], in_=ct1, func=AF.Silu), r=[ck_], w=[MK("pre", m)])
                continue
            P.op("pool", lambda e, m=m, ct3=ct3: e.tensor_scalar(ct3, pv(m, 0), cw[:, 0, m:m + 1], 0.0, ALU.mult, ALU.add), r=[MK("pre", m), ("cw",)], w=[ck_])
            for j in (1, 2, 3):
                P.op("dve", lambda e, m=m, j=j, ct3=ct3: e.scalar_tensor_tensor(ct3, pv(m, j), cw[:, j, m:m + 1], ct3, ALU.mult, ALU.add),
                     r=[MK("pre", m), ("cw",), ck_], w=[ck_])
            P.op("pool", lambda e, m=m: e.tensor_copy(ccar[:, m, :], pre[:, m, 512:515]), r=[MK("pre", m)], w=[("ccar",)])
            P.op("act", lambda e, m=m, cb=cb: e.activation(out=pre[:, m, 3:515], in_=cb, func=AF.Silu), r=[ck_], w=[MK("pre", m)])
        cact = lambda m: pre[:, m, 3:515]
        if KLIM < 6:
            return
        for li, m in enumerate(range(8) if not prepass else range(4, 8)):
            sqb, sqk = ((ctmp[:, :], MK("ctmp")), (tE, MK("tE")))[li % 2]
            rsb, rsk = ((tA, MK("tA")), (tB, MK("tB")))[li % 2]
            if sample:
                dst = gqn[:, m, :] if m < 4 else gkn[:, m - 4, :]
                dk_ = MK("gqn" if m < 4 else "gkn", m % 4)
                sc = 128.0 ** -0.5 if m < 4 else 1.0
                cm = cact(m).rearrange("p (b t) -> p b t", b=4)[:, :, 0:1]
                P.op("pool", lambda e, cm=cm, sqb=sqb: e.tensor_tensor(sqb[:, 0:4].unsqueeze(2), cm, cm, ALU.mult), r=[MK("pre", m)], w=[sqk])
                k = nb()
                P.op("pe", lambda e, k=k, sqb=sqb: e.matmul(ps[:, k, 0:4], ones_f[:, :], sqb[:, 0:4], start=True, stop=True), r=[("ones_f",), sqk], w=[("ps", k)])
                P.op("act", lambda e, k=k, rsb=rsb: e.activation(out=rsb[:, 0:4], in_=ps[:, k, 0:4], func=AF.Ln, bias=EPS), r=[("ps", k)], w=[rsk])
                P.op("act", lambda e, rsb=rsb: e.activation(out=rsb[:, 0:4], in_=rsb[:, 0:4], func=AF.Exp, scale=-0.5), r=[rsk], w=[rsk])
                P.op("pool", lambda e, dst=dst: e.memset(dst, 0.0), w=[dk_])
                P.op("dve", lambda e, cm=cm, dst=dst, sc=sc, rsb=rsb: e.scalar_tensor_tensor(dst.rearrange("p (b t) -> p b t", b=4)[:, :, 0:1], cm, sc, rsb[:, 0:4].unsqueeze(2), ALU.mult, ALU.mult),
                     r=[MK("pre", m), rsk], w=[dk_])
                continue
            P.op("pool", lambda e, m=m, sqb=sqb: e.tensor_tensor(sqb, cact(m), cact(m), ALU.mult), r=[MK("pre", m)], w=[sqk])
            k = nb()
            P.op("pe", lambda e, k=k, sqb=sqb: e.matmul(ps[:, k, :], ones_f[:, :], sqb, start=True, stop=True), r=[("ones_f",), sqk], w=[("ps", k)])
            P.op("act", lambda e, k=k, rsb=rsb: e.activation(out=rsb, in_=ps[:, k, :], func=AF.Ln, bias=EPS), r=[("ps", k)], w=[rsk])
            P.op("act", lambda e, rsb=rsb: e.activation(out=rsb, in_=rsb, func=AF.Exp, scale=-0.5), r=[rsk], w=[rsk])
            dst = gqn[:, m, :] if m < 4 else gkn[:, m - 4, :]
            sc = 128.0 ** -0.5 if m < 4 else 1.0
            P.op("dve", lambda e, m=m, dst=dst, sc=sc, rsb=rsb: e.scalar_tensor_tensor(dst, cact(m), sc, rsb, ALU.mult, ALU.mult),
                 r=[MK("pre", m), rsk], w=[MK("gqn" if m < 4 else "gkn", m % 4)])
        if KLIM < 7:
            return
        gb_v = gbga[:, :, 0:4]
        ga_v = gbga[:, :, 4:8]
        be = sm[:, 0:16].rearrange("p (b h) -> p b h", b=4)
        gg = sm[:, 16:32].rearrange("p (b h) -> p b h", b=4)
        t1 = sm[:, 32:48].rearrange("p (b h) -> p b h", b=4)
        gkeys = [MK("gbga", b) for b in range(4)]
        P.op("act", lambda e: e.activation(out=be, in_=gb_v, func=AF.Exp, scale=-1.0), r=gkeys, w=[MK("be")])
        P.op("dve", lambda e: e.tensor_scalar(be, be, 1.0, 1.0, ALU.add, ALU.mult), r=[MK("be")], w=[MK("be")])
        P.op("dve", lambda e: e.reciprocal(be, be), r=[MK("be")], w=[MK("be")])
        P.op("dve", lambda e: e.tensor_tensor(gg, ga_v, bc(dtb_b[:, :], [128, 4, 4], 1), ALU.add), r=gkeys + [("dtb",)], w=[MK("gg")])
        P.op("dve", lambda e: e.tensor_scalar(t1, gg, -1.0, 0.0, ALU.mult, ALU.add), r=[MK("gg")], w=[MK("t1")])
        P.op("dve", lambda e: e.tensor_tensor(t1, t1, gg, ALU.max), r=[MK("gg"), MK("t1")], w=[MK("t1")])
        P.op("act", lambda e: e.activation(out=t1, in_=t1, func=AF.Exp, scale=-1.0), r=[MK("t1")], w=[MK("t1")])
        P.op("act", lambda e: e.activation(out=t1, in_=t1, func=AF.Ln, bias=1.0), r=[MK("t1")], w=[MK("t1")])
        P.op("dve", lambda e: e.scalar_tensor_tensor(gg, gg, 0.0, t1, ALU.max, ALU.add), r=[MK("gg"), MK("t1")], w=[MK("gg")])
        P.op("dve", lambda e: e.tensor_tensor(gg, gg, bc(negA[:, :], [128, 4, 4], 1), ALU.mult), r=[MK("gg"), ("negA",)], w=[MK("gg")])
        if sample:
            P.op("dve", lambda e: e.tensor_scalar(gg, gg, tok0[:, 0:1], 0.0, ALU.mult, ALU.add), r=[MK("gg"), ("tok0",)], w=[MK("gg")])

        if KLIM < 8:
            return
        v4 = lambda ap: ap.rearrange("p (h t) -> p h t", h=4)
        for b in range(4):
            cols = slice(b * 128, (b + 1) * 128)
            gcc = sm[:, 48:52]
            gl = sm[:, 52:56]
            s1 = sm[:, 56:60]
            s2 = sm[:, 60:63]
            k = nb()
            P.op("pe", lambda e, k=k, b=b: e.matmul(ps[:, k, 0:4], triu[:, :], gg[:, b, :], start=True, stop=True), r=[("triu",), MK("gg")], w=[("ps", k)])
            P.op("dve", lambda e, k=k: e.tensor_copy(gcc, ps[:, k, 0:4]), r=[("ps", k)], w=[MK("gcc")])
            P.op("dve", lambda e, b=b: e.tensor_tensor(v4(tA), bc(triu[:, :], [128, 4, 128], 1), bc(gg[:, b, :], [128, 4, 128], 2), ALU.mult),
                 r=[("triu",), MK("gg")], w=[MK("tA")])
            kr = nb()
            P.op("pe", lambda e, kr=kr: e.matmul(ps[:, kr, :], ones_f[:, :], tA, start=True, stop=True), r=[("ones_f",), MK("tA")], w=[("ps", kr)])
            P.op("dve", lambda e, kr=kr: e.tensor_tensor(v4(tB), v4(ps[:, kr, :]), bc(gcc, [128, 4, 128], 2), ALU.subtract), r=[("ps", kr), MK("gcc")], w=[MK("tB")])
            PQ("dve", lambda e: e.tensor_scalar(tC, tB, 0.0, 0.0, ALU.min, ALU.add), r=[MK("tB")], w=[MK("tC")])
            PQ("act", lambda e: e.activation(out=tC, in_=tC, func=AF.Exp), r=[MK("tC")], w=[MK("tC")])
            PQ("dve", lambda e: e.tensor_tensor(v4(tC), v4(tC), bc(triu[:, :], [128, 4, 128], 1), ALU.mult), r=[MK("tC"), ("triu",)], w=[MK("tC")])
            PD("dve", lambda e: e.tensor_scalar(tD, tB, 0.0, 0.0, ALU.max, ALU.add), r=[MK("tB")], w=[MK("tD")])
            PD("act", lambda e: e.activation(out=tD, in_=tD, func=AF.Exp, scale=-1.0), r=[MK("tD")], w=[MK("tD")])
            PD("dve", lambda e: e.tensor_tensor(v4(tD), v4(tD), bc(strl[:, :], [128, 4, 128], 1), ALU.mult), r=[MK("tD"), ("strl",)], w=[MK("tD")])
            PQ("act", lambda e, kr=kr: e.activation(out=tE, in_=ps[:, kr, :], func=AF.Exp), r=[("ps", kr)], w=[MK("tE")])
            P.op("act", lambda e, kr=kr: e.activation(out=gl, in_=v4(ps[:, kr, :])[:, :, 127], func=AF.Exp), r=[("ps", kr)], w=[MK("gl")])
            P.op("dve", lambda e, kr=kr: e.tensor_tensor(s1, v4(ps[:, kr, :])[:, :, 127], gcc, ALU.subtract), r=[("ps", kr), MK("gcc")], w=[MK("s1")])
            P.op("act", lambda e: e.activation(out=s1, in_=s1, func=AF.Exp), r=[MK("s1")], w=[MK("s1")])
            P.op("act", lambda e: e.activation(out=gcc, in_=gcc, func=AF.Exp), r=[MK("gcc")], w=[MK("gcc")])
            P.op("dve", lambda e, b=b: e.tensor_tensor(gcc, gcc, be[:, b, :], ALU.mult), r=[MK("gcc"), MK("be")], w=[MK("gcc")])
            kt = nb()
            ktb = ps[:, kt, :].bitcast(BF16)
            for h in range(4):
                P.op("pe", lambda e, h=h, ktb=ktb, cols=cols: e.transpose(ktb[:, h * 128:(h + 1) * 128], gkn[:, h, cols], ident[:, :]),
                     r=[MK("gkn", h), ("ident",)], w=[("ps", kt)])
            P.op("dve", lambda e, ktb=ktb: e.tensor_tensor(kb[:, :, :], v4(ktb[:, 0:512]), bc(gcc, [128, 4, 128], 2), ALU.mult), r=[("ps", kt), MK("gcc")], w=[MK("kb")])
            P.op("dve", lambda e, ktb=ktb: e.tensor_tensor(kd[:, :, :], v4(ktb[:, 0:512]), bc(s1, [128, 4, 128], 2), ALU.mult), r=[("ps", kt), MK("s1")], w=[MK("kd")])
            kvv = nb()
            for h in range(4):
                P.op("pe", lambda e, h=h, kvv=kvv, cols=cols: e.transpose(ps[:, kvv, h * 128:(h + 1) * 128], cact(8 + h)[:, cols], ident_f[:, :]),
                     r=[MK("pre", 8 + h), ("identf",)], w=[("ps", kvv)])
            P.op("dve", lambda e, kvv=kvv, b=b: e.tensor_tensor(vb[:, :, :], v4(ps[:, kvv, :]), bc(be[:, b, :], [128, 4, 128], 2), ALU.mult), r=[("ps", kvv), MK("be")], w=[MK("vb")])
            kkk, kqk = nb(), nb()
            for h in range(4):
                PD("pe", lambda e, h=h, kkk=kkk, cols=cols: e.matmul(ps[:, kkk, h * 128:(h + 1) * 128], gkn[:, h, cols], gkn[:, h, cols], start=True, stop=True),
                     r=[MK("gkn", h)], w=[("ps", kkk)])
            for h in range(4):
                PQ("pe", lambda e, h=h, kqk=kqk, cols=cols: e.matmul(ps[:, kqk, h * 128:(h + 1) * 128], gkn[:, h, cols], gqn[:, h, cols], start=True, stop=True),
                     r=[MK("gkn", h), MK("gqn", h)], w=[("ps", kqk)])
            PD("dve", lambda e, kkk=kkk: e.tensor_tensor(tD, ps[:, kkk, :], tD, ALU.mult), r=[("ps", kkk), MK("tD")], w=[MK("tD")])
            PD("dve", lambda e, b=b: e.scalar_tensor_tensor(Xb[0][:, :, :], v4(tD), -1.0, bc(be[:, b, :], [128, 4, 128], 2), ALU.mult, ALU.mult),
                 r=[MK("tD"), MK("be")], w=[MK("X", 0)])
            PQ("dve", lambda e, kqk=kqk: e.tensor_tensor(qkT[:, :, :], v4(ps[:, kqk, :]), v4(tC), ALU.mult), r=[("ps", kqk), MK("tC")], w=[MK("qkT")])
            PQ("pool", lambda e, cols=cols: e.tensor_tensor(qdT[:, :, :], gqn[:, :, cols], v4(tE), ALU.mult), r=[MK("gqn", h) for h in range(4)] + [MK("tE")], w=[MK("qdT")])
            ky = nb()
            kyb = ps[:, ky, :].bitcast(BF16)
            for h in range(4):
                PD("pe", lambda e, h=h, ky=ky: e.transpose(ps[:, ky, h * 128:(h + 1) * 128], Xb[0][:, h, :], ident_f[:, :]), r=[MK("X", 0), ("identf",)], w=[("ps", ky)])
            PD("act", lambda e, ky=ky: e.activation(out=Yb[0][:, :, :], in_=v4(ps[:, ky, :]), func=AF.Identity), r=[("ps", ky)], w=[MK("Y", 0)])
            PD("dve", lambda e: e.tensor_tensor(Nf, Yb[0][:, :, :], bc(ident_f[:, :], [128, 4, 128], 1), ALU.add), r=[MK("Y", 0), ("identf",)], w=[MK("Nf")])
            cur = 0
            for st in range(1, 7):
                nx = 1 - cur
                kx, kyy = nb(), nb()
                for h in range(4):
                    PD("pe", lambda e, h=h, kx=kx, cur=cur: e.matmul(ps[:, kx, h * 128:(h + 1) * 128], Yb[cur][:, h, :], Xb[cur][:, h, :], start=True, stop=True),
                         r=[MK("X", cur), MK("Y", cur)], w=[("ps", kx)])
                for h in (range(4) if st < 6 else ()):
                    PD("pe", lambda e, h=h, kyy=kyy, cur=cur: e.matmul(ps[:, kyy, h * 128:(h + 1) * 128], Xb[cur][:, h, :], Yb[cur][:, h, :], start=True, stop=True),
                         r=[MK("X", cur), MK("Y", cur)], w=[("ps", kyy)])
                PD("act", lambda e, kx=kx, nx=nx: e.activation(out=Xb[nx][:, :, :], in_=v4(ps[:, kx, :]), func=AF.Identity), r=[("ps", kx)], w=[MK("X", nx)])
                if st < 6:
                    PD("dve", lambda e, kyy=kyy, nx=nx: e.tensor_copy(Yb[nx][:, :, :], v4(ps[:, kyy, :])), r=[("ps", kyy)], w=[MK("Y", nx)])
                kn2 = nb()
                for h in range(4):
                    PD("pe", lambda e, h=h, kn2=kn2, nx=nx: e.matmul(ps[:, kn2, h * 128:(h + 1) * 128], Xb[nx][:, h, :], Nf[:, h, :], start=True, stop=True),
                         r=[MK("X", nx), MK("Nf")], w=[("ps", kn2)])
                PD("dve", lambda e, kn2=kn2: e.tensor_tensor(Nf, Nf, v4(ps[:, kn2, :]), ALU.add), r=[("ps", kn2), MK("Nf")], w=[MK("Nf")])
                cur = nx
            PD("act", lambda e: e.activation(out=Nb[:, :, :], in_=Nf, func=AF.Identity), r=[MK("Nf")], w=[MK("N")])
            ku, kw = nb(), nb()
            for h in range(4):
                P.op("pe", lambda e, h=h, ku=ku: e.matmul(ps[:, ku, h * 128:(h + 1) * 128], Nb[:, h, :], vb[:, h, :], start=True, stop=True), r=[MK("N"), MK("vb")], w=[("ps", ku)])
            for h in range(4):
                P.op("pe", lambda e, h=h, kw=kw: e.matmul(ps[:, kw, h * 128:(h + 1) * 128], kb[:, h, :], Nb[:, h, :], start=True, stop=True), r=[MK("N"), MK("kb")], w=[("ps", kw)])
            P.op("act", lambda e, ku=ku: e.activation(out=u_t, in_=ps[:, ku, :], func=AF.Identity), r=[("ps", ku)], w=[MK("u")])
            P.op("act", lambda e, kw=kw: e.activation(out=wT[:, :, :], in_=v4(ps[:, kw, :]), func=AF.Identity), r=[("ps", kw)], w=[MK("wT")])
            if sample:
                sq = tix * 4 + b
                P.op("sp", lambda e, sq=sq: e.dma_start(out=Sst[:, :, :], in_=sgdn[sq].rearrange("h k v -> k h v")), w=[MK("S")], chan="sld")
                P.op("act", lambda e: e.activation(out=Sbf[:, :, :], in_=Sst[:, :, :], func=AF.Identity), r=[MK("S")], w=[MK("Sbf")])
            k1 = nb()
            for h in range(4):
                P.op("pe", lambda e, h=h, k1=k1: e.matmul(ps[:, k1, h * 128:(h + 1) * 128], wT[:, h, :], Sbf[:, h, :], start=True, stop=True), r=[MK("wT"), MK("Sbf")], w=[("ps", k1)])
            P.op("dve", lambda e, k1=k1: e.tensor_tensor(vnew[:, :, :], v4(u_t), v4(ps[:, k1, :]), ALU.subtract), r=[("ps", k1), MK("u")], w=[MK("vnew")])
            k3, k4 = nb(), nb()
            for h in range(4):
                PQ("pe", lambda e, h=h, k3=k3: e.matmul(ps[:, k3, h * 128:(h + 1) * 128], Sbf[:, h, :], qdT[:, h, :], start=True, stop=False), r=[MK("Sbf"), MK("qdT")], w=[("ps", k3)])
                PQ("pe", lambda e, h=h, k3=k3: e.matmul(ps[:, k3, h * 128:(h + 1) * 128], vnew[:, h, :], qkT[:, h, :], start=False, stop=True), r=[MK("vnew"), MK("qkT")], w=[("ps", k3)])
            for h in range(4):
                P.op("pe", lambda e, h=h, k4=k4: e.matmul(ps[:, k4, h * 128:(h + 1) * 128], kd[:, h, :], vnew[:, h, :], start=True, stop=True), r=[MK("kd"), MK("vnew")], w=[("ps", k4)])
            PQ("act", lambda e, k3=k3, cols=cols: e.activation(out=oT[:, :, cols], in_=v4(ps[:, k3, :]), func=AF.Identity), r=[("ps", k3)], w=[MK("oT", b), MK("ncrow")])
            for h in range(4):
                P.op("dve", lambda e, h=h, k4=k4: e.scalar_tensor_tensor(Sst[:, h, :], Sst[:, h, :], gl[:, h:h + 1], ps[:, k4, h * 128:(h + 1) * 128], ALU.mult, ALU.add),
                     r=[("ps", k4), MK("gl"), MK("S")], w=[MK("S")])
            if sample:
                P.op("sp", lambda e, sq=sq: e.dma_start(out=ngs[sq].rearrange("h k v -> k h v"), in_=Sst[:, :, :]), r=[MK("S")], w=[("o_ngs", sq)], chan="ongs")
            else:
                P.op("act", lambda e: e.activation(out=Sbf[:, :, :], in_=Sst[:, :, :], func=AF.Identity), r=[MK("S")], w=[MK("Sbf")])
                if last_tile and b == 3:
                    P.op("sp", lambda e: e.dma_start(out=ngp.rearrange("h k v -> k h v"), in_=Sst[:, :, :]), r=[MK("S")], w=[("o_ngp",)], chan="ongs")
        if KLIM < 9:
            return
        if prepass:
            return
        for h in range(4):
            P.op("pool", lambda e, h=h: e.tensor_tensor(ctmp[:, :], oT[:, h, :], oT[:, h, :], ALU.mult), r=[MK("oT", b) for b in range(4)], w=[MK("ctmp")])
            k = nb()
            P.op("pe", lambda e, k=k: e.matmul(ps[:, k, :], ones_f[:, :], ctmp[:, :], start=True, stop=True), r=[("ones_f",), MK("ctmp")], w=[("ps", k)])
            P.op("act", lambda e, k=k: e.activation(out=tA, in_=ps[:, k, :], func=AF.Ln, scale=1.0 / 128, bias=EPS), r=[("ps", k)], w=[MK("tA")])
            P.op("act", lambda e: e.activation(out=tA, in_=tA, func=AF.Exp, scale=-0.5), r=[MK("tA")], w=[MK("tA")])
            P.op("dve", lambda e, h=h: e.scalar_tensor_tensor(tA, oT[:, h, :], gng_t[:, 0:1], tA, ALU.mult, ALU.mult), r=[MK("oT", b) for b in range(4)] + [MK("tA"), ("gng",)], w=[MK("tA")])
            P.op("dve", lambda e, h=h: e.tensor_tensor(goT[:, h, :], tA, zs[:, h, :], ALU.mult), r=[MK("tA"), MK("zs", h)], w=[MK("goT", h)])
        if KLIM < 10:
            return
        if sample:
            for t_, src in ((aoTs, aoT), (goTs, goT)):
                P.op("pool", lambda e, t_=t_, src=src: e.tensor_copy(t_[:, :, tix * 4:(tix + 1) * 4], src[:, :, :].rearrange("p c (b t) -> p c b t", b=4)[:, :, :, 0]),
                     r=[MK("aoT", b) for b in range(4)] + [MK("goT", h) for h in range(4)], w=[("mixTs", tix)])
        else:
            for b in range(4):
                k0, k1 = nb(), nb()
                for c in range(8):
                    src = aoT if c < 4 else goT
                    for dh, kk in ((0, k0), (1, k1)):
                        P.op("pe", lambda e, c=c, dh=dh, kk=kk, b=b, src=src: e.matmul(ps[:, kk, :], src[:, c % 4, b * 128:(b + 1) * 128], wmo_t[:, c, dh * 512:(dh + 1) * 512], start=(c == 0), stop=(c == 7)),
                             r=[MK("aoT", b), MK("goT", c % 4), ("wmo",)], w=[("ps", kk)])
                for dh, kk in ((0, k0), (1, k1)):
                    P.op("dve", lambda e, dh=dh, kk=kk, b=b: e.tensor_tensor(xres(b, dh), xres(b, dh), ps[:, kk, :], ALU.add), r=[("ps", kk)], w=[xkeys[b]])

    def final_norm(xsrc, xkeys, nblk, np_, ydst, okey, chan):
        keyss = [("ss",)]
        for b in range(nblk):
            P.op("act", lambda e, b=b: e.activation(out=junk[0:np_, :], in_=xsrc(b), func=AF.Square, accum_out=ss[0:np_, b:b + 1]), r=[xkeys[b]], w=[MK("ctmp")] + keyss)
        rstd_cols(nblk, np_, 1.0 / D, keyss)
        for b in range(nblk):
            P.op("dve", lambda e, b=b: e.scalar_tensor_tensor(xsrc(b), xsrc(b), ss[0:np_, b:b + 1], gfin_b[0:np_, :], ALU.mult, ALU.mult), r=keyss + [xkeys[b], ("gfin",)], w=[xkeys[b]])
            P.op("sp", lambda e, b=b: e.dma_start(out=ydst(b), in_=xsrc(b)), r=[xkeys[b]], w=[(okey, b)], chan=chan)

    skeys = [("xs",)]
    P.op("sp", lambda e: e.dma_start(out=xts[:, 0, :], in_=xs), w=skeys, chan="xsld")
    xs_src = lambda b: xts[0:NS, 0, :]
    xs_dst = lambda b, dh: xts[0:NS, 0, dh * 512:(dh + 1) * 512]
    nTs_keys = [("nTs",)]
    ffn_prefetch(0)
    norm_T(xs_src, skeys, 1, NS, 0, nTs, nTs_keys)
    fence()
    ffn(0, nTs, nTs_keys, NS, xs_dst, skeys, 1, NS)
    xk = [("xnT", b) for b in range(4)]
    if stage >= 2:
        norm_T(xs_src, skeys, 1, NS, 1, nTs, nTs_keys)
        fence()
    for tix in range(4 if stage >= 2 else 0):
        P.op("pool", lambda e: e.memset(xnT[:, :, :], 0.0), w=xk)
        P.op("pool", lambda e, tix=tix: e.tensor_copy(xnT[:, :, :].rearrange("p c (b t) -> p c b t", b=4)[:, :, :, 0], nTs[:, :, tix * 4:(tix + 1) * 4]), r=nTs_keys, w=xk)
        mix(xnT, xk, True, tix, False, False, None, None)
    k0, k1 = nb(), nb()
    for c in range(8 if stage >= 2 else 0):
        src = aoTs if c < 4 else goTs
        for dh, kk in ((0, k0), (1, k1)):
            P.op("pe", lambda e, c=c, dh=dh, kk=kk, src=src: e.matmul(ps[0:NS, kk, :], src[:, c % 4, :], wmo_t[:, c, dh * 512:(dh + 1) * 512], start=(c == 0), stop=(c == 7)),
                 r=[("mixTs", t) for t in range(4)] + [("wmo",)], w=[("ps", kk)])
    for dh, kk in (((0, k0), (1, k1)) if stage >= 2 else ()):
        P.op("dve", lambda e, dh=dh, kk=kk: e.tensor_tensor(xs_dst(0, dh), xs_dst(0, dh), ps[0:NS, kk, :], ALU.add), r=[("ps", kk)], w=skeys)
    if stage >= 2:
        ffn_prefetch(1)
        norm_T(xs_src, skeys, 1, NS, 2, nTs, nTs_keys)
        fence()
        ffn(1, nTs, nTs_keys, NS, xs_dst, skeys, 1, NS)
    final_norm(xs_src, skeys, 1, NS, lambda b: ys, "o_ys", "oys")

    tiles = [("pre", i) for i in range(n_pre)] + [("main", i) for i in range(n_tiles)]

    def load_x(gi):
        kind, i = tiles[gi]
        src = xpre if kind == "pre" else xp
        X = xt[gi % 2]
        keys = [("x", gi % 2, b) for b in range(4)]
        if gi == 1:
            keys = keys + [("xs",), ("sct",), MK("ckt"), MK("cvt"), MK("ckd", 0), MK("ckd", 1)]
        P.op("sp", lambda e: [e.dma_start(out=X[:, b, :], in_=src[i * TT + b * 128: i * TT + (b + 1) * 128, :]) for b in range(4)],
             w=keys, chan=f"x{gi % 2}", nd=4)

    if stage >= 3:
        load_x(0)
    for gi, (kind, i) in enumerate(tiles if stage >= 3 else []):
        X = xt[gi % 2]
        xkeys = [("x", gi % 2, b) for b in range(4)]
        xsrc = lambda b, X=X: X[:, b, :]
        xdst = lambda b, dh, X=X: X[:, b, dh * 512:(dh + 1) * 512]
        ffn_prefetch(0)
        norm_T(xsrc, xkeys, 4, 128, 0, xnT, xk)
        fence()
        ffn(0, xnT, xk, TT, xdst, xkeys, 4, 128)
        norm_T(xsrc, xkeys, 4, 128, 1, xnT, xk)
        fence()
        if kind == "pre":
            mix(xnT, xk, False, 0, i == 0, False, xdst, xkeys, prepass=True, pre_last=(i == n_pre - 1))
            if gi + 1 < len(tiles):
                load_x(gi + 1)
            continue
        mix(xnT, xk, False, 0, (n_pre == 0 and i == 0), i == n_tiles - 1, xdst, xkeys, mask_prev=(n_pre > 0 and i == 0))
        if gi + 1 < len(tiles):
            load_x(gi + 1)
        ffn_prefetch(1)
        norm_T(xsrc, xkeys, 4, 128, 2, xnT, xk)
        fence()
        ffn(1, xnT, xk, TT, xdst, xkeys, 4, 128)
        final_norm(xsrc, xkeys, 4, 128, lambda b, i=i: yp[i * TT + b * 128: i * TT + (b + 1) * 128, :], ("o_yp", i), f"oy{gi % 2}")

    with ExitStack() as st:
        sems = {}
        for en in Prog.ENG:
            sems[("e", en)] = st.enter_context(nc.semaphore("e_" + en))
        for ch in P.chan_cnt:
            sems[("c", ch)] = st.enter_context(nc.semaphore("c_" + ch))
        P.run(sems)
    return nc


_NC_CACHE = {}


def run_cores(xp_list, per_core, shared, n_tiles, stage=3, xpre_list=None, hasprev=None):
    n_pre = 0 if xpre_list is None else xpre_list[0].shape[0] // TT
    key = (n_tiles, n_pre, stage)
    if key not in _NC_CACHE:
        _NC_CACHE[key] = build(n_tiles, stage, n_pre)
    nc = _NC_CACHE[key]
    consts = host_consts()
    in_maps = []
    for c in range(len(xp_list)):
        m = dict(shared)
        m.update(per_core[c])
        m["xp"] = xp_list[c]
        if n_pre:
            m["xpre"] = xpre_list[c]
        m["hasprev"] = np.full((128, 1), 0.0 if hasprev is None else hasprev[c], np.float32)
        for k, v in consts.items():
            m["c_" + k] = v
        in_maps.append({k: np.ascontiguousarray(v, dtype=np.float32) for k, v in m.items()})
    res = run_bass_kernel_spmd(nc, in_maps, core_ids=list(range(len(xp_list))))
    return res.results


def kernel(x_prompt, x_sample, cache_attn_k, cache_attn_v, state_conv, state_gdn,
           ffn1_norm_g, ffn1_w_in, ffn1_w_out, mix_norm_g, w_in_mix, attn_sinks, conv_w,
           gdn_A_log, gdn_dt_bias, gdn_norm_g, w_out_mix, ffn2_norm_g, ffn2_w_in, ffn2_w_out,
           final_norm_g):
    f = lambda a: np.asarray(a, dtype=np.float32)
    x_prompt = f(x_prompt)
    B, S, _ = x_prompt.shape
    ncore = 8
    HALF = S // 2
    n_tiles = HALF // TT
    shared = dict(g1=f(ffn1_norm_g)[0], wi1=f(ffn1_w_in)[0], wo1=f(ffn1_w_out)[0], g2=f(mix_norm_g)[0],
                  wmi=f(w_in_mix)[0], sinks=f(attn_sinks)[0], convw=f(conv_w)[0], alog=f(gdn_A_log)[0],
                  dtb=f(gdn_dt_bias)[0], gng=f(gdn_norm_g)[0], wmo=f(w_out_mix)[0], g3=f(ffn2_norm_g)[0],
                  wi2=f(ffn2_w_in)[0], wo2=f(ffn2_w_out)[0], gfin=f(final_norm_g))
    xs_ = f(x_sample)[:, 0, :]
    ck_ = f(cache_attn_k)[0].reshape(-1, 128, 128)
    cv_ = f(cache_attn_v)[0].reshape(-1, 128, 128)
    sc_ = f(state_conv)[0]
    sg_ = f(state_gdn)[0]
    per_core, xp_list, xpre_list, hasprev = [], [], [], []
    zeros = np.zeros((HALF, D), np.float32)
    for c in range(ncore):
        sl = slice(c * NS, (c + 1) * NS)
        per_core.append(dict(xs=xs_[sl], ck=ck_[sl], cv=cv_[sl], sconv=sc_[sl].reshape(NS * 3, 1536), sgdn=sg_[sl]))
        sq, half = c // 2, c % 2
        xp_list.append(x_prompt[sq, half * HALF:(half + 1) * HALF])
        xpre_list.append(x_prompt[sq, 0:HALF] if half else zeros)
        hasprev.append(float(half))
    res = run_cores(xp_list, per_core, shared, n_tiles, 3, xpre_list, hasprev)
    y_prompt = np.stack([np.concatenate([res[2 * b]["yp"], res[2 * b + 1]["yp"]]) for b in range(B)])
    cat = lambda k: np.stack([res[2 * b + 1][k] for b in range(B)])
    y_sample = np.concatenate([res[c]["ys"] for c in range(ncore)])[:, None, :]
    nkp = cat("nkp").reshape(1, B, 128, 2, 64)
    nvp = cat("nvp").reshape(1, B, 128, 2, 64)
    ncp = cat("ncp")[None]
    ngp = cat("ngp")[None]
    nks = np.concatenate([res[c]["nks"] for c in range(ncore)]).reshape(1, ncore * NS, 128, 2, 64)
    nvs = np.concatenate([res[c]["nvs"] for c in range(ncore)]).reshape(1, ncore * NS, 128, 2, 64)
    ncs = np.concatenate([res[c]["ncs"] for c in range(ncore)])[None]
    ngs = np.concatenate([res[c]["ngs"] for c in range(ncore)])[None]
    return (y_prompt, y_sample, nkp, nvp, ncp, ngp, nks, nvs, ncs, ngs)
```

```python
import os
import numpy as np
from contextlib import ExitStack
KLIMS = float(os.environ.get('KLIMS', '99'))
KLIMP = float(os.environ.get('KLIMP', '99'))
import concourse.bass as bass
import concourse.mybir as mybir
from concourse.bass_utils import run_bass_kernel_spmd

F32 = mybir.dt.float32
BF16 = mybir.dt.bfloat16
ALU = mybir.AluOpType
AF = mybir.ActivationFunctionType
AX = mybir.AxisListType

D = 1024
FF = 2816
NJ = 22
EPS = 1e-6
TT = 512
NBLK = 4
NS = 16
INW = 2824


class Prog:
    ENG = ("pe", "act", "dve", "pool", "sp")

    def __init__(self, nc):
        self.nc = nc
        self.ops = []
        self.lastw = {}
        self.readers = {}
        self.chan_cnt = {}

    def op(self, eng, fn, r=(), w=(), chan=None, nd=1):
        i = len(self.ops)
        deps = set()
        for k in r:
            if k in self.lastw:
                deps.add(self.lastw[k])
        for k in w:
            if k in self.lastw:
                deps.add(self.lastw[k])
            deps.update(self.readers.get(k, ()))
        for k in r:
            self.readers.setdefault(k, []).append(i)
        for k in w:
            self.lastw[k] = i
            self.readers[k] = []
        o = dict(eng=eng, fn=fn, deps=deps, chan=chan, nd=nd, sig=False, val=0)
        if chan is not None:
            self.chan_cnt[chan] = self.chan_cnt.get(chan, 0) + 16 * nd
            o["val"] = self.chan_cnt[chan]
        self.ops.append(o)
        return i

    def allkeys(self, pred):
        ks = set(self.lastw) | set(self.readers)
        return [k for k in ks if pred(k)]

    def run(self, sems):
        nc = self.nc
        ops = self.ops
        for o in ops:
            for d in o["deps"]:
                ops[d]["sig"] = True
        cnt = {e: 0 for e in self.ENG}
        for o in ops:
            if o["chan"] is None and o["sig"]:
                cnt[o["eng"]] += 1
                o["val"] = cnt[o["eng"]]
        streams = {e: [] for e in self.ENG}
        for i, o in enumerate(ops):
            streams[o["eng"]].append(i)

        def runner(ename):
            def f(e):
                waited = {}
                for i in streams[ename]:
                    o = ops[i]
                    for d in sorted(o["deps"]):
                        do = ops[d]
                        if do["chan"] is not None:
                            key = ("c", do["chan"])
                        else:
                            if do["eng"] == ename and ename in ("pe", "sp"):
                                continue
                            key = ("e", do["eng"])
                        if waited.get(key, 0) >= do["val"]:
                            continue
                        waited[key] = do["val"]
                        e.wait_ge(sems[key], do["val"])
                    res = o["fn"](e)
                    if o["chan"] is not None:
                        lst = res if isinstance(res, (list, tuple)) else [res]
                        assert len(lst) == o["nd"], (len(lst), o["nd"])
                        for ins in lst:
                            ins.then_inc(sems[("c", o["chan"])], 16)
                    elif o["sig"]:
                        res.then_inc(sems[("e", ename)], 1)
                if ename == "sp":
                    for ch, v in self.chan_cnt.items():
                        if waited.get(("c", ch), 0) < v:
                            e.wait_ge(sems[("c", ch)], v)
            return f

        with nc.Block() as block:
            block.tensor(runner("pe"))
            block.scalar(runner("act"))
            block.vector(runner("dve"))
            block.gpsimd(runner("pool"))
            block.sync(runner("sp"))


def host_consts():
    i = np.arange(128)
    c = {}
    c["ident"] = np.eye(128, dtype=np.float32)
    c["triu"] = (i[:, None] <= i[None, :]).astype(np.float32)
    c["strl"] = (i[:, None] > i[None, :]).astype(np.float32)
    slopes = np.exp2(-8.0 * np.arange(1, 9, dtype=np.float32) / 8.0).astype(np.float32)
    jj = i[:, None, None].astype(np.float32)
    ii = i[None, None, :].astype(np.float32)
    sl = slopes[None, :, None]
    bc = np.where(ii >= jj, -sl * (ii - jj), -30000.0).astype(np.float32)
    bp = np.where(jj >= ii, -sl * (ii - jj + 128.0), -30000.0).astype(np.float32)
    c["bcur"] = np.ascontiguousarray(bc.reshape(128, 1024))
    c["bprev"] = np.ascontiguousarray(bp.reshape(128, 1024))
    tm = np.zeros((128, 1), np.float32)
    tm[0, 0] = 1.0
    c["tok0"] = tm
    cm = np.zeros((128, 512), np.float32)
    cm[:, 0::128] = 1.0
    c["col0"] = cm
    qm = np.zeros((128, 4), np.float32)
    qm[:64, 0] = 0.125; qm[64:, 1] = 0.125; qm[:64, 2] = 1.0; qm[64:, 3] = 1.0
    c["qm"] = qm
    return c


def build(n_tiles, stage=3, n_pre=0):
    nc = bass.Bass("TRN2", target_bir_lowering=False)
    NTOK = n_tiles * TT

    def din(name, shape):
        return nc.dram_tensor(name, list(shape), F32, kind="ExternalInput").ap()

    def dout(name, shape):
        return nc.dram_tensor(name, list(shape), F32, kind="ExternalOutput").ap()

    xp = din("xp", [NTOK, D])
    xpre = din("xpre", [n_pre * TT, D]) if n_pre > 0 else None
    hasprev = din("hasprev", [128, 1])
    xs = din("xs", [NS, D])
    ck = din("ck", [NS, 128, 128])
    cv = din("cv", [NS, 128, 128])
    sconv = din("sconv", [NS * 3, 1536])
    sgdn = din("sgdn", [NS, 4, 128, 128])
    g1 = din("g1", [D]); wi1 = din("wi1", [D, 2 * FF]); wo1 = din("wo1", [FF, D])
    g2 = din("g2", [D]); wmi = din("wmi", [D, INW]); sinks = din("sinks", [8])
    convw = din("convw", [4, 1536]); alog = din("alog", [4]); dtb = din("dtb", [4])
    gng = din("gng", [128]); wmo = din("wmo", [D, D])
    g3 = din("g3", [D]); wi2 = din("wi2", [D, 2 * FF]); wo2 = din("wo2", [FF, D])
    gfin = din("gfin", [D])
    c_ident = din("c_ident", [128, 128]); c_triu = din("c_triu", [128, 128]); c_strl = din("c_strl", [128, 128])
    c_bcur = din("c_bcur", [128, 1024]); c_bprev = din("c_bprev", [128, 1024])
    c_tok0 = din("c_tok0", [128, 1]); c_col0 = din("c_col0", [128, 512]); c_qm = din("c_qm", [128, 4])

    yp = dout("yp", [NTOK, D]); ys = dout("ys", [NS, D])
    nkp = dout("nkp", [128, 128]); nvp = dout("nvp", [128, 128])
    ncp = dout("ncp", [3, 1536]); ngp = dout("ngp", [4, 128, 128])
    nks = dout("nks", [NS, 128, 128]); nvs = dout("nvs", [NS, 128, 128])
    ncs = dout("ncs", [NS, 3, 1536]); ngs = dout("ngs", [NS, 4, 128, 128])

    wi_s = [nc.dram_tensor(f"wi_s{i}", [NJ, 128, 8, 256], BF16).ap() for i in range(2)]
    wo_s = [nc.dram_tensor(f"wo_s{i}", [128, NJ, D], BF16).ap() for i in range(2)]
    wm_s = nc.dram_tensor("wm_s", [11, 128, 8, 256], BF16).ap()
    wtm_s = nc.dram_tensor("wtm_s", [128, 8, 264], BF16).ap()
    wmo_s = nc.dram_tensor("wmo_s", [128, 8, D], BF16).ap()

    A = nc.alloc_sbuf_tensor
    xt = [A(f"xt{i}", [128, NBLK, D], F32) for i in range(2)]
    xts = xt[1][0:NS, 0:1, :]
    x1f = xt[1][:, 1:4, :].rearrange("p b d -> p (b d)")
    xn = A("xn", [128, D], BF16)
    xnT = A("xnT", [128, 8, TT], BF16)
    nTs = A("nTs", [128, 8, NS], BF16)
    wi = [A(f"wibuf{i}", [128, 8, 256], BF16) for i in range(3)]
    wmo_t = A("wmo_t", [128, 8, D], BF16)
    wtm = A("wtm", [128, 8, 264], BF16)
    arena = A("arena", [128, 16896], F32)
    hT = arena[:, 0:5632].bitcast(BF16).rearrange("p (j t) -> p j t", j=NJ)
    wo = arena[:, 5632:16896].bitcast(BF16).rearrange("p (j n) -> p j n", j=NJ)
    off = [0]

    def carve(ncols_f32):
        a = off[0]
        off[0] += ncols_f32
        assert off[0] <= 16896
        return arena[:, a:a + ncols_f32]

    pre = carve(12 * 524).rearrange("p (m t) -> p m t", m=12)
    oT = carve(2048).rearrange("p (h t) -> p h t", h=4)
    ncrow = oT[0:4, :, :].rearrange("p h t -> p (h t)")[:, 0:1536]
    ebuf = carve(1024)
    tC = ebuf[:, 0:512]; tD = ebuf[:, 512:1024]
    tA = carve(512); tB = carve(512); tE = carve(512)
    u_t = carve(512)
    rbuf = tB
    gqn = carve(1024).bitcast(BF16).rearrange("p (h t) -> p h t", h=4)
    gkn = carve(1024).bitcast(BF16).rearrange("p (h t) -> p h t", h=4)
    zs = carve(1024).bitcast(BF16).rearrange("p (h t) -> p h t", h=4)
    Pp = carve(512).bitcast(BF16).rearrange("p (h t) -> p h t", h=8)
    qT = A("qT", [128, 4, TT], BF16)
    kTd = [A(f"kTd{i}", [128, 8, 128], BF16) for i in range(2)]
    Vd = [A(f"Vd{i}", [128, 8, 128], BF16) for i in range(2)]
    aoT = A("aoT", [128, 4, TT], BF16)
    goT = A("goT", [128, 4, TT], BF16)
    aoTs = A("aoTs", [128, 4, NS], BF16)
    goTs = A("goTs", [128, 4, NS], BF16)
    Pc = A("Pc", [128, 8, 128], BF16)
    Xb = [A("Xb0", [128, 4, 128], F32), carve(512).rearrange("p (h t) -> p h t", h=4)]
    Yb = [A("Yb0", [128, 4, 128], F32), carve(512).rearrange("p (h t) -> p h t", h=4)]
    Nf = carve(512).rearrange("p (h t) -> p h t", h=4)
    Nb = A("Nb", [128, 4, 128], BF16)
    vb = A("vb", [128, 4, 128], BF16)
    kb = A("kb", [128, 4, 128], BF16)
    kd = A("kd", [128, 4, 128], BF16)
    wT = A("wT", [128, 4, 128], BF16)
    qdT = A("qdT", [128, 4, 128], BF16)
    qkT = A("qkT", [128, 4, 128], BF16)
    vnew = A("vnew", [128, 4, 128], BF16)
    Sbf = A("Sbf", [128, 4, 128], BF16)
    ctmp = A("ctmp", [128, TT], F32)
    junk = ctmp[:, :].bitcast(BF16)
    Sst = A("Sst", [128, 4, 128], F32)
    ccar = A("ccar", [128, 12, 3], F32)
    kvo = A("kvo", [128, 256], F32)
    gbga = A("gbga", [128, NBLK, 8], F32)
    sm = A("sm", [128, 64], F32)
    ss = A("ss", [128, 8], F32)
    sg = [A("sg0", [128, TT], F32)] * 2
    ident_f = A("ident_f", [128, 128], F32)
    ident = A("ident_b", [128, 128], BF16)
    triu = A("triu", [128, 128], F32)
    strl = A("strl", [128, 128], F32)
    ones_f = A("ones_f", [128, 128], F32)
    ones_b = A("ones_b", [128, 128], BF16)
    bcur = A("bcur", [128, 1024], BF16)
    bprev = A("bprev", [128, 1024], BF16)
    tok0 = A("tok0", [128, 1], F32)
    hp_t = A("hp_t", [128, 1], F32)
    qm = A("qm", [128, 4], F32)
    qT2 = A("qT2", [128, 4, TT], BF16)
    col0 = A("col0", [128, 512], F32)
    gT = [A(f"gT{i}", [128, 8], F32) for i in range(3)]
    gfin_b = A("gfin_b", [128, D], F32)
    cw = A("cw", [128, 4, 12], F32)
    esink = A("esink", [128, 8], F32)
    negA = A("negA", [128, 4], F32)
    dtb_b = A("dtb_b", [128, 4], F32)
    gng_t = A("gng_t", [128, 1], F32)
    sct = x1f[0:NS * 3, 0:1536]
    ckt = x1f[:, 1536:1664]
    cvt = x1f[:, 1664:1792]
    ckd = x1f[:, 1792:2048].rearrange("p (a b d) -> p a b d", a=2, b=2)
    ps = nc.alloc_psum_tensor("ps", [128, 8, 512], F32)

    P = Prog(nc)
    MK = lambda *a: ("m_" + a[0],) + tuple(a[1:])
    bank = [0]

    def nb():
        bank[0] = (bank[0] + 1) % 8
        return bank[0]

    def bc(ap, shape, axis):
        return ap.unsqueeze(axis).broadcast_to(shape)

    def cast_ffn(i, w_in, w_out):
        v = w_in.rearrange("(c p) n -> p c n", p=128)
        for j in range(NJ):
            def f(e, j=j):
                return [e.dma_start(out=wi_s[i][j, :, :, g * 128:(g + 1) * 128],
                                    in_=v[:, :, g * FF + j * 128: g * FF + (j + 1) * 128]) for g in range(2)]
            P.op("pool", f, w=[("wi_s", i, j)], chan=f"cast{i}", nd=2)
        P.op("pool", lambda e: e.dma_start(out=wo_s[i], in_=w_out.rearrange("(j p) n -> p j n", p=128)),
             w=[("wo_s", i), ("castgrp", i)], chan=f"cast{i}")

    cast_ffn(0, wi1, wo1)
    wmv = wmi.rearrange("(c p) n -> p c n", p=128)
    groups = [(128 * p, False) for p in range(4)] + [(512, True), (576, True)] + \
             [(768 + 128 * m, False) for m in range(12)] + [(2304 + 128 * h, False) for h in range(4)]
    for s in range(11):
        def f(e, s=s):
            r = []
            for g in range(2):
                c0, dup = groups[2 * s + g]
                if dup:
                    for q in range(2):
                        r.append(e.dma_start(out=wm_s[s, :, :, g * 128 + q * 64: g * 128 + (q + 1) * 64], in_=wmv[:, :, c0:c0 + 64]))
                else:
                    r.append(e.dma_start(out=wm_s[s, :, :, g * 128:(g + 1) * 128], in_=wmv[:, :, c0:c0 + 128]))
            return r
        nd = sum(2 if groups[2 * s + g][1] else 1 for g in range(2))
        P.op("pool", f, w=[("wm_s", s)], chan="castm", nd=nd)

    def f(e):
        return [e.dma_start(out=wtm_s[:, :, 0:128], in_=wmv[:, :, 640:768]),
                e.dma_start(out=wtm_s[:, :, 128:256], in_=wmv[:, :, 512:640]),
                e.dma_start(out=wtm_s[:, :, 256:264], in_=wmv[:, :, 2816:2824])]
    P.op("pool", f, w=[("wtm_s",)], chan="castm", nd=3)
    P.op("pool", lambda e: e.dma_start(out=wmo_s, in_=wmo.rearrange("(c p) n -> p c n", p=128)), w=[("wmo_s",), ("castgrp", "m")], chan="castm")
    cast_ffn(1, wi2, wo2)

    for sq in range(NS):
        def f(e, sq=sq):
            return [e.dma_start(out=nks[sq, 0:127, :], in_=ck[sq, 1:128, :]),
                    e.dma_start(out=nvs[sq, 0:127, :], in_=cv[sq, 1:128, :]),
                    e.dma_start(out=ncs[sq, 0:2, :], in_=sconv[sq * 3 + 1: sq * 3 + 3, :])]
        P.op("pool", f, w=[("o_hist", sq)], chan="ohist", nd=3)

    ldn = [0]

    def ld(dst, src, key, **kw):
        ldn[0] += 1
        P.op("sp", lambda e: e.dma_start(out=dst, in_=src, **kw), w=[key], chan=f"const{ldn[0]}")

    ld(ident_f[:, :], c_ident, ("identf",)); ld(triu[:, :], c_triu, ("triu",)); ld(strl[:, :], c_strl, ("strl",))
    P.op("pool", lambda e: e.dma_start(out=bcur[:, :], in_=c_bcur), w=[("bcur",)], chan="constb1")
    P.op("pool", lambda e: e.dma_start(out=bprev[:, :], in_=c_bprev), w=[("bprev",)], chan="constb2")
    ld(tok0[:, :], c_tok0, ("tok0",)); ld(hp_t[:, :], hasprev, ("hp",)); ld(qm[:, :], c_qm, ("qm",)); ld(col0[:, :], c_col0, ("col0",))
    for i, g in enumerate((g1, g2, g3)):
        ld(gT[i][:, :], g.rearrange("(c p) -> p c", p=128), ("gT", i), allow_slow_non_contiguous=True)
    ld(gfin_b[:, :], gfin.partition_broadcast(128), ("gfin",))
    P.op("sp", lambda e: [e.dma_start(out=cw[:, j, :], in_=convw[j].rearrange("(m p) -> p m", p=128), allow_slow_non_contiguous=True) for j in range(4)], w=[("cw",)], chan="constcw", nd=4)
    ld(esink[:, :], sinks.partition_broadcast(128), ("esink",))
    ld(negA[:, :], alog.partition_broadcast(128), ("negA",))
    ld(dtb_b[:, :], dtb.partition_broadcast(128), ("dtb",))
    ld(gng_t[:, :], gng.rearrange("(p o) -> p o", o=1), ("gng",))
    ld(wtm[:, :, :], wtm_s, ("wtm",))
    P.ops[-1]["deps"].add(P.lastw[("castgrp", "m")])
    ld(wmo_t[:, :, :], wmo_s, ("wmo",))
    P.ops[-1]["deps"].add(P.lastw[("castgrp", "m")])
    P.op("dve", lambda e: e.tensor_copy(ident[:, :], ident_f[:, :]), r=[("identf",)], w=[("ident",)])
    P.op("dve", lambda e: e.memset(ones_f[:, :], 1.0), w=[("ones_f",)])
    P.op("dve", lambda e: e.memset(ones_b[:, :], 1.0), w=[("ones_b",)])
    P.op("act", lambda e: e.activation(out=esink[:, :], in_=esink[:, :], func=AF.Exp), r=[("esink",)], w=[("esink",)])
    P.op("act", lambda e: e.activation(out=negA[:, :], in_=negA[:, :], func=AF.Exp), r=[("negA",)], w=[("negA",)])
    P.op("dve", lambda e: e.tensor_scalar(negA[:, :], negA[:, :], -1.0, 0.0, ALU.mult, ALU.add), r=[("negA",)], w=[("negA",)])

    def rstd_cols(n, np_, scale, keyss):
        P.op("act", lambda e: e.activation(out=ss[0:np_, 0:n], in_=ss[0:np_, 0:n], func=AF.Ln, scale=scale, bias=EPS),
             r=keyss, w=keyss)
        P.op("act", lambda e: e.activation(out=ss[0:np_, 0:n], in_=ss[0:np_, 0:n], func=AF.Exp, scale=-0.5),
             r=keyss, w=keyss)

    def norm_T(xsrc, xkeys, nblk, np_, gi, dstT, dkeys):
        keyss = [("ss",)]
        for b in range(nblk):
            P.op("act", lambda e, b=b: e.activation(out=junk[0:np_, :], in_=xsrc(b), func=AF.Square, accum_out=ss[0:np_, b:b + 1]),
                 r=[xkeys[b]], w=[MK("ctmp")] + keyss)
        rstd_cols(nblk, np_, 1.0 / D, keyss)
        for b in range(nblk):
            P.op("dve", lambda e, b=b: e.tensor_scalar(xn[0:np_, :], xsrc(b), ss[0:np_, b:b + 1], 1.0, ALU.mult, ALU.mult),
                 r=keyss + [xkeys[b]], w=[("xn",)])
            k = nb()
            pst = ps[:, k, :].bitcast(BF16)
            for c in range(8):
                P.op("pe", lambda e, c=c, pst=pst: e.transpose(pst[:, c * 128:c * 128 + np_], xn[0:np_, c * 128:(c + 1) * 128], ident[0:np_, 0:np_]),
                     r=[("xn",), ("ident",)], w=[("ps", k)])
            P.op("dve", lambda e, b=b, pst=pst: e.tensor_tensor(
                dstT[:, :, b * np_:(b + 1) * np_], pst.rearrange("p (c t) -> p c t", c=8)[:, :, 0:np_],
                bc(gT[gi][:, :], [128, 8, np_], 2), ALU.mult),
                r=[("ps", k), ("gT", gi)], w=[dkeys[b]])

    pref = {"on": False}

    def ffn_prefetch(fi):
        for j in range(3):
            P.op("sp", lambda e, j=j: e.dma_start(out=wi[j][:, :, :], in_=wi_s[fi][j]),
                 r=[("castgrp", fi)], w=[("wi", j)], chan=f"wi{j}")
        pref["on"] = True

    def ffn(fi, srcT, skeys, ntok, xdst, xkeys, nblk, np_):
        hkeys = [("hT", j) for j in range(NJ)]
        skip_first = pref["on"]
        pref["on"] = False
        P.op("sp", lambda e: e.dma_start(out=wo, in_=wo_s[fi]), r=[("wo_s", fi), ("castgrp", fi)], w=[("wo",)], chan="wo")
        for j in range(NJ):
            s = j % 3
            if not (skip_first and j < 3):
                P.op("sp", lambda e, j=j, s=s: e.dma_start(out=wi[s][:, :, :], in_=wi_s[fi][j]),
                     r=[("wi_s", fi, j), ("castgrp", fi)], w=[("wi", s)], chan=f"wi{s}")
            kg, ku = nb(), nb()
            for c in range(8):
                P.op("pe", lambda e, c=c, s=s, kg=kg: e.matmul(ps[:, kg, 0:ntok], wi[s][:, c, 0:128], srcT[:, c, 0:ntok], start=(c == 0), stop=(c == 7)),
                     r=[("wi", s)] + skeys, w=[("ps", kg)])
            for c in range(8):
                P.op("pe", lambda e, c=c, s=s, ku=ku: e.matmul(ps[:, ku, 0:ntok], wi[s][:, c, 128:256], srcT[:, c, 0:ntok], start=(c == 0), stop=(c == 7)),
                     r=[("wi", s)] + skeys, w=[("ps", ku)])
            q = 0
            P.op("act", lambda e, kg=kg, q=q: e.activation(out=sg[q][:, 0:ntok], in_=ps[:, kg, 0:ntok], func=AF.Silu),
                 r=[("ps", kg)], w=[("sg", q)])
            P.op("dve", lambda e, ku=ku, q=q, j=j: e.tensor_tensor(hT[:, j, 0:ntok], sg[q][:, 0:ntok], ps[:, ku, 0:ntok], ALU.mult),
                 r=[("ps", ku), ("sg", q)], w=[hkeys[j]])
        for b in range(nblk):
            k0, k1 = nb(), nb()
            for j in range(NJ):
                for dh, kk in ((0, k0), (1, k1)):
                    P.op("pe", lambda e, j=j, dh=dh, kk=kk, b=b: e.matmul(ps[0:np_, kk, :], hT[:, j, b * np_:(b + 1) * np_], wo[:, j, dh * 512:(dh + 1) * 512], start=(j == 0), stop=(j == NJ - 1)),
                         r=[hkeys[j], ("wo",)], w=[("ps", kk)])
            for dh, kk in ((0, k0), (1, k1)):
                P.op("dve", lambda e, dh=dh, kk=kk, b=b: e.scalar_tensor_tensor(xdst(b, dh), ps[0:np_, kk, :], 0.5, xdst(b, dh), ALU.mult, ALU.add),
                     r=[("ps", kk)], w=[xkeys[b]])

    def fence():
        ks = P.allkeys(lambda k: k[0] in ("hT", "wo") or str(k[0]).startswith("m_"))
        P.op("pool", lambda e: e.memset(sm[:, 63:64], 0.0), w=ks + [("fence",)])


    def mix(srcT, skeys, sample, tix, first_tile, last_tile, xres, xkeys, prepass=False, pre_last=False, mask_prev=False):
        bs = 131 if sample else 128

        def kprev(kv, b):
            return kTd[kv][:, 2 * b, :] if sample else kTd[kv][:, b, :]

        def kcur(kv, b):
            return kTd[kv][:, 2 * b + 1, :] if sample else kTd[kv][:, b + 1, :]

        def kix(b):
            return (2 * b, 2 * b + 1) if sample else (b, b + 1)

        KLIM = KLIMS if sample else KLIMP
        if KLIM < 0.5:
            return
        PQ = (lambda *a, **k: None) if prepass else P.op
        PD = (lambda *a, **k: None) if sample else P.op
        if sample:
            for h in range(4):
                P.op("pool", lambda e, h=h: e.tensor_copy(Nb[:, h, :], ident[:, :]), r=[("ident",)], w=[MK("N")])
        slots = range(11) if not prepass else (range(2, 9) if pre_last else range(5, 9))
        for s in slots:
            sl = s % 3
            P.op("sp", lambda e, s=s, sl=sl: e.dma_start(out=wi[sl][:, :, :], in_=wm_s[s]),
                 r=[("wm_s", s), ("castgrp", "m")], w=[("wi", sl)], chan=f"wi{sl}")
            if 3 <= s <= 8 and (sample or last_tile):
                kq = nb()
                lt = srcT[:, :, :].rearrange("p c (b t) -> p c b t", b=4)[:, :, :, 0] if sample else srcT[:, :, 509:512]
                mrows = 4 if sample else 3
                for c in range(8):
                    P.op("pe", lambda e, c=c, sl=sl, kq=kq, lt=lt, mrows=mrows: e.matmul(ps[0:mrows, kq, 0:256], lt[:, c, :], wi[sl][:, c, :], start=(c == 0), stop=(c == 7)),
                         r=[("wi", sl)] + skeys, w=[("ps", kq)])
                P.op("dve", lambda e, kq=kq, s=s, mrows=mrows: e.tensor_copy(ncrow[0:mrows, (s - 3) * 256:(s - 2) * 256], ps[0:mrows, kq, 0:256]), r=[("ps", kq)], w=[MK("ncrow")])
                if s == 8:
                    if sample:
                        P.op("sp", lambda e: e.dma_start(out=ncs[tix * 4:(tix + 1) * 4, 2, :], in_=ncrow[0:4, :]), r=[MK("ncrow")], w=[("o_ncs", tix)], chan="onc")
                    else:
                        P.op("sp", lambda e: e.dma_start(out=ncp, in_=ncrow[0:3, :]), r=[MK("ncrow")], w=[("o_ncp",)], chan="onc")
            for g in range(2):
                gi = 2 * s + g
                k = nb()
                if sample and 6 <= gi < 18:
                    m = gi - 6
                    rhs4 = srcT[:, :, :].rearrange("p c (b t) -> p c b t", b=4)[:, :, :, 0]
                    for c in range(8):
                        P.op("pe", lambda e, c=c, sl=sl, g=g, k=k, rhs4=rhs4: e.matmul(ps[:, k, 0:4], wi[sl][:, c, g * 128:(g + 1) * 128], rhs4[:, c, :], start=(c == 0), stop=(c == 7)),
                             r=[("wi", sl)] + skeys, w=[("ps", k)])
                    dst4 = pre[:, m, 0:4 * bs].rearrange("p (b t) -> p b t", b=4)[:, :, 3]
                    P.op("act", lambda e, k=k, dst4=dst4: e.activation(out=dst4, in_=ps[:, k, 0:4], func=AF.Identity),
                         r=[("ps", k)], w=[MK("pre", m)])
                    continue
                for c in range(8):
                    P.op("pe", lambda e, c=c, sl=sl, g=g, k=k: e.matmul(ps[:, k, :], wi[sl][:, c, g * 128:(g + 1) * 128], srcT[:, c, :], start=(c == 0), stop=(c == 7)),
                         r=[("wi", sl)] + skeys, w=[("ps", k)])
                if gi < 4:
                    P.op("act", lambda e, k=k, gi=gi: e.activation(out=qT[:, gi, :], in_=ps[:, k, :], func=AF.Identity, scale=qm[:, 0:1]),
                         r=[("ps", k), ("qm",)], w=[MK("qT", gi)])
                    P.op("dve", lambda e, k=k, gi=gi: e.tensor_scalar(qT2[:, gi, :], ps[:, k, :], qm[:, 1:2], 0.0, ALU.mult, ALU.add),
                         r=[("ps", k), ("qm",)], w=[MK("qT", gi)])
                elif gi < 6:
                    kv = gi - 4
                    if sample:
                        dst = kTd[kv][:, :, :].rearrange("p (b two) t -> p b two t", two=2)[:, :, 1, :]
                    else:
                        dst = kTd[kv][:, 1:5, :]
                    P.op("act", lambda e, k=k, dst=dst: e.activation(out=dst, in_=ps[:, k, :].rearrange("p (b t) -> p b t", b=4), func=AF.Identity),
                         r=[("ps", k)], w=[MK("kTd", kv, i) for i in ((1, 3, 5, 7) if sample else (1, 2, 3, 4))])
                elif gi < 18:
                    m = gi - 6
                    dst = pre[:, m, 0:4 * bs].rearrange("p (b t) -> p b t", b=4)[:, :, 3:131] if sample else None
                    if sample:
                        P.op("act", lambda e, k=k, dst=dst: e.activation(out=dst, in_=ps[:, k, :].rearrange("p (b t) -> p b t", b=4), func=AF.Identity),
                             r=[("ps", k)], w=[MK("pre", m)])
                    else:
                        P.op("act", lambda e, k=k, m=m: e.activation(out=pre[:, m, 3:515], in_=ps[:, k, :], func=AF.Identity),
                             r=[("ps", k)], w=[MK("pre", m)])
                else:
                    h = gi - 18
                    P.op("act", lambda e, k=k, h=h: e.activation(out=zs[:, h, :], in_=ps[:, k, :], func=AF.Silu),
                         r=[("ps", k)], w=[MK("zs", h)])
        if KLIM < 1:
            return
        for b in range(4):
            k = nb()
            for c in range(8):
                P.op("pe", lambda e, c=c, k=k, b=b: e.matmul(ps[:, k, 0:264], srcT[:, c, b * 128:(b + 1) * 128], wtm[:, c, :], start=(c == 0), stop=(c == 7)),
                     r=[("wtm",)] + skeys, w=[("ps", k)])
            pi, ci = kix(b)
            for kv in range(2):
                P.op("act", lambda e, k=k, kv=kv, ci=ci: e.activation(out=Vd[kv][:, ci, 0:64], in_=ps[:, k, kv * 64:(kv + 1) * 64], func=AF.Identity),
                     r=[("ps", k)], w=[MK("Vd", kv, ci)])
                P.op("dve", lambda e, k=k, kv=kv, ci=ci: e.tensor_copy(Vd[kv][:, ci, 64:128], ps[:, k, kv * 64:(kv + 1) * 64]),
                     r=[("ps", k)], w=[MK("Vd", kv, ci)])
            P.op("dve", lambda e, k=k, b=b: e.tensor_copy(gbga[:, b, :], ps[:, k, 256:264]), r=[("ps", k)], w=[MK("gbga", b)])
            want_kv = sample or (last_tile and b == 3)
            if want_kv:
                P.op("dve", lambda e, k=k: e.tensor_copy(kvo[:, :], ps[:, k, 0:256]), r=[("ps", k)], w=[MK("kvo")])
                if sample:
                    sq = tix * 4 + b
                    def f(e, sq=sq):
                        return [e.dma_start(out=nvs[sq, 127:128, :], in_=kvo[0:1, 0:128]),
                                e.dma_start(out=nks[sq, 127:128, :], in_=kvo[0:1, 128:256])]
                    P.op("sp", f, r=[MK("kvo")], w=[("o_kvs", sq)], chan="okv", nd=2)
                else:
                    def f(e):
                        return [e.dma_start(out=nvp, in_=kvo[:, 0:128]), e.dma_start(out=nkp, in_=kvo[:, 128:256])]
                    P.op("sp", f, r=[MK("kvo")], w=[("o_kvp",)], chan="okv", nd=2)
        if KLIM < 2:
            return
        if sample:
            for b in range(4):
                sq = tix * 4 + b
                P.op("sp", lambda e, sq=sq: [e.dma_start(out=ckt[:, :], in_=ck[sq]), e.dma_start(out=cvt[:, :], in_=cv[sq])],
                     w=[MK("ckt"), MK("cvt")], chan="ckld", nd=2)
                for kv in range(2):
                    k = nb()
                    for q in range(2):
                        P.op("dve", lambda e, kv=kv, q=q: e.tensor_copy(ckd[:, kv, q, :], ckt[:, kv * 64:(kv + 1) * 64]), r=[MK("ckt")], w=[MK("ckd", kv)])
                    P.op("pe", lambda e, k=k, kv=kv: e.transpose(ps[:, k, 0:128], ckd[:, kv, :, :].rearrange("p a d -> p (a d)"), ident_f[:, :]),
                         r=[MK("ckd", kv), ("identf",)], w=[("ps", k)])
                    P.op("act", lambda e, k=k, kv=kv, b=b: e.activation(out=kTd[kv][:, 2 * b, :], in_=ps[:, k, 0:128], func=AF.Identity),
                         r=[("ps", k)], w=[MK("kTd", kv, 2 * b)])
                    for q in range(2):
                        P.op("dve", lambda e, kv=kv, b=b, q=q: e.tensor_copy(Vd[kv][:, 2 * b, q * 64:(q + 1) * 64], cvt[:, kv * 64:(kv + 1) * 64]),
                             r=[MK("cvt")], w=[MK("Vd", kv, 2 * b)])
            if tix == 0:
                P.op("sp", lambda e: e.dma_start(out=sct[:, :], in_=sconv), w=[("sct",)], chan="sctld")
            for m in range(12):
                k = nb()
                P.op("pe", lambda e, k=k, m=m: e.transpose(ps[:, k, 0:NS * 3], sct[:, m * 128:(m + 1) * 128], ident_f[0:NS * 3, 0:NS * 3]),
                     r=[("sct",), ("identf",)], w=[("ps", k)])
                P.op("dve", lambda e, k=k, m=m: e.tensor_copy(
                    pre[:, m, 0:4 * bs].rearrange("p (b t) -> p b t", b=4)[:, :, 0:3],
                    ps[:, k, tix * 12: tix * 12 + 12].rearrange("p (b r) -> p b r", b=4)),
                    r=[("ps", k)], w=[MK("pre", m)])
        elif not first_tile:
            P.op("pool", lambda e: e.tensor_copy(pre[:, :, 0:3], ccar[:, :, :]), r=[("ccar",)], w=[MK("pre", m) for m in range(12)])
        if (not sample) and first_tile:
            for kv in range(2):
                P.op("pool", lambda e, kv=kv: e.memset(kTd[kv][:, 0, :], 0.0), w=[MK("kTd", kv, 0)])
                P.op("pool", lambda e, kv=kv: e.memset(Vd[kv][:, 0, :], 0.0), w=[MK("Vd", kv, 0)])
            P.op("pool", lambda e: e.memset(pre[:, :, 0:3], 0.0), w=[MK("pre", m) for m in range(12)])
            P.op("pool", lambda e: e.memset(ccar[:, :, :], 0.0), w=[("ccar",)])
            P.op("pool", lambda e: e.memset(Sst[:, :, :], 0.0), w=[MK("S")])
            P.op("pool", lambda e: e.memset(Sbf[:, :, :], 0.0), w=[MK("Sbf")])

        if KLIM < 3:
            return
        for b in (range(4) if not prepass else ()):
            pi, ci = kix(b)
            has_prev = sample or not (first_tile and b == 0)
            kc = [nb(), nb()]
            for h in range(8):
                kv, p, hh = h // 4, h // 2, h % 2
                P.op("pe", lambda e, h=h, kv=kv, p=p, hh=hh, ci=ci, b=b, kc=kc: e.matmul(
                    ps[:, kc[h // 4], (h % 4) * 128:(h % 4 + 1) * 128], kTd[kv][:, ci, :],
                    (qT2 if hh else qT)[:, p, b * 128:(b + 1) * 128], start=True, stop=True),
                    r=[MK("kTd", kv, ci), MK("qT", p)], w=[("ps", kc[h // 4])])
            for half in range(2):
                P.op("dve", lambda e, half=half, kc=kc: e.tensor_tensor(ebuf[:, half * 512:(half + 1) * 512], ps[:, kc[half], :], bcur[:, half * 512:(half + 1) * 512], ALU.add),
                     r=[("ps", kc[half]), ("bcur",)], w=[MK("tC" if half == 0 else "tD")])
            P.op("act", lambda e: e.activation(out=Pc[:, :, :].rearrange("p h t -> p (h t)"), in_=ebuf, func=AF.Exp),
                 r=[MK("tC"), MK("tD")], w=[MK("Pc")])
            if has_prev:
                kp = [nb(), nb()]
                for h in range(8):
                    kv, p, hh = h // 4, h // 2, h % 2
                    P.op("pe", lambda e, h=h, kv=kv, p=p, hh=hh, pi=pi, b=b, kp=kp: e.matmul(
                        ps[:, kp[h // 4], (h % 4) * 128:(h % 4 + 1) * 128], kTd[kv][:, pi, :],
                        (qT2 if hh else qT)[:, p, b * 128:(b + 1) * 128], start=True, stop=True),
                        r=[MK("kTd", kv, pi), MK("qT", p)], w=[("ps", kp[h // 4])])
                for half in range(2):
                    P.op("dve", lambda e, half=half, kp=kp: e.tensor_tensor(ebuf[:, half * 512:(half + 1) * 512], ps[:, kp[half], :], bprev[:, half * 512:(half + 1) * 512], ALU.add),
                         r=[("ps", kp[half]), ("bprev",)], w=[MK("tC" if half == 0 else "tD")])
                P.op("act", lambda e: e.activation(out=Pp[:, :, :].rearrange("p h t -> p (h t)"), in_=ebuf, func=AF.Exp),
                     r=[MK("tC"), MK("tD")], w=[MK("Pp")])
                if mask_prev and b == 0:
                    P.op("dve", lambda e: e.tensor_scalar(Pp[:, :, :].rearrange("p h t -> p (h t)"), Pp[:, :, :].rearrange("p h t -> p (h t)"), hp_t[:, 0:1], 0.0, ALU.mult, ALU.add),
                         r=[MK("Pp"), ("hp",)], w=[MK("Pp")])
            for kv in range(2):
                kn, kdn = nb(), nb()
                pcs = Pc[:, kv * 4:(kv + 1) * 4, :].rearrange("p h t -> p (h t)")
                pps = Pp[:, kv * 4:(kv + 1) * 4, :].rearrange("p h t -> p (h t)")
                P.op("pe", lambda e, kn=kn, kv=kv, ci=ci, pcs=pcs, hp=has_prev: e.matmul(ps[:, kn, :], Vd[kv][:, ci, :], pcs, start=True, stop=not hp),
                     r=[MK("Vd", kv, ci), MK("Pc")], w=[("ps", kn)])
                if has_prev:
                    P.op("pe", lambda e, kn=kn, kv=kv, pi=pi, pps=pps: e.matmul(ps[:, kn, :], Vd[kv][:, pi, :], pps, start=False, stop=True),
                         r=[MK("Vd", kv, pi), MK("Pp")], w=[("ps", kn)])
                P.op("pe", lambda e, kdn=kdn, pcs=pcs, hp=has_prev: e.matmul(ps[:, kdn, :], ones_b[:, :], pcs, start=True, stop=not hp),
                     r=[("ones_b",), MK("Pc")], w=[("ps", kdn)])
                if has_prev:
                    P.op("pe", lambda e, kdn=kdn, pps=pps: e.matmul(ps[:, kdn, :], ones_b[:, :], pps, start=False, stop=True),
                         r=[("ones_b",), MK("Pp")], w=[("ps", kdn)])
                P.op("dve", lambda e, kdn=kdn, kv=kv: e.tensor_tensor(rbuf.rearrange("p (h t) -> p h t", h=4), ps[:, kdn, :].rearrange("p (h t) -> p h t", h=4),
                                                                      bc(esink[:, kv * 4:(kv + 1) * 4], [128, 4, 128], 2), ALU.add),
                     r=[("ps", kdn), ("esink",)], w=[MK("tB")])
                P.op("dve", lambda e: e.reciprocal(rbuf, rbuf), r=[MK("tB")], w=[MK("tB")])
                nv = ps[:, kn, :].rearrange("p (m q t) -> p m q t", m=2, q=2)
                rv = rbuf.rearrange("p (m q t) -> p m q t", m=2, q=2)
                te = tA.rearrange("p (m t) -> p m t", m=4)[:, 0:2, :]
                to = tA.rearrange("p (m t) -> p m t", m=4)[:, 2:4, :]
                P.op("dve", lambda e, nv=nv, rv=rv, te=te: e.tensor_tensor(te, nv[:, :, 0, :], rv[:, :, 0, :], ALU.mult), r=[("ps", kn), MK("tB")], w=[MK("tA")])
                P.op("dve", lambda e, nv=nv, rv=rv, to=to: e.tensor_tensor(to, nv[:, :, 1, :], rv[:, :, 1, :], ALU.mult), r=[("ps", kn), MK("tB")], w=[MK("tA")])
                P.op("dve", lambda e, te=te: e.tensor_scalar(te, te, qm[:, 2:3], 0.0, ALU.mult, ALU.add), r=[MK("tA"), ("qm",)], w=[MK("tA")])
                P.op("dve", lambda e, kv=kv, b=b, te=te, to=to: e.scalar_tensor_tensor(aoT[:, kv * 2:(kv + 1) * 2, b * 128:(b + 1) * 128], to, qm[:, 3:4], te, ALU.mult, ALU.add),
                     r=[MK("tA"), ("qm",)], w=[MK("aoT", b)])
        if (not sample) and (not prepass or pre_last):
            for kv in range(2):
                P.op("pool", lambda e, kv=kv: e.tensor_copy(kTd[kv][:, 0, :], kTd[kv][:, 4, :]), r=[MK("kTd", kv, 4)], w=[MK("kTd", kv, 0)])
                P.op("pool", lambda e, kv=kv: e.tensor_copy(Vd[kv][:, 0, :], Vd[kv][:, 4, :]), r=[MK("Vd", kv, 4)], w=[MK("Vd", kv, 0)])

        if KLIM < 4:
            return
        if KLIM < 5:
            return
        def pv(m, j):
            if sample:
                return pre[:, m, 0:4 * bs].rearrange("p (b t) -> p b t", b=4)[:, :, j:j + 128]
            return pre[:, m, j:j + 512].rearrange("p (b t) -> p b t", b=4)
        cbufs = [(ctmp[:, :], MK("ctmp")), (tA, MK("tA")), (tE, MK("tE"))]
        for ci_, m in enumerate(range(12) if (not prepass or pre_last) else range(4, 12)):
            cb, ck_ = cbufs[ci_ % 3]
            ct3 = cb.rearrange("p (b t) -> p b t", b=4)
            if sample:
                ct1 = ct3[:, :, 0:1]
                P.op("pool", lambda e, m=m, ct1=ct1: e.tensor_scalar(ct1, pv(m, 0)[:, :, 0:1], cw[:, 0, m:m + 1], 0.0, ALU.mult, ALU.add), r=[MK("pre", m), ("cw",)], w=[ck_])
                for j in (1, 2, 3):
                    P.op("dve", lambda e, m=m, j=j, ct1=ct1: e.scalar_tensor_tensor(ct1, pv(m, j)[:, :, 0:1], cw[:, j, m:m + 1], ct1, ALU.mult, ALU.add),
                         r=[MK("pre", m), ("cw",), ck_], w=[ck_])
                P.op("pool", lambda e, m=m: e.memset(pre[:, m, 3:515], 0.0), w=[MK("pre", m)])
                P.op("act", lambda e, m=m, ct1=ct1: e.activation(out=pre[:, m, 3:515].rearrange("p (b t) -> p b t", b=4)[:, :, 0:1], in_=ct1, func=AF.Silu), r=[ck_], w=[MK("pre", m)])
                continue
            P.op("pool", lambda e, m=m, ct3=ct3: e.tensor_scalar(ct3, pv(m, 0), cw[:, 0, m:m + 1], 0.0, ALU.mult, ALU.add), r=[MK("pre", m), ("cw",)], w=[ck_])
            for j in (1, 2, 3):
                P.op("dve", lambda e, m=m, j=j, ct3=ct3: e.scalar_tensor_tensor(ct3, pv(m, j), cw[:, j, m:m + 1], ct3, ALU.mult, ALU.add),
                     r=[MK("pre", m), ("cw",), ck_], w=[ck_])
            P.op("pool", lambda e, m=m: e.tensor_copy(ccar[:, m, :], pre[:, m, 512:515]), r=[MK("pre", m)], w=[("ccar",)])
            P.op("act", lambda e, m=m, cb=cb: e.activation(out=pre[:, m, 3:515], in_=cb, func=AF.Silu), r=[ck_], w=[MK("pre", m)])
        cact = lambda m: pre[:, m, 3:515]
        if KLIM < 6:
            return
        for li, m in enumerate(range(8) if not prepass else range(4, 8)):
            sqb, sqk = ((ctmp[:, :], MK("ctmp")), (tE, MK("tE")))[li % 2]
            rsb, rsk = ((tA, MK("tA")), (tB, MK("tB")))[li % 2]
            P.op("pool", lambda e, m=m, sqb=sqb: e.tensor_tensor(sqb, cact(m), cact(m), ALU.mult), r=[MK("pre", m)], w=[sqk])
            k = nb()
            P.op("pe", lambda e, k=k, sqb=sqb: e.matmul(ps[:, k, :], ones_f[:, :], sqb, start=True, stop=True), r=[("ones_f",), sqk], w=[("ps", k)])
            P.op("act", lambda e, k=k, rsb=rsb: e.activation(out=rsb, in_=ps[:, k, :], func=AF.Ln, bias=EPS), r=[("ps", k)], w=[rsk])
            P.op("act", lambda e, rsb=rsb: e.activation(out=rsb, in_=rsb, func=AF.Exp, scale=-0.5), r=[rsk], w=[rsk])
            dst = gqn[:, m, :] if m < 4 else gkn[:, m - 4, :]
            sc = 128.0 ** -0.5 if m < 4 else 1.0
            P.op("dve", lambda e, m=m, dst=dst, sc=sc, rsb=rsb: e.scalar_tensor_tensor(dst, cact(m), sc, rsb, ALU.mult, ALU.mult),
                 r=[MK("pre", m), rsk], w=[MK("gqn" if m < 4 else "gkn", m % 4)])
        if KLIM < 7:
            return
        gb_v = gbga[:, :, 0:4]
        ga_v = gbga[:, :, 4:8]
        be = sm[:, 0:16].rearrange("p (b h) -> p b h", b=4)
        gg = sm[:, 16:32].rearrange("p (b h) -> p b h", b=4)
        t1 = sm[:, 32:48].rearrange("p (b h) -> p b h", b=4)
        gkeys = [MK("gbga", b) for b in range(4)]
        P.op("act", lambda e: e.activation(out=be, in_=gb_v, func=AF.Exp, scale=-1.0), r=gkeys, w=[MK("be")])
        P.op("dve", lambda e: e.tensor_scalar(be, be, 1.0, 1.0, ALU.add, ALU.mult), r=[MK("be")], w=[MK("be")])
        P.op("dve", lambda e: e.reciprocal(be, be), r=[MK("be")], w=[MK("be")])
        P.op("dve", lambda e: e.tensor_tensor(gg, ga_v, bc(dtb_b[:, :], [128, 4, 4], 1), ALU.add), r=gkeys + [("dtb",)], w=[MK("gg")])
        P.op("dve", lambda e: e.tensor_scalar(t1, gg, -1.0, 0.0, ALU.mult, ALU.add), r=[MK("gg")], w=[MK("t1")])
        P.op("dve", lambda e: e.tensor_tensor(t1, t1, gg, ALU.max), r=[MK("gg"), MK("t1")], w=[MK("t1")])
        P.op("act", lambda e: e.activation(out=t1, in_=t1, func=AF.Exp, scale=-1.0), r=[MK("t1")], w=[MK("t1")])
        P.op("act", lambda e: e.activation(out=t1, in_=t1, func=AF.Ln, bias=1.0), r=[MK("t1")], w=[MK("t1")])
        P.op("dve", lambda e: e.scalar_tensor_tensor(gg, gg, 0.0, t1, ALU.max, ALU.add), r=[MK("gg"), MK("t1")], w=[MK("gg")])
        P.op("dve", lambda e: e.tensor_tensor(gg, gg, bc(negA[:, :], [128, 4, 4], 1), ALU.mult), r=[MK("gg"), ("negA",)], w=[MK("gg")])
        if sample:
            P.op("dve", lambda e: e.tensor_scalar(gg, gg, tok0[:, 0:1], 0.0, ALU.mult, ALU.add), r=[MK("gg"), ("tok0",)], w=[MK("gg")])

        if KLIM < 8:
            return
        v4 = lambda ap: ap.rearrange("p (h t) -> p h t", h=4)
        for b in range(4):
            cols = slice(b * 128, (b + 1) * 128)
            gcc = sm[:, 48:52]
            gl = sm[:, 52:56]
            s1 = sm[:, 56:60]
            s2 = sm[:, 60:63]
            k = nb()
            P.op("pe", lambda e, k=k, b=b: e.matmul(ps[:, k, 0:4], triu[:, :], gg[:, b, :], start=True, stop=True), r=[("triu",), MK("gg")], w=[("ps", k)])
            P.op("dve", lambda e, k=k: e.tensor_copy(gcc, ps[:, k, 0:4]), r=[("ps", k)], w=[MK("gcc")])
            P.op("dve", lambda e, b=b: e.tensor_tensor(v4(tA), bc(triu[:, :], [128, 4, 128], 1), bc(gg[:, b, :], [128, 4, 128], 2), ALU.mult),
                 r=[("triu",), MK("gg")], w=[MK("tA")])
            kr = nb()
            P.op("pe", lambda e, kr=kr: e.matmul(ps[:, kr, :], ones_f[:, :], tA, start=True, stop=True), r=[("ones_f",), MK("tA")], w=[("ps", kr)])
            P.op("dve", lambda e, kr=kr: e.tensor_tensor(v4(tB), v4(ps[:, kr, :]), bc(gcc, [128, 4, 128], 2), ALU.subtract), r=[("ps", kr), MK("gcc")], w=[MK("tB")])
            PQ("dve", lambda e: e.tensor_scalar(tC, tB, 0.0, 0.0, ALU.min, ALU.add), r=[MK("tB")], w=[MK("tC")])
            PQ("act", lambda e: e.activation(out=tC, in_=tC, func=AF.Exp), r=[MK("tC")], w=[MK("tC")])
            PQ("dve", lambda e: e.tensor_tensor(v4(tC), v4(tC), bc(triu[:, :], [128, 4, 128], 1), ALU.mult), r=[MK("tC"), ("triu",)], w=[MK("tC")])
            PD("dve", lambda e: e.tensor_scalar(tD, tB, 0.0, 0.0, ALU.max, ALU.add), r=[MK("tB")], w=[MK("tD")])
            PD("act", lambda e: e.activation(out=tD, in_=tD, func=AF.Exp, scale=-1.0), r=[MK("tD")], w=[MK("tD")])
            PD("dve", lambda e: e.tensor_tensor(v4(tD), v4(tD), bc(strl[:, :], [128, 4, 128], 1), ALU.mult), r=[MK("tD"), ("strl",)], w=[MK("tD")])
            PQ("act", lambda e, kr=kr: e.activation(out=tE, in_=ps[:, kr, :], func=AF.Exp), r=[("ps", kr)], w=[MK("tE")])
            P.op("act", lambda e, kr=kr: e.activation(out=gl, in_=v4(ps[:, kr, :])[:, :, 127], func=AF.Exp), r=[("ps", kr)], w=[MK("gl")])
            P.op("dve", lambda e, kr=kr: e.tensor_tensor(s1, v4(ps[:, kr, :])[:, :, 127], gcc, ALU.subtract), r=[("ps", kr), MK("gcc")], w=[MK("s1")])
            P.op("act", lambda e: e.activation(out=s1, in_=s1, func=AF.Exp), r=[MK("s1")], w=[MK("s1")])
            P.op("act", lambda e: e.activation(out=gcc, in_=gcc, func=AF.Exp), r=[MK("gcc")], w=[MK("gcc")])
            P.op("dve", lambda e, b=b: e.tensor_tensor(gcc, gcc, be[:, b, :], ALU.mult), r=[MK("gcc"), MK("be")], w=[MK("gcc")])
            kt = nb()
            ktb = ps[:, kt, :].bitcast(BF16)
            for h in range(4):
                P.op("pe", lambda e, h=h, ktb=ktb, cols=cols: e.transpose(ktb[:, h * 128:(h + 1) * 128], gkn[:, h, cols], ident[:, :]),
                     r=[MK("gkn", h), ("ident",)], w=[("ps", kt)])
            P.op("dve", lambda e, ktb=ktb: e.tensor_tensor(kb[:, :, :], v4(ktb[:, 0:512]), bc(gcc, [128, 4, 128], 2), ALU.mult), r=[("ps", kt), MK("gcc")], w=[MK("kb")])
            P.op("dve", lambda e, ktb=ktb: e.tensor_tensor(kd[:, :, :], v4(ktb[:, 0:512]), bc(s1, [128, 4, 128], 2), ALU.mult), r=[("ps", kt), MK("s1")], w=[MK("kd")])
            kvv = nb()
            for h in range(4):
                P.op("pe", lambda e, h=h, kvv=kvv, cols=cols: e.transpose(ps[:, kvv, h * 128:(h + 1) * 128], cact(8 + h)[:, cols], ident_f[:, :]),
                     r=[MK("pre", 8 + h), ("identf",)], w=[("ps", kvv)])
            P.op("dve", lambda e, kvv=kvv, b=b: e.tensor_tensor(vb[:, :, :], v4(ps[:, kvv, :]), bc(be[:, b, :], [128, 4, 128], 2), ALU.mult), r=[("ps", kvv), MK("be")], w=[MK("vb")])
            kkk, kqk = nb(), nb()
            for h in range(4):
                PD("pe", lambda e, h=h, kkk=kkk, cols=cols: e.matmul(ps[:, kkk, h * 128:(h + 1) * 128], gkn[:, h, cols], gkn[:, h, cols], start=True, stop=True),
                     r=[MK("gkn", h)], w=[("ps", kkk)])
            for h in range(4):
                PQ("pe", lambda e, h=h, kqk=kqk, cols=cols: e.matmul(ps[:, kqk, h * 128:(h + 1) * 128], gkn[:, h, cols], gqn[:, h, cols], start=True, stop=True),
                     r=[MK("gkn", h), MK("gqn", h)], w=[("ps", kqk)])
            PD("dve", lambda e, kkk=kkk: e.tensor_tensor(tD, ps[:, kkk, :], tD, ALU.mult), r=[("ps", kkk), MK("tD")], w=[MK("tD")])
            PD("dve", lambda e, b=b: e.scalar_tensor_tensor(Xb[0][:, :, :], v4(tD), -1.0, bc(be[:, b, :], [128, 4, 128], 2), ALU.mult, ALU.mult),
                 r=[MK("tD"), MK("be")], w=[MK("X", 0)])
            PQ("dve", lambda e, kqk=kqk: e.tensor_tensor(qkT[:, :, :], v4(ps[:, kqk, :]), v4(tC), ALU.mult), r=[("ps", kqk), MK("tC")], w=[MK("qkT")])
            PQ("pool", lambda e, cols=cols: e.tensor_tensor(qdT[:, :, :], gqn[:, :, cols], v4(tE), ALU.mult), r=[MK("gqn", h) for h in range(4)] + [MK("tE")], w=[MK("qdT")])
            ky = nb()
            kyb = ps[:, ky, :].bitcast(BF16)
            for h in range(4):
                PD("pe", lambda e, h=h, ky=ky: e.transpose(ps[:, ky, h * 128:(h + 1) * 128], Xb[0][:, h, :], ident_f[:, :]), r=[MK("X", 0), ("identf",)], w=[("ps", ky)])
            PD("act", lambda e, ky=ky: e.activation(out=Yb[0][:, :, :], in_=v4(ps[:, ky, :]), func=AF.Identity), r=[("ps", ky)], w=[MK("Y", 0)])
            PD("dve", lambda e: e.tensor_tensor(Nf, Yb[0][:, :, :], bc(ident_f[:, :], [128, 4, 128], 1), ALU.add), r=[MK("Y", 0), ("identf",)], w=[MK("Nf")])
            cur = 0
            for st in range(1, 7):
                nx = 1 - cur
                kx, kyy = nb(), nb()
                for h in range(4):
                    PD("pe", lambda e, h=h, kx=kx, cur=cur: e.matmul(ps[:, kx, h * 128:(h + 1) * 128], Yb[cur][:, h, :], Xb[cur][:, h, :], start=True, stop=True),
                         r=[MK("X", cur), MK("Y", cur)], w=[("ps", kx)])
                for h in (range(4) if st < 6 else ()):
                    PD("pe", lambda e, h=h, kyy=kyy, cur=cur: e.matmul(ps[:, kyy, h * 128:(h + 1) * 128], Xb[cur][:, h, :], Yb[cur][:, h, :], start=True, stop=True),
                         r=[MK("X", cur), MK("Y", cur)], w=[("ps", kyy)])
                PD("act", lambda e, kx=kx, nx=nx: e.activation(out=Xb[nx][:, :, :], in_=v4(ps[:, kx, :]), func=AF.Identity), r=[("ps", kx)], w=[MK("X", nx)])
                if st < 6:
                    PD("dve", lambda e, kyy=kyy, nx=nx: e.tensor_copy(Yb[nx][:, :, :], v4(ps[:, kyy, :])), r=[("ps", kyy)], w=[MK("Y", nx)])
                kn2 = nb()
                for h in range(4):
                    PD("pe", lambda e, h=h, kn2=kn2, nx=nx: e.matmul(ps[:, kn2, h * 128:(h + 1) * 128], Xb[nx][:, h, :], Nf[:, h, :], start=True, stop=True),
                         r=[MK("X", nx), MK("Nf")], w=[("ps", kn2)])
                PD("dve", lambda e, kn2=kn2: e.tensor_tensor(Nf, Nf, v4(ps[:, kn2, :]), ALU.add), r=[("ps", kn2), MK("Nf")], w=[MK("Nf")])
                cur = nx
            PD("act", lambda e: e.activation(out=Nb[:, :, :], in_=Nf, func=AF.Identity), r=[MK("Nf")], w=[MK("N")])
            ku, kw = nb(), nb()
            for h in range(4):
                P.op("pe", lambda e, h=h, ku=ku: e.matmul(ps[:, ku, h * 128:(h + 1) * 128], Nb[:, h, :], vb[:, h, :], start=True, stop=True), r=[MK("N"), MK("vb")], w=[("ps", ku)])
            for h in range(4):
                P.op("pe", lambda e, h=h, kw=kw: e.matmul(ps[:, kw, h * 128:(h + 1) * 128], kb[:, h, :], Nb[:, h, :], start=True, stop=True), r=[MK("N"), MK("kb")], w=[("ps", kw)])
            P.op("act", lambda e, ku=ku: e.activation(out=u_t, in_=ps[:, ku, :], func=AF.Identity), r=[("ps", ku)], w=[MK("u")])
            P.op("act", lambda e, kw=kw: e.activation(out=wT[:, :, :], in_=v4(ps[:, kw, :]), func=AF.Identity), r=[("ps", kw)], w=[MK("wT")])
            if sample:
                sq = tix * 4 + b
                P.op("sp", lambda e, sq=sq: e.dma_start(out=Sst[:, :, :], in_=sgdn[sq].rearrange("h k v -> k h v")), w=[MK("S")], chan="sld")
                P.op("act", lambda e: e.activation(out=Sbf[:, :, :], in_=Sst[:, :, :], func=AF.Identity), r=[MK("S")], w=[MK("Sbf")])
            k1 = nb()
            for h in range(4):
                P.op("pe", lambda e, h=h, k1=k1: e.matmul(ps[:, k1, h * 128:(h + 1) * 128], wT[:, h, :], Sbf[:, h, :], start=True, stop=True), r=[MK("wT"), MK("Sbf")], w=[("ps", k1)])
            P.op("dve", lambda e, k1=k1: e.tensor_tensor(vnew[:, :, :], v4(u_t), v4(ps[:, k1, :]), ALU.subtract), r=[("ps", k1), MK("u")], w=[MK("vnew")])
            k3, k4 = nb(), nb()
            for h in range(4):
                PQ("pe", lambda e, h=h, k3=k3: e.matmul(ps[:, k3, h * 128:(h + 1) * 128], Sbf[:, h, :], qdT[:, h, :], start=True, stop=False), r=[MK("Sbf"), MK("qdT")], w=[("ps", k3)])
                PQ("pe", lambda e, h=h, k3=k3: e.matmul(ps[:, k3, h * 128:(h + 1) * 128], vnew[:, h, :], qkT[:, h, :], start=False, stop=True), r=[MK("vnew"), MK("qkT")], w=[("ps", k3)])
            for h in range(4):
                P.op("pe", lambda e, h=h, k4=k4: e.matmul(ps[:, k4, h * 128:(h + 1) * 128], kd[:, h, :], vnew[:, h, :], start=True, stop=True), r=[MK("kd"), MK("vnew")], w=[("ps", k4)])
            PQ("act", lambda e, k3=k3, cols=cols: e.activation(out=oT[:, :, cols], in_=v4(ps[:, k3, :]), func=AF.Identity), r=[("ps", k3)], w=[MK("oT", b), MK("ncrow")])
            for h in range(4):
                P.op("dve", lambda e, h=h, k4=k4: e.scalar_tensor_tensor(Sst[:, h, :], Sst[:, h, :], gl[:, h:h + 1], ps[:, k4, h * 128:(h + 1) * 128], ALU.mult, ALU.add),
                     r=[("ps", k4), MK("gl"), MK("S")], w=[MK("S")])
            if sample:
                P.op("sp", lambda e, sq=sq: e.dma_start(out=ngs[sq].rearrange("h k v -> k h v"), in_=Sst[:, :, :]), r=[MK("S")], w=[("o_ngs", sq)], chan="ongs")
            else:
                P.op("act", lambda e: e.activation(out=Sbf[:, :, :], in_=Sst[:, :, :], func=AF.Identity), r=[MK("S")], w=[MK("Sbf")])
                if last_tile and b == 3:
                    P.op("sp", lambda e: e.dma_start(out=ngp.rearrange("h k v -> k h v"), in_=Sst[:, :, :]), r=[MK("S")], w=[("o_ngp",)], chan="ongs")
        if KLIM < 9:
            return
        if prepass:
            return
        for h in range(4):
            P.op("pool", lambda e, h=h: e.tensor_tensor(ctmp[:, :], oT[:, h, :], oT[:, h, :], ALU.mult), r=[MK("oT", b) for b in range(4)], w=[MK("ctmp")])
            k = nb()
            P.op("pe", lambda e, k=k: e.matmul(ps[:, k, :], ones_f[:, :], ctmp[:, :], start=True, stop=True), r=[("ones_f",), MK("ctmp")], w=[("ps", k)])
            P.op("act", lambda e, k=k: e.activation(out=tA, in_=ps[:, k, :], func=AF.Ln, scale=1.0 / 128, bias=EPS), r=[("ps", k)], w=[MK("tA")])
            P.op("act", lambda e: e.activation(out=tA, in_=tA, func=AF.Exp, scale=-0.5), r=[MK("tA")], w=[MK("tA")])
            P.op("dve", lambda e, h=h: e.scalar_tensor_tensor(tA, oT[:, h, :], gng_t[:, 0:1], tA, ALU.mult, ALU.mult), r=[MK("oT", b) for b in range(4)] + [MK("tA"), ("gng",)], w=[MK("tA")])
            P.op("dve", lambda e, h=h: e.tensor_tensor(goT[:, h, :], tA, zs[:, h, :], ALU.mult), r=[MK("tA"), MK("zs", h)], w=[MK("goT", h)])
        if KLIM < 10:
            return
        if sample:
            for t_, src in ((aoTs, aoT), (goTs, goT)):
                P.op("pool", lambda e, t_=t_, src=src: e.tensor_copy(t_[:, :, tix * 4:(tix + 1) * 4], src[:, :, :].rearrange("p c (b t) -> p c b t", b=4)[:, :, :, 0]),
                     r=[MK("aoT", b) for b in range(4)] + [MK("goT", h) for h in range(4)], w=[("mixTs", tix)])
        else:
            for b in range(4):
                k0, k1 = nb(), nb()
                for c in range(8):
                    src = aoT if c < 4 else goT
                    for dh, kk in ((0, k0), (1, k1)):
                        P.op("pe", lambda e, c=c, dh=dh, kk=kk, b=b, src=src: e.matmul(ps[:, kk, :], src[:, c % 4, b * 128:(b + 1) * 128], wmo_t[:, c, dh * 512:(dh + 1) * 512], start=(c == 0), stop=(c == 7)),
                             r=[MK("aoT", b), MK("goT", c % 4), ("wmo",)], w=[("ps", kk)])
                for dh, kk in ((0, k0), (1, k1)):
                    P.op("dve", lambda e, dh=dh, kk=kk, b=b: e.tensor_tensor(xres(b, dh), xres(b, dh), ps[:, kk, :], ALU.add), r=[("ps", kk)], w=[xkeys[b]])

    def final_norm(xsrc, xkeys, nblk, np_, ydst, okey, chan):
        keyss = [("ss",)]
        for b in range(nblk):
            P.op("act", lambda e, b=b: e.activation(out=junk[0:np_, :], in_=xsrc(b), func=AF.Square, accum_out=ss[0:np_, b:b + 1]), r=[xkeys[b]], w=[MK("ctmp")] + keyss)
        rstd_cols(nblk, np_, 1.0 / D, keyss)
        for b in range(nblk):
            P.op("dve", lambda e, b=b: e.scalar_tensor_tensor(xsrc(b), xsrc(b), ss[0:np_, b:b + 1], gfin_b[0:np_, :], ALU.mult, ALU.mult), r=keyss + [xkeys[b], ("gfin",)], w=[xkeys[b]])
            P.op("sp", lambda e, b=b: e.dma_start(out=ydst(b), in_=xsrc(b)), r=[xkeys[b]], w=[(okey, b)], chan=chan)

    skeys = [("xs",)]
    P.op("sp", lambda e: e.dma_start(out=xts[:, 0, :], in_=xs), w=skeys, chan="xsld")
    xs_src = lambda b: xts[0:NS, 0, :]
    xs_dst = lambda b, dh: xts[0:NS, 0, dh * 512:(dh + 1) * 512]
    nTs_keys = [("nTs",)]
    ffn_prefetch(0)
    norm_T(xs_src, skeys, 1, NS, 0, nTs, nTs_keys)
    fence()
    ffn(0, nTs, nTs_keys, NS, xs_dst, skeys, 1, NS)
    xk = [("xnT", b) for b in range(4)]
    if stage >= 2:
        norm_T(xs_src, skeys, 1, NS, 1, nTs, nTs_keys)
        fence()
    for tix in range(4 if stage >= 2 else 0):
        P.op("pool", lambda e: e.memset(xnT[:, :, :], 0.0), w=xk)
        P.op("pool", lambda e, tix=tix: e.tensor_copy(xnT[:, :, :].rearrange("p c (b t) -> p c b t", b=4)[:, :, :, 0], nTs[:, :, tix * 4:(tix + 1) * 4]), r=nTs_keys, w=xk)
        mix(xnT, xk, True, tix, False, False, None, None)
    k0, k1 = nb(), nb()
    for c in range(8 if stage >= 2 else 0):
        src = aoTs if c < 4 else goTs
        for dh, kk in ((0, k0), (1, k1)):
            P.op("pe", lambda e, c=c, dh=dh, kk=kk, src=src: e.matmul(ps[0:NS, kk, :], src[:, c % 4, :], wmo_t[:, c, dh * 512:(dh + 1) * 512], start=(c == 0), stop=(c == 7)),
                 r=[("mixTs", t) for t in range(4)] + [("wmo",)], w=[("ps", kk)])
    for dh, kk in (((0, k0), (1, k1)) if stage >= 2 else ()):
        P.op("dve", lambda e, dh=dh, kk=kk: e.tensor_tensor(xs_dst(0, dh), xs_dst(0, dh), ps[0:NS, kk, :], ALU.add), r=[("ps", kk)], w=skeys)
    if stage >= 2:
        ffn_prefetch(1)
        norm_T(xs_src, skeys, 1, NS, 2, nTs, nTs_keys)
        fence()
        ffn(1, nTs, nTs_keys, NS, xs_dst, skeys, 1, NS)
    final_norm(xs_src, skeys, 1, NS, lambda b: ys, "o_ys", "oys")

    tiles = [("pre", i) for i in range(n_pre)] + [("main", i) for i in range(n_tiles)]

    def load_x(gi):
        kind, i = tiles[gi]
        src = xpre if kind == "pre" else xp
        X = xt[gi % 2]
        keys = [("x", gi % 2, b) for b in range(4)]
        if gi == 1:
            keys = keys + [("xs",), ("sct",), MK("ckt"), MK("cvt"), MK("ckd", 0), MK("ckd", 1)]
        P.op("sp", lambda e: [e.dma_start(out=X[:, b, :], in_=src[i * TT + b * 128: i * TT + (b + 1) * 128, :]) for b in range(4)],
             w=keys, chan=f"x{gi % 2}", nd=4)

    if stage >= 3:
        load_x(0)
    for gi, (kind, i) in enumerate(tiles if stage >= 3 else []):
        X = xt[gi % 2]
        xkeys = [("x", gi % 2, b) for b in range(4)]
        xsrc = lambda b, X=X: X[:, b, :]
        xdst = lambda b, dh, X=X: X[:, b, dh * 512:(dh + 1) * 512]
        ffn_prefetch(0)
        norm_T(xsrc, xkeys, 4, 128, 0, xnT, xk)
        fence()
        ffn(0, xnT, xk, TT, xdst, xkeys, 4, 128)
        norm_T(xsrc, xkeys, 4, 128, 1, xnT, xk)
        fence()
        if kind == "pre":
            mix(xnT, xk, False, 0, i == 0, False, xdst, xkeys, prepass=True, pre_last=(i == n_pre - 1))
            if gi + 1 < len(tiles):
                load_x(gi + 1)
            continue
        mix(xnT, xk, False, 0, (n_pre == 0 and i == 0), i == n_tiles - 1, xdst, xkeys, mask_prev=(n_pre > 0 and i == 0))
        if gi + 1 < len(tiles):
            load_x(gi + 1)
        ffn_prefetch(1)
        norm_T(xsrc, xkeys, 4, 128, 2, xnT, xk)
        fence()
        ffn(1, xnT, xk, TT, xdst, xkeys, 4, 128)
        final_norm(xsrc, xkeys, 4, 128, lambda b, i=i: yp[i * TT + b * 128: i * TT + (b + 1) * 128, :], ("o_yp", i), f"oy{gi % 2}")

    with ExitStack() as st:
        sems = {}
        for en in Prog.ENG:
            sems[("e", en)] = st.enter_context(nc.semaphore("e_" + en))
        for ch in P.chan_cnt:
            sems[("c", ch)] = st.enter_context(nc.semaphore("c_" + ch))
        P.run(sems)
    return nc


_NC_CACHE = {}


def run_cores(xp_list, per_core, shared, n_tiles, stage=3, xpre_list=None, hasprev=None):
    n_pre = 0 if xpre_list is None else xpre_list[0].shape[0] // TT
    key = (n_tiles, n_pre, stage)
    if key not in _NC_CACHE:
        _NC_CACHE[key] = build(n_tiles, stage, n_pre)
    nc = _NC_CACHE[key]
    consts = host_consts()
    in_maps = []
    for c in range(len(xp_list)):
        m = dict(shared)
        m.update(per_core[c])
        m["xp"] = xp_list[c]
        if n_pre:
            m["xpre"] = xpre_list[c]
        m["hasprev"] = np.full((128, 1), 0.0 if hasprev is None else hasprev[c], np.float32)
        for k, v in consts.items():
            m["c_" + k] = v
        in_maps.append({k: np.ascontiguousarray(v, dtype=np.float32) for k, v in m.items()})
    res = run_bass_kernel_spmd(nc, in_maps, core_ids=list(range(len(xp_list))))
    return res.results


def kernel(x_prompt, x_sample, cache_attn_k, cache_attn_v, state_conv, state_gdn,
           ffn1_norm_g, ffn1_w_in, ffn1_w_out, mix_norm_g, w_in_mix, attn_sinks, conv_w,
           gdn_A_log, gdn_dt_bias, gdn_norm_g, w_out_mix, ffn2_norm_g, ffn2_w_in, ffn2_w_out,
           final_norm_g):
    f = lambda a: np.asarray(a, dtype=np.float32)
    x_prompt = f(x_prompt)
    B, S, _ = x_prompt.shape
    ncore = 8
    HALF = S // 2
    n_tiles = HALF // TT
    shared = dict(g1=f(ffn1_norm_g)[0], wi1=f(ffn1_w_in)[0], wo1=f(ffn1_w_out)[0], g2=f(mix_norm_g)[0],
                  wmi=f(w_in_mix)[0], sinks=f(attn_sinks)[0], convw=f(conv_w)[0], alog=f(gdn_A_log)[0],
                  dtb=f(gdn_dt_bias)[0], gng=f(gdn_norm_g)[0], wmo=f(w_out_mix)[0], g3=f(ffn2_norm_g)[0],
                  wi2=f(ffn2_w_in)[0], wo2=f(ffn2_w_out)[0], gfin=f(final_norm_g))
    xs_ = f(x_sample)[:, 0, :]
    ck_ = f(cache_attn_k)[0].reshape(-1, 128, 128)
    cv_ = f(cache_attn_v)[0].reshape(-1, 128, 128)
    sc_ = f(state_conv)[0]
    sg_ = f(state_gdn)[0]
    per_core, xp_list, xpre_list, hasprev = [], [], [], []
    zeros = np.zeros((HALF, D), np.float32)
    for c in range(ncore):
        sl = slice(c * NS, (c + 1) * NS)
        per_core.append(dict(xs=xs_[sl], ck=ck_[sl], cv=cv_[sl], sconv=sc_[sl].reshape(NS * 3, 1536), sgdn=sg_[sl]))
        sq, half = c // 2, c % 2
        xp_list.append(x_prompt[sq, half * HALF:(half + 1) * HALF])
        xpre_list.append(x_prompt[sq, 0:HALF] if half else zeros)
        hasprev.append(float(half))
    res = run_cores(xp_list, per_core, shared, n_tiles, 3, xpre_list, hasprev)
    y_prompt = np.stack([np.concatenate([res[2 * b]["yp"], res[2 * b + 1]["yp"]]) for b in range(B)])
    cat = lambda k: np.stack([res[2 * b + 1][k] for b in range(B)])
    y_sample = np.concatenate([res[c]["ys"] for c in range(ncore)])[:, None, :]
    nkp = cat("nkp").reshape(1, B, 128, 2, 64)
    nvp = cat("nvp").reshape(1, B, 128, 2, 64)
    ncp = cat("ncp")[None]
    ngp = cat("ngp")[None]
    nks = np.concatenate([res[c]["nks"] for c in range(ncore)]).reshape(1, ncore * NS, 128, 2, 64)
    nvs = np.concatenate([res[c]["nvs"] for c in range(ncore)]).reshape(1, ncore * NS, 128, 2, 64)
    ncs = np.concatenate([res[c]["ncs"] for c in range(ncore)])[None]
    ngs = np.concatenate([res[c]["ngs"] for c in range(ncore)])[None]
    return (y_prompt, y_sample, nkp, nvp, ncp, ngp, nks, nvs, ncs, ngs)
```

```python
import os
import numpy as np
from contextlib import ExitStack
KLIMS = float(os.environ.get('KLIMS', '99'))
KLIMP = float(os.environ.get('KLIMP', '99'))
import concourse.bass as bass
import concourse.mybir as mybir
from concourse.bass_utils import run_bass_kernel_spmd

F32 = mybir.dt.float32
BF16 = mybir.dt.bfloat16
ALU = mybir.AluOpType
AF = mybir.ActivationFunctionType
AX = mybir.AxisListType

D = 1024
FF = 2816
NJ = 22
EPS = 1e-6
TT = 512
NBLK = 4
NS = 16
INW = 2824


class Prog:
    ENG = ("pe", "act", "dve", "pool", "sp")

    def __init__(self, nc):
        self.nc = nc
        self.ops = []
        self.lastw = {}
        self.readers = {}
        self.chan_cnt = {}

    def op(self, eng, fn, r=(), w=(), chan=None, nd=1):
        i = len(self.ops)
        deps = set()
        for k in r:
            if k in self.lastw:
                deps.add(self.lastw[k])
        for k in w:
            if k in self.lastw:
                deps.add(self.lastw[k])
            deps.update(self.readers.get(k, ()))
        for k in r:
            self.readers.setdefault(k, []).append(i)
        for k in w:
            self.lastw[k] = i
            self.readers[k] = []
        o = dict(eng=eng, fn=fn, deps=deps, chan=chan, nd=nd, sig=False, val=0)
        if chan is not None:
            self.chan_cnt[chan] = self.chan_cnt.get(chan, 0) + 16 * nd
            o["val"] = self.chan_cnt[chan]
        self.ops.append(o)
        return i

    def allkeys(self, pred):
        ks = set(self.lastw) | set(self.readers)
        return [k for k in ks if pred(k)]

    def run(self, sems):
        nc = self.nc
        ops = self.ops
        for o in ops:
            for d in o["deps"]:
                ops[d]["sig"] = True
        cnt = {e: 0 for e in self.ENG}
        for o in ops:
            if o["chan"] is None and o["sig"]:
                cnt[o["eng"]] += 1
                o["val"] = cnt[o["eng"]]
        streams = {e: [] for e in self.ENG}
        for i, o in enumerate(ops):
            streams[o["eng"]].append(i)

        def runner(ename):
            def f(e):
                waited = {}
                for i in streams[ename]:
                    o = ops[i]
                    for d in sorted(o["deps"]):
                        do = ops[d]
                        if do["chan"] is not None:
                            key = ("c", do["chan"])
                        else:
                            if do["eng"] == ename and ename in ("pe", "sp"):
                                continue
                            key = ("e", do["eng"])
                        if waited.get(key, 0) >= do["val"]:
                            continue
                        waited[key] = do["val"]
                        e.wait_ge(sems[key], do["val"])
                    res = o["fn"](e)
                    if o["chan"] is not None:
                        lst = res if isinstance(res, (list, tuple)) else [res]
                        assert len(lst) == o["nd"], (len(lst), o["nd"])
                        for ins in lst:
                            ins.then_inc(sems[("c", o["chan"])], 16)
                    elif o["sig"]:
                        res.then_inc(sems[("e", ename)], 1)
                if ename == "sp":
                    for ch, v in self.chan_cnt.items():
                        if waited.get(("c", ch), 0) < v:
                            e.wait_ge(sems[("c", ch)], v)
            return f

        with nc.Block() as block:
            block.tensor(runner("pe"))
            block.scalar(runner("act"))
            block.vector(runner("dve"))
            block.gpsimd(runner("pool"))
            block.sync(runner("sp"))


def host_consts():
    i = np.arange(128)
    c = {}
    c["ident"] = np.eye(128, dtype=np.float32)
    c["triu"] = (i[:, None] <= i[None, :]).astype(np.float32)
    c["strl"] = (i[:, None] > i[None, :]).astype(np.float32)
    slopes = np.exp2(-8.0 * np.arange(1, 9, dtype=np.float32) / 8.0).astype(np.float32)
    jj = i[:, None, None].astype(np.float32)
    ii = i[None, None, :].astype(np.float32)
    sl = slopes[None, :, None]
    bc = np.where(ii >= jj, -sl * (ii - jj), -30000.0).astype(np.float32)
    bp = np.where(jj >= ii, -sl * (ii - jj + 128.0), -30000.0).astype(np.float32)
    c["bcur"] = np.ascontiguousarray(bc.reshape(128, 1024))
    c["bprev"] = np.ascontiguousarray(bp.reshape(128, 1024))
    tm = np.zeros((128, 1), np.float32)
    tm[0, 0] = 1.0
    c["tok0"] = tm
    cm = np.zeros((128, 512), np.float32)
    cm[:, 0::128] = 1.0
    c["col0"] = cm
    qm = np.zeros((128, 4), np.float32)
    qm[:64, 0] = 0.125; qm[64:, 1] = 0.125; qm[:64, 2] = 1.0; qm[64:, 3] = 1.0
    c["qm"] = qm
    return c


def build(n_tiles, stage=3, n_pre=0):
    nc = bass.Bass("TRN2", target_bir_lowering=False)
    NTOK = n_tiles * TT

    def din(name, shape):
        return nc.dram_tensor(name, list(shape), F32, kind="ExternalInput").ap()

    def dout(name, shape):
        return nc.dram_tensor(name, list(shape), F32, kind="ExternalOutput").ap()

    xp = din("xp", [NTOK, D])
    xpre = din("xpre", [n_pre * TT, D]) if n_pre > 0 else None
    hasprev = din("hasprev", [128, 1])
    xs = din("xs", [NS, D])
    ck = din("ck", [NS, 128, 128])
    cv = din("cv", [NS, 128, 128])
    sconv = din("sconv", [NS * 3, 1536])
    sgdn = din("sgdn", [NS, 4, 128, 128])
    g1 = din("g1", [D]); wi1 = din("wi1", [D, 2 * FF]); wo1 = din("wo1", [FF, D])
    g2 = din("g2", [D]); wmi = din("wmi", [D, INW]); sinks = din("sinks", [8])
    convw = din("convw", [4, 1536]); alog = din("alog", [4]); dtb = din("dtb", [4])
    gng = din("gng", [128]); wmo = din("wmo", [D, D])
    g3 = din("g3", [D]); wi2 = din("wi2", [D, 2 * FF]); wo2 = din("wo2", [FF, D])
    gfin = din("gfin", [D])
    c_ident = din("c_ident", [128, 128]); c_triu = din("c_triu", [128, 128]); c_strl = din("c_strl", [128, 128])
    c_bcur = din("c_bcur", [128, 1024]); c_bprev = din("c_bprev", [128, 1024])
    c_tok0 = din("c_tok0", [128, 1]); c_col0 = din("c_col0", [128, 512]); c_qm = din("c_qm", [128, 4])

    yp = dout("yp", [NTOK, D]); ys = dout("ys", [NS, D])
    nkp = dout("nkp", [128, 128]); nvp = dout("nvp", [128, 128])
    ncp = dout("ncp", [3, 1536]); ngp = dout("ngp", [4, 128, 128])
    nks = dout("nks", [NS, 128, 128]); nvs = dout("nvs", [NS, 128, 128])
    ncs = dout("ncs", [NS, 3, 1536]); ngs = dout("ngs", [NS, 4, 128, 128])

    wi_s = [nc.dram_tensor(f"wi_s{i}", [NJ, 128, 8, 256], BF16).ap() for i in range(2)]
    wo_s = [nc.dram_tensor(f"wo_s{i}", [128, NJ, D], BF16).ap() for i in range(2)]
    wm_s = nc.dram_tensor("wm_s", [11, 128, 8, 256], BF16).ap()
    wtm_s = nc.dram_tensor("wtm_s", [128, 8, 264], BF16).ap()
    wmo_s = nc.dram_tensor("wmo_s", [128, 8, D], BF16).ap()

    A = nc.alloc_sbuf_tensor
    xt = [A(f"xt{i}", [128, NBLK, D], F32) for i in range(2)]
    xts = xt[1][0:NS, 0:1, :]
    x1f = xt[1][:, 1:4, :].rearrange("p b d -> p (b d)")
    xn = A("xn", [128, D], BF16)
    xnT = A("xnT", [128, 8, TT], BF16)
    nTs = A("nTs", [128, 8, NS], BF16)
    wi = [A(f"wibuf{i}", [128, 8, 256], BF16) for i in range(3)]
    wmo_t = A("wmo_t", [128, 8, D], BF16)
    wtm = A("wtm", [128, 8, 264], BF16)
    arena = A("arena", [128, 16896], F32)
    hT = arena[:, 0:5632].bitcast(BF16).rearrange("p (j t) -> p j t", j=NJ)
    wo = arena[:, 5632:16896].bitcast(BF16).rearrange("p (j n) -> p j n", j=NJ)
    off = [0]

    def carve(ncols_f32):
        a = off[0]
        off[0] += ncols_f32
        assert off[0] <= 16896
        return arena[:, a:a + ncols_f32]

    pre = carve(12 * 524).rearrange("p (m t) -> p m t", m=12)
    oT = carve(2048).rearrange("p (h t) -> p h t", h=4)
    ncrow = oT[0:4, :, :].rearrange("p h t -> p (h t)")[:, 0:1536]
    ebuf = carve(1024)
    tC = ebuf[:, 0:512]; tD = ebuf[:, 512:1024]
    tA = carve(512); tB = carve(512); tE = carve(512)
    u_t = carve(512)
    rbuf = tB
    gqn = carve(1024).bitcast(BF16).rearrange("p (h t) -> p h t", h=4)
    gkn = carve(1024).bitcast(BF16).rearrange("p (h t) -> p h t", h=4)
    zs = carve(1024).bitcast(BF16).rearrange("p (h t) -> p h t", h=4)
    Pp = carve(512).bitcast(BF16).rearrange("p (h t) -> p h t", h=8)
    qT = A("qT", [128, 4, TT], BF16)
    kTd = [A(f"kTd{i}", [128, 8, 128], BF16) for i in range(2)]
    Vd = [A(f"Vd{i}", [128, 8, 128], BF16) for i in range(2)]
    aoT = A("aoT", [128, 4, TT], BF16)
    goT = A("goT", [128, 4, TT], BF16)
    aoTs = A("aoTs", [128, 4, NS], BF16)
    goTs = A("goTs", [128, 4, NS], BF16)
    Pc = A("Pc", [128, 8, 128], BF16)
    Xb = [A("Xb0", [128, 4, 128], F32), carve(512).rearrange("p (h t) -> p h t", h=4)]
    Yb = [A("Yb0", [128, 4, 128], F32), carve(512).rearrange("p (h t) -> p h t", h=4)]
    Nf = carve(512).rearrange("p (h t) -> p h t", h=4)
    Nb = A("Nb", [128, 4, 128], BF16)
    vb = A("vb", [128, 4, 128], BF16)
    kb = A("kb", [128, 4, 128], BF16)
    kd = A("kd", [128, 4, 128], BF16)
    wT = A("wT", [128, 4, 128], BF16)
    qdT = A("qdT", [128, 4, 128], BF16)
    qkT = A("qkT", [128, 4, 128], BF16)
    vnew = A("vnew", [128, 4, 128], BF16)
    Sbf = A("Sbf", [128, 4, 128], BF16)
    ctmp = A("ctmp", [128, TT], F32)
    junk = ctmp[:, :].bitcast(BF16)
    Sst = A("Sst", [128, 4, 128], F32)
    ccar = A("ccar", [128, 12, 3], F32)
    kvo = A("kvo", [128, 256], F32)
    gbga = A("gbga", [128, NBLK, 8], F32)
    sm = A("sm", [128, 64], F32)
    ss = A("ss", [128, 8], F32)
    sg = [A("sg0", [128, TT], F32)] * 2
    ident_f = A("ident_f", [128, 128], F32)
    ident = A("ident_b", [128, 128], BF16)
    triu = A("triu", [128, 128], F32)
    strl = A("strl", [128, 128], F32)
    ones_f = A("ones_f", [128, 128], F32)
    ones_b = A("ones_b", [128, 128], BF16)
    bcur = A("bcur", [128, 1024], BF16)
    bprev = A("bprev", [128, 1024], BF16)
    tok0 = A("tok0", [128, 1], F32)
    hp_t = A("hp_t", [128, 1], F32)
    qm = A("qm", [128, 4], F32)
    qT2 = A("qT2", [128, 4, TT], BF16)
    col0 = A("col0", [128, 512], F32)
    gT = [A(f"gT{i}", [128, 8], F32) for i in range(3)]
    gfin_b = A("gfin_b", [128, D], F32)
    cw = A("cw", [128, 4, 12], F32)
    esink = A("esink", [128, 8], F32)
    negA = A("negA", [128, 4], F32)
    dtb_b = A("dtb_b", [128, 4], F32)
    gng_t = A("gng_t", [128, 1], F32)
    sct = x1f[0:NS * 3, 0:1536]
    ckt = x1f[:, 1536:1664]
    cvt = x1f[:, 1664:1792]
    ckd = x1f[:, 1792:2048].rearrange("p (a b d) -> p a b d", a=2, b=2)
    ps = nc.alloc_psum_tensor("ps", [128, 8, 512], F32)

    P = Prog(nc)
    MK = lambda *a: ("m_" + a[0],) + tuple(a[1:])
    bank = [0]

    def nb():
        bank[0] = (bank[0] + 1) % 8
        return bank[0]

    def bc(ap, shape, axis):
        return ap.unsqueeze(axis).broadcast_to(shape)

    def cast_ffn(i, w_in, w_out):
        v = w_in.rearrange("(c p) n -> p c n", p=128)
        for j in range(NJ):
            def f(e, j=j):
                return [e.dma_start(out=wi_s[i][j, :, :, g * 128:(g + 1) * 128],
                                    in_=v[:, :, g * FF + j * 128: g * FF + (j + 1) * 128]) for g in range(2)]
            P.op("pool", f, w=[("wi_s", i, j)], chan=f"cast{i}", nd=2)
        P.op("pool", lambda e: e.dma_start(out=wo_s[i], in_=w_out.rearrange("(j p) n -> p j n", p=128)),
             w=[("wo_s", i), ("castgrp", i)], chan=f"cast{i}")

    cast_ffn(0, wi1, wo1)
    wmv = wmi.rearrange("(c p) n -> p c n", p=128)
    groups = [(128 * p, False) for p in range(4)] + [(512, True), (576, True)] + \
             [(768 + 128 * m, False) for m in range(12)] + [(2304 + 128 * h, False) for h in range(4)]
    for s in range(11):
        def f(e, s=s):
            r = []
            for g in range(2):
                c0, dup = groups[2 * s + g]
                if dup:
                    for q in range(2):
                        r.append(e.dma_start(out=wm_s[s, :, :, g * 128 + q * 64: g * 128 + (q + 1) * 64], in_=wmv[:, :, c0:c0 + 64]))
                else:
                    r.append(e.dma_start(out=wm_s[s, :, :, g * 128:(g + 1) * 128], in_=wmv[:, :, c0:c0 + 128]))
            return r
        nd = sum(2 if groups[2 * s + g][1] else 1 for g in range(2))
        P.op("pool", f, w=[("wm_s", s)], chan="castm", nd=nd)

    def f(e):
        return [e.dma_start(out=wtm_s[:, :, 0:128], in_=wmv[:, :, 640:768]),
                e.dma_start(out=wtm_s[:, :, 128:256], in_=wmv[:, :, 512:640]),
                e.dma_start(out=wtm_s[:, :, 256:264], in_=wmv[:, :, 2816:2824])]
    P.op("pool", f, w=[("wtm_s",)], chan="castm", nd=3)
    P.op("pool", lambda e: e.dma_start(out=wmo_s, in_=wmo.rearrange("(c p) n -> p c n", p=128)), w=[("wmo_s",), ("castgrp", "m")], chan="castm")
    cast_ffn(1, wi2, wo2)

    for sq in range(NS):
        def f(e, sq=sq):
            return [e.dma_start(out=nks[sq, 0:127, :], in_=ck[sq, 1:128, :]),
                    e.dma_start(out=nvs[sq, 0:127, :], in_=cv[sq, 1:128, :]),
                    e.dma_start(out=ncs[sq, 0:2, :], in_=sconv[sq * 3 + 1: sq * 3 + 3, :])]
        P.op("pool", f, w=[("o_hist", sq)], chan="ohist", nd=3)

    ldn = [0]

    def ld(dst, src, key, **kw):
        ldn[0] += 1
        P.op("sp", lambda e: e.dma_start(out=dst, in_=src, **kw), w=[key], chan=f"const{ldn[0]}")

    ld(ident_f[:, :], c_ident, ("identf",)); ld(triu[:, :], c_triu, ("triu",)); ld(strl[:, :], c_strl, ("strl",))
    P.op("pool", lambda e: e.dma_start(out=bcur[:, :], in_=c_bcur), w=[("bcur",)], chan="constb1")
    P.op("pool", lambda e: e.dma_start(out=bprev[:, :], in_=c_bprev), w=[("bprev",)], chan="constb2")
    ld(tok0[:, :], c_tok0, ("tok0",)); ld(hp_t[:, :], hasprev, ("hp",)); ld(qm[:, :], c_qm, ("qm",)); ld(col0[:, :], c_col0, ("col0",))
    for i, g in enumerate((g1, g2, g3)):
        ld(gT[i][:, :], g.rearrange("(c p) -> p c", p=128), ("gT", i), allow_slow_non_contiguous=True)
    ld(gfin_b[:, :], gfin.partition_broadcast(128), ("gfin",))
    P.op("sp", lambda e: [e.dma_start(out=cw[:, j, :], in_=convw[j].rearrange("(m p) -> p m", p=128), allow_slow_non_contiguous=True) for j in range(4)], w=[("cw",)], chan="constcw", nd=4)
    ld(esink[:, :], sinks.partition_broadcast(128), ("esink",))
    ld(negA[:, :], alog.partition_broadcast(128), ("negA",))
    ld(dtb_b[:, :], dtb.partition_broadcast(128), ("dtb",))
    ld(gng_t[:, :], gng.rearrange("(p o) -> p o", o=1), ("gng",))
    ld(wtm[:, :, :], wtm_s, ("wtm",))
    P.ops[-1]["deps"].add(P.lastw[("castgrp", "m")])
    ld(wmo_t[:, :, :], wmo_s, ("wmo",))
    P.ops[-1]["deps"].add(P.lastw[("castgrp", "m")])
    P.op("dve", lambda e: e.tensor_copy(ident[:, :], ident_f[:, :]), r=[("identf",)], w=[("ident",)])
    P.op("dve", lambda e: e.memset(ones_f[:, :], 1.0), w=[("ones_f",)])
    P.op("dve", lambda e: e.memset(ones_b[:, :], 1.0), w=[("ones_b",)])
    P.op("act", lambda e: e.activation(out=esink[:, :], in_=esink[:, :], func=AF.Exp), r=[("esink",)], w=[("esink",)])
    P.op("act", lambda e: e.activation(out=negA[:, :], in_=negA[:, :], func=AF.Exp), r=[("negA",)], w=[("negA",)])
    P.op("dve", lambda e: e.tensor_scalar(negA[:, :], negA[:, :], -1.0, 0.0, ALU.mult, ALU.add), r=[("negA",)], w=[("negA",)])

    def rstd_cols(n, np_, scale, keyss):
        P.op("act", lambda e: e.activation(out=ss[0:np_, 0:n], in_=ss[0:np_, 0:n], func=AF.Ln, scale=scale, bias=EPS),
             r=keyss, w=keyss)
        P.op("act", lambda e: e.activation(out=ss[0:np_, 0:n], in_=ss[0:np_, 0:n], func=AF.Exp, scale=-0.5),
             r=keyss, w=keyss)

    def norm_T(xsrc, xkeys, nblk, np_, gi, dstT, dkeys):
        keyss = [("ss",)]
        for b in range(nblk):
            P.op("act", lambda e, b=b: e.activation(out=junk[0:np_, :], in_=xsrc(b), func=AF.Square, accum_out=ss[0:np_, b:b + 1]),
                 r=[xkeys[b]], w=[MK("ctmp")] + keyss)
        rstd_cols(nblk, np_, 1.0 / D, keyss)
        for b in range(nblk):
            P.op("dve", lambda e, b=b: e.tensor_scalar(xn[0:np_, :], xsrc(b), ss[0:np_, b:b + 1], 1.0, ALU.mult, ALU.mult),
                 r=keyss + [xkeys[b]], w=[("xn",)])
            k = nb()
            pst = ps[:, k, :].bitcast(BF16)
            for c in range(8):
                P.op("pe", lambda e, c=c, pst=pst: e.transpose(pst[:, c * 128:c * 128 + np_], xn[0:np_, c * 128:(c + 1) * 128], ident[0:np_, 0:np_]),
                     r=[("xn",), ("ident",)], w=[("ps", k)])
            P.op("dve", lambda e, b=b, pst=pst: e.tensor_tensor(
                dstT[:, :, b * np_:(b + 1) * np_], pst.rearrange("p (c t) -> p c t", c=8)[:, :, 0:np_],
                bc(gT[gi][:, :], [128, 8, np_], 2), ALU.mult),
                r=[("ps", k), ("gT", gi)], w=[dkeys[b]])

    pref = {"on": False}

    def ffn_prefetch(fi):
        for j in range(3):
            P.op("sp", lambda e, j=j: e.dma_start(out=wi[j][:, :, :], in_=wi_s[fi][j]),
                 r=[("castgrp", fi)], w=[("wi", j)], chan=f"wi{j}")
        pref["on"] = True

    def ffn(fi, srcT, skeys, ntok, xdst, xkeys, nblk, np_):
        hkeys = [("hT", j) for j in range(NJ)]
        skip_first = pref["on"]
        pref["on"] = False
        P.op("sp", lambda e: e.dma_start(out=wo, in_=wo_s[fi]), r=[("wo_s", fi), ("castgrp", fi)], w=[("wo",)], chan="wo")
        for j in range(NJ):
            s = j % 3
            if not (skip_first and j < 3):
                P.op("sp", lambda e, j=j, s=s: e.dma_start(out=wi[s][:, :, :], in_=wi_s[fi][j]),
                     r=[("wi_s", fi, j), ("castgrp", fi)], w=[("wi", s)], chan=f"wi{s}")
            kg, ku = nb(), nb()
            for c in range(8):
                P.op("pe", lambda e, c=c, s=s, kg=kg: e.matmul(ps[:, kg, 0:ntok], wi[s][:, c, 0:128], srcT[:, c, 0:ntok], start=(c == 0), stop=(c == 7)),
                     r=[("wi", s)] + skeys, w=[("ps", kg)])
            for c in range(8):
                P.op("pe", lambda e, c=c, s=s, ku=ku: e.matmul(ps[:, ku, 0:ntok], wi[s][:, c, 128:256], srcT[:, c, 0:ntok], start=(c == 0), stop=(c == 7)),
                     r=[("wi", s)] + skeys, w=[("ps", ku)])
            q = 0
            P.op("act", lambda e, kg=kg, q=q: e.activation(out=sg[q][:, 0:ntok], in_=ps[:, kg, 0:ntok], func=AF.Silu),
                 r=[("ps", kg)], w=[("sg", q)])
            P.op("dve", lambda e, ku=ku, q=q, j=j: e.tensor_tensor(hT[:, j, 0:ntok], sg[q][:, 0:ntok], ps[:, ku, 0:ntok], ALU.mult),
                 r=[("ps", ku), ("sg", q)], w=[hkeys[j]])
        for b in range(nblk):
            k0, k1 = nb(), nb()
            for j in range(NJ):
                for dh, kk in ((0, k0), (1, k1)):
                    P.op("pe", lambda e, j=j, dh=dh, kk=kk, b=b: e.matmul(ps[0:np_, kk, :], hT[:, j, b * np_:(b + 1) * np_], wo[:, j, dh * 512:(dh + 1) * 512], start=(j == 0), stop=(j == NJ - 1)),
                         r=[hkeys[j], ("wo",)], w=[("ps", kk)])
            for dh, kk in ((0, k0), (1, k1)):
                P.op("dve", lambda e, dh=dh, kk=kk, b=b: e.scalar_tensor_tensor(xdst(b, dh), ps[0:np_, kk, :], 0.5, xdst(b, dh), ALU.mult, ALU.add),
                     r=[("ps", kk)], w=[xkeys[b]])

    def fence():
        ks = P.allkeys(lambda k: k[0] in ("hT", "wo") or str(k[0]).startswith("m_"))
        P.op("pool", lambda e: e.memset(sm[:, 63:64], 0.0), w=ks + [("fence",)])


    def mix(srcT, skeys, sample, tix, first_tile, last_tile, xres, xkeys, prepass=False, pre_last=False, mask_prev=False):
        bs = 131 if sample else 128

        def kprev(kv, b):
            return kTd[kv][:, 2 * b, :] if sample else kTd[kv][:, b, :]

        def kcur(kv, b):
            return kTd[kv][:, 2 * b + 1, :] if sample else kTd[kv][:, b + 1, :]

        def kix(b):
            return (2 * b, 2 * b + 1) if sample else (b, b + 1)

        KLIM = KLIMS if sample else KLIMP
        if KLIM < 0.5:
            return
        PQ = (lambda *a, **k: None) if prepass else P.op
        PD = (lambda *a, **k: None) if sample else P.op
        if sample:
            for h in range(4):
                P.op("pool", lambda e, h=h: e.tensor_copy(Nb[:, h, :], ident[:, :]), r=[("ident",)], w=[MK("N")])
        slots = range(11) if not prepass else (range(2, 9) if pre_last else range(5, 9))
        for s in slots:
            sl = s % 3
            P.op("sp", lambda e, s=s, sl=sl: e.dma_start(out=wi[sl][:, :, :], in_=wm_s[s]),
                 r=[("wm_s", s), ("castgrp", "m")], w=[("wi", sl)], chan=f"wi{sl}")
            if 3 <= s <= 8 and (sample or last_tile):
                kq = nb()
                lt = srcT[:, :, :].rearrange("p c (b t) -> p c b t", b=4)[:, :, :, 0] if sample else srcT[:, :, 509:512]
                mrows = 4 if sample else 3
                for c in range(8):
                    P.op("pe", lambda e, c=c, sl=sl, kq=kq, lt=lt, mrows=mrows: e.matmul(ps[0:mrows, kq, 0:256], lt[:, c, :], wi[sl][:, c, :], start=(c == 0), stop=(c == 7)),
                         r=[("wi", sl)] + skeys, w=[("ps", kq)])
                P.op("dve", lambda e, kq=kq, s=s, mrows=mrows: e.tensor_copy(ncrow[0:mrows, (s - 3) * 256:(s - 2) * 256], ps[0:mrows, kq, 0:256]), r=[("ps", kq)], w=[MK("ncrow")])
                if s == 8:
                    if sample:
                        P.op("sp", lambda e: e.dma_start(out=ncs[tix * 4:(tix + 1) * 4, 2, :], in_=ncrow[0:4, :]), r=[MK("ncrow")], w=[("o_ncs", tix)], chan="onc")
                    else:
                        P.op("sp", lambda e: e.dma_start(out=ncp, in_=ncrow[0:3, :]), r=[MK("ncrow")], w=[("o_ncp",)], chan="onc")
            for g in range(2):
                gi = 2 * s + g
                k = nb()
                if sample and 6 <= gi < 18:
                    m = gi - 6
                    rhs4 = srcT[:, :, :].rearrange("p c (b t) -> p c b t", b=4)[:, :, :, 0]
                    for c in range(8):
                        P.op("pe", lambda e, c=c, sl=sl, g=g, k=k, rhs4=rhs4: e.matmul(ps[:, k, 0:4], wi[sl][:, c, g * 128:(g + 1) * 128], rhs4[:, c, :], start=(c == 0), stop=(c == 7)),
                             r=[("wi", sl)] + skeys, w=[("ps", k)])
                    dst4 = pre[:, m, 0:4 * bs].rearrange("p (b t) -> p b t", b=4)[:, :, 3]
                    P.op("act", lambda e, k=k, dst4=dst4: e.activation(out=dst4, in_=ps[:, k, 0:4], func=AF.Identity),
                         r=[("ps", k)], w=[MK("pre", m)])
                    continue
                for c in range(8):
                    P.op("pe", lambda e, c=c, sl=sl, g=g, k=k: e.matmul(ps[:, k, :], wi[sl][:, c, g * 128:(g + 1) * 128], srcT[:, c, :], start=(c == 0), stop=(c == 7)),
                         r=[("wi", sl)] + skeys, w=[("ps", k)])
                if gi < 4:
                    P.op("act", lambda e, k=k, gi=gi: e.activation(out=qT[:, gi, :], in_=ps[:, k, :], func=AF.Identity, scale=qm[:, 0:1]),
                         r=[("ps", k), ("qm",)], w=[MK("qT", gi)])
                    P.op("dve", lambda e, k=k, gi=gi: e.tensor_scalar(qT2[:, gi, :], ps[:, k, :], qm[:, 1:2], 0.0, ALU.mult, ALU.add),
                         r=[("ps", k), ("qm",)], w=[MK("qT", gi)])
                elif gi < 6:
                    kv = gi - 4
                    if sample:
                        dst = kTd[kv][:, :, :].rearrange("p (b two) t -> p b two t", two=2)[:, :, 1, :]
                    else:
                        dst = kTd[kv][:, 1:5, :]
                    P.op("act", lambda e, k=k, dst=dst: e.activation(out=dst, in_=ps[:, k, :].rearrange("p (b t) -> p b t", b=4), func=AF.Identity),
                         r=[("ps", k)], w=[MK("kTd", kv, i) for i in ((1, 3, 5, 7) if sample else (1, 2, 3, 4))])
                elif gi < 18:
                    m = gi - 6
                    dst = pre[:, m, 0:4 * bs].rearrange("p (b t) -> p b t", b=4)[:, :, 3:131] if sample else None
                    if sample:
                        P.op("act", lambda e, k=k, dst=dst: e.activation(out=dst, in_=ps[:, k, :].rearrange("p (b t) -> p b t", b=4), func=AF.Identity),
                             r=[("ps", k)], w=[MK("pre", m)])
                    else:
                        P.op("act", lambda e, k=k, m=m: e.activation(out=pre[:, m, 3:515], in_=ps[:, k, :], func=AF.Identity),
                             r=[("ps", k)], w=[MK("pre", m)])
                else:
                    h = gi - 18
                    P.op("act", lambda e, k=k, h=h: e.activation(out=zs[:, h, :], in_=ps[:, k, :], func=AF.Silu),
                         r=[("ps", k)], w=[MK("zs", h)])
        if KLIM < 1:
            return
        for b in range(4):
            k = nb()
            for c in range(8):
                P.op("pe", lambda e, c=c, k=k, b=b: e.matmul(ps[:, k, 0:264], srcT[:, c, b * 128:(b + 1) * 128], wtm[:, c, :], start=(c == 0), stop=(c == 7)),
                     r=[("wtm",)] + skeys, w=[("ps", k)])
            pi, ci = kix(b)
            for kv in range(2):
                P.op("act", lambda e, k=k, kv=kv, ci=ci: e.activation(out=Vd[kv][:, ci, 0:64], in_=ps[:, k, kv * 64:(kv + 1) * 64], func=AF.Identity),
                     r=[("ps", k)], w=[MK("Vd", kv, ci)])
                P.op("dve", lambda e, k=k, kv=kv, ci=ci: e.tensor_copy(Vd[kv][:, ci, 64:128], ps[:, k, kv * 64:(kv + 1) * 64]),
                     r=[("ps", k)], w=[MK("Vd", kv, ci)])
            P.op("dve", lambda e, k=k, b=b: e.tensor_copy(gbga[:, b, :], ps[:, k, 256:264]), r=[("ps", k)], w=[MK("gbga", b)])
            want_kv = sample or (last_tile and b == 3)
            if want_kv:
                P.op("dve", lambda e, k=k: e.tensor_copy(kvo[:, :], ps[:, k, 0:256]), r=[("ps", k)], w=[MK("kvo")])
                if sample:
                    sq = tix * 4 + b
                    def f(e, sq=sq):
                        return [e.dma_start(out=nvs[sq, 127:128, :], in_=kvo[0:1, 0:128]),
                                e.dma_start(out=nks[sq, 127:128, :], in_=kvo[0:1, 128:256])]
                    P.op("sp", f, r=[MK("kvo")], w=[("o_kvs", sq)], chan="okv", nd=2)
                else:
                    def f(e):
                        return [e.dma_start(out=nvp, in_=kvo[:, 0:128]), e.dma_start(out=nkp, in_=kvo[:, 128:256])]
                    P.op("sp", f, r=[MK("kvo")], w=[("o_kvp",)], chan="okv", nd=2)
        if KLIM < 2:
            return
        if sample:
            for b in range(4):
                sq = tix * 4 + b
                P.op("sp", lambda e, sq=sq: [e.dma_start(out=ckt[:, :], in_=ck[sq]), e.dma_start(out=cvt[:, :], in_=cv[sq])],
                     w=[MK("ckt"), MK("cvt")], chan="ckld", nd=2)
                for kv in range(2):
                    k = nb()
                    for q in range(2):
                        P.op("dve", lambda e, kv=kv, q=q: e.tensor_copy(ckd[:, kv, q, :], ckt[:, kv * 64:(kv + 1) * 64]), r=[MK("ckt")], w=[MK("ckd", kv)])
                    P.op("pe", lambda e, k=k, kv=kv: e.transpose(ps[:, k, 0:128], ckd[:, kv, :, :].rearrange("p a d -> p (a d)"), ident_f[:, :]),
                         r=[MK("ckd", kv), ("identf",)], w=[("ps", k)])
                    P.op("act", lambda e, k=k, kv=kv, b=b: e.activation(out=kTd[kv][:, 2 * b, :], in_=ps[:, k, 0:128], func=AF.Identity),
                         r=[("ps", k)], w=[MK("kTd", kv, 2 * b)])
                    for q in range(2):
                        P.op("dve", lambda e, kv=kv, b=b, q=q: e.tensor_copy(Vd[kv][:, 2 * b, q * 64:(q + 1) * 64], cvt[:, kv * 64:(kv + 1) * 64]),
                             r=[MK("cvt")], w=[MK("Vd", kv, 2 * b)])
            if tix == 0:
                P.op("sp", lambda e: e.dma_start(out=sct[:, :], in_=sconv), w=[("sct",)], chan="sctld")
            for m in range(12):
                k = nb()
                P.op("pe", lambda e, k=k, m=m: e.transpose(ps[:, k, 0:NS * 3], sct[:, m * 128:(m + 1) * 128], ident_f[0:NS * 3, 0:NS * 3]),
                     r=[("sct",), ("identf",)], w=[("ps", k)])
                P.op("dve", lambda e, k=k, m=m: e.tensor_copy(
                    pre[:, m, 0:4 * bs].rearrange("p (b t) -> p b t", b=4)[:, :, 0:3],
                    ps[:, k, tix * 12: tix * 12 + 12].rearrange("p (b r) -> p b r", b=4)),
                    r=[("ps", k)], w=[MK("pre", m)])
        elif not first_tile:
            P.op("pool", lambda e: e.tensor_copy(pre[:, :, 0:3], ccar[:, :, :]), r=[("ccar",)], w=[MK("pre", m) for m in range(12)])
        if (not sample) and first_tile:
            for kv in range(2):
                P.op("pool", lambda e, kv=kv: e.memset(kTd[kv][:, 0, :], 0.0), w=[MK("kTd", kv, 0)])
                P.op("pool", lambda e, kv=kv: e.memset(Vd[kv][:, 0, :], 0.0), w=[MK("Vd", kv, 0)])
            P.op("pool", lambda e: e.memset(pre[:, :, 0:3], 0.0), w=[MK("pre", m) for m in range(12)])
            P.op("pool", lambda e: e.memset(ccar[:, :, :], 0.0), w=[("ccar",)])
            P.op("pool", lambda e: e.memset(Sst[:, :, :], 0.0), w=[MK("S")])
            P.op("pool", lambda e: e.memset(Sbf[:, :, :], 0.0), w=[MK("Sbf")])

        if KLIM < 3:
            return
        for b in (range(4) if not prepass else ()):
            pi, ci = kix(b)
            has_prev = sample or not (first_tile and b == 0)
            kc = [nb(), nb()]
            for h in range(8):
                kv, p, hh = h // 4, h // 2, h % 2
                P.op("pe", lambda e, h=h, kv=kv, p=p, hh=hh, ci=ci, b=b, kc=kc: e.matmul(
                    ps[:, kc[h // 4], (h % 4) * 128:(h % 4 + 1) * 128], kTd[kv][:, ci, :],
                    (qT2 if hh else qT)[:, p, b * 128:(b + 1) * 128], start=True, stop=True),
                    r=[MK("kTd", kv, ci), MK("qT", p)], w=[("ps", kc[h // 4])])
            for half in range(2):
                P.op("dve", lambda e, half=half, kc=kc: e.tensor_tensor(ebuf[:, half * 512:(half + 1) * 512], ps[:, kc[half], :], bcur[:, half * 512:(half + 1) * 512], ALU.add),
                     r=[("ps", kc[half]), ("bcur",)], w=[MK("tC" if half == 0 else "tD")])
            P.op("act", lambda e: e.activation(out=Pc[:, :, :].rearrange("p h t -> p (h t)"), in_=ebuf, func=AF.Exp),
                 r=[MK("tC"), MK("tD")], w=[MK("Pc")])
            if has_prev:
                kp = [nb(), nb()]
                for h in range(8):
                    kv, p, hh = h // 4, h // 2, h % 2
                    P.op("pe", lambda e, h=h, kv=kv, p=p, hh=hh, pi=pi, b=b, kp=kp: e.matmul(
                        ps[:, kp[h // 4], (h % 4) * 128:(h % 4 + 1) * 128], kTd[kv][:, pi, :],
                        (qT2 if hh else qT)[:, p, b * 128:(b + 1) * 128], start=True, stop=True),
                        r=[MK("kTd", kv, pi), MK("qT", p)], w=[("ps", kp[h // 4])])
                for half in range(2):
                    P.op("dve", lambda e, half=half, kp=kp: e.tensor_tensor(ebuf[:, half * 512:(half + 1) * 512], ps[:, kp[half], :], bprev[:, half * 512:(half + 1) * 512], ALU.add),
                         r=[("ps", kp[half]), ("bprev",)], w=[MK("tC" if half == 0 else "tD")])
                P.op("act", lambda e: e.activation(out=Pp[:, :, :].rearrange("p h t -> p (h t)"), in_=ebuf, func=AF.Exp),
                     r=[MK("tC"), MK("tD")], w=[MK("Pp")])
                if mask_prev and b == 0:
                    P.op("dve", lambda e: e.tensor_scalar(Pp[:, :, :].rearrange("p h t -> p (h t)"), Pp[:, :, :].rearrange("p h t -> p (h t)"), hp_t[:, 0:1], 0.0, ALU.mult, ALU.add),
                         r=[MK("Pp"), ("hp",)], w=[MK("Pp")])
            for kv in range(2):
                kn, kdn = nb(), nb()
                pcs = Pc[:, kv * 4:(kv + 1) * 4, :].rearrange("p h t -> p (h t)")
                pps = Pp[:, kv * 4:(kv + 1) * 4, :].rearrange("p h t -> p (h t)")
                P.op("pe", lambda e, kn=kn, kv=kv, ci=ci, pcs=pcs, hp=has_prev: e.matmul(ps[:, kn, :], Vd[kv][:, ci, :], pcs, start=True, stop=not hp),
                     r=[MK("Vd", kv, ci), MK("Pc")], w=[("ps", kn)])
                if has_prev:
                    P.op("pe", lambda e, kn=kn, kv=kv, pi=pi, pps=pps: e.matmul(ps[:, kn, :], Vd[kv][:, pi, :], pps, start=False, stop=True),
                         r=[MK("Vd", kv, pi), MK("Pp")], w=[("ps", kn)])
                P.op("pe", lambda e, kdn=kdn, pcs=pcs, hp=has_prev: e.matmul(ps[:, kdn, :], ones_b[:, :], pcs, start=True, stop=not hp),
                     r=[("ones_b",), MK("Pc")], w=[("ps", kdn)])
                if has_prev:
                    P.op("pe", lambda e, kdn=kdn, pps=pps: e.matmul(ps[:, kdn, :], ones_b[:, :], pps, start=False, stop=True),
                         r=[("ones_b",), MK("Pp")], w=[("ps", kdn)])
                P.op("dve", lambda e, kdn=kdn, kv=kv: e.tensor_tensor(rbuf.rearrange("p (h t) -> p h t", h=4), ps[:, kdn, :].rearrange("p (h t) -> p h t", h=4),
                                                                      bc(esink[:, kv * 4:(kv + 1) * 4], [128, 4, 128], 2), ALU.add),
                     r=[("ps", kdn), ("esink",)], w=[MK("tB")])
                P.op("dve", lambda e: e.reciprocal(rbuf, rbuf), r=[MK("tB")], w=[MK("tB")])
                nv = ps[:, kn, :].rearrange("p (m q t) -> p m q t", m=2, q=2)
                rv = rbuf.rearrange("p (m q t) -> p m q t", m=2, q=2)
                te = tA.rearrange("p (m t) -> p m t", m=4)[:, 0:2, :]
                to = tA.rearrange("p (m t) -> p m t", m=4)[:, 2:4, :]
                P.op("dve", lambda e, nv=nv, rv=rv, te=te: e.tensor_tensor(te, nv[:, :, 0, :], rv[:, :, 0, :], ALU.mult), r=[("ps", kn), MK("tB")], w=[MK("tA")])
                P.op("dve", lambda e, nv=nv, rv=rv, to=to: e.tensor_tensor(to, nv[:, :, 1, :], rv[:, :, 1, :], ALU.mult), r=[("ps", kn), MK("tB")], w=[MK("tA")])
                P.op("dve", lambda e, te=te: e.tensor_scalar(te, te, qm[:, 2:3], 0.0, ALU.mult, ALU.add), r=[MK("tA"), ("qm",)], w=[MK("tA")])
                P.op("dve", lambda e, kv=kv, b=b, te=te, to=to: e.scalar_tensor_tensor(aoT[:, kv * 2:(kv + 1) * 2, b * 128:(b + 1) * 128], to, qm[:, 3:4], te, ALU.mult, ALU.add),
                     r=[MK("tA"), ("qm",)], w=[MK("aoT", b)])
        if (not sample) and (not prepass or pre_last):
            for kv in range(2):
                P.op("pool", lambda e, kv=kv: e.tensor_copy(kTd[kv][:, 0, :], kTd[kv][:, 4, :]), r=[MK("kTd", kv, 4)], w=[MK("kTd", kv, 0)])
                P.op("pool", lambda e, kv=kv: e.tensor_copy(Vd[kv][:, 0, :], Vd[kv][:, 4, :]), r=[MK("Vd", kv, 4)], w=[MK("Vd", kv, 0)])

        if KLIM < 4:
            return
        if KLIM < 5:
            return
        def pv(m, j):
            if sample:
                return pre[:, m, 0:4 * bs].rearrange("p (b t) -> p b t", b=4)[:, :, j:j + 128]
            return pre[:, m, j:j + 512].rearrange("p (b t) -> p b t", b=4)
        cbufs = [(ctmp[:, :], MK("ctmp")), (tA, MK("tA")), (tE, MK("tE"))]
        for ci_, m in enumerate(range(12) if (not prepass or pre_last) else range(4, 12)):
            cb, ck_ = cbufs[ci_ % 3]
            ct3 = cb.rearrange("p (b t) -> p b t", b=4)
            if sample:
                ct1 = ct3[:, :, 0:1]
                P.op("pool", lambda e, m=m, ct1=ct1: e.tensor_scalar(ct1, pv(m, 0)[:, :, 0:1], cw[:, 0, m:m + 1], 0.0, ALU.mult, ALU.add), r=[MK("pre", m), ("cw",)], w=[ck_])
                for j in (1, 2, 3):
                    P.op("dve", lambda e, m=m, j=j, ct1=ct1: e.scalar_tensor_tensor(ct1, pv(m, j)[:, :, 0:1], cw[:, j, m:m + 1], ct1, ALU.mult, ALU.add),
                         r=[MK("pre", m), ("cw",), ck_], w=[ck_])
                P.op("pool", lambda e, m=m: e.memset(pre[:, m, 3:515], 0.0), w=[MK("pre", m)])
                P.op("act", lambda e, m=m, ct1=ct1: e.activation(out=pre[:, m, 3:515].rearrange("p (b t) -> p b t", b=4)[:, :, 0:1], in_=ct1, func=AF.Silu), r=[ck_], w=[MK("pre", m)])
                continue
            P.op("pool", lambda e, m=m, ct3=ct3: e.tensor_scalar(ct3, pv(m, 0), cw[:, 0, m:m + 1], 0.0, ALU.mult, ALU.add), r=[MK("pre", m), ("cw",)], w=[ck_])
            for j in (1, 2, 3):
                P.op("dve", lambda e, m=m, j=j, ct3=ct3: e.scalar_tensor_tensor(ct3, pv(m, j), cw[:, j, m:m + 1], ct3, ALU.mult, ALU.add),
                     r=[MK("pre", m), ("cw",), ck_], w=[ck_])
            P.op("pool", lambda e, m=m: e.tensor_copy(ccar[:, m, :], pre[:, m, 512:515]), r=[MK("pre", m)], w=[("ccar",)])
            P.op("act", lambda e, m=m, cb=cb: e.activation(out=pre[:, m, 3:515], in_=cb, func=AF.Silu), r=[ck_], w=[MK("pre", m)])
        cact = lambda m: pre[:, m, 3:515]
        if KLIM < 6:
            return
        for li, m in enumerate(range(8) if not prepass else range(4, 8)):
            sqb, sqk = ((ctmp[:, :], MK("ctmp")), (tE, MK("tE")))[li % 2]
            rsb, rsk = ((tA, MK("tA")), (tB, MK("tB")))[li % 2]
            if sample:
                dst = gqn[:, m, :] if m < 4 else gkn[:, m - 4, :]
                dk_ = MK("gqn" if m < 4 else "gkn", m % 4)
                sc = 128.0 ** -0.5 if m < 4 else 1.0
                cm = cact(m).rearrange("p (b t) -> p b t", b=4)[:, :, 0:1]
                P.op("pool", lambda e, cm=cm, sqb=sqb: e.tensor_tensor(sqb[:, 0:4].unsqueeze(2), cm, cm, ALU.mult), r=[MK("pre", m)], w=[sqk])
                k = nb()
                P.op("pe", lambda e, k=k, sqb=sqb: e.matmul(ps[:, k, 0:4], ones_f[:, :], sqb[:, 0:4], start=True, stop=True), r=[("ones_f",), sqk], w=[("ps", k)])
                P.op("act", lambda e, k=k, rsb=rsb: e.activation(out=rsb[:, 0:4], in_=ps[:, k, 0:4], func=AF.Ln, bias=EPS), r=[("ps", k)], w=[rsk])
                P.op("act", lambda e, rsb=rsb: e.activation(out=rsb[:, 0:4], in_=rsb[:, 0:4], func=AF.Exp, scale=-0.5), r=[rsk], w=[rsk])
                P.op("pool", lambda e, dst=dst: e.memset(dst, 0.0), w=[dk_])
                P.op("dve", lambda e, cm=cm, dst=dst, sc=sc, rsb=rsb: e.scalar_tensor_tensor(dst.rearrange("p (b t) -> p b t", b=4)[:, :, 0:1], cm, sc, rsb[:, 0:4].unsqueeze(2), ALU.mult, ALU.mult),
                     r=[MK("pre", m), rsk], w=[dk_])
                continue
            P.op("pool", lambda e, m=m, sqb=sqb: e.tensor_tensor(sqb, cact(m), cact(m), ALU.mult), r=[MK("pre", m)], w=[sqk])
            k = nb()
            P.op("pe", lambda e, k=k, sqb=sqb: e.matmul(ps[:, k, :], ones_f[:, :], sqb, start=True, stop=True), r=[("ones_f",), sqk], w=[("ps", k)])
            P.op("act", lambda e, k=k, rsb=rsb: e.activation(out=rsb, in_=ps[:, k, :], func=AF.Ln, bias=EPS), r=[("ps", k)], w=[rsk])
            P.op("act", lambda e, rsb=rsb: e.activation(out=rsb, in_=rsb, func=AF.Exp, scale=-0.5), r=[rsk], w=[rsk])
            dst = gqn[:, m, :] if m < 4 else gkn[:, m - 4, :]
            sc = 128.0 ** -0.5 if m < 4 else 1.0
            P.op("dve", lambda e, m=m, dst=dst, sc=sc, rsb=rsb: e.scalar_tensor_tensor(dst, cact(m), sc, rsb, ALU.mult, ALU.mult),
                 r=[MK("pre", m), rsk], w=[MK("gqn" if m < 4 else "gkn", m % 4)])
        if KLIM < 7:
            return
        gb_v = gbga[:, :, 0:4]
        ga_v = gbga[:, :, 4:8]
        be = sm[:, 0:16].rearrange("p (b h) -> p b h", b=4)
        gg = sm[:, 16:32].rearrange("p (b h) -> p b h", b=4)
        t1 = sm[:, 32:48].rearrange("p (b h) -> p b h", b=4)
        gkeys = [MK("gbga", b) for b in range(4)]
        P.op("act", lambda e: e.activation(out=be, in_=gb_v, func=AF.Exp, scale=-1.0), r=gkeys, w=[MK("be")])
        P.op("dve", lambda e: e.tensor_scalar(be, be, 1.0, 1.0, ALU.add, ALU.mult), r=[MK("be")], w=[MK("be")])
        P.op("dve", lambda e: e.reciprocal(be, be), r=[MK("be")], w=[MK("be")])
        P.op("dve", lambda e: e.tensor_tensor(gg, ga_v, bc(dtb_b[:, :], [128, 4, 4], 1), ALU.add), r=gkeys + [("dtb",)], w=[MK("gg")])
        P.op("dve", lambda e: e.tensor_scalar(t1, gg, -1.0, 0.0, ALU.mult, ALU.add), r=[MK("gg")], w=[MK("t1")])
        P.op("dve", lambda e: e.tensor_tensor(t1, t1, gg, ALU.max), r=[MK("gg"), MK("t1")], w=[MK("t1")])
        P.op("act", lambda e: e.activation(out=t1, in_=t1, func=AF.Exp, scale=-1.0), r=[MK("t1")], w=[MK("t1")])
        P.op("act", lambda e: e.activation(out=t1, in_=t1, func=AF.Ln, bias=1.0), r=[MK("t1")], w=[MK("t1")])
        P.op("dve", lambda e: e.scalar_tensor_tensor(gg, gg, 0.0, t1, ALU.max, ALU.add), r=[MK("gg"), MK("t1")], w=[MK("gg")])
        P.op("dve", lambda e: e.tensor_tensor(gg, gg, bc(negA[:, :], [128, 4, 4], 1), ALU.mult), r=[MK("gg"), ("negA",)], w=[MK("gg")])
        if sample:
            P.op("dve", lambda e: e.tensor_scalar(gg, gg, tok0[:, 0:1], 0.0, ALU.mult, ALU.add), r=[MK("gg"), ("tok0",)], w=[MK("gg")])

        if KLIM < 8:
            return
        v4 = lambda ap: ap.rearrange("p (h t) -> p h t", h=4)
        for b in range(4):
            cols = slice(b * 128, (b + 1) * 128)
            gcc = sm[:, 48:52]
            gl = sm[:, 52:56]
            s1 = sm[:, 56:60]
            s2 = sm[:, 60:63]
            k = nb()
            P.op("pe", lambda e, k=k, b=b: e.matmul(ps[:, k, 0:4], triu[:, :], gg[:, b, :], start=True, stop=True), r=[("triu",), MK("gg")], w=[("ps", k)])
            P.op("dve", lambda e, k=k: e.tensor_copy(gcc, ps[:, k, 0:4]), r=[("ps", k)], w=[MK("gcc")])
            P.op("dve", lambda e, b=b: e.tensor_tensor(v4(tA), bc(triu[:, :], [128, 4, 128], 1), bc(gg[:, b, :], [128, 4, 128], 2), ALU.mult),
                 r=[("triu",), MK("gg")], w=[MK("tA")])
            kr = nb()
            P.op("pe", lambda e, kr=kr: e.matmul(ps[:, kr, :], ones_f[:, :], tA, start=True, stop=True), r=[("ones_f",), MK("tA")], w=[("ps", kr)])
            P.op("dve", lambda e, kr=kr: e.tensor_tensor(v4(tB), v4(ps[:, kr, :]), bc(gcc, [128, 4, 128], 2), ALU.subtract), r=[("ps", kr), MK("gcc")], w=[MK("tB")])
            PQ("dve", lambda e: e.tensor_scalar(tC, tB, 0.0, 0.0, ALU.min, ALU.add), r=[MK("tB")], w=[MK("tC")])
            PQ("act", lambda e: e.activation(out=tC, in_=tC, func=AF.Exp), r=[MK("tC")], w=[MK("tC")])
            PQ("dve", lambda e: e.tensor_tensor(v4(tC), v4(tC), bc(triu[:, :], [128, 4, 128], 1), ALU.mult), r=[MK("tC"), ("triu",)], w=[MK("tC")])
            PD("dve", lambda e: e.tensor_scalar(tD, tB, 0.0, 0.0, ALU.max, ALU.add), r=[MK("tB")], w=[MK("tD")])
            PD("act", lambda e: e.activation(out=tD, in_=tD, func=AF.Exp, scale=-1.0), r=[MK("tD")], w=[MK("tD")])
            PD("dve", lambda e: e.tensor_tensor(v4(tD), v4(tD), bc(strl[:, :], [128, 4, 128], 1), ALU.mult), r=[MK("tD"), ("strl",)], w=[MK("tD")])
            PQ("act", lambda e, kr=kr: e.activation(out=tE, in_=ps[:, kr, :], func=AF.Exp), r=[("ps", kr)], w=[MK("tE")])
            P.op("act", lambda e, kr=kr: e.activation(out=gl, in_=v4(ps[:, kr, :])[:, :, 127], func=AF.Exp), r=[("ps", kr)], w=[MK("gl")])
            P.op("dve", lambda e, kr=kr: e.tensor_tensor(s1, v4(ps[:, kr, :])[:, :, 127], gcc, ALU.subtract), r=[("ps", kr), MK("gcc")], w=[MK("s1")])
            P.op("act", lambda e: e.activation(out=s1, in_=s1, func=AF.Exp), r=[MK("s1")], w=[MK("s1")])
            P.op("act", lambda e: e.activation(out=gcc, in_=gcc, func=AF.Exp), r=[MK("gcc")], w=[MK("gcc")])
            P.op("dve", lambda e, b=b: e.tensor_tensor(gcc, gcc, be[:, b, :], ALU.mult), r=[MK("gcc"), MK("be")], w=[MK("gcc")])
            kt = nb()
            ktb = ps[:, kt, :].bitcast(BF16)
            for h in range(4):
                P.op("pe", lambda e, h=h, ktb=ktb, cols=cols: e.transpose(ktb[:, h * 128:(h + 1) * 128], gkn[:, h, cols], ident[:, :]),
                     r=[MK("gkn", h), ("ident",)], w=[("ps", kt)])
            P.op("dve", lambda e, ktb=ktb: e.tensor_tensor(kb[:, :, :], v4(ktb[:, 0:512]), bc(gcc, [128, 4, 128], 2), ALU.mult), r=[("ps", kt), MK("gcc")], w=[MK("kb")])
            P.op("dve", lambda e, ktb=ktb: e.tensor_tensor(kd[:, :, :], v4(ktb[:, 0:512]), bc(s1, [128, 4, 128], 2), ALU.mult), r=[("ps", kt), MK("s1")], w=[MK("kd")])
            kvv = nb()
            for h in range(4):
                P.op("pe", lambda e, h=h, kvv=kvv, cols=cols: e.transpose(ps[:, kvv, h * 128:(h + 1) * 128], cact(8 + h)[:, cols], ident_f[:, :]),
                     r=[MK("pre", 8 + h), ("identf",)], w=[("ps", kvv)])
            P.op("dve", lambda e, kvv=kvv, b=b: e.tensor_tensor(vb[:, :, :], v4(ps[:, kvv, :]), bc(be[:, b, :], [128, 4, 128], 2), ALU.mult), r=[("ps", kvv), MK("be")], w=[MK("vb")])
            kkk, kqk = nb(), nb()
            for h in range(4):
                PD("pe", lambda e, h=h, kkk=kkk, cols=cols: e.matmul(ps[:, kkk, h * 128:(h + 1) * 128], gkn[:, h, cols], gkn[:, h, cols], start=True, stop=True),
                     r=[MK("gkn", h)], w=[("ps", kkk)])
            for h in range(4):
                PQ("pe", lambda e, h=h, kqk=kqk, cols=cols: e.matmul(ps[:, kqk, h * 128:(h + 1) * 128], gkn[:, h, cols], gqn[:, h, cols], start=True, stop=True),
                     r=[MK("gkn", h), MK("gqn", h)], w=[("ps", kqk)])
            PD("dve", lambda e, kkk=kkk: e.tensor_tensor(tD, ps[:, kkk, :], tD, ALU.mult), r=[("ps", kkk), MK("tD")], w=[MK("tD")])
            PD("dve", lambda e, b=b: e.scalar_tensor_tensor(Xb[0][:, :, :], v4(tD), -1.0, bc(be[:, b, :], [128, 4, 128], 2), ALU.mult, ALU.mult),
                 r=[MK("tD"), MK("be")], w=[MK("X", 0)])
            PQ("dve", lambda e, kqk=kqk: e.tensor_tensor(qkT[:, :, :], v4(ps[:, kqk, :]), v4(tC), ALU.mult), r=[("ps", kqk), MK("tC")], w=[MK("qkT")])
            PQ("pool", lambda e, cols=cols: e.tensor_tensor(qdT[:, :, :], gqn[:, :, cols], v4(tE), ALU.mult), r=[MK("gqn", h) for h in range(4)] + [MK("tE")], w=[MK("qdT")])
            ky = nb()
            kyb = ps[:, ky, :].bitcast(BF16)
            for h in range(4):
                PD("pe", lambda e, h=h, ky=ky: e.transpose(ps[:, ky, h * 128:(h + 1) * 128], Xb[0][:, h, :], ident_f[:, :]), r=[MK("X", 0), ("identf",)], w=[("ps", ky)])
            PD("act", lambda e, ky=ky: e.activation(out=Yb[0][:, :, :], in_=v4(ps[:, ky, :]), func=AF.Identity), r=[("ps", ky)], w=[MK("Y", 0)])
            PD("dve", lambda e: e.tensor_tensor(Nf, Yb[0][:, :, :], bc(ident_f[:, :], [128, 4, 128], 1), ALU.add), r=[MK("Y", 0), ("identf",)], w=[MK("Nf")])
            cur = 0
            for st in range(1, 7):
                nx = 1 - cur
                kx, kyy = nb(), nb()
                for h in range(4):
                    PD("pe", lambda e, h=h, kx=kx, cur=cur: e.matmul(ps[:, kx, h * 128:(h + 1) * 128], Yb[cur][:, h, :], Xb[cur][:, h, :], start=True, stop=True),
                         r=[MK("X", cur), MK("Y", cur)], w=[("ps", kx)])
                for h in (range(4) if st < 6 else ()):
                    PD("pe", lambda e, h=h, kyy=kyy, cur=cur: e.matmul(ps[:, kyy, h * 128:(h + 1) * 128], Xb[cur][:, h, :], Yb[cur][:, h, :], start=True, stop=True),
                         r=[MK("X", cur), MK("Y", cur)], w=[("ps", kyy)])
                PD("act", lambda e, kx=kx, nx=nx: e.activation(out=Xb[nx][:, :, :], in_=v4(ps[:, kx, :]), func=AF.Identity), r=[("ps", kx)], w=[MK("X", nx)])
                if st < 6:
                    PD("dve", lambda e, kyy=kyy, nx=nx: e.tensor_copy(Yb[nx][:, :, :], v4(ps[:, kyy, :])), r=[("ps", kyy)], w=[MK("Y", nx)])
                kn2 = nb()
                for h in range(4):
                    PD("pe", lambda e, h=h, kn2=kn2, nx=nx: e.matmul(ps[:, kn2, h * 128:(h + 1) * 128], Xb[nx][:, h, :], Nf[:, h, :], start=True, stop=True),
                         r=[MK("X", nx), MK("Nf")], w=[("ps", kn2)])
                PD("dve", lambda e, kn2=kn2: e.tensor_tensor(Nf, Nf, v4(ps[:, kn2, :]), ALU.add), r=[("ps", kn2), MK("Nf")], w=[MK("Nf")])
                cur = nx
            PD("act", lambda e: e.activation(out=Nb[:, :, :], in_=Nf, func=AF.Identity), r=[MK("Nf")], w=[MK("N")])
            ku, kw = nb(), nb()
            for h in range(4):
                P.op("pe", lambda e, h=h, ku=ku: e.matmul(ps[:, ku, h * 128:(h + 1) * 128], Nb[:, h, :], vb[:, h, :], start=True, stop=True), r=[MK("N"), MK("vb")], w=[("ps", ku)])
            for h in range(4):
                P.op("pe", lambda e, h=h, kw=kw: e.matmul(ps[:, kw, h * 128:(h + 1) * 128], kb[:, h, :], Nb[:, h, :], start=True, stop=True), r=[MK("N"), MK("kb")], w=[("ps", kw)])
            P.op("act", lambda e, ku=ku: e.activation(out=u_t, in_=ps[:, ku, :], func=AF.Identity), r=[("ps", ku)], w=[MK("u")])
            P.op("act", lambda e, kw=kw: e.activation(out=wT[:, :, :], in_=v4(ps[:, kw, :]), func=AF.Identity), r=[("ps", kw)], w=[MK("wT")])
            if sample:
                sq = tix * 4 + b
                P.op("sp", lambda e, sq=sq: e.dma_start(out=Sst[:, :, :], in_=sgdn[sq].rearrange("h k v -> k h v")), w=[MK("S")], chan="sld")
                P.op("act", lambda e: e.activation(out=Sbf[:, :, :], in_=Sst[:, :, :], func=AF.Identity), r=[MK("S")], w=[MK("Sbf")])
            k1 = nb()
            for h in range(4):
                P.op("pe", lambda e, h=h, k1=k1: e.matmul(ps[:, k1, h * 128:(h + 1) * 128], wT[:, h, :], Sbf[:, h, :], start=True, stop=True), r=[MK("wT"), MK("Sbf")], w=[("ps", k1)])
            P.op("dve", lambda e, k1=k1: e.tensor_tensor(vnew[:, :, :], v4(u_t), v4(ps[:, k1, :]), ALU.subtract), r=[("ps", k1), MK("u")], w=[MK("vnew")])
            k3, k4 = nb(), nb()
            for h in range(4):
                PQ("pe", lambda e, h=h, k3=k3: e.matmul(ps[:, k3, h * 128:(h + 1) * 128], Sbf[:, h, :], qdT[:, h, :], start=True, stop=False), r=[MK("Sbf"), MK("qdT")], w=[("ps", k3)])
                PQ("pe", lambda e, h=h, k3=k3: e.matmul(ps[:, k3, h * 128:(h + 1) * 128], vnew[:, h, :], qkT[:, h, :], start=False, stop=True), r=[MK("vnew"), MK("qkT")], w=[("ps", k3)])
            for h in range(4):
                P.op("pe", lambda e, h=h, k4=k4: e.matmul(ps[:, k4, h * 128:(h + 1) * 128], kd[:, h, :], vnew[:, h, :], start=True, stop=True), r=[MK("kd"), MK("vnew")], w=[("ps", k4)])
            PQ("act", lambda e, k3=k3, cols=cols: e.activation(out=oT[:, :, cols], in_=v4(ps[:, k3, :]), func=AF.Identity), r=[("ps", k3)], w=[MK("oT", b), MK("ncrow")])
            for h in range(4):
                P.op("dve", lambda e, h=h, k4=k4: e.scalar_tensor_tensor(Sst[:, h, :], Sst[:, h, :], gl[:, h:h + 1], ps[:, k4, h * 128:(h + 1) * 128], ALU.mult, ALU.add),
                     r=[("ps", k4), MK("gl"), MK("S")], w=[MK("S")])
            if sample:
                P.op("sp", lambda e, sq=sq: e.dma_start(out=ngs[sq].rearrange("h k v -> k h v"), in_=Sst[:, :, :]), r=[MK("S")], w=[("o_ngs", sq)], chan="ongs")
            else:
                P.op("act", lambda e: e.activation(out=Sbf[:, :, :], in_=Sst[:, :, :], func=AF.Identity), r=[MK("S")], w=[MK("Sbf")])
                if last_tile and b == 3:
                    P.op("sp", lambda e: e.dma_start(out=ngp.rearrange("h k v -> k h v"), in_=Sst[:, :, :]), r=[MK("S")], w=[("o_ngp",)], chan="ongs")
        if KLIM < 9:
            return
        if prepass:
            return
        for h in range(4):
            P.op("pool", lambda e, h=h: e.tensor_tensor(ctmp[:, :], oT[:, h, :], oT[:, h, :], ALU.mult), r=[MK("oT", b) for b in range(4)], w=[MK("ctmp")])
            k = nb()
            P.op("pe", lambda e, k=k: e.matmul(ps[:, k, :], ones_f[:, :], ctmp[:, :], start=True, stop=True), r=[("ones_f",), MK("ctmp")], w=[("ps", k)])
            P.op("act", lambda e, k=k: e.activation(out=tA, in_=ps[:, k, :], func=AF.Ln, scale=1.0 / 128, bias=EPS), r=[("ps", k)], w=[MK("tA")])
            P.op("act", lambda e: e.activation(out=tA, in_=tA, func=AF.Exp, scale=-0.5), r=[MK("tA")], w=[MK("tA")])
            P.op("dve", lambda e, h=h: e.scalar_tensor_tensor(tA, oT[:, h, :], gng_t[:, 0:1], tA, ALU.mult, ALU.mult), r=[MK("oT", b) for b in range(4)] + [MK("tA"), ("gng",)], w=[MK("tA")])
            P.op("dve", lambda e, h=h: e.tensor_tensor(goT[:, h, :], tA, zs[:, h, :], ALU.mult), r=[MK("tA"), MK("zs", h)], w=[MK("goT", h)])
        if KLIM < 10:
            return
        if sample:
            for t_, src in ((aoTs, aoT), (goTs, goT)):
                P.op("pool", lambda e, t_=t_, src=src: e.tensor_copy(t_[:, :, tix * 4:(tix + 1) * 4], src[:, :, :].rearrange("p c (b t) -> p c b t", b=4)[:, :, :, 0]),
                     r=[MK("aoT", b) for b in range(4)] + [MK("goT", h) for h in range(4)], w=[("mixTs", tix)])
        else:
            for b in range(4):
                k0, k1 = nb(), nb()
                for c in range(8):
                    src = aoT if c < 4 else goT
                    for dh, kk in ((0, k0), (1, k1)):
                        P.op("pe", lambda e, c=c, dh=dh, kk=kk, b=b, src=src: e.matmul(ps[:, kk, :], src[:, c % 4, b * 128:(b + 1) * 128], wmo_t[:, c, dh * 512:(dh + 1) * 512], start=(c == 0), stop=(c == 7)),
                             r=[MK("aoT", b), MK("goT", c % 4), ("wmo",)], w=[("ps", kk)])
                for dh, kk in ((0, k0), (1, k1)):
                    P.op("dve", lambda e, dh=dh, kk=kk, b=b: e.tensor_tensor(xres(b, dh), xres(b, dh), ps[:, kk, :], ALU.add), r=[("ps", kk)], w=[xkeys[b]])

    def final_norm(xsrc, xkeys, nblk, np_, ydst, okey, chan):
        keyss = [("ss",)]
        for b in range(nblk):
            P.op("act", lambda e, b=b: e.activation(out=junk[0:np_, :], in_=xsrc(b), func=AF.Square, accum_out=ss[0:np_, b:b + 1]), r=[xkeys[b]], w=[MK("ctmp")] + keyss)
        rstd_cols(nblk, np_, 1.0 / D, keyss)
        for b in range(nblk):
            P.op("dve", lambda e, b=b: e.scalar_tensor_tensor(xsrc(b), xsrc(b), ss[0:np_, b:b + 1], gfin_b[0:np_, :], ALU.mult, ALU.mult), r=keyss + [xkeys[b], ("gfin",)], w=[xkeys[b]])
            P.op("sp", lambda e, b=b: e.dma_start(out=ydst(b), in_=xsrc(b)), r=[xkeys[b]], w=[(okey, b)], chan=chan)

    skeys = [("xs",)]
    P.op("sp", lambda e: e.dma_start(out=xts[:, 0, :], in_=xs), w=skeys, chan="xsld")
    xs_src = lambda b: xts[0:NS, 0, :]
    xs_dst = lambda b, dh: xts[0:NS, 0, dh * 512:(dh + 1) * 512]
    nTs_keys = [("nTs",)]
    ffn_prefetch(0)
    norm_T(xs_src, skeys, 1, NS, 0, nTs, nTs_keys)
    fence()
    ffn(0, nTs, nTs_keys, NS, xs_dst, skeys, 1, NS)
    xk = [("xnT", b) for b in range(4)]
    if stage >= 2:
        norm_T(xs_src, skeys, 1, NS, 1, nTs, nTs_keys)
        fence()
    for tix in range(4 if stage >= 2 else 0):
        P.op("pool", lambda e: e.memset(xnT[:, :, :], 0.0), w=xk)
        P.op("pool", lambda e, tix=tix: e.tensor_copy(xnT[:, :, :].rearrange("p c (b t) -> p c b t", b=4)[:, :, :, 0], nTs[:, :, tix * 4:(tix + 1) * 4]), r=nTs_keys, w=xk)
        mix(xnT, xk, True, tix, False, False, None, None)
    k0, k1 = nb(), nb()
    for c in range(8 if stage >= 2 else 0):
        src = aoTs if c < 4 else goTs
        for dh, kk in ((0, k0), (1, k1)):
            P.op("pe", lambda e, c=c, dh=dh, kk=kk, src=src: e.matmul(ps[0:NS, kk, :], src[:, c % 4, :], wmo_t[:, c, dh * 512:(dh + 1) * 512], start=(c == 0), stop=(c == 7)),
                 r=[("mixTs", t) for t in range(4)] + [("wmo",)], w=[("ps", kk)])
    for dh, kk in (((0, k0), (1, k1)) if stage >= 2 else ()):
        P.op("dve", lambda e, dh=dh, kk=kk: e.tensor_tensor(xs_dst(0, dh), xs_dst(0, dh), ps[0:NS, kk, :], ALU.add), r=[("ps", kk)], w=skeys)
    if stage >= 2:
        ffn_prefetch(1)
        norm_T(xs_src, skeys, 1, NS, 2, nTs, nTs_keys)
        fence()
        ffn(1, nTs, nTs_keys, NS, xs_dst, skeys, 1, NS)
    final_norm(xs_src, skeys, 1, NS, lambda b: ys, "o_ys", "oys")

    tiles = [("pre", i) for i in range(n_pre)] + [("main", i) for i in range(n_tiles)]

    def load_x(gi):
        kind, i = tiles[gi]
        src = xpre if kind == "pre" else xp
        X = xt[gi % 2]
        keys = [("x", gi % 2, b) for b in range(4)]
        if gi == 1:
            keys = keys + [("xs",), ("sct",), MK("ckt"), MK("cvt"), MK("ckd", 0), MK("ckd", 1)]
        P.op("sp", lambda e: [e.dma_start(out=X[:, b, :], in_=src[i * TT + b * 128: i * TT + (b + 1) * 128, :]) for b in range(4)],
             w=keys, chan=f"x{gi % 2}", nd=4)

    if stage >= 3:
        load_x(0)
    for gi, (kind, i) in enumerate(tiles if stage >= 3 else []):
        X = xt[gi % 2]
        xkeys = [("x", gi % 2, b) for b in range(4)]
        xsrc = lambda b, X=X: X[:, b, :]
        xdst = lambda b, dh, X=X: X[:, b, dh * 512:(dh + 1) * 512]
        ffn_prefetch(0)
        norm_T(xsrc, xkeys, 4, 128, 0, xnT, xk)
        fence()
        ffn(0, xnT, xk, TT, xdst, xkeys, 4, 128)
        norm_T(xsrc, xkeys, 4, 128, 1, xnT, xk)
        fence()
        if kind == "pre":
            mix(xnT, xk, False, 0, i == 0, False, xdst, xkeys, prepass=True, pre_last=(i == n_pre - 1))
            if gi + 1 < len(tiles):
                load_x(gi + 1)
            continue
        mix(xnT, xk, False, 0, (n_pre == 0 and i == 0), i == n_tiles - 1, xdst, xkeys, mask_prev=(n_pre > 0 and i == 0))
        if gi + 1 < len(tiles):
            load_x(gi + 1)
        ffn_prefetch(1)
        norm_T(xsrc, xkeys, 4, 128, 2, xnT, xk)
        fence()
        ffn(1, xnT, xk, TT, xdst, xkeys, 4, 128)
        final_norm(xsrc, xkeys, 4, 128, lambda b, i=i: yp[i * TT + b * 128: i * TT + (b + 1) * 128, :], ("o_yp", i), f"oy{gi % 2}")

    with ExitStack() as st:
        sems = {}
        for en in Prog.ENG:
            sems[("e", en)] = st.enter_context(nc.semaphore("e_" + en))
        for ch in P.chan_cnt:
            sems[("c", ch)] = st.enter_context(nc.semaphore("c_" + ch))
        P.run(sems)
    return nc


_NC_CACHE = {}


def run_cores(xp_list, per_core, shared, n_tiles, stage=3, xpre_list=None, hasprev=None):
    n_pre = 0 if xpre_list is None else xpre_list[0].shape[0] // TT
    key = (n_tiles, n_pre, stage)
    if key not in _NC_CACHE:
        _NC_CACHE[key] = build(n_tiles, stage, n_pre)
    nc = _NC_CACHE[key]
    consts = host_consts()
    in_maps = []
    for c in range(len(xp_list)):
        m = dict(shared)
        m.update(per_core[c])
        m["xp"] = xp_list[c]
        if n_pre:
            m["xpre"] = xpre_list[c]
        m["hasprev"] = np.full((128, 1), 0.0 if hasprev is None else hasprev[c], np.float32)
        for k, v in consts.items():
            m["c_" + k] = v
        in_maps.append({k: np.ascontiguousarray(v, dtype=np.float32) for k, v in m.items()})
    res = run_bass_kernel_spmd(nc, in_maps, core_ids=list(range(len(xp_list))))
    return res.results


def kernel(x_prompt, x_sample, cache_attn_k, cache_attn_v, state_conv, state_gdn,
           ffn1_norm_g, ffn1_w_in, ffn1_w_out, mix_norm_g, w_in_mix, attn_sinks, conv_w,
           gdn_A_log, gdn_dt_bias, gdn_norm_g, w_out_mix, ffn2_norm_g, ffn2_w_in, ffn2_w_out,
           final_norm_g):
    f = lambda a: np.asarray(a, dtype=np.float32)
    x_prompt = f(x_prompt)
    B, S, _ = x_prompt.shape
    ncore = 8
    HALF = S // 2
    n_tiles = HALF // TT
    shared = dict(g1=f(ffn1_norm_g)[0], wi1=f(ffn1_w_in)[0], wo1=f(ffn1_w_out)[0], g2=f(mix_norm_g)[0],
                  wmi=f(w_in_mix)[0], sinks=f(attn_sinks)[0], convw=f(conv_w)[0], alog=f(gdn_A_log)[0],
                  dtb=f(gdn_dt_bias)[0], gng=f(gdn_norm_g)[0], wmo=f(w_out_mix)[0], g3=f(ffn2_norm_g)[0],
                  wi2=f(ffn2_w_in)[0], wo2=f(ffn2_w_out)[0], gfin=f(final_norm_g))
    xs_ = f(x_sample)[:, 0, :]
    ck_ = f(cache_attn_k)[0].reshape(-1, 128, 128)
    cv_ = f(cache_attn_v)[0].reshape(-1, 128, 128)
    sc_ = f(state_conv)[0]
    sg_ = f(state_gdn)[0]
    per_core, xp_list, xpre_list, hasprev = [], [], [], []
    zeros = np.zeros((HALF, D), np.float32)
    for c in range(ncore):
        sl = slice(c * NS, (c + 1) * NS)
        per_core.append(dict(xs=xs_[sl], ck=ck_[sl], cv=cv_[sl], sconv=sc_[sl].reshape(NS * 3, 1536), sgdn=sg_[sl]))
        sq, half = c // 2, c % 2
        xp_list.append(x_prompt[sq, half * HALF:(half + 1) * HALF])
        xpre_list.append(x_prompt[sq, 0:HALF] if half else zeros)
        hasprev.append(float(half))
    res = run_cores(xp_list, per_core, shared, n_tiles, 3, xpre_list, hasprev)
    y_prompt = np.stack([np.concatenate([res[2 * b]["yp"], res[2 * b + 1]["yp"]]) for b in range(B)])
    cat = lambda k: np.stack([res[2 * b + 1][k] for b in range(B)])
    y_sample = np.concatenate([res[c]["ys"] for c in range(ncore)])[:, None, :]
    nkp = cat("nkp").reshape(1, B, 128, 2, 64)
    nvp = cat("nvp").reshape(1, B, 128, 2, 64)
    ncp = cat("ncp")[None]
    ngp = cat("ngp")[None]
    nks = np.concatenate([res[c]["nks"] for c in range(ncore)]).reshape(1, ncore * NS, 128, 2, 64)
    nvs = np.concatenate([res[c]["nvs"] for c in range(ncore)]).reshape(1, ncore * NS, 128, 2, 64)
    ncs = np.concatenate([res[c]["ncs"] for c in range(ncore)])[None]
    ngs = np.concatenate([res[c]["ngs"] for c in range(ncore)])[None]
    return (y_prompt, y_sample, nkp, nvp, ncp, ngp, nks, nvs, ncs, ngs)
```
